# Optimizing a Trainium2 kernel written in Bass

```python
import jax
import jax.numpy as jnp
from jax import lax
import numpy as np

D_MODEL = 1024
BATCH = 4
SEQ = 4096
DEPTH = 2

MEM_LEN = 256
N_MIXERS = 4
GROUP_WIDTH = D_MODEL // N_MIXERS
HEAD_DIM = 64
GROUP_HEADS = GROUP_WIDTH // HEAD_DIM
Q_BLOCK = 128
RET_CHUNK = 128
RET_THETA = 10000.0
CONV_WIDTH = 4
LRU_C = 8.0
RWKV_W_RANK = 64
RWKV_A_RANK = 64
RWKV_V_RANK = 32
RWKV_G_RANK = 128
RWKV_GN_EPS = 64e-5
XATTN_HEADS = 4
XATTN_HEAD_DIM = D_MODEL // XATTN_HEADS
D_FF = 4 * D_MODEL
NORM_EPS = 1e-6

FOX_WIDTH = 3 * GROUP_WIDTH + GROUP_HEADS
LRU_WIDTH = 2 * GROUP_WIDTH
RWKV_WIDTH = 3 * GROUP_WIDTH + RWKV_W_RANK + RWKV_A_RANK + RWKV_G_RANK
RET_WIDTH = 4 * GROUP_WIDTH
FOX_OFF = 0
LRU_OFF = FOX_OFF + FOX_WIDTH
RWKV_OFF = LRU_OFF + LRU_WIDTH
RET_OFF = RWKV_OFF + RWKV_WIDTH
IN_WIDTH = RET_OFF + RET_WIDTH

kernel_name = 'hybrid_parallel_heads_decoder'

F32 = jnp.float32


def rms_norm(x, g):
    xf = x.astype(F32)
    y = xf * lax.rsqrt(jnp.mean(xf * xf, axis=-1, keepdims=True) + NORM_EPS)
    return (y * g.astype(F32)).astype(x.dtype)


def split_heads(t):
    return t.reshape(t.shape[0], t.shape[1], GROUP_HEADS, HEAD_DIM)


def head_layer_norm(y, w, b, eps):
    mu = jnp.mean(y, axis=-1, keepdims=True)
    var = jnp.mean(jnp.square(y - mu), axis=-1, keepdims=True)
    y = ((y - mu) * lax.rsqrt(var + eps)).reshape(y.shape[0], y.shape[1], -1)
    return y * w.astype(F32) + b.astype(F32)


def head_rms_norm(y, w):
    y = (y * lax.rsqrt(jnp.mean(y * y, axis=-1, keepdims=True) + NORM_EPS)).reshape(y.shape[0], y.shape[1], -1)
    return y * w.astype(F32)


def fox_mixer(q, k, v, f_logit, f_bias):
    B, S, _ = q.shape
    q = split_heads(q).transpose(0, 2, 1, 3)
    k = split_heads(k).transpose(0, 2, 1, 3)
    v = split_heads(v).transpose(0, 2, 1, 3)
    log_f = jax.nn.log_sigmoid(f_logit.astype(F32) + f_bias.astype(F32))
    cum = jnp.cumsum(log_f, axis=1).transpose(0, 2, 1)
    nb = S // Q_BLOCK
    q_blocks = q.reshape(B, GROUP_HEADS, nb, Q_BLOCK, HEAD_DIM).transpose(2, 0, 1, 3, 4)
    c_blocks = cum.reshape(B, GROUP_HEADS, nb, Q_BLOCK).transpose(2, 0, 1, 3)
    q_pos = jnp.arange(S).reshape(nb, Q_BLOCK)
    k_pos = jnp.arange(S)
    scale = HEAD_DIM ** -0.5

    def block(args):
        q_i, c_i, p_i = args
        s = jnp.einsum('bhqd,bhkd->bhqk', q_i, k).astype(F32) * scale
        s = s + c_i[..., :, None] - cum[..., None, :]
        s = jnp.where(k_pos[None, :] <= p_i[:, None], s, -jnp.inf)
        p = jax.nn.softmax(s, axis=-1)
        return jnp.einsum('bhqk,bhkd->bhqd', p.astype(v.dtype), v)

    o = lax.map(block, (q_blocks, c_blocks, q_pos))
    return o.transpose(1, 0, 3, 2, 4).reshape(B, S, GROUP_WIDTH)


def rglru_mixer(xb, yb, conv_w, conv_b, ra_w, ra_b, ri_w, ri_b, lam):
    B, S, W = xb.shape
    xp = jnp.pad(xb, ((0, 0), (CONV_WIDTH - 1, 0), (0, 0)))
    xc = conv_b + xp[:, 0:S] * conv_w[0]
    for j in range(1, CONV_WIDTH):
        xc = xc + xp[:, j:j + S] * conv_w[j]
    xh = split_heads(xc)
    r = jax.nn.sigmoid(jnp.einsum('bshi,hij->bshj', xh, ra_w).reshape(B, S, W) + ra_b)
    i = jax.nn.sigmoid(jnp.einsum('bshi,hij->bshj', xh, ri_w).reshape(B, S, W) + ri_b)
    log_a = LRU_C * r.astype(F32) * jax.nn.log_sigmoid(lam.astype(F32))
    a = jnp.exp(log_a)
    mult = jnp.sqrt(jnp.maximum(-jnp.expm1(2.0 * log_a), 0.0))
    u = mult * (i * xc).astype(F32)

    def combine(left, right):
        a1, b1 = left
        a2, b2 = right
        return a1 * a2, a2 * b1 + b2

    _, h = lax.associative_scan(combine, (a, u), axis=1)
    return (h * jax.nn.gelu(yb.astype(F32))).astype(xb.dtype)


def rwkv7_mixer(slab, v_first, mu, w0, w2, a0, a2, g2, k_k, k_a, r_k, gn_w, gn_b, vres):
    out_dtype = slab.dtype
    s = slab.astype(F32)
    B, S, _ = s.shape
    shifted = jnp.pad(s, ((0, 0), (1, 0), (0, 0)))[:, :-1]
    s = s + (shifted - s) * mu.astype(F32)
    G = GROUP_WIDTH
    r = s[..., 0:G]
    k = s[..., G:2 * G]
    v = s[..., 2 * G:3 * G]
    o = 3 * G
    wd = s[..., o:o + RWKV_W_RANK]
    o = o + RWKV_W_RANK
    ad = s[..., o:o + RWKV_A_RANK]
    o = o + RWKV_A_RANK
    gd = s[..., o:o + RWKV_G_RANK]
    w_log = -jax.nn.softplus(-(w0.astype(F32) + jnp.tanh(wd) @ w2.astype(F32))) - 0.5
    decay = jnp.exp(-jnp.exp(w_log))
    a = jax.nn.sigmoid(a0.astype(F32) + ad @ a2.astype(F32))
    g = jax.nn.sigmoid(gd) @ g2.astype(F32)
    if vres is not None:
        v0, v1, v2 = vres
        v = v + (v_first - v) * jax.nn.sigmoid(v0.astype(F32) + (v @ v1.astype(F32)) @ v2.astype(F32))
    v_out = v
    kk = split_heads(k * k_k.astype(F32))
    kk = kk / jnp.maximum(jnp.sqrt(jnp.sum(kk * kk, axis=-1, keepdims=True)), 1e-12)
    k = k * (1.0 + (a - 1.0) * k_a.astype(F32))
    rh, wh, kh, vh, ah = (split_heads(t) for t in (r, decay, k, v, a))
    a_vec = -kk
    b_vec = kk * ah

    def step(state, inp):
        r_t, w_t, k_t, v_t, a_t, b_t = inp
        sa = jnp.einsum('bhij,bhj->bhi', state, a_t)
        state = state * w_t[:, :, None, :] + sa[..., None] * b_t[:, :, None, :] + v_t[..., None] * k_t[:, :, None, :]
        return state, jnp.einsum('bhij,bhj->bhi', state, r_t)

    xs = tuple(t.transpose(1, 0, 2, 3) for t in (rh, wh, kh, vh, a_vec, b_vec))
    state0 = jnp.zeros((B, GROUP_HEADS, HEAD_DIM, HEAD_DIM), F32)
    _, y = lax.scan(step, state0, xs)
    y = head_layer_norm(y.transpose(1, 0, 2, 3), gn_w, gn_b, RWKV_GN_EPS)
    bonus = jnp.sum(rh * kh * r_k.astype(F32), axis=-1, keepdims=True) * vh
    y = y + bonus.reshape(B, S, G)
    return (y * g).astype(out_dtype), v_out


def rotary(t):
    half = HEAD_DIM // 2
    inv = 1.0 / (RET_THETA ** jnp.linspace(0.0, 1.0, half, dtype=F32))
    ang = jnp.arange(t.shape[2], dtype=F32)[:, None] * inv[None, :]
    cos, sin = jnp.cos(ang), jnp.sin(ang)
    t1, t2 = t[..., :half], t[..., half:]
    return jnp.concatenate([t1 * cos - t2 * sin, t1 * sin + t2 * cos], axis=-1)


def retention_mixer(q, k, v, g, gn_w):
    B, S, _ = q.shape
    H, C = GROUP_HEADS, RET_CHUNK
    nc = S // C
    qh = rotary(split_heads(q).astype(F32).transpose(0, 2, 1, 3))
    kh = rotary(split_heads(k).astype(F32).transpose(0, 2, 1, 3)) * (HEAD_DIM ** -0.5)
    vh = split_heads(v).astype(F32).transpose(0, 2, 1, 3)
    qc = qh.reshape(B, H, nc, C, HEAD_DIM)
    kc = kh.reshape(B, H, nc, C, HEAD_DIM)
    vc = vh.reshape(B, H, nc, C, HEAD_DIM)
    lg = jnp.log(1.0 - 2.0 ** (-5.0 - jnp.arange(H, dtype=F32)))
    n = jnp.arange(C, dtype=F32)
    diff = n[:, None] - n[None, :]
    dmat = jnp.where(diff >= 0, jnp.exp(lg[:, None, None] * jnp.maximum(diff, 0.0)), 0.0)
    inner = jnp.einsum('bhcnd,bhcmd->bhcnm', qc, kc) * dmat[None, :, None]
    intra = jnp.einsum('bhcnm,bhcme->bhcne', inner, vc)
    zeta = jnp.exp(lg[:, None] * (C - 1.0 - n)[None, :])
    kv = jnp.einsum('bhcmd,bhcme->bhcde', kc * zeta[None, :, None, :, None], vc)
    chunk_decay = jnp.exp(lg * C)[None, :, None, None]

    def step(R, kv_c):
        return R * chunk_decay + kv_c, R

    _, r_prev = lax.scan(step, jnp.zeros((B, H, HEAD_DIM, HEAD_DIM), F32), kv.transpose(2, 0, 1, 3, 4))
    xi = jnp.exp(lg[:, None] * (n + 1.0)[None, :])
    cross = jnp.einsum('bhcnd,cbhde->bhcne', qc * xi[None, :, None, :, None], r_prev)
    o = (intra + cross).transpose(0, 2, 3, 1, 4).reshape(B, S, H, HEAD_DIM)
    o = head_rms_norm(o, gn_w)
    return (o * jax.nn.silu(g.astype(F32))).astype(q.dtype)


def cross_attention(xn, memn, wq, wk, wv, wo):
    B, S, _ = xn.shape
    M = memn.shape[1]
    q = (xn @ wq).reshape(B, S, XATTN_HEADS, XATTN_HEAD_DIM)
    k = (memn @ wk).reshape(B, M, XATTN_HEADS, XATTN_HEAD_DIM)
    v = (memn @ wv).reshape(B, M, XATTN_HEADS, XATTN_HEAD_DIM)
    s = jnp.einsum('bshd,bmhd->bhsm', q, k).astype(F32) * (XATTN_HEAD_DIM ** -0.5)
    p = jax.nn.softmax(s, axis=-1)
    o = jnp.einsum('bhsm,bmhd->bshd', p.astype(v.dtype), v).reshape(B, S, D_MODEL)
    return o @ wo


def setup_inputs(seed: int = 0) -> dict:
    key = jax.random.key(seed)
    ks = iter(jax.random.split(key, 64))
    nrm = lambda shape, scale: jax.random.normal(next(ks), shape, F32) * scale
    gain = lambda shape: 1.0 + nrm(shape, 0.05)
    L, D, G, H, N = DEPTH, D_MODEL, GROUP_WIDTH, GROUP_HEADS, HEAD_DIM
    a8 = jax.random.uniform(next(ks), (L, G), F32, minval=0.9, maxval=0.999)
    a_lru = a8 ** (1.0 / LRU_C)
    return {
        'x': nrm((BATCH, SEQ, D), 1.0),
        'mem': nrm((BATCH, MEM_LEN, D), 1.0),
        'norm_mix_pre': gain((L, D)),
        'norm_mix_post': gain((L, D)),
        'norm_xa_pre': gain((L, D)),
        'norm_xa_post': gain((L, D)),
        'norm_mem': gain((L, D)),
        'norm_mlp_pre': gain((L, D)),
        'norm_mlp_post': gain((L, D)),
        'w_in': nrm((L, D, IN_WIDTH), D ** -0.5),
        'w_out': nrm((L, D, D), D ** -0.5),
        'fox_f_bias': 2.0 + nrm((L, H), 0.1),
        'lru_conv_w': nrm((L, CONV_WIDTH, G), CONV_WIDTH ** -0.5),
        'lru_conv_b': nrm((L, G), 0.01),
        'lru_ra_w': nrm((L, H, N, N), N ** -0.5),
        'lru_ra_b': nrm((L, G), 0.01),
        'lru_ri_w': nrm((L, H, N, N), N ** -0.5),
        'lru_ri_b': nrm((L, G), 0.01),
        'lru_lambda': jnp.log(a_lru) - jnp.log1p(-a_lru),
        'rwkv_mu': jax.random.uniform(next(ks), (L, RWKV_WIDTH), F32),
        'rwkv_w0': jnp.linspace(-6.5, -1.5, G, dtype=F32)[None, :] + nrm((L, G), 0.3),
        'rwkv_w2': nrm((L, RWKV_W_RANK, G), RWKV_W_RANK ** -0.5),
        'rwkv_a0': nrm((L, G), 0.1),
        'rwkv_a2': nrm((L, RWKV_A_RANK, G), RWKV_A_RANK ** -0.5),
        'rwkv_g2': nrm((L, RWKV_G_RANK, G), RWKV_G_RANK ** -0.5),
        'rwkv_k_k': 0.85 + nrm((L, G), 0.05),
        'rwkv_k_a': gain((L, G)),
        'rwkv_r_k': nrm((L, H, N), 0.1),
        'rwkv_gn_w': gain((L, G)),
        'rwkv_gn_b': nrm((L, G), 0.01),
        'rwkv_v0': nrm((L - 1, G), 0.1),
        'rwkv_v1': nrm((L - 1, G, RWKV_V_RANK), G ** -0.5),
        'rwkv_v2': nrm((L - 1, RWKV_V_RANK, G), RWKV_V_RANK ** -0.5),
        'ret_gn_w': gain((L, G)),
        'xa_wq': nrm((L, D, D), D ** -0.5),
        'xa_wk': nrm((L, D, D), D ** -0.5),
        'xa_wv': nrm((L, D, D), D ** -0.5),
        'xa_wo': nrm((L, D, D), D ** -0.5),
        'mlp_w1': nrm((L, D, D_FF), D ** -0.5),
        'mlp_w2': nrm((L, D_FF, D), D_FF ** -0.5),
    }


def reference(x, mem, norm_mix_pre, norm_mix_post, norm_xa_pre, norm_xa_post, norm_mem, norm_mlp_pre, norm_mlp_post,
              w_in, w_out, fox_f_bias, lru_conv_w, lru_conv_b, lru_ra_w, lru_ra_b, lru_ri_w, lru_ri_b, lru_lambda,
              rwkv_mu, rwkv_w0, rwkv_w2, rwkv_a0, rwkv_a2, rwkv_g2, rwkv_k_k, rwkv_k_a, rwkv_r_k, rwkv_gn_w, rwkv_gn_b,
              rwkv_v0, rwkv_v1, rwkv_v2, ret_gn_w, xa_wq, xa_wk, xa_wv, xa_wo, mlp_w1, mlp_w2):
    G = GROUP_WIDTH
    v_first = None
    for l in range(DEPTH):
        h = rms_norm(x, norm_mix_pre[l])
        proj = h @ w_in[l]
        fox_out = fox_mixer(proj[..., FOX_OFF:FOX_OFF + G], proj[..., FOX_OFF + G:FOX_OFF + 2 * G],
                            proj[..., FOX_OFF + 2 * G:FOX_OFF + 3 * G], proj[..., FOX_OFF + 3 * G:FOX_OFF + FOX_WIDTH],
                            fox_f_bias[l])
        lru_out = rglru_mixer(proj[..., LRU_OFF:LRU_OFF + G], proj[..., LRU_OFF + G:LRU_OFF + 2 * G],
                              lru_conv_w[l], lru_conv_b[l], lru_ra_w[l], lru_ra_b[l], lru_ri_w[l], lru_ri_b[l],
                              lru_lambda[l])
        vres = None if l == 0 else (rwkv_v0[l - 1], rwkv_v1[l - 1], rwkv_v2[l - 1])
        rwkv_out, v_l = rwkv7_mixer(proj[..., RWKV_OFF:RWKV_OFF + RWKV_WIDTH], v_first, rwkv_mu[l], rwkv_w0[l],
                                    rwkv_w2[l], rwkv_a0[l], rwkv_a2[l], rwkv_g2[l], rwkv_k_k[l], rwkv_k_a[l],
                                    rwkv_r_k[l], rwkv_gn_w[l], rwkv_gn_b[l], vres)
        if l == 0:
            v_first = v_l
        ret_out = retention_mixer(proj[..., RET_OFF:RET_OFF + G], proj[..., RET_OFF + G:RET_OFF + 2 * G],
                                  proj[..., RET_OFF + 2 * G:RET_OFF + 3 * G], proj[..., RET_OFF + 3 * G:RET_OFF + 4 * G],
                                  ret_gn_w[l])
        mixed = jnp.concatenate([fox_out, lru_out, rwkv_out, ret_out], axis=-1) @ w_out[l]
        x = x + rms_norm(mixed, norm_mix_post[l])
        xa = cross_attention(rms_norm(x, norm_xa_pre[l]), rms_norm(mem, norm_mem[l]),
                             xa_wq[l], xa_wk[l], xa_wv[l], xa_wo[l])
        x = x + rms_norm(xa, norm_xa_post[l])
        hid = jnp.square(jax.nn.relu(rms_norm(x, norm_mlp_pre[l]) @ mlp_w1[l]))
        x = x + rms_norm(hid @ mlp_w2[l], norm_mlp_post[l])
    return x
```

```python
import contextlib, math
import numpy as np
import concourse.bass as bass
import concourse.mybir as mybir
from concourse.bass_utils import run_bass_kernel_spmd


F32 = mybir.dt.float32
BF16 = mybir.dt.bfloat16
I32 = mybir.dt.int32
AF = mybir.ActivationFunctionType
ALU = mybir.AluOpType


class Reg:
    __slots__ = ("last_w", "reads")

    def __init__(self):
        self.last_w = None
        self.reads = []


class T:
    def __init__(self, ap, name):
        self.ap = ap
        self.name = name
        self.regs = {None: Reg()}

    def __getitem__(self, idx):
        return self.ap[idx]


class K:
    ENGS = ("pe", "dve", "act", "pool", "sp")

    def __init__(self, nc, stack, n_dma_sems=6):
        self.nc = nc
        self.stack = stack
        self.eng = {"pe": nc.tensor, "dve": nc.vector, "act": nc.scalar, "pool": nc.gpsimd, "sp": nc.sync}
        self.sem = {}
        self.tick = {}
        for e in self.ENGS:
            self.sem[e] = stack.enter_context(nc.semaphore("s_" + e))
            self.tick[e] = 0
        self.dq = {}
        for q in ("sp", "pool", "act"):
            lst = []
            for i in range(n_dma_sems):
                key = "d_%s%d" % (q, i)
                self.sem[key] = stack.enter_context(nc.semaphore(key))
                self.tick[key] = 0
                lst.append(key)
            self.dq[q] = [lst, 0]
        self.sem["cc"] = stack.enter_context(nc.semaphore("s_cc"))
        self.tick["cc"] = 0
        self.waited = {e: {} for e in self.ENGS}
        self.ninst = {e: 0 for e in self.ENGS}
        self.nwait = 0

    @contextlib.contextmanager
    def scope(self):
        old = self.stack
        with contextlib.ExitStack() as st:
            self.stack = st
            try:
                yield
            finally:
                self.barrier()
                self.stack = old

    def barrier(self):
        deps = [(sk, v) for sk, v in self.tick.items() if v > 0]
        for e in self.ENGS:
            self._wait(e, [d for d in deps if d[0] != e])

    def cc_allgather(self, in_t, out_t, groups):
        reads = self._norm([in_t])
        writes = self._norm([out_t])
        self._wait("pool", self._deps(reads, writes))
        inst = self.eng["pool"].collective_compute("AllGather", ALU.bypass, replica_groups=groups, ins=[in_t.ap], outs=[out_t.ap])
        self.tick["cc"] += 1
        inst.then_inc(self.sem["cc"], 1)
        self.ninst["pool"] += 1
        self._record(("cc", self.tick["cc"]), reads, writes)
        return inst

    _uid = 0

    def sb(self, name, shape, dtype=F32):
        K._uid += 1
        name = "%s_u%d" % (name, K._uid)
        return T(self.stack.enter_context(self.nc.sbuf_tensor(name, list(shape), dtype)), name)

    def ps(self, name, shape, dtype=F32):
        return T(self.stack.enter_context(self.nc.psum_tensor(name, list(shape), dtype)), name)

    def dram(self, name, shape, dtype=F32, kind="Internal"):
        return T(self.nc.dram_tensor(name, list(shape), dtype, kind=kind).ap(), name)

    @staticmethod
    def _norm(lst):
        out = []
        for x in lst:
            if x is None:
                continue
            if isinstance(x, T):
                out.append((x, None))
            else:
                out.append(x)
        return out

    def _deps(self, reads, writes):
        deps = []
        for (t, k) in reads:
            if k is None:
                for r in t.regs.values():
                    if r.last_w:
                        deps.append(r.last_w)
            else:
                r = t.regs.get(k)
                if r is not None and r.last_w:
                    deps.append(r.last_w)
                if t.regs[None].last_w:
                    deps.append(t.regs[None].last_w)
        for (t, k) in writes:
            if k is None:
                rs = list(t.regs.values())
            else:
                rs = [t.regs[None]]
                if k in t.regs:
                    rs.append(t.regs[k])
            for r in rs:
                if r.last_w:
                    deps.append(r.last_w)
                deps.extend(r.reads)
        return deps

    def _record(self, me, reads, writes):
        for (t, k) in reads:
            t.regs.setdefault(k, Reg()).reads.append(me)
        for (t, k) in writes:
            if k is None:
                for kk in list(t.regs.keys()):
                    if kk is not None:
                        del t.regs[kk]
                r = t.regs[None]
            else:
                r = t.regs.setdefault(k, Reg())
            r.last_w = me
            r.reads = []

    def _wait(self, e, deps):
        best = {}
        for (sk, v) in deps:
            if sk == e and e == "pe":
                continue
            if best.get(sk, 0) < v:
                best[sk] = v
        w = self.waited[e]
        for sk, v in best.items():
            if w.get(sk, 0) < v:
                self.eng[e].wait_ge(self.sem[sk], v)
                w[sk] = v
                self.nwait += 1

    def op(self, e, fn, reads=(), writes=()):
        reads = self._norm(reads)
        writes = self._norm(writes)
        self._wait(e, self._deps(reads, writes))
        inst = fn(self.eng[e])
        self.tick[e] += 1
        inst.then_inc(self.sem[e], 1)
        self.ninst[e] += 1
        self._record((e, self.tick[e]), reads, writes)
        return inst

    def dma(self, q, out, in_, reads=(), writes=(), **kw):
        reads = self._norm(reads)
        writes = self._norm(writes)
        lst, i = self.dq[q]
        sk = lst[i % len(lst)]
        self.dq[q][1] = i + 1
        deps = self._deps(reads, writes)
        if self.tick[sk] > 0:
            deps.append((sk, self.tick[sk]))
        self._wait(q, deps)
        inst = self.eng[q].dma_start(out=out, in_=in_, **kw)
        self.tick[sk] += 16
        inst.then_inc(self.sem[sk], 16)
        self.ninst[q] += 1
        self._record((sk, self.tick[sk]), reads, writes)
        return inst

    def finish(self, e, tiles):
        deps = []
        for t in tiles:
            for r in t.regs.values():
                if r.last_w:
                    deps.append(r.last_w)
        self._wait(e, deps)

    def mm(self, out, lhsT, rhs, start=True, stop=True, reads=(), writes=()):
        return self.op("pe", lambda e: e.matmul(out, lhsT, rhs, start=start, stop=stop), reads, writes)

    def tr(self, out, in_, ident, reads=(), writes=()):
        return self.op("pe", lambda e: e.transpose(out, in_, ident), reads, writes)

    def act(self, out, in_, func, reads=(), writes=(), bias=None, scale=None, e="act"):
        kw = {}
        if bias is not None:
            kw["bias"] = bias
        if scale is not None:
            kw["scale"] = scale
        return self.op(e, lambda g: g.activation(out, in_, func, **kw), reads, writes)

    def tt(self, out, in0, in1, op, reads=(), writes=(), e="dve"):
        return self.op(e, lambda g: g.tensor_tensor(out, in0, in1, op), reads, writes)

    def ts(self, out, in0, s1, op0, s2=None, op1=None, reads=(), writes=(), e="dve"):
        if op1 is None:
            return self.op(e, lambda g: g.tensor_scalar(out, in0, s1, None, op0), reads, writes)
        return self.op(e, lambda g: g.tensor_scalar(out, in0, s1, s2, op0, op1), reads, writes)

    def stt(self, out, in0, scalar, in1, op0, op1, reads=(), writes=()):
        return self.op("dve", lambda g: g.scalar_tensor_tensor(out, in0, scalar, in1, op0, op1), reads, writes)

    def cp(self, out, in_, reads=(), writes=(), e="dve"):
        if e == "act":
            return self.op(e, lambda g: g.copy(out, in_), reads, writes)
        return self.op(e, lambda g: g.tensor_copy(out, in_), reads, writes)

    def memset(self, out, val, writes=(), e="pool"):
        return self.op(e, lambda g: g.memset(out, val), (), writes)


TT = 512
EPS = 1e-6


class Rot:
    def __init__(self, k, name, shape, dtype, n):
        self.t = [k.sb("%s%d" % (name, i), shape, dtype) for i in range(n)]
        self.i = 0

    def next(self):
        t = self.t[self.i % len(self.t)]
        self.i += 1
        return t


class Ctx:
    def __init__(self, k):
        self.k = k
        self.ones = k.sb("ones_bf", [128, 128], BF16)
        k.memset(self.ones[:], 1.0, writes=[self.ones])
        self.eps = k.sb("eps_c", [128, 1], F32)
        k.memset(self.eps[:], EPS, writes=[self.eps])
        self.psums = [k.ps("ps%d" % i, [128, 512], F32) for i in range(8)]
        self.pi = 0
        self.sq = Rot(k, "sq", [128, TT], BF16, 3)
        self.rstd = Rot(k, "rstd", [128, TT], F32, 2)
        self.tmp = Rot(k, "tmpf", [128, TT], F32, 3)

    def psum(self):
        p = self.psums[self.pi % 8]
        self.pi += 1
        return p


def rstd_from(c, srcs, n=TT, nfeat=1024.0):
    k = c.k
    pst = c.psum()
    nk = len(srcs)
    for i, (ap, t) in enumerate(srcs):
        sq = c.sq.next()
        k.act(sq[:, 0:n], ap, AF.Square, reads=[t], writes=[sq])
        k.mm(pst[:, 0:n], c.ones[:], sq[:, 0:n], start=(i == 0), stop=(i == nk - 1), reads=[c.ones, sq], writes=[pst])
    r = c.rstd.next()
    k.act(r[:, 0:n], pst[:, 0:n], AF.Sqrt, reads=[pst, c.eps], writes=[r], bias=c.eps[:], scale=1.0 / nfeat)
    k.op("dve", lambda g: g.reciprocal(r[:, 0:n], r[:, 0:n]), reads=[r], writes=[r])
    return r


def pre_norm(c, xt, gain, gi, out, n=TT):
    k = c.k
    r = rstd_from(c, [(xt[:, kc, 0:n], xt) for kc in range(8)], n)
    for kc in range(8):
        k.stt(out[:, kc, 0:n], xt[:, kc, 0:n], gain[:, gi, kc:kc + 1], r[:, 0:n], ALU.mult, ALU.mult,
              reads=[xt, gain, r], writes=[(out, kc)])


def post_norm_res(c, m, gain, gi, xt, n=TT):
    k = c.k
    r = rstd_from(c, [(m[:, kc, 0:n], m) for kc in range(8)], n)
    for kc in range(8):
        t = c.tmp.next()
        k.stt(t[:, 0:n], m[:, kc, 0:n], gain[:, gi, kc:kc + 1], r[:, 0:n], ALU.mult, ALU.mult,
              reads=[m, gain, r], writes=[t])
        k.tt(xt[:, kc, 0:n], xt[:, kc, 0:n], t[:, 0:n], ALU.add, reads=[t, (xt, kc)], writes=[(xt, kc)], e="pool")


class WStream:
    def __init__(self, k, nbuf=4, elems=8 * 512):
        self.k = k
        self.elems = elems
        self.pool = Rot(k, "wbuf", [128, elems], BF16, nbuf)

    def load(self, w_ap, r0, nk, c0, ncols):
        buf = self.pool.next()
        view = buf[:, 0:nk * ncols].rearrange("p (a c) -> p a c", a=nk)
        if isinstance(w_ap, T):
            src = w_ap[r0:r0 + nk * 128, c0:c0 + ncols].rearrange("(a p) c -> p a c", p=128)
            self.qi = getattr(self, "qi", 0) + 1
            self.k.dma("sp", view, src, reads=[w_ap], writes=[buf])
        else:
            src = w_ap[r0:r0 + nk * 128, c0:c0 + ncols].rearrange("(a p) c -> p a c", p=128)
            self.k.dma("pool", view, src, writes=[buf])
        return buf, view


def pre_norm_split(c, xt, gain, gi, out, n=TT):
    k = c.k
    for kc in range(8):
        k.ts(out[:, kc, 0:n], xt[:, kc, 0:n], gain[:, gi, kc:kc + 1], ALU.mult, reads=[xt, gain], writes=[(out, kc)])
    return lambda: rstd_from(c, [(xt[:, kc, 0:n], xt) for kc in range(8)], n)


def dense(c, ws, w_ap, nk, nout_chunks, rhs_fn, rhs_reads, sink, n=TT, cols_per_load=512, hook=None):
    k = c.k
    per = cols_per_load // 128
    for o0 in range(0, nout_chunks, per):
        buf, view = ws.load(w_ap, 0, nk, o0 * 128, cols_per_load)
        for j in range(per):
            oc = o0 + j
            ps = c.psum()
            for kc in range(nk):
                k.mm(ps[:, 0:n], view[:, kc, j * 128:(j + 1) * 128], rhs_fn(kc), start=(kc == 0), stop=(kc == nk - 1),
                     reads=[buf] + rhs_reads(kc), writes=[ps])
            if hook is not None and oc == 0:
                hook()
            sink(oc, ps)


def stage_b(k, c, ws, l, catT, xT_in, xT_out, memT, W, gain, NT, tok0=0, cat_load=None, x_in=None, x_out=None):
    m = k.sb("m_%d" % l, [128, 8, TT], F32)
    memt = m
    k.dma("sp", memt[:, :, 0:256], memT[:, :].rearrange("(a p) c -> p a c", p=128), reads=[memT], writes=[memt])
    mn = k.sb("mn_%d" % l, [128, 8, 256], BF16)
    pre_norm(c, memt, gain, 3, mn, n=256)
    kT = k.sb("kT_%d" % l, [128, 8, 256], BF16)
    vM = k.sb("vM_%d" % l, [128, 2, 1024], BF16)

    def sink_k(oc, ps):
        k.cp(kT[:, oc, :], ps[:, 0:256], reads=[ps], writes=[(kT, oc)], e="act")

    dense(c, ws, W["wk"], 8, 8, lambda kc: mn[:, kc, :], lambda kc: [(mn, kc)], sink_k, n=256)
    for half in range(2):
        buf, view = ws.load(W["wv"], 0, 8, half * 512, 512)
        for mc in range(2):
            ps = c.psum()
            for kc in range(8):
                k.mm(ps[:, :], mn[:, kc, mc * 128:(mc + 1) * 128], view[:, kc, :], start=(kc == 0), stop=(kc == 7),
                     reads=[buf, (mn, kc)], writes=[ps])
            k.cp(vM[:, mc, half * 512:(half + 1) * 512], ps[:, :], reads=[ps], writes=[(vM, (mc, half))], e="act")

    xpool = Rot(k, "xt_%d" % l, [128, 8, TT], F32, 2)
    cpool = Rot(k, "ct_%d" % l, [128, 8, TT], BF16, 1)
    hb = k.sb("hb_%d" % l, [128, 8, TT], BF16)
    qT = k.sb("qT_%d" % l, [128, 8, TT], BF16)
    oT = k.sb("oT_%d" % l, [128, 8, TT], BF16)
    hid = k.sb("hid_%d" % l, [128, 32, TT], BF16)
    pT = Rot(k, "pT_%d" % l, [128, TT], BF16, 4)
    relu = Rot(k, "relu_%d" % l, [128, TT], F32, 3)
    rden = Rot(k, "rden_%d" % l, [128, TT], F32, 2)
    r2buf = k.sb("r2buf_%d" % l, [128, TT], F32)

    def sink_m(oc, ps):
        k.cp(m[:, oc, :], ps[:, :], reads=[ps], writes=[(m, oc)], e="dve")

    for ti in range(NT // TT):
        t0 = tok0 + ti * TT
        xt = xpool.next()
        if x_in is None:
            k.dma("sp", xt[:], xT_in[:, t0:t0 + TT].rearrange("(a p) c -> p a c", p=128), reads=[xT_in], writes=[xt])
        elif ti == 0:
            x_in(ti, xt)
        if x_in is not None and ti + 1 < NT // TT:
            xnext = xpool.t[(xpool.i) % len(xpool.t)]
            x_in(ti + 1, xnext)
        ct = cpool.next()
        if cat_load is None:
            k.dma("pool", ct[:], catT[:, t0:t0 + TT].rearrange("(a p) c -> p a c", p=128), reads=[catT], writes=[ct])
        elif ti == 0:
            cat_load(ti, ct)
        dense(c, ws, W["w_out"], 8, 8, lambda kc: ct[:, kc, :], lambda kc: [ct], sink_m)
        if cat_load is not None and ti + 1 < NT // TT:
            cat_load(ti + 1, ct)
        post_norm_res(c, m, gain, 0, xt)
        stats_q = pre_norm_split(c, xt, gain, 1, hb)
        rq = {}

        def hook_q():
            rq["r"] = stats_q()

        def sink_q(oc, ps):
            r_ = rq["r"]
            k.tt(qT[:, oc, :], ps[:, :], r_[:], ALU.mult, reads=[ps, r_], writes=[(qT, oc)])

        dense(c, ws, W["wq"], 8, 8, lambda kc: hb[:, kc, :], lambda kc: [(hb, kc)], sink_q, hook=hook_q)
        for hd in range(4):
            pts = []
            for mc in range(2):
                ps = c.psum()
                for j in range(2):
                    dc = hd * 2 + j
                    k.mm(ps[:, :], kT[:, dc, mc * 128:(mc + 1) * 128], qT[:, dc, :], start=(j == 0), stop=(j == 1),
                         reads=[(kT, dc), (qT, dc)], writes=[ps])
                pt = pT.next()
                k.act(pt[:], ps[:, :], AF.Exp, reads=[ps], writes=[pt], scale=1.0 / 16.0)
                pts.append(pt)
            pden = c.psum()
            for mc in range(2):
                k.mm(pden[:, :], c.ones[:], pts[mc][:], start=(mc == 0), stop=(mc == 1), reads=[c.ones, pts[mc]], writes=[pden])
            rd = rden.next()
            k.op("dve", lambda g: g.reciprocal(rd[:], pden[:, :]), reads=[pden], writes=[rd])
            for j in range(2):
                dc = hd * 2 + j
                ps = c.psum()
                for mc in range(2):
                    k.mm(ps[:, :], vM[:, mc, dc * 128:(dc + 1) * 128], pts[mc][:], start=(mc == 0), stop=(mc == 1),
                         reads=[vM, pts[mc]], writes=[ps])
                k.tt(oT[:, dc, :], ps[:, :], rd[:], ALU.mult, reads=[ps, rd], writes=[(oT, dc)])
        dense(c, ws, W["wo"], 8, 8, lambda kc: oT[:, kc, :], lambda kc: [(oT, kc)], sink_m)
        post_norm_res(c, m, gain, 2, xt)
        stats_m = pre_norm_split(c, xt, gain, 4, hb)

        def hook_m():
            r_ = stats_m()
            k.tt(r2buf[:], r_[:], r_[:], ALU.mult, reads=[r_], writes=[r2buf])

        def sink_h(f, ps):
            r = relu.next()
            k.act(r[:], ps[:, :], AF.Relu, reads=[ps], writes=[r])
            k.tt(hid[:, f, :], r[:], r[:], ALU.mult, reads=[r], writes=[(hid, f)], e="pool")

        dense(c, ws, W["w1"], 8, 32, lambda kc: hb[:, kc, :], lambda kc: [(hb, kc)], sink_h, hook=hook_m)
        for fb in range(8):
            buf, view = ws.load(W["w2"], fb * 512, 4, 0, 1024)
            for oc in range(8):
                ps = c.psums[oc]
                for f4 in range(4):
                    f = fb * 4 + f4
                    k.mm(ps[:, :], view[:, f4, oc * 128:(oc + 1) * 128], hid[:, f, :], start=(f == 0), stop=(f == 31),
                         reads=[buf, (hid, f)], writes=[ps])
        for oc in range(8):
            k.tt(m[:, oc, :], c.psums[oc][:, :], r2buf[:], ALU.mult, reads=[c.psums[oc], r2buf], writes=[(m, oc)])
        post_norm_res(c, m, gain, 5, xt)
        if x_out is None:
            k.dma("sp", xT_out[:, t0:t0 + TT].rearrange("(a p) c -> p a c", p=128), xt[:], reads=[xt], writes=[xT_out])
        else:
            x_out(ti, xt)


(CH_FQ, CH_FK, CH_FV, CH_LX, CH_LY, CH_RR, CH_RK, CH_RV, CH_RWA, CH_RGD,
 CH_TQ, CH_TK, CH_TV, CH_TG, CH_TQS, CH_TKS, CH_RVO) = range(17)
NCH = 17
NCOLS_A = NCH * 128 + 2

(PV_CW0, PV_CW1, PV_CW2, PV_CW3, PV_CB, PV_RAB, PV_RIB, PV_LAM,
 PV_MU_R, PV_MU_K, PV_MU_V, PV_MU_VO, PV_MU_WA, PV_MU_GD,
 PV_W0, PV_A0, PV_KK, PV_KA, PV_GNW, PV_GNB, PV_RK, PV_V0,
 PV_TGN, PV_FB, PV_HP, PV_H0, PV_H1) = range(27)
PV_GAIN = 27
NPV = 35


class ConstA:
    def __init__(self, k, c):
        self.k = k
        nc = k.nc
        g = k.eng["pool"]
        self.ident = k.sb("identf", [128, 128], F32)
        k.memset(self.ident[:], 1.0, writes=[self.ident])
        k.op("pool", lambda e: e.affine_select(out=self.ident[:], in_=self.ident[:], pattern=[[-1, 128]], compare_op=ALU.is_equal,
                                               fill=0.0, base=0, channel_multiplier=1), reads=[self.ident], writes=[self.ident])
        self.ut_f = k.sb("ut_f", [128, 128], F32)
        k.memset(self.ut_f[:], 1.0, writes=[self.ut_f])
        k.op("pool", lambda e: e.affine_select(out=self.ut_f[:], in_=self.ut_f[:], pattern=[[1, 128]], compare_op=ALU.is_ge,
                                               fill=0.0, base=0, channel_multiplier=-1), reads=[self.ut_f], writes=[self.ut_f])
        self.ut_b = k.sb("ut_b", [128, 128], BF16)
        k.cp(self.ut_b[:], self.ut_f[:], reads=[self.ut_f], writes=[self.ut_b])
        self.blk = k.sb("blk_f", [128, 128], F32)
        k.memset(self.blk[:], 0.0, writes=[self.blk])
        k.memset(self.blk[0:64, 0:64], 1.0, writes=[self.blk])
        k.memset(self.blk[64:128, 64:128], 1.0, writes=[self.blk])
        self.ones_f = k.sb("ones_f", [128, 512], F32)
        k.memset(self.ones_f[:], 1.0, writes=[self.ones_f])
        self.one_c = k.sb("one_c", [128, 1], F32)
        k.memset(self.one_c[:], 1.0, writes=[self.one_c])
        self.zero_c = k.sb("zero_c", [128, 1], F32)
        k.memset(self.zero_c[:], 0.0, writes=[self.zero_c])


def gelu_tanh(k, c, y_ps, y_t, tmps):
    y = tmps.next()
    k.cp(y[:], y_ps, reads=[y_t], writes=[y], e="act")
    t = tmps.next()
    k.tt(t[:], y[:], y[:], ALU.mult, reads=[y], writes=[t])
    k.ts(t[:], t[:], 0.044715, ALU.mult, 1.0, ALU.add, reads=[t], writes=[t])
    k.tt(t[:], t[:], y[:], ALU.mult, reads=[t, y], writes=[t])
    k.act(t[:], t[:], AF.Sigmoid, reads=[t], writes=[t], scale=2.0 * math.sqrt(2.0 / math.pi))
    return y, t


class StageA:
    def __init__(self, k, c, ca, l, xT, outT, Wown, pvd, mats, vfirst_in=None, vfirst_out=None, do=("lru", "ret", "fox", "rwkv")):
        self.k, self.c, self.ca, self.l = k, c, ca, l
        self.xT, self.outT = xT, outT
        self.do = do
        self.x_src = self.out_sink = self.tile_done = None
        self.results = {}
        self.interleave = ()
        self.f32r = False
        self.vf_in, self.vf_out = vfirst_in, vfirst_out
        L = "a%d_" % l
        self.L = L
        self.W = k.sb(L + "W", [128, 8, NCOLS_A], BF16)
        for kc in range(8):
            k.dma("pool", self.W[:, kc, :], Wown[kc * 128:(kc + 1) * 128, :], writes=[(self.W, kc)], max_dma_last_dim=4096)
        self.pv = k.sb(L + "pv", [128, NPV], F32)
        k.dma("sp", self.pv[:], pvd, writes=[self.pv])
        self.mats = {}
        for name, (ap, shape) in mats.items():
            t = k.sb(L + name, shape, F32)
            k.dma("sp", t[:], ap, writes=[t])
            self.mats[name] = t
        self.xpool = Rot(k, L + "xt", [128, 8, TT], F32, 1)
        self.hb = k.sb(L + "hb", [128, 8, TT], BF16)
        self.tmps = Rot(k, L + "t", [128, TT], F32, 12)
        self.outs = Rot(k, L + "o", [128, TT], F32, 3)
        self.named_ps = c.psums[5:8]
        if "lru" in do:
            self.init_lru()
        if "ret" in do:
            self.init_ret()
        if "fox" in do:
            self.init_fox()
        if "rwkv" in do:
            self.init_rwkv()

    def pcol(self, i):
        return self.pv[:, i:i + 1]

    def psum(self):
        c = self.c
        p = c.psums[c.pi % 5]
        c.pi += 1
        return p

    def proj(self, ch, n0=0, n1=TT):
        k = self.k
        ps = self.psum()
        for kc in range(8):
            k.mm(ps[:, n0:n1], self.W[:, kc, ch * 128:(ch + 1) * 128], self.hb[:, kc, n0:n1], start=(kc == 0), stop=(kc == 7),
                 reads=[(self.W, kc), (self.hb, kc)], writes=[ps])
        return ps

    def init_lru(self):
        k, L = self.k, self.L
        self.l_xbuf = k.sb(L + "lxb", [128, 3 + TT], F32)
        k.memset(self.l_xbuf[:, 0:3], 0.0, writes=[self.l_xbuf])
        self.l_h = k.sb(L + "lh", [128, 1], F32)
        k.memset(self.l_h[:], 0.0, writes=[self.l_h])
        self.l_c1 = k.sb(L + "lc1", [128, 2], F32)
        k.act(self.l_c1[:, 0:1], self.pcol(PV_LAM), AF.Exp, reads=[self.pv], writes=[self.l_c1], scale=-1.0)
        k.act(self.l_c1[:, 0:1], self.l_c1[:, 0:1], AF.Ln, reads=[self.l_c1, self.ca.one_c], writes=[self.l_c1], bias=self.ca.one_c[:])
        k.ts(self.l_c1[:, 1:2], self.l_c1[:, 0:1], -16.0, ALU.mult, reads=[self.l_c1], writes=[self.l_c1])
        k.ts(self.l_c1[:, 0:1], self.l_c1[:, 0:1], -8.0, ALU.mult, reads=[self.l_c1], writes=[self.l_c1])

    def lru_tile(self, ti):
        k, c, ca = self.k, self.c, self.ca
        xb = self.l_xbuf
        ps = self.proj(CH_LX)
        k.cp(xb[:, 3:3 + TT], ps[:, :], reads=[ps], writes=[xb], e="act")
        xc = self.tmps.next()
        k.ts(xc[:], xb[:, 0:TT], self.pcol(PV_CW0), ALU.mult, self.pcol(PV_CB), ALU.add, reads=[xb, self.pv], writes=[xc])
        for j in range(1, 4):
            k.stt(xc[:], xb[:, j:j + TT], self.pcol(PV_CW0 + j), xc[:], ALU.mult, ALU.add, reads=[xb, self.pv, xc], writes=[xc])
        k.cp(xb[:, 0:3], xb[:, TT:TT + 3], reads=[xb], writes=[xb], e="dve")
        pr = self.psum()
        k.mm(pr[:, :], self.mats["RA"][:], xc[:], reads=[self.mats["RA"], xc], writes=[pr])
        pi_ = self.psum()
        k.mm(pi_[:, :], self.mats["RI"][:], xc[:], reads=[self.mats["RI"], xc], writes=[pi_])
        r = self.tmps.next()
        k.act(r[:], pr[:, :], AF.Sigmoid, reads=[pr, self.pv], writes=[r], bias=self.pcol(PV_RAB))
        ig = self.tmps.next()
        k.act(ig[:], pi_[:, :], AF.Sigmoid, reads=[pi_, self.pv], writes=[ig], bias=self.pcol(PV_RIB))
        a = self.tmps.next()
        k.act(a[:], r[:], AF.Exp, reads=[r, self.l_c1], writes=[a], scale=self.l_c1[:, 0:1])
        mult = self.tmps.next()
        k.act(mult[:], r[:], AF.Exp, reads=[r, self.l_c1], writes=[mult], scale=self.l_c1[:, 1:2])
        k.act(mult[:], mult[:], AF.Sqrt, reads=[mult, ca.one_c], writes=[mult], scale=-1.0, bias=ca.one_c[:])
        k.tt(ig[:], ig[:], xc[:], ALU.mult, reads=[ig, xc], writes=[ig])
        k.tt(ig[:], ig[:], mult[:], ALU.mult, reads=[ig, mult], writes=[ig])
        h = self.tmps.next()
        k.op("dve", lambda g: g.tensor_tensor_scan(h[:], a[:], ig[:], self.l_h[:, 0:1], ALU.mult, ALU.add),
             reads=[a, ig, self.l_h], writes=[h])
        k.cp(self.l_h[:], h[:, TT - 1:TT], reads=[h], writes=[self.l_h], e="dve")
        py = self.proj(CH_LY)
        o = self.outs.next()
        y, t = gelu_tanh(k, c, py[:, :], py, self.tmps)
        k.tt(o[:], t[:], y[:], ALU.mult, reads=[t, y], writes=[o])
        k.tt(o[:], o[:], h[:], ALU.mult, reads=[o, h], writes=[o])
        return o

    def init_ret(self):
        k, L, ca = self.k, self.L, self.ca
        self.t_lg = k.sb(L + "tlg", [128, 3], F32)
        tmp = k.sb(L + "tlgt", [128, 3], F32)
        k.ts(tmp[:], self.pv[:, PV_HP:PV_HP + 3], 5.0, ALU.add, reads=[self.pv], writes=[tmp])
        k.act(tmp[:], tmp[:], AF.Exp, reads=[tmp], writes=[tmp], scale=-math.log(2.0))
        k.act(self.t_lg[:], tmp[:], AF.Ln, reads=[tmp, ca.one_c], writes=[self.t_lg], scale=-1.0, bias=ca.one_c[:])
        ni = k.sb(L + "tni", [128, 128], I32)
        k.op("pool", lambda e: e.iota(ni[:], pattern=[[1, 128]], base=0, channel_multiplier=0), writes=[ni])
        nf = k.sb(L + "tnf", [128, 128], F32)
        k.cp(nf[:], ni[:], reads=[ni], writes=[nf])
        self.t_xi = k.sb(L + "txi", [128, 128], F32)
        self.t_zt = k.sb(L + "tzt", [128, 128], F32)
        tt_ = k.sb(L + "ttmp", [128, 128], F32)
        k.ts(tt_[:], nf[:], 1.0, ALU.add, reads=[nf], writes=[tt_])
        k.act(self.t_xi[:], tt_[:], AF.Exp, reads=[tt_, self.t_lg], writes=[self.t_xi], scale=self.t_lg[:, 0:1])
        k.ts(tt_[:], nf[:], -1.0, ALU.mult, 127.0, ALU.add, reads=[nf], writes=[tt_])
        k.act(self.t_zt[:], tt_[:], AF.Exp, reads=[tt_, self.t_lg], writes=[self.t_zt], scale=self.t_lg[:, 0:1])
        k.ts(self.t_zt[:], self.t_zt[:], 0.125, ALU.mult, reads=[self.t_zt], writes=[self.t_zt])
        self.t_gc = k.sb(L + "tgc", [128, 1], F32)
        k.act(self.t_gc[:], self.t_lg[:, 0:1], AF.Exp, reads=[self.t_lg], writes=[self.t_gc], scale=128.0)
        di = k.sb(L + "tdi", [128, 128], I32)
        k.op("pool", lambda e: e.iota(di[:], pattern=[[1, 128]], base=0, channel_multiplier=-1), writes=[di])
        df = k.sb(L + "tdf", [128, 128], F32)
        k.cp(df[:], di[:], reads=[di], writes=[df])
        k.ts(df[:], df[:], 0.0, ALU.max, reads=[df], writes=[df])
        self.t_dt = k.sb(L + "tdt", [128, 2, 128], F32)
        for h in range(2):
            k.act(self.t_dt[:, h, :], df[:], AF.Exp, reads=[df, self.t_lg], writes=[self.t_dt], scale=self.t_lg[:, 1 + h:2 + h])
            k.stt(self.t_dt[:, h, :], self.t_dt[:, h, :], 0.125, ca.ut_f[:], ALU.mult, ALU.mult, reads=[self.t_dt, ca.ut_f], writes=[self.t_dt])
        pi_ = k.sb(L + "tpi", [128, 2], I32)
        k.op("pool", lambda e: e.iota(pi_[:, 0:1], pattern=[[0, 1]], base=0, channel_multiplier=1), writes=[pi_])
        k.ts(pi_[:, 1:2], pi_[:, 0:1], 32, ALU.bitwise_and, reads=[pi_], writes=[pi_])
        k.ts(pi_[:, 0:1], pi_[:, 0:1], 31, ALU.bitwise_and, reads=[pi_], writes=[pi_])
        pf = k.sb(L + "tpf", [128, 2], F32)
        k.cp(pf[:], pi_[:], reads=[pi_], writes=[pf])
        self.t_inv = k.sb(L + "tinv", [128, 1], F32)
        k.act(self.t_inv[:], pf[:, 0:1], AF.Exp, reads=[pf], writes=[self.t_inv], scale=-math.log(RET_THETA) / 31.0)
        self.t_sgn = k.sb(L + "tsgn", [128, 1], F32)
        k.ts(self.t_sgn[:], pf[:, 1:2], 1.0 / 16.0, ALU.mult, -1.0, ALU.add, reads=[pf], writes=[self.t_sgn])
        qi = k.sb(L + "tqi", [128, TT], I32)
        k.op("pool", lambda e: e.iota(qi[:], pattern=[[1, TT]], base=0, channel_multiplier=0), writes=[qi])
        self.t_pos = k.sb(L + "tpos", [128, TT], F32)
        k.cp(self.t_pos[:], qi[:], reads=[qi], writes=[self.t_pos])
        self.t_R = k.sb(L + "tR", [128, 64], F32)
        k.memset(self.t_R[:], 0.0, writes=[self.t_R])
        self.t_Rb = k.sb(L + "tRb", [128, 64], BF16)
        k.memset(self.t_Rb[:], 0.0, writes=[self.t_Rb])
        self.t_small = Rot(k, L + "tsm", [128, 128], BF16, 4)
        self.t_vtm = Rot(k, L + "tvtm", [128, 4, 128], BF16, 1)
        self.t_bf = Rot(k, L + "tbf", [128, TT], BF16, 4)

    def sincos(self, t0):
        k, ca = self.k, self.ca
        TWO_PI = 2.0 * math.pi
        C1 = 6.28125
        C2 = TWO_PI - C1
        res = []
        for shift in (math.pi / 2.0, 0.0):
            ang = self.tmps.next()
            k.ts(ang[:], self.t_pos[:], float(t0), ALU.add, self.t_inv[:, 0:1], ALU.mult, reads=[self.t_pos, self.t_inv], writes=[ang])
            if shift:
                k.ts(ang[:], ang[:], shift, ALU.add, reads=[ang], writes=[ang])
            kf = self.tmps.next()
            ki = kf[:].bitcast(I32)
            k.ts(kf[:], ang[:], 1.0 / TWO_PI, ALU.mult, reads=[ang], writes=[kf])
            k.cp(ki, kf[:], reads=[kf], writes=[kf])
            k.cp(kf[:], ki, reads=[kf], writes=[kf])
            k.stt(ang[:], kf[:], -C1, ang[:], ALU.mult, ALU.add, reads=[kf, ang], writes=[ang])
            k.stt(ang[:], kf[:], -C2, ang[:], ALU.mult, ALU.add, reads=[kf, ang], writes=[ang])
            k.ts(kf[:], ang[:], math.pi, ALU.is_gt, -TWO_PI, ALU.mult, reads=[ang], writes=[kf])
            k.tt(ang[:], ang[:], kf[:], ALU.add, reads=[ang, kf], writes=[ang])
            k.ts(kf[:], ang[:], -math.pi, ALU.is_lt, TWO_PI, ALU.mult, reads=[ang], writes=[kf])
            k.tt(ang[:], ang[:], kf[:], ALU.add, reads=[ang, kf], writes=[ang])
            k.ts(ang[:], ang[:], math.pi, ALU.min, -math.pi, ALU.max, reads=[ang], writes=[ang])
            k.act(ang[:], ang[:], AF.Sin, reads=[ang], writes=[ang])
            res.append(ang)
        C, S = res
        k.ts(S[:], S[:], self.t_sgn[:, 0:1], ALU.mult, reads=[S, self.t_sgn], writes=[S])
        return C, S

    def ret_gen(self, ti):
        k, c, ca = self.k, self.c, self.ca
        t0 = ti * TT
        C, S = self.sincos(t0)
        yield
        rot = []
        for (cha, chs) in ((CH_TQ, CH_TQS), (CH_TK, CH_TKS)):
            pa = self.proj(cha)
            a = self.tmps.next()
            k.tt(a[:], pa[:, :], C[:], ALU.mult, reads=[pa, C], writes=[a])
            pb = self.proj(chs)
            b = self.tmps.next()
            k.tt(b[:], pb[:, :], S[:], ALU.mult, reads=[pb, S], writes=[b])
            k.tt(a[:], a[:], b[:], ALU.add, reads=[a, b], writes=[a], e="pool")
            rot.append(a)
            yield
        qr, kr = rot
        qb = self.t_bf.next()
        k.cp(qb[:], qr[:], reads=[qr], writes=[qb], e="act")
        kb = self.t_bf.next()
        k.cp(kb[:], kr[:], reads=[kr], writes=[kb], e="act")
        qx = self.t_bf.next()
        k.tt(qx[:].rearrange("p (c n) -> p c n", n=128), qr[:].rearrange("p (c n) -> p c n", n=128),
             self.t_xi[:].unsqueeze(1).to_broadcast([128, 4, 128]), ALU.mult, reads=[qr, self.t_xi], writes=[qx])
        kz = self.tmps.next()
        k.tt(kz[:].rearrange("p (c n) -> p c n", n=128), kr[:].rearrange("p (c n) -> p c n", n=128),
             self.t_zt[:].unsqueeze(1).to_broadcast([128, 4, 128]), ALU.mult, reads=[kr, self.t_zt], writes=[kz])
        vtm = self.t_vtm.next()
        for cc in range(4):
            ps = self.psum()
            for kc in range(8):
                k.mm(ps[:, 0:128], self.hb[:, kc, cc * 128:(cc + 1) * 128], self.W[:, kc, CH_TV * 128:(CH_TV + 1) * 128],
                     start=(kc == 0), stop=(kc == 7), reads=[(self.W, kc), (self.hb, kc)], writes=[ps])
            k.cp(vtm[:, cc, :], ps[:, 0:128], reads=[ps], writes=[(vtm, cc)], e="act")
            yield
        osb = self.tmps.next()
        for cc in range(4):
            cs = slice(cc * 128, (cc + 1) * 128)
            ptr = self.psum()
            k.tr(ptr[:, 0:128], kz[:, cs], ca.ident[:], reads=[kz, ca.ident], writes=[ptr])
            kzt = self.t_small.next()
            k.cp(kzt[:], ptr[:, 0:128], reads=[ptr], writes=[kzt], e="act")
            pkv = self.psum()
            po = self.psum()
            for h in range(2):
                hp = slice(h * 64, (h + 1) * 64)
                pin = self.psum()
                k.mm(pin[:, 0:128], kb[hp, cs], qb[hp, cs], reads=[kb, qb], writes=[pin])
                ind = self.t_small.next()
                k.tt(ind[:], pin[:, 0:128], self.t_dt[:, h, :], ALU.mult, reads=[pin, self.t_dt], writes=[ind])
                k.mm(po[hp, 0:128], vtm[:, cc, hp], ind[:], start=True, stop=False, reads=[(vtm, cc), ind], writes=[po])
                k.mm(po[hp, 0:128], self.t_Rb[hp, :], qx[hp, cs], start=False, stop=True, reads=[self.t_Rb, qx], writes=[po])
                k.mm(pkv[hp, 0:64], kzt[:, hp], vtm[:, cc, hp], reads=[kzt, (vtm, cc)], writes=[pkv])
            k.stt(self.t_R[:], self.t_R[:], self.t_gc[:, 0:1], pkv[:, 0:64], ALU.mult, ALU.add, reads=[self.t_R, self.t_gc, pkv], writes=[self.t_R])
            k.cp(self.t_Rb[:], self.t_R[:], reads=[self.t_R], writes=[self.t_Rb], e="act")
            k.cp(osb[:, cs], po[:, 0:128], reads=[po], writes=[osb], e="act")
            yield
        sq = self.tmps.next()
        k.tt(sq[:], osb[:], osb[:], ALU.mult, reads=[osb], writes=[sq], e="pool")
        pss = self.psum()
        k.mm(pss[:, :], ca.blk[:], sq[:], reads=[ca.blk, sq], writes=[pss])
        rs = self.tmps.next()
        k.act(rs[:], pss[:, :], AF.Sqrt, reads=[pss, c.eps], writes=[rs], bias=c.eps[:], scale=1.0 / 64.0)
        k.op("dve", lambda g: g.reciprocal(rs[:], rs[:]), reads=[rs], writes=[rs])
        pg = self.proj(CH_TG)
        sg = self.tmps.next()
        k.act(sg[:], pg[:, :], AF.Silu, reads=[pg], writes=[sg])
        o = self.outs.next()
        k.stt(o[:], osb[:], self.pcol(PV_TGN), rs[:], ALU.mult, ALU.mult, reads=[osb, self.pv, rs], writes=[o])
        k.tt(o[:], o[:], sg[:], ALU.mult, reads=[o, sg], writes=[o])
        self.results["ret"] = o

    def ret_tile(self, ti):
        for _ in self.ret_gen(ti):
            pass
        return self.results["ret"]


    def init_fox(self):
        k, L, ca = self.k, self.L, self.ca
        self.f_kT = k.sb(L + "fkT", [128, 4096], BF16)
        self.f_v = k.sb(L + "fv", [128, 32, 128], BF16)
        self.f_q = k.sb(L + "fq", [128, TT], BF16)
        self.f_row = Rot(k, L + "frow", [2, TT], F32, 3)
        self.f_clast = k.sb(L + "fcl", [2, 1], F32)
        k.memset(self.f_clast[:], 0.0, writes=[self.f_clast])
        self.f_ccol = k.sb(L + "fcc", [128, 32, 2], F32)
        self.f_cend = k.sb(L + "fce", [128, 4, 2], F32)
        self.f_B = Rot(k, L + "fB", [128, 32, 4], F32, 2)
        self.f_pT = Rot(k, L + "fpT", [128, TT], BF16, 4)
        self.f_rd = k.sb(L + "frd", [128, TT], F32)
        self.f_sel = k.sb(L + "fsel", [128, 128], F32)
        k.memset(self.f_sel[:], 1.0, writes=[self.f_sel])
        k.op("pool", lambda e: e.affine_select(out=self.f_sel[:], in_=self.f_sel[:], pattern=[[0, 128]], compare_op=ALU.is_equal,
                                               fill=0.0, base=-127, channel_multiplier=1), reads=[self.f_sel], writes=[self.f_sel])
        self.f_nb = k.sb(L + "fnb", [2, 1], F32)
        k.ts(self.f_nb[:], self.pv[0:2, PV_FB:PV_FB + 1], -1.0, ALU.mult, reads=[self.pv], writes=[self.f_nb])

    def fox_gen(self, ti):
        k, c, ca = self.k, self.c, self.ca
        t0 = ti * TT
        pq = self.proj(CH_FQ)
        k.cp(self.f_q[:], pq[:, :], reads=[pq], writes=[self.f_q], e="act")
        pk = self.proj(CH_FK)
        k.cp(self.f_kT[:, t0:t0 + TT], pk[:, :], reads=[pk], writes=[(self.f_kT, ti)], e="act")
        for cc in range(4):
            ps = self.psum()
            for kc in range(8):
                k.mm(ps[:, 0:128], self.hb[:, kc, cc * 128:(cc + 1) * 128], self.W[:, kc, CH_FV * 128:(CH_FV + 1) * 128],
                     start=(kc == 0), stop=(kc == 7), reads=[(self.W, kc), (self.hb, kc)], writes=[ps])
            k.cp(self.f_v[:, ti * 4 + cc, :], ps[:, 0:128], reads=[ps], writes=[(self.f_v, ti * 4 + cc)], e="act")
        pf = self.psum()
        for kc in range(8):
            k.mm(pf[0:2, :], self.W[:, kc, NCH * 128:NCH * 128 + 2], self.hb[:, kc, :], start=(kc == 0), stop=(kc == 7),
                 reads=[(self.W, kc), (self.hb, kc)], writes=[pf])
        e_ = self.f_row.next()
        k.act(e_[:], pf[0:2, :], AF.Exp, reads=[pf, self.f_nb], writes=[e_], scale=-1.0, bias=self.f_nb[:])
        k.act(e_[:], e_[:], AF.Ln, reads=[e_, ca.one_c], writes=[e_], bias=ca.one_c[0:2, :])
        cs = self.f_row.next()
        k.op("dve", lambda g: g.tensor_tensor_scan(cs[:], ca.ones_f[0:2, :], e_[:], self.f_clast[:, 0:1], ALU.mult, ALU.add),
             reads=[ca.ones_f, e_, self.f_clast], writes=[cs])
        k.cp(self.f_clast[:], cs[:, TT - 1:TT], reads=[cs], writes=[self.f_clast])
        pc = self.psum()
        for cc in range(4):
            k.tr(pc[:, 2 * cc:2 * cc + 2], cs[:, cc * 128:(cc + 1) * 128], ca.ident[0:2, 0:2], reads=[cs, ca.ident], writes=[pc])
        k.cp(self.f_ccol[:, ti * 4:(ti + 1) * 4, :].rearrange("p a b -> p (a b)"), pc[:, 0:8], reads=[pc], writes=[(self.f_ccol, ti)])
        pe = self.psum()
        k.mm(pe[:, 0:8], self.f_sel[:], self.f_ccol[:, ti * 4:(ti + 1) * 4, :].rearrange("p a b -> p (a b)"),
             reads=[self.f_sel, (self.f_ccol, ti)], writes=[pe])
        k.cp(self.f_cend[:].rearrange("p a b -> p (a b)"), pe[:, 0:8], reads=[pe], writes=[self.f_cend])
        po, pd = self.named_ps[0], self.named_ps[1]
        nkc = 4 * (ti + 1)
        yield
        for h in range(2):
            hp = slice(h * 64, (h + 1) * 64)
            B = self.f_B.next()
            k.tt(B[:, 0:nkc, :], self.f_ccol[:, 0:nkc, h:h + 1].to_broadcast([128, nkc, 4]),
                 self.f_cend[:, :, h].unsqueeze(1).to_broadcast([128, nkc, 4]), ALU.subtract,
                 reads=[self.f_ccol, self.f_cend], writes=[B])

            def pv_step(kc, pt, col0):
                k.mm(po[hp, col0:TT], self.f_v[:, kc, hp], pt[:, col0:TT], start=(kc == 0), stop=(kc == nkc - 1),
                     reads=[(self.f_v, kc), pt], writes=[po])
                k.mm(pd[hp, col0:TT], c.ones[:, 0:64], pt[:, col0:TT], start=(kc == 0), stop=(kc == nkc - 1),
                     reads=[c.ones, pt], writes=[pd])

            pend = None
            for kc in range(nkc):
                j = kc - 4 * ti
                col0 = 128 * j if j >= 0 else 0
                ps = self.psum()
                k.mm(ps[:, col0:TT], self.f_kT[hp, kc * 128:(kc + 1) * 128], self.f_q[hp, col0:TT],
                     reads=[(self.f_kT, kc // 4), self.f_q], writes=[ps])
                pt = self.f_pT.next()
                for qbl in range(col0 // 128, 4):
                    qs = slice(qbl * 128, (qbl + 1) * 128)
                    k.act(pt[:, qs], ps[:, qs], AF.Exp, reads=[ps, B], writes=[pt], scale=0.125, bias=B[:, kc, qbl:qbl + 1])
                if j >= 0:
                    qs = slice(j * 128, (j + 1) * 128)
                    k.tt(pt[:, qs], pt[:, qs], ca.ut_b[:], ALU.mult, reads=[pt, ca.ut_b], writes=[pt], e="pool")
                if pend is not None:
                    pv_step(*pend)
                pend = (kc, pt, col0)
                yield
            pv_step(*pend)
            yield
        rd = self.f_rd
        k.op("dve", lambda g: g.reciprocal(rd[:], pd[:, :]), reads=[pd], writes=[rd])
        o = self.outs.next()
        k.tt(o[:], po[:, :], rd[:], ALU.mult, reads=[po, rd], writes=[o])
        self.results["fox"] = o

    def fox_tile(self, ti):
        for _ in self.fox_gen(ti):
            pass
        return self.results["fox"]

    def init_rwkv(self):
        k, L, ca = self.k, self.L, self.ca
        self.TR = 256
        self.NCK = self.TR // 64
        NCK = self.NCK
        self.w_carry = k.sb(L + "wcar", [128, 6], F32)
        k.memset(self.w_carry[:], 0.0, writes=[self.w_carry])
        self.w_AR = k.sb(L + "wAR", [128, NCK, 2, 2, 64], F32)
        self.w_BK = k.sb(L + "wBK", [128, NCK, 2, 2, 64], F32)
        self.w_HAT = k.sb(L + "wHAT", [128, NCK, 3, 2, 64], F32)
        for t in (self.w_AR, self.w_BK, self.w_HAT):
            k.memset(t[:], 0.0, writes=[t])
        self.w_H = k.sb(L + "wH", [128, 128], F32)
        k.memset(self.w_H[:], 0.0, writes=[self.w_H])
        self.w_ms = k.sb(L + "wms", [128, 128], F32)
        k.memset(self.w_ms[:], 1.0, writes=[self.w_ms])
        k.op("pool", lambda e: e.affine_select(out=self.w_ms[:], in_=self.w_ms[:], pattern=[[1, 128]], compare_op=ALU.is_gt,
                                               fill=0.0, base=0, channel_multiplier=-1), reads=[self.w_ms], writes=[self.w_ms])
        self.w_mst = k.sb(L + "wmst", [128, 128], F32)
        k.memset(self.w_mst[:], 1.0, writes=[self.w_mst])
        k.op("pool", lambda e: e.affine_select(out=self.w_mst[:], in_=self.w_mst[:], pattern=[[-1, 128]], compare_op=ALU.is_gt,
                                               fill=0.0, base=0, channel_multiplier=1), reads=[self.w_mst], writes=[self.w_mst])
        self.w_mist = k.sb(L + "wmist", [128, 2, 64], F32)
        for j in range(2):
            k.cp(self.w_mist[0:64, j, :], ca.ut_f[0:64, 0:64], reads=[ca.ut_f], writes=[self.w_mist])
            k.cp(self.w_mist[64:128, j, :], ca.ut_f[64:128, 64:128], reads=[ca.ut_f], writes=[self.w_mist])
        self.w_chm = k.sb(L + "wchm", [128, self.TR], F32)
        k.memset(self.w_chm[:], 1.0, writes=[self.w_chm])
        k.memset(self.w_chm[:].rearrange("p (c n) -> p c n", n=64)[:, :, 0:1], 0.0, writes=[self.w_chm])
        self.w_omka = k.sb(L + "womka", [128, 1], F32)
        k.ts(self.w_omka[:], self.pcol(PV_KA), -1.0, ALU.mult, 1.0, ALU.add, reads=[self.pv], writes=[self.w_omka])
        self.w_gc = k.sb(L + "wgc", [128, NCK], F32)
        self.w_rot = {n: Rot(k, L + "w" + n, [128, 128], F32, 2) for n in ("w1", "u")}
        self.w_out3 = {n: [k.sb(L + "w%s%d" % (n, i), [128, 128], F32) for i in range(3)] for n in ("nak", "nst", "bh", "kh", "vb", "x")}
        self.w_ptb = [[k.sb(L + "wpt%d_%d" % (i, j), [128, 128], F32) for j in range(2)] for i in range(3)]
        self.w_pxb = [[k.sb(L + "wpx%d_%d" % (i, j), [128, 256], F32) for j in range(2)] for i in range(3)]
        self.w_v32 = k.sb(L + "wv32", [32, self.TR], F32)
        self.w_gneps = k.sb(L + "wgne", [128, 1], F32)
        k.memset(self.w_gneps[:], 64e-5, writes=[self.w_gneps])

    def nb(self, i):
        TR = self.TR
        xt = self.cur_xt
        return xt[:, i // 2, (i % 2) * TR:(i % 2 + 1) * TR], (xt, ("nb", i))

    def rwkv_gen(self, ti):
        o = self.outs.next()
        for half in range(self.TT_R):
            yield from self.rwkv_half(ti, half, o)
        self.results["rwkv"] = o

    def rwkv_tile(self, ti):
        for _ in self.rwkv_gen(ti):
            pass
        return self.results["rwkv"]

    TT_R = 2

    def rwkv_half(self, ti, half, o):
        k, c, ca, l = self.k, self.c, self.ca, self.l
        TR, NCK = self.TR, self.NCK
        n0 = half * TR
        n1 = n0 + TR
        g0 = ti * TT + n0
        pv = self.pv
        (R, dR), (Kk, dK), (V, dV), (WA, dWA), (GD, dGD), (VO, dVO), (A, dA), (LW, dLW), (LG, dLG), (KKN, dKKN), (BV, dBV), \
            (RT, dRT), (TA, dTA), (TB, dTB), (TC, dTC), (Y, dY) = [self.nb(i) for i in range(16)]
        car = self.w_carry

        def lerp(X, dX, ch, mu_i, ci):
            ps = self.proj(ch, n0, n1)
            k.cp(X, ps[:, n0:n1], reads=[ps], writes=[dX], e="act")
            k.tt(TA[:, 1:TR], X[:, 0:TR - 1], X[:, 1:TR], ALU.subtract, reads=[dX], writes=[dTA])
            k.tt(TA[:, 0:1], car[:, ci:ci + 1], X[:, 0:1], ALU.subtract, reads=[dX, car], writes=[dTA])
            k.cp(car[:, ci:ci + 1], X[:, TR - 1:TR], reads=[dX], writes=[car])
            k.stt(X, TA, self.pcol(mu_i), X, ALU.mult, ALU.add, reads=[dTA, pv, dX], writes=[dX])

        lerp(R, dR, CH_RR, PV_MU_R, 0)
        yield
        lerp(Kk, dK, CH_RK, PV_MU_K, 1)
        yield
        lerp(V, dV, CH_RV, PV_MU_V, 2)
        yield
        lerp(WA, dWA, CH_RWA, PV_MU_WA, 3)
        yield
        lerp(GD, dGD, CH_RGD, PV_MU_GD, 4)
        yield
        if l > 0:
            lerp(VO, dVO, CH_RVO, PV_MU_VO, 5)
            yield
        WA2, G2 = self.mats["WA2"], self.mats["G2"]
        k.act(TB[0:64, :], WA[0:64, :], AF.Tanh, reads=[dWA], writes=[dTB])
        pz = self.psum()
        k.mm(pz[:, 0:TR], WA2[0:64, :], TB[0:64, :], reads=[WA2, dTB], writes=[pz])
        k.act(LW, pz[:, 0:TR], AF.Sigmoid, reads=[pz, pv], writes=[dLW], bias=self.pcol(PV_W0))
        k.ts(LW, LW, -math.exp(-0.5), ALU.mult, reads=[dLW], writes=[dLW])
        pa = self.psum()
        k.mm(pa[:, 0:TR], WA2[64:128, :], WA[64:128, :], reads=[WA2, dWA], writes=[pa])
        k.act(A, pa[:, 0:TR], AF.Sigmoid, reads=[pa, pv], writes=[dA], bias=self.pcol(PV_A0))
        k.act(GD, GD, AF.Sigmoid, reads=[dGD], writes=[dGD])
        pg = self.psum()
        k.mm(pg[:, 0:TR], G2[:], GD, reads=[G2, dGD], writes=[pg])
        k.cp(GD, pg[:, 0:TR], reads=[pg], writes=[dGD], e="act")
        if l > 0:
            V1, V2 = self.mats["V1"], self.mats["V2"]
            p1 = self.psum()
            k.mm(p1[0:32, 0:TR], V1[:, 0, :], V, start=True, stop=False, reads=[V1, dV], writes=[p1])
            k.mm(p1[0:32, 0:TR], V1[:, 1, :], VO, start=False, stop=True, reads=[V1, dVO], writes=[p1])
            k.cp(self.w_v32[:], p1[0:32, 0:TR], reads=[p1], writes=[self.w_v32], e="act")
            p2 = self.psum()
            k.mm(p2[:, 0:TR], V2[:], self.w_v32[:], reads=[V2, self.w_v32], writes=[p2])
            k.act(TB, p2[:, 0:TR], AF.Sigmoid, reads=[p2, pv], writes=[dTB], bias=self.pcol(PV_V0))
            k.dma("sp", TC, self.vf_in[:, g0:g0 + TR], reads=[self.vf_in], writes=[dTC])
            k.tt(TC, TC, V, ALU.subtract, reads=[dTC, dV], writes=[dTC])
            k.tt(TC, TC, TB, ALU.mult, reads=[dTC, dTB], writes=[dTC])
            k.tt(V, V, TC, ALU.add, reads=[dV, dTC], writes=[dV])
        else:
            k.dma("sp", self.vf_out[:, g0:g0 + TR], V, reads=[dV], writes=[self.vf_out])
        yield
        k.ts(KKN, Kk, self.pcol(PV_KK), ALU.mult, reads=[dK, pv], writes=[dKKN])
        k.tt(TB, KKN, KKN, ALU.mult, reads=[dKKN], writes=[dTB])
        pn = self.psum()
        k.mm(pn[:, 0:TR], ca.blk[:], TB, reads=[ca.blk, dTB], writes=[pn])
        k.act(TB, pn[:, 0:TR], AF.Sqrt, reads=[pn], writes=[dTB])
        k.ts(TB, TB, 1e-12, ALU.max, reads=[dTB], writes=[dTB])
        k.op("dve", lambda g: g.reciprocal(TB, TB), reads=[dTB], writes=[dTB])
        k.tt(KKN, KKN, TB, ALU.mult, reads=[dKKN, dTB], writes=[dKKN])
        k.ts(TB, A, self.pcol(PV_KA), ALU.mult, self.w_omka[:, 0:1], ALU.add, reads=[dA, pv, self.w_omka], writes=[dTB])
        k.tt(Kk, Kk, TB, ALU.mult, reads=[dK, dTB], writes=[dK])
        k.tt(BV, KKN, A, ALU.mult, reads=[dKKN, dA], writes=[dBV])
        yield
        k.op("dve", lambda g: g.tensor_tensor_scan(LG, self.w_chm[:], LW, 0.0, ALU.mult, ALU.add),
             reads=[self.w_chm, dLW], writes=[dLG])
        AR, BK, HAT = self.w_AR, self.w_BK, self.w_HAT
        v3 = lambda ap: ap.rearrange("p (c n) -> p c n", n=64)

        def to_blk(dst, which, src, dsrc, mul=None, dmul=None, op="mult"):
            for h in range(2):
                hp = slice(h * 64, (h + 1) * 64)
                if mul is None:
                    k.cp(dst[hp, :, which, h, :], v3(src[hp, :]), reads=[dsrc], writes=[dst])
                else:
                    k.tt(dst[hp, :, which, h, :], v3(src[hp, :]), v3(mul[hp, :]), ALU.mult, reads=[dsrc, dmul], writes=[dst])

        k.act(TB, LG, AF.Exp, reads=[dLG], writes=[dTB])
        k.tt(RT, R, TB, ALU.mult, reads=[dR, dTB], writes=[dRT])
        to_blk(AR, 1, RT, dRT)
        k.act(self.w_gc[:], v3(LG)[:, :, 63], AF.Exp, reads=[dLG], writes=[self.w_gc])
        yield
        k.tt(TB, LG, LW, ALU.subtract, reads=[dLG, dLW], writes=[dTB])
        k.act(TB, TB, AF.Exp, reads=[dTB], writes=[dTB])
        k.stt(TC, KKN, -1.0, TB, ALU.mult, ALU.mult, reads=[dKKN, dTB], writes=[dTC])
        to_blk(AR, 0, TC, dTC)
        k.act(TB, LG, AF.Exp, reads=[dLG], writes=[dTB], scale=-1.0)
        to_blk(BK, 0, BV, dBV, TB, dTB)
        to_blk(BK, 1, Kk, dK, TB, dTB)
        yield
        k.tt(v3(TB), v3(LG)[:, :, 63:64].to_broadcast([128, NCK, 64]), v3(LG), ALU.subtract, reads=[dLG], writes=[dTB])
        k.act(TB, TB, AF.Exp, reads=[dTB], writes=[dTB])
        to_blk(HAT, 0, BV, dBV, TB, dTB)
        to_blk(HAT, 1, Kk, dK, TB, dTB)
        to_blk(HAT, 2, V, dV)
        py = self.named_ps[2]
        ident = ca.ident
        m2 = lambda ap: ap.rearrange("p a b -> p (a b)")
        H = self.w_H
        P = {}
        if self.f32r:
            F32R = mybir.dt.float32r

            def mmr(out, lhsT, rhs, **kw):
                return k.mm(out, lhsT.bitcast(F32R), rhs.bitcast(F32R), **kw)
        else:
            mmr = k.mm

        def prep(cc):
            slot = cc % 3
            pxs, pts = self.w_pxb[slot], self.w_ptb[slot]
            o3 = {n: self.w_out3[n][slot] for n in self.w_out3}
            cs = slice(cc * 64, (cc + 1) * 64)
            Ab = m2(AR[:, cc, 0])
            Bb, Kb = m2(BK[:, cc, 0]), m2(BK[:, cc, 1])
            pA = self.psum()
            mmr(pA[:, 0:128], Bb, Ab, reads=[BK, AR], writes=[pA])
            mmr(pA[:, 128:256], Kb, Ab, reads=[BK, AR], writes=[pA])
            pC = self.psum()
            mmr(pC[:, 0:128], Ab, Bb, reads=[BK, AR], writes=[pC])
            pS = self.psum()
            mmr(pS[:, 0:64], Bb, RT[:, cs], reads=[BK, dRT], writes=[pS])
            mmr(pS[:, 64:128], Kb, RT[:, cs], reads=[BK, dRT], writes=[pS])
            pi_ = 0
            px = pxs[pi_]
            k.tt(px[:, 0:128], pA[:, 0:128], self.w_ms[:], ALU.mult, reads=[pA, self.w_ms], writes=[px])
            k.tt(px[:, 128:256], px[:, 0:128], ident[:], ALU.add, reads=[px, ident], writes=[px], e="pool")
            pt = pts[pi_]
            k.tt(pt[:], pC[:, 0:128], self.w_mst[:], ALU.mult, reads=[pC, self.w_mst], writes=[pt])
            nak = o3["nak"]
            k.tt(nak[:], pA[:, 128:256], self.w_ms[:], ALU.mult, reads=[pA, self.w_ms], writes=[nak])
            nst = o3["nst"]
            k.tt(nst[:], pS[:, 0:128], m2(self.w_mist[:]), ALU.mult, reads=[pS, self.w_mist], writes=[nst])
            yield
            pP = self.psum()
            mmr(pP[:, 0:128], pt[:], px[:, 0:128], reads=[pt, px], writes=[pP])
            pQ = self.psum()
            mmr(pQ[:, 0:128], px[:, 0:128], pt[:], reads=[pt, px], writes=[pQ])
            pi_ ^= 1
            px2, pt2 = pxs[pi_], pts[pi_]
            k.cp(px2[:, 0:128], pP[:, 0:128], reads=[pP], writes=[px2], e="act")
            k.cp(px2[:, 128:256], px[:, 128:256], reads=[px], writes=[px2], e="pool")
            k.cp(pt2[:], pQ[:, 0:128], reads=[pQ], writes=[pt2], e="act")
            px, pt = px2, pt2
            yield
            X = o3["x"]
            for lvl in range(1, 6):
                last = (lvl == 5)
                pP = self.psum()
                if not last:
                    mmr(pP[:, 0:256], pt[:], px[:, 0:256], reads=[pt, px], writes=[pP])
                    pQ = self.psum()
                    mmr(pQ[:, 0:128], px[:, 0:128], pt[:], reads=[pt, px], writes=[pQ])
                    pi_ ^= 1
                    px2, pt2 = pxs[pi_], pts[pi_]
                    k.tt(px2[:, 128:256], pP[:, 128:256], px[:, 128:256], ALU.add, reads=[pP, px], writes=[px2])
                    k.cp(px2[:, 0:128], pP[:, 0:128], reads=[pP], writes=[px2], e="act")
                    k.cp(pt2[:], pQ[:, 0:128], reads=[pQ], writes=[pt2], e="act")
                    px, pt = px2, pt2
                else:
                    mmr(pP[:, 128:256], pt[:], px[:, 128:256], reads=[pt, px], writes=[pP])
                    k.tt(X[:], pP[:, 128:256], px[:, 128:256], ALU.add, reads=[pP, px], writes=[X])
                yield
            tm = []
            for wi, nm in enumerate(("bh", "kh", "vb")):
                ptr = self.psum()
                k.tr(ptr[:, 0:128], m2(HAT[:, cc, wi]), ident[:], reads=[HAT, ident], writes=[ptr])
                tbuf = o3[nm]
                k.cp(tbuf[:], ptr[:, 0:128], reads=[ptr], writes=[tbuf], e="act")
                tm.append(tbuf)
                yield
            P[cc] = (X, nak, nst) + tuple(tm)

        def chain(cc):
            cs = slice(cc * 64, (cc + 1) * 64)
            Ab = m2(AR[:, cc, 0])
            X, nak, nst, bh, kh, vb = P[cc]
            pW = self.psum()
            mmr(pW[:, 0:128], Ab, H[:], start=True, stop=False, reads=[AR, H], writes=[pW])
            mmr(pW[:, 0:128], nak[:], vb[:], start=False, stop=True, reads=[nak, vb], writes=[pW])
            w1 = self.w_rot["w1"].next()
            k.cp(w1[:], pW[:, 0:128], reads=[pW], writes=[w1], e="dve")
            yield
            pU = self.psum()
            mmr(pU[:, 0:128], X[:], w1[:], reads=[X, w1], writes=[pU])
            u = self.w_rot["u"].next()
            k.cp(u[:], pU[:, 0:128], reads=[pU], writes=[u], e="dve")
            yield
            mmr(py[:, cs], H[:], RT[:, cs], start=True, stop=False, reads=[H, dRT], writes=[py])
            mmr(py[:, cs], u[:], nst[:, 0:64], start=False, stop=False, reads=[u, nst], writes=[py])
            mmr(py[:, cs], vb[:], nst[:, 64:128], start=False, stop=True, reads=[vb, nst], writes=[py])
            pH = self.psum()
            mmr(pH[:, 0:128], bh[:], u[:], start=True, stop=False, reads=[bh, u], writes=[pH])
            mmr(pH[:, 0:128], kh[:], vb[:], start=False, stop=True, reads=[kh, vb], writes=[pH])
            k.stt(H[:], H[:], self.w_gc[:, cc:cc + 1], pH[:, 0:128], ALU.mult, ALU.add, reads=[H, self.w_gc, pH], writes=[H])
            yield

        active = {}

        def start(cc_):
            if cc_ < NCK:
                active[cc_] = prep(cc_)

        def step_preps():
            for cc_ in list(active):
                try:
                    next(active[cc_])
                except StopIteration:
                    del active[cc_]

        start(0)
        start(1)
        while 0 in active:
            step_preps()
            yield
        for cc in range(NCK):
            start(cc + 2)
            while cc in active:
                step_preps()
                yield
            gc = chain(cc)
            done = False
            while not done:
                try:
                    next(gc)
                except StopIteration:
                    done = True
                step_preps()
                yield
        k.cp(Y, py[:, 0:TR], reads=[py], writes=[dY], e="act")
        pm = self.psum()
        k.mm(pm[:, 0:TR], ca.blk[:], Y, reads=[ca.blk, dY], writes=[pm])
        k.stt(Y, pm[:, 0:TR], -1.0 / 64.0, Y, ALU.mult, ALU.add, reads=[pm, dY], writes=[dY])
        k.tt(TB, Y, Y, ALU.mult, reads=[dY], writes=[dTB])
        pvv = self.psum()
        k.mm(pvv[:, 0:TR], ca.blk[:], TB, reads=[ca.blk, dTB], writes=[pvv])
        k.act(TB, pvv[:, 0:TR], AF.Sqrt, reads=[pvv, self.w_gneps], writes=[dTB], bias=self.w_gneps[:], scale=1.0 / 64.0)
        k.op("dve", lambda g: g.reciprocal(TB, TB), reads=[dTB], writes=[dTB])
        k.tt(Y, Y, TB, ALU.mult, reads=[dY, dTB], writes=[dY])
        k.ts(Y, Y, self.pcol(PV_GNW), ALU.mult, self.pcol(PV_GNB), ALU.add, reads=[dY, pv], writes=[dY])
        yield
        k.stt(TB, R, self.pcol(PV_RK), Kk, ALU.mult, ALU.mult, reads=[dR, pv, dK], writes=[dTB])
        pb = self.psum()
        k.mm(pb[:, 0:TR], ca.blk[:], TB, reads=[ca.blk, dTB], writes=[pb])
        k.tt(TB, pb[:, 0:TR], V, ALU.mult, reads=[pb, dV], writes=[dTB])
        k.tt(Y, Y, TB, ALU.add, reads=[dY, dTB], writes=[dY])
        k.tt(o[:, n0:n1], Y, GD, ALU.mult, reads=[dY, dGD], writes=[o])

    def run(self, ntiles):
        k, c = self.k, self.c
        gainA = self.pv
        for ti in range(ntiles):
            t0 = ti * TT
            xt = self.xpool.next()
            self.cur_xt = xt
            if self.x_src is None:
                k.dma("sp", xt[:], self.xT[:, t0:t0 + TT].rearrange("(a p) c -> p a c", p=128), reads=[self.xT], writes=[xt])
            else:
                self.x_src(ti, xt)
            r = rstd_from(c, [(xt[:, kc, :], xt) for kc in range(8)])
            for kc in range(8):
                k.stt(self.hb[:, kc, :], xt[:, kc, :], self.pv[:, PV_GAIN + kc:PV_GAIN + kc + 1], r[:], ALU.mult, ALU.mult,
                      reads=[xt, self.pv, r], writes=[(self.hb, kc)])
            def emit(mi, o):
                if self.out_sink is None:
                    k.dma("sp", self.outT[mi * 128:(mi + 1) * 128, t0:t0 + TT], o[:], reads=[o], writes=[self.outT])
                else:
                    self.out_sink(ti, mi, o)
            for mi, name in ((1, "lru"), (3, "ret")):
                if name in self.do and name not in self.interleave:
                    emit(mi, getattr(self, name + "_tile")(ti))
            gens = []
            if "ret" in self.do and "ret" in self.interleave:
                gens.append((3, "ret", self.ret_gen(ti)))
            if "fox" in self.do:
                gens.append((0, "fox", self.fox_gen(ti)))
            if "rwkv" in self.do:
                gens.append((2, "rwkv", self.rwkv_gen(ti)))
            while gens:
                for item in list(gens):
                    try:
                        next(item[2])
                    except StopIteration:
                        gens.remove(item)
                        emit(item[0], self.results[item[1]])
            if self.tile_done is not None:
                self.tile_done(ti)


RET_THETA = 10000.0

G = 256
FOX_OFF = 0; LRU_OFF = 772; RWKV_OFF = 1284; RET_OFF = 2308
NCH = 17
NPV = 35

def prep_A(d, l, hh):
    w_in = d["w_in"][l]
    own = np.arange(hh * 128, hh * 128 + 128)
    oth = np.arange((1 - hh) * 128, (1 - hh) * 128 + 128)
    swap = np.concatenate([own[h * 64:(h + 1) * 64][np.r_[32:64, 0:32]] for h in range(2)])
    cols = [FOX_OFF + own, FOX_OFF + G + own, FOX_OFF + 2 * G + own,
            LRU_OFF + own, LRU_OFF + G + own,
            RWKV_OFF + own, RWKV_OFF + G + own, RWKV_OFF + 2 * G + own, RWKV_OFF + 3 * G + np.arange(128), RWKV_OFF + 3 * G + 128 + np.arange(128),
            RET_OFF + own, RET_OFF + G + own, RET_OFF + 2 * G + own, RET_OFF + 3 * G + own, RET_OFF + swap, RET_OFF + G + swap,
            RWKV_OFF + 2 * G + oth,
            FOX_OFF + 3 * G + hh * 2 + np.arange(2)]
    cols = np.concatenate(cols)
    Wown = np.ascontiguousarray(w_in[:, cols])
    pv = np.zeros((128, NPV), np.float32)
    c = 0
    def put(v):
        nonlocal c
        pv[:len(v), c] = v; c += 1
    for j in range(4): put(d["lru_conv_w"][l, j, own])
    put(d["lru_conv_b"][l, own]); put(d["lru_ra_b"][l, own]); put(d["lru_ri_b"][l, own]); put(d["lru_lambda"][l, own])
    mu = d["rwkv_mu"][l]
    put(mu[own]); put(mu[G + own]); put(mu[2 * G + own]); put(mu[2 * G + oth]); put(mu[3 * G:3 * G + 128]); put(mu[3 * G + 128:3 * G + 256])
    put(d["rwkv_w0"][l, own]); put(d["rwkv_a0"][l, own]); put(d["rwkv_k_k"][l, own]); put(d["rwkv_k_a"][l, own])
    put(d["rwkv_gn_w"][l, own]); put(d["rwkv_gn_b"][l, own]); put(d["rwkv_r_k"][l].reshape(-1)[own])
    put(d["rwkv_v0"][l - 1, own] if l > 0 else np.zeros(128, np.float32))
    put(d["ret_gn_w"][l, own])
    put(d["fox_f_bias"][l, hh * 2:hh * 2 + 2])
    put((np.arange(128) // 64 + 2 * hh).astype(np.float32)); put(np.full(128, 2 * hh, np.float32)); put(np.full(128, 2 * hh + 1, np.float32))
    assert c == 27
    pv[:, 27:35] = d["norm_mix_pre"][l].reshape(8, 128).T
    def blockdiag(w):
        m = np.zeros((128, 128), np.float32)
        for h in range(2):
            m[h * 64:(h + 1) * 64, h * 64:(h + 1) * 64] = w[2 * hh + h]
        return m
    mats = {"RA": blockdiag(d["lru_ra_w"][l]), "RI": blockdiag(d["lru_ri_w"][l]),
            "WA2": np.ascontiguousarray(np.concatenate([d["rwkv_w2"][l][:, own], d["rwkv_a2"][l][:, own]], axis=0)),
            "G2": np.ascontiguousarray(d["rwkv_g2"][l][:, own])}
    if l > 0:
        v1 = d["rwkv_v1"][l - 1]
        mats["V1"] = np.ascontiguousarray(np.stack([v1[own], v1[oth]], axis=1))
        mats["V2"] = np.ascontiguousarray(d["rwkv_v2"][l - 1][:, own])
    return Wown, pv, mats

MATSH = {"RA": [128, 128], "RI": [128, 128], "WA2": [128, 128], "G2": [128, 128], "V1": [128, 2, 32], "V2": [32, 128]}
NTOK_B = 2048
PAIRS = [[0, 1], [2, 3], [4, 5], [6, 7]]
DEPTH = 2


def _build_fused():
    nc = bass.Bass("TRN2", target_bir_lowering=False)
    D = lambda name, shape, kind="ExternalInput": nc.dram_tensor(name, shape, F32, kind=kind).ap()
    xT = D("xT", [1024, 4096])
    xown = D("xown", [1024, NTOK_B])
    memT = D("memT", [1024, 256])
    selv = D("selv", [128, 2])
    A_in, B_in = [], []
    for l in range(DEPTH):
        names = ["RA", "RI", "WA2", "G2"] + (["V1", "V2"] if l > 0 else [])
        A_in.append({"Wown": D("Wown%d" % l, [1024, NCOLS_A]), "pv": D("pv%d" % l, [128, NPV]),
                     "mats": {n: (D("%s_%d" % (n, l), MATSH[n]), MATSH[n]) for n in names}})
        W = {n: D("%s_%d" % (n, l), [1024, 1024]) for n in ("w_out", "wq", "wk", "wv", "wo")}
        W["w1"] = D("w1_%d" % l, [1024, 4096])
        W["w2"] = D("w2_%d" % l, [4096, 1024])
        B_in.append({"W": W, "gain": D("gain%d" % l, [128, 6, 8])})
    xo = D("xo", [1024, NTOK_B], "ExternalOutput")
    with contextlib.ExitStack() as st:
        k = K(nc, st)
        c = Ctx(k)
        ca = ConstA(k, c)
        sel = k.sb("selv_sb", [128, 2], F32)
        k.dma("sp", sel[:], selv, writes=[sel])
        xo_t = T(xo, "xo")
        xT_t, xown_t, memT_t = T(xT, "xT"), T(xown, "xown"), T(memT, "memT")
        vf_t = k.dram("vf_scr", [128, 4096])
        WB = [None] * DEPTH
        XO = k.dram("xo_scr", [1024, NTOK_B])
        XG = None
        for l in range(DEPTH):
            CA = [k.dram("ca%d_%d" % (l, i), [512, TT]) for i in range(8)]
            CG = [k.dram("cg%d_%d" % (l, i), [1024, TT]) for i in range(8)]
            with k.scope():
                a = A_in[l]
                sa = StageA(k, c, ca, l, xT_t, None, a["Wown"], a["pv"], a["mats"],
                            vfirst_in=vf_t if l > 0 else None, vfirst_out=vf_t if l == 0 else None)
                wb = {}
                for n, ap in B_in[l]["W"].items():
                    R_, C_ = ap.shape
                    t = k.dram("wb_%s_%d" % (n, l), [R_, C_], BF16)
                    src, dst = ap, t.ap
                    if C_ > 1024:
                        src = src.rearrange("r (a c) -> (r a) c", c=1024)
                        dst = dst.rearrange("r (a c) -> (r a) c", c=1024)
                    for r0 in range(0, src.shape[0], 512):
                        k.dma("pool", dst[r0:r0 + 512, :], src[r0:r0 + 512, :], writes=[(t, r0)])
                    wb[n] = t
                WB[l] = wb
                if l > 0:
                    XGl = XG

                    def x_src(ti, xt, XGl=XGl):
                        g = XGl[ti % 4]
                        r = ti // 4
                        k.dma("sp", xt[:], g[r * 1024:(r + 1) * 1024, :].rearrange("(a p) c -> p a c", p=128), reads=[g], writes=[xt])
                    sa.x_src = x_src

                def out_sink(ti, mi, o, CA=CA):
                    k.dma("sp", CA[ti][mi * 128:(mi + 1) * 128, :], o[:], reads=[o], writes=[(CA[ti], mi)])

                def tile_done(ti, CA=CA, CG=CG):
                    k.cc_allgather(CA[ti], CG[ti], PAIRS)
                sa.out_sink, sa.tile_done = out_sink, tile_done
                sa.run(8)
            with k.scope():
                b = B_in[l]
                ws = WStream(k, nbuf=4)
                gain = k.sb("gain_sb%d" % l, [128, 6, 8], F32)
                k.dma("sp", gain[:], b["gain"], writes=[gain])
                ctb = k.sb("ctb_%d" % l, [128, 8, TT], BF16)
                last = (l == DEPTH - 1)
                XA = [k.dram("xa%d_%d" % (l, i), [1024, TT]) for i in range(4)] if not last else None
                XGn = [k.dram("xg%d_%d" % (l, i), [2048, TT]) for i in range(4)] if not last else None
                x_res = xown_t if l == 0 else XO

                def cat_load(j, ct, CG=CG, ctb=ctb):
                    k.dma("pool", ct[:], CG[j][:, :].rearrange("(a p) c -> p a c", p=128), reads=[CG[j]], writes=[ct])
                    k.dma("pool", ctb[:], CG[4 + j][:, :].rearrange("(a p) c -> p a c", p=128), reads=[CG[4 + j]], writes=[ctb])
                    fl = lambda t: t[:].rearrange("p a c -> p (a c)")
                    k.ts(fl(ct), fl(ct), sel[:, 1:2], ALU.mult, reads=[ct, sel], writes=[ct])
                    k.stt(fl(ct), fl(ctb), sel[:, 0:1], fl(ct), ALU.mult, ALU.add, reads=[ctb, sel, ct], writes=[ct])

                def x_in(j, xt, x_res=x_res):
                    k.dma("sp", xt[:], x_res[:, j * TT:(j + 1) * TT].rearrange("(a p) c -> p a c", p=128), reads=[(x_res, j)], writes=[xt])

                def x_out(j, xt, last=last, XA=XA, XGn=XGn):
                    if last:
                        k.dma("sp", xo_t[:, j * TT:(j + 1) * TT].rearrange("(a p) c -> p a c", p=128), xt[:], reads=[xt], writes=[(xo_t, j)])
                    else:
                        k.dma("sp", XO[:, j * TT:(j + 1) * TT].rearrange("(a p) c -> p a c", p=128), xt[:], reads=[xt], writes=[(XO, j)])
                        k.dma("sp", XA[j][:, :].rearrange("(a p) c -> p a c", p=128), xt[:], reads=[xt], writes=[XA[j]])
                        k.cc_allgather(XA[j], XGn[j], PAIRS)

                stage_b(k, c, ws, l, None, None, None, memT_t, WB[l], gain, NTOK_B, cat_load=cat_load, x_in=x_in, x_out=x_out)
                XG = XGn
        k.finish("sp", [xo_t])
        k.finish("pool", [xo_t])
    return nc


def prep_B_gain(d, l):
    gl = lambda name: np.ascontiguousarray(d[name][l].reshape(8, 128).T)
    return np.ascontiguousarray(np.stack([gl(n) for n in ("norm_mix_post", "norm_xa_pre", "norm_xa_post", "norm_mem", "norm_mlp_pre", "norm_mlp_post")], axis=1))


def kernel(**inputs):
    d = {k_: np.asarray(v, dtype=np.float32) for k_, v in inputs.items()}
    B = 4
    nc = _build_fused()
    perm = np.concatenate([np.arange(mi * 256 + r * 128, mi * 256 + r * 128 + 128) for r in range(2) for mi in range(4)])
    shared = {}
    for l in range(DEPTH):
        shared["w_out_%d" % l] = np.ascontiguousarray(d["w_out"][l][perm])
        shared["wq_%d" % l] = d["xa_wq"][l]; shared["wk_%d" % l] = d["xa_wk"][l]
        shared["wv_%d" % l] = d["xa_wv"][l]; shared["wo_%d" % l] = d["xa_wo"][l]
        shared["w1_%d" % l] = d["mlp_w1"][l]; shared["w2_%d" % l] = d["mlp_w2"][l]
        shared["gain%d" % l] = prep_B_gain(d, l)
    prepA = {(l, hh): prep_A(d, l, hh) for l in range(DEPTH) for hh in range(2)}
    ins = []
    for cid in range(8):
        b, r = cid // 2, cid % 2
        xTb = np.ascontiguousarray(d["x"][b].T)
        m = dict(shared)
        m["xT"] = xTb
        m["xown"] = np.ascontiguousarray(xTb[:, r * NTOK_B:(r + 1) * NTOK_B])
        m["memT"] = np.ascontiguousarray(d["mem"][b].T)
        m["selv"] = np.ascontiguousarray(np.tile(np.array([[float(r), 1.0 - float(r)]], np.float32), (128, 1)))
        for l in range(DEPTH):
            Wown, pv, mats = prepA[(l, r)]
            m["Wown%d" % l] = Wown
            m["pv%d" % l] = pv
            for n, v in mats.items():
                m["%s_%d" % (n, l)] = v
        ins.append(m)
    res = run_bass_kernel_spmd(nc, ins, core_ids=list(range(8)))
    out = np.empty((B, 4096, 1024), np.float32)
    for cid in range(8):
        b, r = cid // 2, cid % 2
        out[b, r * NTOK_B:(r + 1) * NTOK_B, :] = res.results[cid]["xo"].T
    return out
```

```python
import contextlib, math
import numpy as np
import concourse.bass as bass
import concourse.mybir as mybir
from concourse.bass_utils import run_bass_kernel_spmd


F32 = mybir.dt.float32
BF16 = mybir.dt.bfloat16
I32 = mybir.dt.int32
AF = mybir.ActivationFunctionType
ALU = mybir.AluOpType


class Reg:
    __slots__ = ("last_w", "reads")

    def __init__(self):
        self.last_w = None
        self.reads = []


class T:
    def __init__(self, ap, name):
        self.ap = ap
        self.name = name
        self.regs = {None: Reg()}

    def __getitem__(self, idx):
        return self.ap[idx]


class K:
    ENGS = ("pe", "dve", "act", "pool", "sp")

    def __init__(self, nc, stack, n_dma_sems=6):
        self.nc = nc
        self.stack = stack
        self.eng = {"pe": nc.tensor, "dve": nc.vector, "act": nc.scalar, "pool": nc.gpsimd, "sp": nc.sync}
        self.sem = {}
        self.tick = {}
        for e in self.ENGS:
            self.sem[e] = stack.enter_context(nc.semaphore("s_" + e))
            self.tick[e] = 0
        self.dq = {}
        for q in ("sp", "pool", "act"):
            lst = []
            for i in range(n_dma_sems):
                key = "d_%s%d" % (q, i)
                self.sem[key] = stack.enter_context(nc.semaphore(key))
                self.tick[key] = 0
                lst.append(key)
            self.dq[q] = [lst, 0]
        self.sem["cc"] = stack.enter_context(nc.semaphore("s_cc"))
        self.tick["cc"] = 0
        self.waited = {e: {} for e in self.ENGS}
        self.ninst = {e: 0 for e in self.ENGS}
        self.nwait = 0

    @contextlib.contextmanager
    def scope(self):
        old = self.stack
        with contextlib.ExitStack() as st:
            self.stack = st
            try:
                yield
            finally:
                self.barrier()
                self.stack = old

    def barrier(self):
        deps = [(sk, v) for sk, v in self.tick.items() if v > 0]
        for e in self.ENGS:
            self._wait(e, [d for d in deps if d[0] != e])

    def cc_allgather(self, in_t, out_t, groups):
        reads = self._norm([in_t])
        writes = self._norm([out_t])
        self._wait("pool", self._deps(reads, writes))
        inst = self.eng["pool"].collective_compute("AllGather", ALU.bypass, replica_groups=groups, ins=[in_t.ap], outs=[out_t.ap])
        self.tick["cc"] += 1
        inst.then_inc(self.sem["cc"], 1)
        self.ninst["pool"] += 1
        self._record(("cc", self.tick["cc"]), reads, writes)
        return inst

    _uid = 0

    def sb(self, name, shape, dtype=F32):
        K._uid += 1
        name = "%s_u%d" % (name, K._uid)
        return T(self.stack.enter_context(self.nc.sbuf_tensor(name, list(shape), dtype)), name)

    def ps(self, name, shape, dtype=F32):
        return T(self.stack.enter_context(self.nc.psum_tensor(name, list(shape), dtype)), name)

    def dram(self, name, shape, dtype=F32, kind="Internal"):
        return T(self.nc.dram_tensor(name, list(shape), dtype, kind=kind).ap(), name)

    @staticmethod
    def _norm(lst):
        out = []
        for x in lst:
            if x is None:
                continue
            if isinstance(x, T):
                out.append((x, None))
            else:
                out.append(x)
        return out

    def _deps(self, reads, writes):
        deps = []
        for (t, k) in reads:
            if k is None:
                for r in t.regs.values():
                    if r.last_w:
                        deps.append(r.last_w)
            else:
                r = t.regs.get(k)
                if r is not None and r.last_w:
                    deps.append(r.last_w)
                if t.regs[None].last_w:
                    deps.append(t.regs[None].last_w)
        for (t, k) in writes:
            if k is None:
                rs = list(t.regs.values())
            else:
                rs = [t.regs[None]]
                if k in t.regs:
                    rs.append(t.regs[k])
            for r in rs:
                if r.last_w:
                    deps.append(r.last_w)
                deps.extend(r.reads)
        return deps

    def _record(self, me, reads, writes):
        for (t, k) in reads:
            t.regs.setdefault(k, Reg()).reads.append(me)
        for (t, k) in writes:
            if k is None:
                for kk in list(t.regs.keys()):
                    if kk is not None:
                        del t.regs[kk]
                r = t.regs[None]
            else:
                r = t.regs.setdefault(k, Reg())
            r.last_w = me
            r.reads = []

    def _wait(self, e, deps):
        best = {}
        for (sk, v) in deps:
            if sk == e and e == "pe":
                continue
            if best.get(sk, 0) < v:
                best[sk] = v
        w = self.waited[e]
        for sk, v in best.items():
            if w.get(sk, 0) < v:
                self.eng[e].wait_ge(self.sem[sk], v)
                w[sk] = v
                self.nwait += 1

    def op(self, e, fn, reads=(), writes=()):
        reads = self._norm(reads)
        writes = self._norm(writes)
        self._wait(e, self._deps(reads, writes))
        inst = fn(self.eng[e])
        self.tick[e] += 1
        inst.then_inc(self.sem[e], 1)
        self.ninst[e] += 1
        self._record((e, self.tick[e]), reads, writes)
        return inst

    def dma(self, q, out, in_, reads=(), writes=(), **kw):
        reads = self._norm(reads)
        writes = self._norm(writes)
        lst, i = self.dq[q]
        sk = lst[i % len(lst)]
        self.dq[q][1] = i + 1
        deps = self._deps(reads, writes)
        if self.tick[sk] > 0:
            deps.append((sk, self.tick[sk]))
        self._wait(q, deps)
        inst = self.eng[q].dma_start(out=out, in_=in_, **kw)
        self.tick[sk] += 16
        inst.then_inc(self.sem[sk], 16)
        self.ninst[q] += 1
        self._record((sk, self.tick[sk]), reads, writes)
        return inst

    def finish(self, e, tiles):
        deps = []
        for t in tiles:
            for r in t.regs.values():
                if r.last_w:
                    deps.append(r.last_w)
        self._wait(e, deps)

    def mm(self, out, lhsT, rhs, start=True, stop=True, reads=(), writes=()):
        return self.op("pe", lambda e: e.matmul(out, lhsT, rhs, start=start, stop=stop), reads, writes)

    def tr(self, out, in_, ident, reads=(), writes=()):
        return self.op("pe", lambda e: e.transpose(out, in_, ident), reads, writes)

    def act(self, out, in_, func, reads=(), writes=(), bias=None, scale=None, e="act"):
        kw = {}
        if bias is not None:
            kw["bias"] = bias
        if scale is not None:
            kw["scale"] = scale
        return self.op(e, lambda g: g.activation(out, in_, func, **kw), reads, writes)

    def tt(self, out, in0, in1, op, reads=(), writes=(), e="dve"):
        return self.op(e, lambda g: g.tensor_tensor(out, in0, in1, op), reads, writes)

    def ts(self, out, in0, s1, op0, s2=None, op1=None, reads=(), writes=(), e="dve"):
        if op1 is None:
            return self.op(e, lambda g: g.tensor_scalar(out, in0, s1, None, op0), reads, writes)
        return self.op(e, lambda g: g.tensor_scalar(out, in0, s1, s2, op0, op1), reads, writes)

    def stt(self, out, in0, scalar, in1, op0, op1, reads=(), writes=()):
        return self.op("dve", lambda g: g.scalar_tensor_tensor(out, in0, scalar, in1, op0, op1), reads, writes)

    def cp(self, out, in_, reads=(), writes=(), e="dve"):
        if e == "act":
            return self.op(e, lambda g: g.copy(out, in_), reads, writes)
        return self.op(e, lambda g: g.tensor_copy(out, in_), reads, writes)

    def memset(self, out, val, writes=(), e="pool"):
        return self.op(e, lambda g: g.memset(out, val), (), writes)


TT = 512
EPS = 1e-6


class Rot:
    def __init__(self, k, name, shape, dtype, n):
        self.t = [k.sb("%s%d" % (name, i), shape, dtype) for i in range(n)]
        self.i = 0

    def next(self):
        t = self.t[self.i % len(self.t)]
        self.i += 1
        return t


class Ctx:
    def __init__(self, k):
        self.k = k
        self.ones = k.sb("ones_bf", [128, 128], BF16)
        k.memset(self.ones[:], 1.0, writes=[self.ones])
        self.eps = k.sb("eps_c", [128, 1], F32)
        k.memset(self.eps[:], EPS, writes=[self.eps])
        self.psums = [k.ps("ps%d" % i, [128, 512], F32) for i in range(8)]
        self.pi = 0
        self.sq = Rot(k, "sq", [128, TT], BF16, 3)
        self.rstd = Rot(k, "rstd", [128, TT], F32, 2)
        self.tmp = Rot(k, "tmpf", [128, TT], F32, 3)

    def psum(self):
        p = self.psums[self.pi % 8]
        self.pi += 1
        return p


def rstd_from(c, srcs, n=TT, nfeat=1024.0):
    k = c.k
    pst = c.psum()
    nk = len(srcs)
    for i, (ap, t) in enumerate(srcs):
        sq = c.sq.next()
        k.act(sq[:, 0:n], ap, AF.Square, reads=[t], writes=[sq])
        k.mm(pst[:, 0:n], c.ones[:], sq[:, 0:n], start=(i == 0), stop=(i == nk - 1), reads=[c.ones, sq], writes=[pst])
    r = c.rstd.next()
    k.act(r[:, 0:n], pst[:, 0:n], AF.Sqrt, reads=[pst, c.eps], writes=[r], bias=c.eps[:], scale=1.0 / nfeat)
    k.op("dve", lambda g: g.reciprocal(r[:, 0:n], r[:, 0:n]), reads=[r], writes=[r])
    return r


def pre_norm(c, xt, gain, gi, out, n=TT):
    k = c.k
    r = rstd_from(c, [(xt[:, kc, 0:n], xt) for kc in range(8)], n)
    for kc in range(8):
        k.stt(out[:, kc, 0:n], xt[:, kc, 0:n], gain[:, gi, kc:kc + 1], r[:, 0:n], ALU.mult, ALU.mult,
              reads=[xt, gain, r], writes=[(out, kc)])


def post_norm_res(c, m, gain, gi, xt, n=TT):
    k = c.k
    r = rstd_from(c, [(m[:, kc, 0:n], m) for kc in range(8)], n)
    for kc in range(8):
        t = c.tmp.next()
        k.stt(t[:, 0:n], m[:, kc, 0:n], gain[:, gi, kc:kc + 1], r[:, 0:n], ALU.mult, ALU.mult,
              reads=[m, gain, r], writes=[t])
        k.tt(xt[:, kc, 0:n], xt[:, kc, 0:n], t[:, 0:n], ALU.add, reads=[t, (xt, kc)], writes=[(xt, kc)], e="pool")


class WStream:
    def __init__(self, k, nbuf=4, elems=8 * 512):
        self.k = k
        self.elems = elems
        self.pool = Rot(k, "wbuf", [128, elems], BF16, nbuf)

    def load(self, w_ap, r0, nk, c0, ncols):
        buf = self.pool.next()
        view = buf[:, 0:nk * ncols].rearrange("p (a c) -> p a c", a=nk)
        if isinstance(w_ap, T):
            src = w_ap[r0:r0 + nk * 128, c0:c0 + ncols].rearrange("(a p) c -> p a c", p=128)
            self.qi = getattr(self, "qi", 0) + 1
            self.k.dma("sp", view, src, reads=[w_ap], writes=[buf])
        else:
            src = w_ap[r0:r0 + nk * 128, c0:c0 + ncols].rearrange("(a p) c -> p a c", p=128)
            self.k.dma("pool", view, src, writes=[buf])
        return buf, view


def pre_norm_split(c, xt, gain, gi, out, n=TT):
    k = c.k
    for kc in range(8):
        k.ts(out[:, kc, 0:n], xt[:, kc, 0:n], gain[:, gi, kc:kc + 1], ALU.mult, reads=[xt, gain], writes=[(out, kc)])
    return lambda: rstd_from(c, [(xt[:, kc, 0:n], xt) for kc in range(8)], n)


def dense(c, ws, w_ap, nk, nout_chunks, rhs_fn, rhs_reads, sink, n=TT, cols_per_load=512, hook=None):
    k = c.k
    per = cols_per_load // 128
    for o0 in range(0, nout_chunks, per):
        buf, view = ws.load(w_ap, 0, nk, o0 * 128, cols_per_load)
        for j in range(per):
            oc = o0 + j
            ps = c.psum()
            for kc in range(nk):
                k.mm(ps[:, 0:n], view[:, kc, j * 128:(j + 1) * 128], rhs_fn(kc), start=(kc == 0), stop=(kc == nk - 1),
                     reads=[buf] + rhs_reads(kc), writes=[ps])
            if hook is not None and oc == 0:
                hook()
            sink(oc, ps)


def stage_b(k, c, ws, l, catT, xT_in, xT_out, memT, W, gain, NT, tok0=0, cat_load=None, x_in=None, x_out=None):
    m = k.sb("m_%d" % l, [128, 8, TT], F32)
    memt = m
    k.dma("sp", memt[:, :, 0:256], memT[:, :].rearrange("(a p) c -> p a c", p=128), reads=[memT], writes=[memt])
    mn = k.sb("mn_%d" % l, [128, 8, 256], BF16)
    pre_norm(c, memt, gain, 3, mn, n=256)
    kT = k.sb("kT_%d" % l, [128, 8, 256], BF16)
    vM = k.sb("vM_%d" % l, [128, 2, 1024], BF16)

    def sink_k(oc, ps):
        k.cp(kT[:, oc, :], ps[:, 0:256], reads=[ps], writes=[(kT, oc)], e="act")

    dense(c, ws, W["wk"], 8, 8, lambda kc: mn[:, kc, :], lambda kc: [(mn, kc)], sink_k, n=256)
    for half in range(2):
        buf, view = ws.load(W["wv"], 0, 8, half * 512, 512)
        for mc in range(2):
            ps = c.psum()
            for kc in range(8):
                k.mm(ps[:, :], mn[:, kc, mc * 128:(mc + 1) * 128], view[:, kc, :], start=(kc == 0), stop=(kc == 7),
                     reads=[buf, (mn, kc)], writes=[ps])
            k.cp(vM[:, mc, half * 512:(half + 1) * 512], ps[:, :], reads=[ps], writes=[(vM, (mc, half))], e="act")

    xpool = Rot(k, "xt_%d" % l, [128, 8, TT], F32, 2)
    cpool = Rot(k, "ct_%d" % l, [128, 8, TT], BF16, 1)
    hb = k.sb("hb_%d" % l, [128, 8, TT], BF16)
    qT = k.sb("qT_%d" % l, [128, 8, TT], BF16)
    oT = k.sb("oT_%d" % l, [128, 8, TT], BF16)
    hid = k.sb("hid_%d" % l, [128, 32, TT], BF16)
    pT = Rot(k, "pT_%d" % l, [128, TT], BF16, 4)
    relu = Rot(k, "relu_%d" % l, [128, TT], F32, 3)
    rden = Rot(k, "rden_%d" % l, [128, TT], F32, 2)
    r2buf = k.sb("r2buf_%d" % l, [128, TT], F32)

    def sink_m(oc, ps):
        k.cp(m[:, oc, :], ps[:, :], reads=[ps], writes=[(m, oc)], e="dve")

    for ti in range(NT // TT):
        t0 = tok0 + ti * TT
        xt = xpool.next()
        if x_in is None:
            k.dma("sp", xt[:], xT_in[:, t0:t0 + TT].rearrange("(a p) c -> p a c", p=128), reads=[xT_in], writes=[xt])
        elif ti == 0:
            x_in(ti, xt)
        if x_in is not None and ti + 1 < NT // TT:
            xnext = xpool.t[(xpool.i) % len(xpool.t)]
            x_in(ti + 1, xnext)
        ct = cpool.next()
        if cat_load is None:
            k.dma("pool", ct[:], catT[:, t0:t0 + TT].rearrange("(a p) c -> p a c", p=128), reads=[catT], writes=[ct])
        elif ti == 0:
            cat_load(ti, ct)
        dense(c, ws, W["w_out"], 8, 8, lambda kc: ct[:, kc, :], lambda kc: [ct], sink_m)
        if cat_load is not None and ti + 1 < NT // TT:
            cat_load(ti + 1, ct)
        post_norm_res(c, m, gain, 0, xt)
        stats_q = pre_norm_split(c, xt, gain, 1, hb)
        rq = {}

        def hook_q():
            rq["r"] = stats_q()

        def sink_q(oc, ps):
            r_ = rq["r"]
            k.tt(qT[:, oc, :], ps[:, :], r_[:], ALU.mult, reads=[ps, r_], writes=[(qT, oc)])

        dense(c, ws, W["wq"], 8, 8, lambda kc: hb[:, kc, :], lambda kc: [(hb, kc)], sink_q, hook=hook_q)
        for hd in range(4):
            pts = []
            for mc in range(2):
                ps = c.psum()
                for j in range(2):
                    dc = hd * 2 + j
                    k.mm(ps[:, :], kT[:, dc, mc * 128:(mc + 1) * 128], qT[:, dc, :], start=(j == 0), stop=(j == 1),
                         reads=[(kT, dc), (qT, dc)], writes=[ps])
                pt = pT.next()
                k.act(pt[:], ps[:, :], AF.Exp, reads=[ps], writes=[pt], scale=1.0 / 16.0)
                pts.append(pt)
            pden = c.psum()
            for mc in range(2):
                k.mm(pden[:, :], c.ones[:], pts[mc][:], start=(mc == 0), stop=(mc == 1), reads=[c.ones, pts[mc]], writes=[pden])
            rd = rden.next()
            k.op("dve", lambda g: g.reciprocal(rd[:], pden[:, :]), reads=[pden], writes=[rd])
            for j in range(2):
                dc = hd * 2 + j
                ps = c.psum()
                for mc in range(2):
                    k.mm(ps[:, :], vM[:, mc, dc * 128:(dc + 1) * 128], pts[mc][:], start=(mc == 0), stop=(mc == 1),
                         reads=[vM, pts[mc]], writes=[ps])
                k.tt(oT[:, dc, :], ps[:, :], rd[:], ALU.mult, reads=[ps, rd], writes=[(oT, dc)])
        dense(c, ws, W["wo"], 8, 8, lambda kc: oT[:, kc, :], lambda kc: [(oT, kc)], sink_m)
        post_norm_res(c, m, gain, 2, xt)
        stats_m = pre_norm_split(c, xt, gain, 4, hb)

        def hook_m():
            r_ = stats_m()
            k.tt(r2buf[:], r_[:], r_[:], ALU.mult, reads=[r_], writes=[r2buf])

        def sink_h(f, ps):
            r = relu.next()
            k.act(r[:], ps[:, :], AF.Relu, reads=[ps], writes=[r])
            k.tt(hid[:, f, :], r[:], r[:], ALU.mult, reads=[r], writes=[(hid, f)], e="pool")

        dense(c, ws, W["w1"], 8, 32, lambda kc: hb[:, kc, :], lambda kc: [(hb, kc)], sink_h, hook=hook_m)
        for fb in range(8):
            buf, view = ws.load(W["w2"], fb * 512, 4, 0, 1024)
            for oc in range(8):
                ps = c.psums[oc]
                for f4 in range(4):
                    f = fb * 4 + f4
                    k.mm(ps[:, :], view[:, f4, oc * 128:(oc + 1) * 128], hid[:, f, :], start=(f == 0), stop=(f == 31),
                         reads=[buf, (hid, f)], writes=[ps])
        for oc in range(8):
            k.tt(m[:, oc, :], c.psums[oc][:, :], r2buf[:], ALU.mult, reads=[c.psums[oc], r2buf], writes=[(m, oc)])
        post_norm_res(c, m, gain, 5, xt)
        if x_out is None:
            k.dma("sp", xT_out[:, t0:t0 + TT].rearrange("(a p) c -> p a c", p=128), xt[:], reads=[xt], writes=[xT_out])
        else:
            x_out(ti, xt)


(CH_FQ, CH_FK, CH_FV, CH_LX, CH_LY, CH_RR, CH_RK, CH_RV, CH_RWA, CH_RGD,
 CH_TQ, CH_TK, CH_TV, CH_TG, CH_TQS, CH_TKS, CH_RVO) = range(17)
NCH = 17
NCOLS_A = NCH * 128 + 2

(PV_CW0, PV_CW1, PV_CW2, PV_CW3, PV_CB, PV_RAB, PV_RIB, PV_LAM,
 PV_MU_R, PV_MU_K, PV_MU_V, PV_MU_VO, PV_MU_WA, PV_MU_GD,
 PV_W0, PV_A0, PV_KK, PV_KA, PV_GNW, PV_GNB, PV_RK, PV_V0,
 PV_TGN, PV_FB, PV_HP, PV_H0, PV_H1) = range(27)
PV_GAIN = 27
NPV = 35


class ConstA:
    def __init__(self, k, c):
        self.k = k
        nc = k.nc
        g = k.eng["pool"]
        self.ident = k.sb("identf", [128, 128], F32)
        k.memset(self.ident[:], 1.0, writes=[self.ident])
        k.op("pool", lambda e: e.affine_select(out=self.ident[:], in_=self.ident[:], pattern=[[-1, 128]], compare_op=ALU.is_equal,
                                               fill=0.0, base=0, channel_multiplier=1), reads=[self.ident], writes=[self.ident])
        self.ut_f = k.sb("ut_f", [128, 128], F32)
        k.memset(self.ut_f[:], 1.0, writes=[self.ut_f])
        k.op("pool", lambda e: e.affine_select(out=self.ut_f[:], in_=self.ut_f[:], pattern=[[1, 128]], compare_op=ALU.is_ge,
                                               fill=0.0, base=0, channel_multiplier=-1), reads=[self.ut_f], writes=[self.ut_f])
        self.ut_b = k.sb("ut_b", [128, 128], BF16)
        k.cp(self.ut_b[:], self.ut_f[:], reads=[self.ut_f], writes=[self.ut_b])
        self.blk = k.sb("blk_f", [128, 128], F32)
        k.memset(self.blk[:], 0.0, writes=[self.blk])
        k.memset(self.blk[0:64, 0:64], 1.0, writes=[self.blk])
        k.memset(self.blk[64:128, 64:128], 1.0, writes=[self.blk])
        self.ones_f = k.sb("ones_f", [128, 512], F32)
        k.memset(self.ones_f[:], 1.0, writes=[self.ones_f])
        self.one_c = k.sb("one_c", [128, 1], F32)
        k.memset(self.one_c[:], 1.0, writes=[self.one_c])
        self.zero_c = k.sb("zero_c", [128, 1], F32)
        k.memset(self.zero_c[:], 0.0, writes=[self.zero_c])


def gelu_tanh(k, c, y_ps, y_t, tmps):
    y = tmps.next()
    k.cp(y[:], y_ps, reads=[y_t], writes=[y], e="act")
    t = tmps.next()
    k.tt(t[:], y[:], y[:], ALU.mult, reads=[y], writes=[t])
    k.ts(t[:], t[:], 0.044715, ALU.mult, 1.0, ALU.add, reads=[t], writes=[t])
    k.tt(t[:], t[:], y[:], ALU.mult, reads=[t, y], writes=[t])
    k.act(t[:], t[:], AF.Sigmoid, reads=[t], writes=[t], scale=2.0 * math.sqrt(2.0 / math.pi))
    return y, t


class StageA:
    def __init__(self, k, c, ca, l, xT, outT, Wown, pvd, mats, vfirst_in=None, vfirst_out=None, do=("lru", "ret", "fox", "rwkv")):
        self.k, self.c, self.ca, self.l = k, c, ca, l
        self.xT, self.outT = xT, outT
        self.do = do
        self.x_src = self.out_sink = self.tile_done = None
        self.results = {}
        self.interleave = ()
        self.f32r = False
        self.vf_in, self.vf_out = vfirst_in, vfirst_out
        L = "a%d_" % l
        self.L = L
        self.W = k.sb(L + "W", [128, 8, NCOLS_A], BF16)
        for kc in range(8):
            k.dma("pool", self.W[:, kc, :], Wown[kc * 128:(kc + 1) * 128, :], writes=[(self.W, kc)], max_dma_last_dim=4096)
        self.pv = k.sb(L + "pv", [128, NPV], F32)
        k.dma("sp", self.pv[:], pvd, writes=[self.pv])
        self.mats = {}
        for name, (ap, shape) in mats.items():
            t = k.sb(L + name, shape, F32)
            k.dma("sp", t[:], ap, writes=[t])
            self.mats[name] = t
        self.xpool = Rot(k, L + "xt", [128, 8, TT], F32, 1)
        self.hb = k.sb(L + "hb", [128, 8, TT], BF16)
        self.tmps = Rot(k, L + "t", [128, TT], F32, 12)
        self.outs = Rot(k, L + "o", [128, TT], F32, 3)
        self.named_ps = c.psums[5:8]
        if "lru" in do:
            self.init_lru()
        if "ret" in do:
            self.init_ret()
        if "fox" in do:
            self.init_fox()
        if "rwkv" in do:
            self.init_rwkv()

    def pcol(self, i):
        return self.pv[:, i:i + 1]

    def psum(self):
        c = self.c
        p = c.psums[c.pi % 5]
        c.pi += 1
        return p

    def proj(self, ch, n0=0, n1=TT):
        k = self.k
        ps = self.psum()
        for kc in range(8):
            k.mm(ps[:, n0:n1], self.W[:, kc, ch * 128:(ch + 1) * 128], self.hb[:, kc, n0:n1], start=(kc == 0), stop=(kc == 7),
                 reads=[(self.W, kc), (self.hb, kc)], writes=[ps])
        return ps

    def init_lru(self):
        k, L = self.k, self.L
        self.l_xbuf = k.sb(L + "lxb", [128, 3 + TT], F32)
        k.memset(self.l_xbuf[:, 0:3], 0.0, writes=[self.l_xbuf])
        self.l_h = k.sb(L + "lh", [128, 1], F32)
        k.memset(self.l_h[:], 0.0, writes=[self.l_h])
        self.l_c1 = k.sb(L + "lc1", [128, 2], F32)
        k.act(self.l_c1[:, 0:1], self.pcol(PV_LAM), AF.Exp, reads=[self.pv], writes=[self.l_c1], scale=-1.0)
        k.act(self.l_c1[:, 0:1], self.l_c1[:, 0:1], AF.Ln, reads=[self.l_c1, self.ca.one_c], writes=[self.l_c1], bias=self.ca.one_c[:])
        k.ts(self.l_c1[:, 1:2], self.l_c1[:, 0:1], -16.0, ALU.mult, reads=[self.l_c1], writes=[self.l_c1])
        k.ts(self.l_c1[:, 0:1], self.l_c1[:, 0:1], -8.0, ALU.mult, reads=[self.l_c1], writes=[self.l_c1])

    def lru_tile(self, ti):
        k, c, ca = self.k, self.c, self.ca
        xb = self.l_xbuf
        ps = self.proj(CH_LX)
        k.cp(xb[:, 3:3 + TT], ps[:, :], reads=[ps], writes=[xb], e="act")
        xc = self.tmps.next()
        k.ts(xc[:], xb[:, 0:TT], self.pcol(PV_CW0), ALU.mult, self.pcol(PV_CB), ALU.add, reads=[xb, self.pv], writes=[xc])
        for j in range(1, 4):
            k.stt(xc[:], xb[:, j:j + TT], self.pcol(PV_CW0 + j), xc[:], ALU.mult, ALU.add, reads=[xb, self.pv, xc], writes=[xc])
        k.cp(xb[:, 0:3], xb[:, TT:TT + 3], reads=[xb], writes=[xb], e="dve")
        pr = self.psum()
        k.mm(pr[:, :], self.mats["RA"][:], xc[:], reads=[self.mats["RA"], xc], writes=[pr])
        pi_ = self.psum()
        k.mm(pi_[:, :], self.mats["RI"][:], xc[:], reads=[self.mats["RI"], xc], writes=[pi_])
        r = self.tmps.next()
        k.act(r[:], pr[:, :], AF.Sigmoid, reads=[pr, self.pv], writes=[r], bias=self.pcol(PV_RAB))
        ig = self.tmps.next()
        k.act(ig[:], pi_[:, :], AF.Sigmoid, reads=[pi_, self.pv], writes=[ig], bias=self.pcol(PV_RIB))
        a = self.tmps.next()
        k.act(a[:], r[:], AF.Exp, reads=[r, self.l_c1], writes=[a], scale=self.l_c1[:, 0:1])
        mult = self.tmps.next()
        k.act(mult[:], r[:], AF.Exp, reads=[r, self.l_c1], writes=[mult], scale=self.l_c1[:, 1:2])
        k.act(mult[:], mult[:], AF.Sqrt, reads=[mult, ca.one_c], writes=[mult], scale=-1.0, bias=ca.one_c[:])
        k.tt(ig[:], ig[:], xc[:], ALU.mult, reads=[ig, xc], writes=[ig])
        k.tt(ig[:], ig[:], mult[:], ALU.mult, reads=[ig, mult], writes=[ig])
        h = self.tmps.next()
        k.op("dve", lambda g: g.tensor_tensor_scan(h[:], a[:], ig[:], self.l_h[:, 0:1], ALU.mult, ALU.add),
             reads=[a, ig, self.l_h], writes=[h])
        k.cp(self.l_h[:], h[:, TT - 1:TT], reads=[h], writes=[self.l_h], e="dve")
        py = self.proj(CH_LY)
        o = self.outs.next()
        y, t = gelu_tanh(k, c, py[:, :], py, self.tmps)
        k.tt(o[:], t[:], y[:], ALU.mult, reads=[t, y], writes=[o])
        k.tt(o[:], o[:], h[:], ALU.mult, reads=[o, h], writes=[o])
        return o

    def init_ret(self):
        k, L, ca = self.k, self.L, self.ca
        self.t_lg = k.sb(L + "tlg", [128, 3], F32)
        tmp = k.sb(L + "tlgt", [128, 3], F32)
        k.ts(tmp[:], self.pv[:, PV_HP:PV_HP + 3], 5.0, ALU.add, reads=[self.pv], writes=[tmp])
        k.act(tmp[:], tmp[:], AF.Exp, reads=[tmp], writes=[tmp], scale=-math.log(2.0))
        k.act(self.t_lg[:], tmp[:], AF.Ln, reads=[tmp, ca.one_c], writes=[self.t_lg], scale=-1.0, bias=ca.one_c[:])
        ni = k.sb(L + "tni", [128, 128], I32)
        k.op("pool", lambda e: e.iota(ni[:], pattern=[[1, 128]], base=0, channel_multiplier=0), writes=[ni])
        nf = k.sb(L + "tnf", [128, 128], F32)
        k.cp(nf[:], ni[:], reads=[ni], writes=[nf])
        self.t_xi = k.sb(L + "txi", [128, 128], F32)
        self.t_zt = k.sb(L + "tzt", [128, 128], F32)
        tt_ = k.sb(L + "ttmp", [128, 128], F32)
        k.ts(tt_[:], nf[:], 1.0, ALU.add, reads=[nf], writes=[tt_])
        k.act(self.t_xi[:], tt_[:], AF.Exp, reads=[tt_, self.t_lg], writes=[self.t_xi], scale=self.t_lg[:, 0:1])
        k.ts(tt_[:], nf[:], -1.0, ALU.mult, 127.0, ALU.add, reads=[nf], writes=[tt_])
        k.act(self.t_zt[:], tt_[:], AF.Exp, reads=[tt_, self.t_lg], writes=[self.t_zt], scale=self.t_lg[:, 0:1])
        k.ts(self.t_zt[:], self.t_zt[:], 0.125, ALU.mult, reads=[self.t_zt], writes=[self.t_zt])
        self.t_gc = k.sb(L + "tgc", [128, 1], F32)
        k.act(self.t_gc[:], self.t_lg[:, 0:1], AF.Exp, reads=[self.t_lg], writes=[self.t_gc], scale=128.0)
        di = k.sb(L + "tdi", [128, 128], I32)
        k.op("pool", lambda e: e.iota(di[:], pattern=[[1, 128]], base=0, channel_multiplier=-1), writes=[di])
        df = k.sb(L + "tdf", [128, 128], F32)
        k.cp(df[:], di[:], reads=[di], writes=[df])
        k.ts(df[:], df[:], 0.0, ALU.max, reads=[df], writes=[df])
        self.t_dt = k.sb(L + "tdt", [128, 2, 128], F32)
        for h in range(2):
            k.act(self.t_dt[:, h, :], df[:], AF.Exp, reads=[df, self.t_lg], writes=[self.t_dt], scale=self.t_lg[:, 1 + h:2 + h])
            k.stt(self.t_dt[:, h, :], self.t_dt[:, h, :], 0.125, ca.ut_f[:], ALU.mult, ALU.mult, reads=[self.t_dt, ca.ut_f], writes=[self.t_dt])
        pi_ = k.sb(L + "tpi", [128, 2], I32)
        k.op("pool", lambda e: e.iota(pi_[:, 0:1], pattern=[[0, 1]], base=0, channel_multiplier=1), writes=[pi_])
        k.ts(pi_[:, 1:2], pi_[:, 0:1], 32, ALU.bitwise_and, reads=[pi_], writes=[pi_])
        k.ts(pi_[:, 0:1], pi_[:, 0:1], 31, ALU.bitwise_and, reads=[pi_], writes=[pi_])
        pf = k.sb(L + "tpf", [128, 2], F32)
        k.cp(pf[:], pi_[:], reads=[pi_], writes=[pf])
        self.t_inv = k.sb(L + "tinv", [128, 1], F32)
        k.act(self.t_inv[:], pf[:, 0:1], AF.Exp, reads=[pf], writes=[self.t_inv], scale=-math.log(RET_THETA) / 31.0)
        self.t_sgn = k.sb(L + "tsgn", [128, 1], F32)
        k.ts(self.t_sgn[:], pf[:, 1:2], 1.0 / 16.0, ALU.mult, -1.0, ALU.add, reads=[pf], writes=[self.t_sgn])
        qi = k.sb(L + "tqi", [128, TT], I32)
        k.op("pool", lambda e: e.iota(qi[:], pattern=[[1, TT]], base=0, channel_multiplier=0), writes=[qi])
        self.t_pos = k.sb(L + "tpos", [128, TT], F32)
        k.cp(self.t_pos[:], qi[:], reads=[qi], writes=[self.t_pos])
        self.t_R = k.sb(L + "tR", [128, 64], F32)
        k.memset(self.t_R[:], 0.0, writes=[self.t_R])
        self.t_Rb = k.sb(L + "tRb", [128, 64], BF16)
        k.memset(self.t_Rb[:], 0.0, writes=[self.t_Rb])
        self.t_small = Rot(k, L + "tsm", [128, 128], BF16, 4)
        self.t_vtm = Rot(k, L + "tvtm", [128, 4, 128], BF16, 1)
        self.t_bf = Rot(k, L + "tbf", [128, TT], BF16, 4)

    def sincos(self, t0):
        k, ca = self.k, self.ca
        TWO_PI = 2.0 * math.pi
        C1 = 6.28125
        C2 = TWO_PI - C1
        res = []
        for shift in (math.pi / 2.0, 0.0):
            ang = self.tmps.next()
            k.ts(ang[:], self.t_pos[:], float(t0), ALU.add, self.t_inv[:, 0:1], ALU.mult, reads=[self.t_pos, self.t_inv], writes=[ang])
            if shift:
                k.ts(ang[:], ang[:], shift, ALU.add, reads=[ang], writes=[ang])
            kf = self.tmps.next()
            ki = kf[:].bitcast(I32)
            k.ts(kf[:], ang[:], 1.0 / TWO_PI, ALU.mult, reads=[ang], writes=[kf])
            k.cp(ki, kf[:], reads=[kf], writes=[kf])
            k.cp(kf[:], ki, reads=[kf], writes=[kf])
            k.stt(ang[:], kf[:], -C1, ang[:], ALU.mult, ALU.add, reads=[kf, ang], writes=[ang])
            k.stt(ang[:], kf[:], -C2, ang[:], ALU.mult, ALU.add, reads=[kf, ang], writes=[ang])
            k.ts(kf[:], ang[:], math.pi, ALU.is_gt, -TWO_PI, ALU.mult, reads=[ang], writes=[kf])
            k.tt(ang[:], ang[:], kf[:], ALU.add, reads=[ang, kf], writes=[ang])
            k.ts(kf[:], ang[:], -math.pi, ALU.is_lt, TWO_PI, ALU.mult, reads=[ang], writes=[kf])
            k.tt(ang[:], ang[:], kf[:], ALU.add, reads=[ang, kf], writes=[ang])
            k.ts(ang[:], ang[:], math.pi, ALU.min, -math.pi, ALU.max, reads=[ang], writes=[ang])
            k.act(ang[:], ang[:], AF.Sin, reads=[ang], writes=[ang])
            res.append(ang)
        C, S = res
        k.ts(S[:], S[:], self.t_sgn[:, 0:1], ALU.mult, reads=[S, self.t_sgn], writes=[S])
        return C, S

    def ret_gen(self, ti):
        k, c, ca = self.k, self.c, self.ca
        t0 = ti * TT
        C, S = self.sincos(t0)
        yield
        rot = []
        for (cha, chs) in ((CH_TQ, CH_TQS), (CH_TK, CH_TKS)):
            pa = self.proj(cha)
            a = self.tmps.next()
            k.tt(a[:], pa[:, :], C[:], ALU.mult, reads=[pa, C], writes=[a])
            pb = self.proj(chs)
            b = self.tmps.next()
            k.tt(b[:], pb[:, :], S[:], ALU.mult, reads=[pb, S], writes=[b])
            k.tt(a[:], a[:], b[:], ALU.add, reads=[a, b], writes=[a], e="pool")
            rot.append(a)
            yield
        qr, kr = rot
        qb = self.t_bf.next()
        k.cp(qb[:], qr[:], reads=[qr], writes=[qb], e="act")
        kb = self.t_bf.next()
        k.cp(kb[:], kr[:], reads=[kr], writes=[kb], e="act")
        qx = self.t_bf.next()
        k.tt(qx[:].rearrange("p (c n) -> p c n", n=128), qr[:].rearrange("p (c n) -> p c n", n=128),
             self.t_xi[:].unsqueeze(1).to_broadcast([128, 4, 128]), ALU.mult, reads=[qr, self.t_xi], writes=[qx])
        kz = self.tmps.next()
        k.tt(kz[:].rearrange("p (c n) -> p c n", n=128), kr[:].rearrange("p (c n) -> p c n", n=128),
             self.t_zt[:].unsqueeze(1).to_broadcast([128, 4, 128]), ALU.mult, reads=[kr, self.t_zt], writes=[kz])
        vtm = self.t_vtm.next()
        for cc in range(4):
            ps = self.psum()
            for kc in range(8):
                k.mm(ps[:, 0:128], self.hb[:, kc, cc * 128:(cc + 1) * 128], self.W[:, kc, CH_TV * 128:(CH_TV + 1) * 128],
                     start=(kc == 0), stop=(kc == 7), reads=[(self.W, kc), (self.hb, kc)], writes=[ps])
            k.cp(vtm[:, cc, :], ps[:, 0:128], reads=[ps], writes=[(vtm, cc)], e="act")
            yield
        osb = self.tmps.next()
        for cc in range(4):
            cs = slice(cc * 128, (cc + 1) * 128)
            ptr = self.psum()
            k.tr(ptr[:, 0:128], kz[:, cs], ca.ident[:], reads=[kz, ca.ident], writes=[ptr])
            kzt = self.t_small.next()
            k.cp(kzt[:], ptr[:, 0:128], reads=[ptr], writes=[kzt], e="act")
            pkv = self.psum()
            po = self.psum()
            for h in range(2):
                hp = slice(h * 64, (h + 1) * 64)
                pin = self.psum()
                k.mm(pin[:, 0:128], kb[hp, cs], qb[hp, cs], reads=[kb, qb], writes=[pin])
                ind = self.t_small.next()
                k.tt(ind[:], pin[:, 0:128], self.t_dt[:, h, :], ALU.mult, reads=[pin, self.t_dt], writes=[ind])
                k.mm(po[hp, 0:128], vtm[:, cc, hp], ind[:], start=True, stop=False, reads=[(vtm, cc), ind], writes=[po])
                k.mm(po[hp, 0:128], self.t_Rb[hp, :], qx[hp, cs], start=False, stop=True, reads=[self.t_Rb, qx], writes=[po])
                k.mm(pkv[hp, 0:64], kzt[:, hp], vtm[:, cc, hp], reads=[kzt, (vtm, cc)], writes=[pkv])
            k.stt(self.t_R[:], self.t_R[:], self.t_gc[:, 0:1], pkv[:, 0:64], ALU.mult, ALU.add, reads=[self.t_R, self.t_gc, pkv], writes=[self.t_R])
            k.cp(self.t_Rb[:], self.t_R[:], reads=[self.t_R], writes=[self.t_Rb], e="act")
            k.cp(osb[:, cs], po[:, 0:128], reads=[po], writes=[osb], e="act")
            yield
        sq = self.tmps.next()
        k.tt(sq[:], osb[:], osb[:], ALU.mult, reads=[osb], writes=[sq], e="pool")
        pss = self.psum()
        k.mm(pss[:, :], ca.blk[:], sq[:], reads=[ca.blk, sq], writes=[pss])
        rs = self.tmps.next()
        k.act(rs[:], pss[:, :], AF.Sqrt, reads=[pss, c.eps], writes=[rs], bias=c.eps[:], scale=1.0 / 64.0)
        k.op("dve", lambda g: g.reciprocal(rs[:], rs[:]), reads=[rs], writes=[rs])
        pg = self.proj(CH_TG)
        sg = self.tmps.next()
        k.act(sg[:], pg[:, :], AF.Silu, reads=[pg], writes=[sg])
        o = self.outs.next()
        k.stt(o[:], osb[:], self.pcol(PV_TGN), rs[:], ALU.mult, ALU.mult, reads=[osb, self.pv, rs], writes=[o])
        k.tt(o[:], o[:], sg[:], ALU.mult, reads=[o, sg], writes=[o])
        self.results["ret"] = o

    def ret_tile(self, ti):
        for _ in self.ret_gen(ti):
            pass
        return self.results["ret"]


    def init_fox(self):
        k, L, ca = self.k, self.L, self.ca
        self.f_kT = k.sb(L + "fkT", [128, 4096], BF16)
        self.f_v = k.sb(L + "fv", [128, 32, 128], BF16)
        self.f_q = k.sb(L + "fq", [128, TT], BF16)
        self.f_row = Rot(k, L + "frow", [2, TT], F32, 3)
        self.f_clast = k.sb(L + "fcl", [2, 1], F32)
        k.memset(self.f_clast[:], 0.0, writes=[self.f_clast])
        self.f_ccol = k.sb(L + "fcc", [128, 32, 2], F32)
        self.f_cend = k.sb(L + "fce", [128, 4, 2], F32)
        self.f_B = Rot(k, L + "fB", [128, 32, 4], F32, 2)
        self.f_pT = Rot(k, L + "fpT", [128, TT], BF16, 4)
        self.f_rd = k.sb(L + "frd", [128, TT], F32)
        self.f_sel = k.sb(L + "fsel", [128, 128], F32)
        k.memset(self.f_sel[:], 1.0, writes=[self.f_sel])
        k.op("pool", lambda e: e.affine_select(out=self.f_sel[:], in_=self.f_sel[:], pattern=[[0, 128]], compare_op=ALU.is_equal,
                                               fill=0.0, base=-127, channel_multiplier=1), reads=[self.f_sel], writes=[self.f_sel])
        self.f_nb = k.sb(L + "fnb", [2, 1], F32)
        k.ts(self.f_nb[:], self.pv[0:2, PV_FB:PV_FB + 1], -1.0, ALU.mult, reads=[self.pv], writes=[self.f_nb])

    def fox_gen(self, ti):
        k, c, ca = self.k, self.c, self.ca
        t0 = ti * TT
        pq = self.proj(CH_FQ)
        k.cp(self.f_q[:], pq[:, :], reads=[pq], writes=[self.f_q], e="act")
        pk = self.proj(CH_FK)
        k.cp(self.f_kT[:, t0:t0 + TT], pk[:, :], reads=[pk], writes=[(self.f_kT, ti)], e="act")
        for cc in range(4):
            ps = self.psum()
            for kc in range(8):
                k.mm(ps[:, 0:128], self.hb[:, kc, cc * 128:(cc + 1) * 128], self.W[:, kc, CH_FV * 128:(CH_FV + 1) * 128],
                     start=(kc == 0), stop=(kc == 7), reads=[(self.W, kc), (self.hb, kc)], writes=[ps])
            k.cp(self.f_v[:, ti * 4 + cc, :], ps[:, 0:128], reads=[ps], writes=[(self.f_v, ti * 4 + cc)], e="act")
        pf = self.psum()
        for kc in range(8):
            k.mm(pf[0:2, :], self.W[:, kc, NCH * 128:NCH * 128 + 2], self.hb[:, kc, :], start=(kc == 0), stop=(kc == 7),
                 reads=[(self.W, kc), (self.hb, kc)], writes=[pf])
        e_ = self.f_row.next()
        k.act(e_[:], pf[0:2, :], AF.Exp, reads=[pf, self.f_nb], writes=[e_], scale=-1.0, bias=self.f_nb[:])
        k.act(e_[:], e_[:], AF.Ln, reads=[e_, ca.one_c], writes=[e_], bias=ca.one_c[0:2, :])
        cs = self.f_row.next()
        k.op("dve", lambda g: g.tensor_tensor_scan(cs[:], ca.ones_f[0:2, :], e_[:], self.f_clast[:, 0:1], ALU.mult, ALU.add),
             reads=[ca.ones_f, e_, self.f_clast], writes=[cs])
        k.cp(self.f_clast[:], cs[:, TT - 1:TT], reads=[cs], writes=[self.f_clast])
        pc = self.psum()
        for cc in range(4):
            k.tr(pc[:, 2 * cc:2 * cc + 2], cs[:, cc * 128:(cc + 1) * 128], ca.ident[0:2, 0:2], reads=[cs, ca.ident], writes=[pc])
        k.cp(self.f_ccol[:, ti * 4:(ti + 1) * 4, :].rearrange("p a b -> p (a b)"), pc[:, 0:8], reads=[pc], writes=[(self.f_ccol, ti)])
        pe = self.psum()
        k.mm(pe[:, 0:8], self.f_sel[:], self.f_ccol[:, ti * 4:(ti + 1) * 4, :].rearrange("p a b -> p (a b)"),
             reads=[self.f_sel, (self.f_ccol, ti)], writes=[pe])
        k.cp(self.f_cend[:].rearrange("p a b -> p (a b)"), pe[:, 0:8], reads=[pe], writes=[self.f_cend])
        po, pd = self.named_ps[0], self.named_ps[1]
        nkc = 4 * (ti + 1)
        yield
        for h in range(2):
            hp = slice(h * 64, (h + 1) * 64)
            B = self.f_B.next()
            k.tt(B[:, 0:nkc, :], self.f_ccol[:, 0:nkc, h:h + 1].to_broadcast([128, nkc, 4]),
                 self.f_cend[:, :, h].unsqueeze(1).to_broadcast([128, nkc, 4]), ALU.subtract,
                 reads=[self.f_ccol, self.f_cend], writes=[B])

            def pv_step(kc, pt, col0):
                k.mm(po[hp, col0:TT], self.f_v[:, kc, hp], pt[:, col0:TT], start=(kc == 0), stop=(kc == nkc - 1),
                     reads=[(self.f_v, kc), pt], writes=[po])
                k.mm(pd[hp, col0:TT], c.ones[:, 0:64], pt[:, col0:TT], start=(kc == 0), stop=(kc == nkc - 1),
                     reads=[c.ones, pt], writes=[pd])

            pend = None
            for kc in range(nkc):
                j = kc - 4 * ti
                col0 = 128 * j if j >= 0 else 0
                ps = self.psum()
                k.mm(ps[:, col0:TT], self.f_kT[hp, kc * 128:(kc + 1) * 128], self.f_q[hp, col0:TT],
                     reads=[(self.f_kT, kc // 4), self.f_q], writes=[ps])
                pt = self.f_pT.next()
                for qbl in range(col0 // 128, 4):
                    qs = slice(qbl * 128, (qbl + 1) * 128)
                    k.act(pt[:, qs], ps[:, qs], AF.Exp, reads=[ps, B], writes=[pt], scale=0.125, bias=B[:, kc, qbl:qbl + 1])
                if j >= 0:
                    qs = slice(j * 128, (j + 1) * 128)
                    k.tt(pt[:, qs], pt[:, qs], ca.ut_b[:], ALU.mult, reads=[pt, ca.ut_b], writes=[pt], e="pool")
                if pend is not None:
                    pv_step(*pend)
                pend = (kc, pt, col0)
                yield
            pv_step(*pend)
            yield
        rd = self.f_rd
        k.op("dve", lambda g: g.reciprocal(rd[:], pd[:, :]), reads=[pd], writes=[rd])
        o = self.outs.next()
        k.tt(o[:], po[:, :], rd[:], ALU.mult, reads=[po, rd], writes=[o])
        self.results["fox"] = o

    def fox_tile(self, ti):
        for _ in self.fox_gen(ti):
            pass
        return self.results["fox"]

    def init_rwkv(self):
        k, L, ca = self.k, self.L, self.ca
        self.TR = 256
        self.NCK = self.TR // 64
        NCK = self.NCK
        self.w_carry = k.sb(L + "wcar", [128, 6], F32)
        k.memset(self.w_carry[:], 0.0, writes=[self.w_carry])
        self.w_AR = k.sb(L + "wAR", [128, NCK, 2, 2, 64], F32)
        self.w_BK = k.sb(L + "wBK", [128, NCK, 2, 2, 64], F32)
        self.w_HAT = k.sb(L + "wHAT", [128, NCK, 3, 2, 64], F32)
        for t in (self.w_AR, self.w_BK, self.w_HAT):
            k.memset(t[:], 0.0, writes=[t])
        self.w_H = k.sb(L + "wH", [128, 128], F32)
        k.memset(self.w_H[:], 0.0, writes=[self.w_H])
        self.w_ms = k.sb(L + "wms", [128, 128], F32)
        k.memset(self.w_ms[:], 1.0, writes=[self.w_ms])
        k.op("pool", lambda e: e.affine_select(out=self.w_ms[:], in_=self.w_ms[:], pattern=[[1, 128]], compare_op=ALU.is_gt,
                                               fill=0.0, base=0, channel_multiplier=-1), reads=[self.w_ms], writes=[self.w_ms])
        self.w_mst = k.sb(L + "wmst", [128, 128], F32)
        k.memset(self.w_mst[:], 1.0, writes=[self.w_mst])
        k.op("pool", lambda e: e.affine_select(out=self.w_mst[:], in_=self.w_mst[:], pattern=[[-1, 128]], compare_op=ALU.is_gt,
                                               fill=0.0, base=0, channel_multiplier=1), reads=[self.w_mst], writes=[self.w_mst])
        self.w_mist = k.sb(L + "wmist", [128, 2, 64], F32)
        for j in range(2):
            k.cp(self.w_mist[0:64, j, :], ca.ut_f[0:64, 0:64], reads=[ca.ut_f], writes=[self.w_mist])
            k.cp(self.w_mist[64:128, j, :], ca.ut_f[64:128, 64:128], reads=[ca.ut_f], writes=[self.w_mist])
        self.w_chm = k.sb(L + "wchm", [128, self.TR], F32)
        k.memset(self.w_chm[:], 1.0, writes=[self.w_chm])
        k.memset(self.w_chm[:].rearrange("p (c n) -> p c n", n=64)[:, :, 0:1], 0.0, writes=[self.w_chm])
        self.w_omka = k.sb(L + "womka", [128, 1], F32)
        k.ts(self.w_omka[:], self.pcol(PV_KA), -1.0, ALU.mult, 1.0, ALU.add, reads=[self.pv], writes=[self.w_omka])
        self.w_gc = k.sb(L + "wgc", [128, NCK], F32)
        self.w_rot = {n: Rot(k, L + "w" + n, [128, 128], F32, 2) for n in ("w1", "u")}
        self.w_out3 = {n: [k.sb(L + "w%s%d" % (n, i), [128, 128], F32) for i in range(3)] for n in ("nak", "nst", "bh", "kh", "vb", "x")}
        self.w_ptb = [[k.sb(L + "wpt%d_%d" % (i, j), [128, 128], F32) for j in range(2)] for i in range(3)]
        self.w_pxb = [[k.sb(L + "wpx%d_%d" % (i, j), [128, 256], F32) for j in range(2)] for i in range(3)]
        self.w_v32 = k.sb(L + "wv32", [32, self.TR], F32)
        self.w_gneps = k.sb(L + "wgne", [128, 1], F32)
        k.memset(self.w_gneps[:], 64e-5, writes=[self.w_gneps])

    def nb(self, i):
        TR = self.TR
        xt = self.cur_xt
        return xt[:, i // 2, (i % 2) * TR:(i % 2 + 1) * TR], (xt, ("nb", i))

    def rwkv_gen(self, ti):
        o = self.outs.next()
        for half in range(self.TT_R):
            yield from self.rwkv_half(ti, half, o)
        self.results["rwkv"] = o

    def rwkv_tile(self, ti):
        for _ in self.rwkv_gen(ti):
            pass
        return self.results["rwkv"]

    TT_R = 2

    def rwkv_half(self, ti, half, o):
        k, c, ca, l = self.k, self.c, self.ca, self.l
        TR, NCK = self.TR, self.NCK
        n0 = half * TR
        n1 = n0 + TR
        g0 = ti * TT + n0
        pv = self.pv
        (R, dR), (Kk, dK), (V, dV), (WA, dWA), (GD, dGD), (VO, dVO), (A, dA), (LW, dLW), (LG, dLG), (KKN, dKKN), (BV, dBV), \
            (RT, dRT), (TA, dTA), (TB, dTB), (TC, dTC), (Y, dY) = [self.nb(i) for i in range(16)]
        car = self.w_carry

        def lerp(X, dX, ch, mu_i, ci):
            ps = self.proj(ch, n0, n1)
            k.cp(X, ps[:, n0:n1], reads=[ps], writes=[dX], e="act")
            k.tt(TA[:, 1:TR], X[:, 0:TR - 1], X[:, 1:TR], ALU.subtract, reads=[dX], writes=[dTA])
            k.tt(TA[:, 0:1], car[:, ci:ci + 1], X[:, 0:1], ALU.subtract, reads=[dX, car], writes=[dTA])
            k.cp(car[:, ci:ci + 1], X[:, TR - 1:TR], reads=[dX], writes=[car])
            k.stt(X, TA, self.pcol(mu_i), X, ALU.mult, ALU.add, reads=[dTA, pv, dX], writes=[dX])

        lerp(R, dR, CH_RR, PV_MU_R, 0)
        yield
        lerp(Kk, dK, CH_RK, PV_MU_K, 1)
        yield
        lerp(V, dV, CH_RV, PV_MU_V, 2)
        yield
        lerp(WA, dWA, CH_RWA, PV_MU_WA, 3)
        yield
        lerp(GD, dGD, CH_RGD, PV_MU_GD, 4)
        yield
        if l > 0:
            lerp(VO, dVO, CH_RVO, PV_MU_VO, 5)
            yield
        WA2, G2 = self.mats["WA2"], self.mats["G2"]
        k.act(TB[0:64, :], WA[0:64, :], AF.Tanh, reads=[dWA], writes=[dTB])
        pz = self.psum()
        k.mm(pz[:, 0:TR], WA2[0:64, :], TB[0:64, :], reads=[WA2, dTB], writes=[pz])
        k.act(LW, pz[:, 0:TR], AF.Sigmoid, reads=[pz, pv], writes=[dLW], bias=self.pcol(PV_W0))
        k.ts(LW, LW, -math.exp(-0.5), ALU.mult, reads=[dLW], writes=[dLW])
        pa = self.psum()
        k.mm(pa[:, 0:TR], WA2[64:128, :], WA[64:128, :], reads=[WA2, dWA], writes=[pa])
        k.act(A, pa[:, 0:TR], AF.Sigmoid, reads=[pa, pv], writes=[dA], bias=self.pcol(PV_A0))
        k.act(GD, GD, AF.Sigmoid, reads=[dGD], writes=[dGD])
        pg = self.psum()
        k.mm(pg[:, 0:TR], G2[:], GD, reads=[G2, dGD], writes=[pg])
        k.cp(GD, pg[:, 0:TR], reads=[pg], writes=[dGD], e="act")
        if l > 0:
            V1, V2 = self.mats["V1"], self.mats["V2"]
            p1 = self.psum()
            k.mm(p1[0:32, 0:TR], V1[:, 0, :], V, start=True, stop=False, reads=[V1, dV], writes=[p1])
            k.mm(p1[0:32, 0:TR], V1[:, 1, :], VO, start=False, stop=True, reads=[V1, dVO], writes=[p1])
            k.cp(self.w_v32[:], p1[0:32, 0:TR], reads=[p1], writes=[self.w_v32], e="act")
            p2 = self.psum()
            k.mm(p2[:, 0:TR], V2[:], self.w_v32[:], reads=[V2, self.w_v32], writes=[p2])
            k.act(TB, p2[:, 0:TR], AF.Sigmoid, reads=[p2, pv], writes=[dTB], bias=self.pcol(PV_V0))
            k.dma("sp", TC, self.vf_in[:, g0:g0 + TR], reads=[self.vf_in], writes=[dTC])
            k.tt(TC, TC, V, ALU.subtract, reads=[dTC, dV], writes=[dTC])
            k.tt(TC, TC, TB, ALU.mult, reads=[dTC, dTB], writes=[dTC])
            k.tt(V, V, TC, ALU.add, reads=[dV, dTC], writes=[dV])
        else:
            k.dma("sp", self.vf_out[:, g0:g0 + TR], V, reads=[dV], writes=[self.vf_out])
        yield
        k.ts(KKN, Kk, self.pcol(PV_KK), ALU.mult, reads=[dK, pv], writes=[dKKN])
        k.tt(TB, KKN, KKN, ALU.mult, reads=[dKKN], writes=[dTB])
        pn = self.psum()
        k.mm(pn[:, 0:TR], ca.blk[:], TB, reads=[ca.blk, dTB], writes=[pn])
        k.act(TB, pn[:, 0:TR], AF.Sqrt, reads=[pn], writes=[dTB])
        k.ts(TB, TB, 1e-12, ALU.max, reads=[dTB], writes=[dTB])
        k.op("dve", lambda g: g.reciprocal(TB, TB), reads=[dTB], writes=[dTB])
        k.tt(KKN, KKN, TB, ALU.mult, reads=[dKKN, dTB], writes=[dKKN])
        k.ts(TB, A, self.pcol(PV_KA), ALU.mult, self.w_omka[:, 0:1], ALU.add, reads=[dA, pv, self.w_omka], writes=[dTB])
        k.tt(Kk, Kk, TB, ALU.mult, reads=[dK, dTB], writes=[dK])
        k.tt(BV, KKN, A, ALU.mult, reads=[dKKN, dA], writes=[dBV])
        yield
        k.op("dve", lambda g: g.tensor_tensor_scan(LG, self.w_chm[:], LW, 0.0, ALU.mult, ALU.add),
             reads=[self.w_chm, dLW], writes=[dLG])
        AR, BK, HAT = self.w_AR, self.w_BK, self.w_HAT
        v3 = lambda ap: ap.rearrange("p (c n) -> p c n", n=64)

        def to_blk(dst, which, src, dsrc, mul=None, dmul=None, op="mult"):
            for h in range(2):
                hp = slice(h * 64, (h + 1) * 64)
                if mul is None:
                    k.cp(dst[hp, :, which, h, :], v3(src[hp, :]), reads=[dsrc], writes=[dst])
                else:
                    k.tt(dst[hp, :, which, h, :], v3(src[hp, :]), v3(mul[hp, :]), ALU.mult, reads=[dsrc, dmul], writes=[dst])

        k.act(TB, LG, AF.Exp, reads=[dLG], writes=[dTB])
        k.tt(RT, R, TB, ALU.mult, reads=[dR, dTB], writes=[dRT])
        to_blk(AR, 1, RT, dRT)
        k.act(self.w_gc[:], v3(LG)[:, :, 63], AF.Exp, reads=[dLG], writes=[self.w_gc])
        yield
        k.tt(TB, LG, LW, ALU.subtract, reads=[dLG, dLW], writes=[dTB])
        k.act(TB, TB, AF.Exp, reads=[dTB], writes=[dTB])
        k.stt(TC, KKN, -1.0, TB, ALU.mult, ALU.mult, reads=[dKKN, dTB], writes=[dTC])
        to_blk(AR, 0, TC, dTC)
        k.act(TB, LG, AF.Exp, reads=[dLG], writes=[dTB], scale=-1.0)
        to_blk(BK, 0, BV, dBV, TB, dTB)
        to_blk(BK, 1, Kk, dK, TB, dTB)
        yield
        k.tt(v3(TB), v3(LG)[:, :, 63:64].to_broadcast([128, NCK, 64]), v3(LG), ALU.subtract, reads=[dLG], writes=[dTB])
        k.act(TB, TB, AF.Exp, reads=[dTB], writes=[dTB])
        to_blk(HAT, 0, BV, dBV, TB, dTB)
        to_blk(HAT, 1, Kk, dK, TB, dTB)
        to_blk(HAT, 2, V, dV)
        py = self.named_ps[2]
        ident = ca.ident
        m2 = lambda ap: ap.rearrange("p a b -> p (a b)")
        H = self.w_H
        P = {}
        if self.f32r:
            F32R = mybir.dt.float32r

            def mmr(out, lhsT, rhs, **kw):
                return k.mm(out, lhsT.bitcast(F32R), rhs.bitcast(F32R), **kw)
        else:
            mmr = k.mm

        def prep(cc):
            slot = cc % 3
            pxs, pts = self.w_pxb[slot], self.w_ptb[slot]
            o3 = {n: self.w_out3[n][slot] for n in self.w_out3}
            cs = slice(cc * 64, (cc + 1) * 64)
            Ab = m2(AR[:, cc, 0])
            Bb, Kb = m2(BK[:, cc, 0]), m2(BK[:, cc, 1])
            pA = self.psum()
            mmr(pA[:, 0:128], Bb, Ab, reads=[BK, AR], writes=[pA])
            mmr(pA[:, 128:256], Kb, Ab, reads=[BK, AR], writes=[pA])
            pC = self.psum()
            mmr(pC[:, 0:128], Ab, Bb, reads=[BK, AR], writes=[pC])
            pS = self.psum()
            mmr(pS[:, 0:64], Bb, RT[:, cs], reads=[BK, dRT], writes=[pS])
            mmr(pS[:, 64:128], Kb, RT[:, cs], reads=[BK, dRT], writes=[pS])
            pi_ = 0
            px = pxs[pi_]
            k.tt(px[:, 0:128], pA[:, 0:128], self.w_ms[:], ALU.mult, reads=[pA, self.w_ms], writes=[px])
            k.tt(px[:, 128:256], px[:, 0:128], ident[:], ALU.add, reads=[px, ident], writes=[px], e="pool")
            pt = pts[pi_]
            k.tt(pt[:], pC[:, 0:128], self.w_mst[:], ALU.mult, reads=[pC, self.w_mst], writes=[pt])
            nak = o3["nak"]
            k.tt(nak[:], pA[:, 128:256], self.w_ms[:], ALU.mult, reads=[pA, self.w_ms], writes=[nak])
            nst = o3["nst"]
            k.tt(nst[:], pS[:, 0:128], m2(self.w_mist[:]), ALU.mult, reads=[pS, self.w_mist], writes=[nst])
            yield
            pP = self.psum()
            mmr(pP[:, 0:128], pt[:], px[:, 0:128], reads=[pt, px], writes=[pP])
            pQ = self.psum()
            mmr(pQ[:, 0:128], px[:, 0:128], pt[:], reads=[pt, px], writes=[pQ])
            pi_ ^= 1
            px2, pt2 = pxs[pi_], pts[pi_]
            k.cp(px2[:, 0:128], pP[:, 0:128], reads=[pP], writes=[px2], e="act")
            k.cp(px2[:, 128:256], px[:, 128:256], reads=[px], writes=[px2], e="pool")
            k.cp(pt2[:], pQ[:, 0:128], reads=[pQ], writes=[pt2], e="act")
            px, pt = px2, pt2
            yield
            X = o3["x"]
            for lvl in range(1, 6):
                last = (lvl == 5)
                pP = self.psum()
                if not last:
                    mmr(pP[:, 0:256], pt[:], px[:, 0:256], reads=[pt, px], writes=[pP])
                    pQ = self.psum()
                    mmr(pQ[:, 0:128], px[:, 0:128], pt[:], reads=[pt, px], writes=[pQ])
                    pi_ ^= 1
                    px2, pt2 = pxs[pi_], pts[pi_]
                    k.tt(px2[:, 128:256], pP[:, 128:256], px[:, 128:256], ALU.add, reads=[pP, px], writes=[px2])
                    k.cp(px2[:, 0:128], pP[:, 0:128], reads=[pP], writes=[px2], e="act")
                    k.cp(pt2[:], pQ[:, 0:128], reads=[pQ], writes=[pt2], e="act")
                    px, pt = px2, pt2
                else:
                    mmr(pP[:, 128:256], pt[:], px[:, 128:256], reads=[pt, px], writes=[pP])
                    k.tt(X[:], pP[:, 128:256], px[:, 128:256], ALU.add, reads=[pP, px], writes=[X])
                yield
            tm = []
            for wi, nm in enumerate(("bh", "kh", "vb")):
                ptr = self.psum()
                k.tr(ptr[:, 0:128], m2(HAT[:, cc, wi]), ident[:], reads=[HAT, ident], writes=[ptr])
                tbuf = o3[nm]
                k.cp(tbuf[:], ptr[:, 0:128], reads=[ptr], writes=[tbuf], e="act")
                tm.append(tbuf)
                yield
            P[cc] = (X, nak, nst) + tuple(tm)

        def chain(cc):
            cs = slice(cc * 64, (cc + 1) * 64)
            Ab = m2(AR[:, cc, 0])
            X, nak, nst, bh, kh, vb = P[cc]
            pW = self.psum()
            mmr(pW[:, 0:128], Ab, H[:], start=True, stop=False, reads=[AR, H], writes=[pW])
            mmr(pW[:, 0:128], nak[:], vb[:], start=False, stop=True, reads=[nak, vb], writes=[pW])
            w1 = self.w_rot["w1"].next()
            k.cp(w1[:], pW[:, 0:128], reads=[pW], writes=[w1], e="dve")
            yield
            pU = self.psum()
            mmr(pU[:, 0:128], X[:], w1[:], reads=[X, w1], writes=[pU])
            u = self.w_rot["u"].next()
            k.cp(u[:], pU[:, 0:128], reads=[pU], writes=[u], e="dve")
            yield
            mmr(py[:, cs], H[:], RT[:, cs], start=True, stop=False, reads=[H, dRT], writes=[py])
            mmr(py[:, cs], u[:], nst[:, 0:64], start=False, stop=False, reads=[u, nst], writes=[py])
            mmr(py[:, cs], vb[:], nst[:, 64:128], start=False, stop=True, reads=[vb, nst], writes=[py])
            pH = self.psum()
            mmr(pH[:, 0:128], bh[:], u[:], start=True, stop=False, reads=[bh, u], writes=[pH])
            mmr(pH[:, 0:128], kh[:], vb[:], start=False, stop=True, reads=[kh, vb], writes=[pH])
            k.stt(H[:], H[:], self.w_gc[:, cc:cc + 1], pH[:, 0:128], ALU.mult, ALU.add, reads=[H, self.w_gc, pH], writes=[H])
            yield

        active = {}

        def start(cc_):
            if cc_ < NCK:
                active[cc_] = prep(cc_)

        def step_preps():
            for cc_ in list(active):
                try:
                    next(active[cc_])
                except StopIteration:
                    del active[cc_]

        start(0)
        start(1)
        while 0 in active:
            step_preps()
            yield
        for cc in range(NCK):
            start(cc + 2)
            while cc in active:
                step_preps()
                yield
            gc = chain(cc)
            done = False
            while not done:
                try:
                    next(gc)
                except StopIteration:
                    done = True
                step_preps()
                yield
        k.cp(Y, py[:, 0:TR], reads=[py], writes=[dY], e="act")
        pm = self.psum()
        k.mm(pm[:, 0:TR], ca.blk[:], Y, reads=[ca.blk, dY], writes=[pm])
        k.stt(Y, pm[:, 0:TR], -1.0 / 64.0, Y, ALU.mult, ALU.add, reads=[pm, dY], writes=[dY])
        k.tt(TB, Y, Y, ALU.mult, reads=[dY], writes=[dTB])
        pvv = self.psum()
        k.mm(pvv[:, 0:TR], ca.blk[:], TB, reads=[ca.blk, dTB], writes=[pvv])
        k.act(TB, pvv[:, 0:TR], AF.Sqrt, reads=[pvv, self.w_gneps], writes=[dTB], bias=self.w_gneps[:], scale=1.0 / 64.0)
        k.op("dve", lambda g: g.reciprocal(TB, TB), reads=[dTB], writes=[dTB])
        k.tt(Y, Y, TB, ALU.mult, reads=[dY, dTB], writes=[dY])
        k.ts(Y, Y, self.pcol(PV_GNW), ALU.mult, self.pcol(PV_GNB), ALU.add, reads=[dY, pv], writes=[dY])
        yield
        k.stt(TB, R, self.pcol(PV_RK), Kk, ALU.mult, ALU.mult, reads=[dR, pv, dK], writes=[dTB])
        pb = self.psum()
        k.mm(pb[:, 0:TR], ca.blk[:], TB, reads=[ca.blk, dTB], writes=[pb])
        k.tt(TB, pb[:, 0:TR], V, ALU.mult, reads=[pb, dV], writes=[dTB])
        k.tt(Y, Y, TB, ALU.add, reads=[dY, dTB], writes=[dY])
        k.tt(o[:, n0:n1], Y, GD, ALU.mult, reads=[dY, dGD], writes=[o])

    def run(self, ntiles):
        k, c = self.k, self.c
        gainA = self.pv
        for ti in range(ntiles):
            t0 = ti * TT
            xt = self.xpool.next()
            self.cur_xt = xt
            if self.x_src is None:
                k.dma("sp", xt[:], self.xT[:, t0:t0 + TT].rearrange("(a p) c -> p a c", p=128), reads=[self.xT], writes=[xt])
            else:
                self.x_src(ti, xt)
            r = rstd_from(c, [(xt[:, kc, :], xt) for kc in range(8)])
            for kc in range(8):
                k.stt(self.hb[:, kc, :], xt[:, kc, :], self.pv[:, PV_GAIN + kc:PV_GAIN + kc + 1], r[:], ALU.mult, ALU.mult,
                      reads=[xt, self.pv, r], writes=[(self.hb, kc)])
            def emit(mi, o):
                if self.out_sink is None:
                    k.dma("sp", self.outT[mi * 128:(mi + 1) * 128, t0:t0 + TT], o[:], reads=[o], writes=[self.outT])
                else:
                    self.out_sink(ti, mi, o)
            for mi, name in ((1, "lru"), (3, "ret")):
                if name in self.do and name not in self.interleave:
                    emit(mi, getattr(self, name + "_tile")(ti))
            gens = []
            if "ret" in self.do and "ret" in self.interleave:
                gens.append((3, "ret", self.ret_gen(ti)))
            if "fox" in self.do:
                gens.append((0, "fox", self.fox_gen(ti)))
            if "rwkv" in self.do:
                gens.append((2, "rwkv", self.rwkv_gen(ti)))
            while gens:
                for item in list(gens):
                    try:
                        next(item[2])
                    except StopIteration:
                        gens.remove(item)
                        emit(item[0], self.results[item[1]])
            if self.tile_done is not None:
                self.tile_done(ti)


RET_THETA = 10000.0

G = 256
FOX_OFF = 0; LRU_OFF = 772; RWKV_OFF = 1284; RET_OFF = 2308
NCH = 17
NPV = 35

def prep_A(d, l, hh):
    w_in = d["w_in"][l]
    own = np.arange(hh * 128, hh * 128 + 128)
    oth = np.arange((1 - hh) * 128, (1 - hh) * 128 + 128)
    swap = np.concatenate([own[h * 64:(h + 1) * 64][np.r_[32:64, 0:32]] for h in range(2)])
    cols = [FOX_OFF + own, FOX_OFF + G + own, FOX_OFF + 2 * G + own,
            LRU_OFF + own, LRU_OFF + G + own,
            RWKV_OFF + own, RWKV_OFF + G + own, RWKV_OFF + 2 * G + own, RWKV_OFF + 3 * G + np.arange(128), RWKV_OFF + 3 * G + 128 + np.arange(128),
            RET_OFF + own, RET_OFF + G + own, RET_OFF + 2 * G + own, RET_OFF + 3 * G + own, RET_OFF + swap, RET_OFF + G + swap,
            RWKV_OFF + 2 * G + oth,
            FOX_OFF + 3 * G + hh * 2 + np.arange(2)]
    cols = np.concatenate(cols)
    Wown = np.ascontiguousarray(w_in[:, cols])
    pv = np.zeros((128, NPV), np.float32)
    c = 0
    def put(v):
        nonlocal c
        pv[:len(v), c] = v; c += 1
    for j in range(4): put(d["lru_conv_w"][l, j, own])
    put(d["lru_conv_b"][l, own]); put(d["lru_ra_b"][l, own]); put(d["lru_ri_b"][l, own]); put(d["lru_lambda"][l, own])
    mu = d["rwkv_mu"][l]
    put(mu[own]); put(mu[G + own]); put(mu[2 * G + own]); put(mu[2 * G + oth]); put(mu[3 * G:3 * G + 128]); put(mu[3 * G + 128:3 * G + 256])
    put(d["rwkv_w0"][l, own]); put(d["rwkv_a0"][l, own]); put(d["rwkv_k_k"][l, own]); put(d["rwkv_k_a"][l, own])
    put(d["rwkv_gn_w"][l, own]); put(d["rwkv_gn_b"][l, own]); put(d["rwkv_r_k"][l].reshape(-1)[own])
    put(d["rwkv_v0"][l - 1, own] if l > 0 else np.zeros(128, np.float32))
    put(d["ret_gn_w"][l, own])
    put(d["fox_f_bias"][l, hh * 2:hh * 2 + 2])
    put((np.arange(128) // 64 + 2 * hh).astype(np.float32)); put(np.full(128, 2 * hh, np.float32)); put(np.full(128, 2 * hh + 1, np.float32))
    assert c == 27
    pv[:, 27:35] = d["norm_mix_pre"][l].reshape(8, 128).T
    def blockdiag(w):
        m = np.zeros((128, 128), np.float32)
        for h in range(2):
            m[h * 64:(h + 1) * 64, h * 64:(h + 1) * 64] = w[2 * hh + h]
        return m
    mats = {"RA": blockdiag(d["lru_ra_w"][l]), "RI": blockdiag(d["lru_ri_w"][l]),
            "WA2": np.ascontiguousarray(np.concatenate([d["rwkv_w2"][l][:, own], d["rwkv_a2"][l][:, own]], axis=0)),
            "G2": np.ascontiguousarray(d["rwkv_g2"][l][:, own])}
    if l > 0:
        v1 = d["rwkv_v1"][l - 1]
        mats["V1"] = np.ascontiguousarray(np.stack([v1[own], v1[oth]], axis=1))
        mats["V2"] = np.ascontiguousarray(d["rwkv_v2"][l - 1][:, own])
    return Wown, pv, mats

MATSH = {"RA": [128, 128], "RI": [128, 128], "WA2": [128, 128], "G2": [128, 128], "V1": [128, 2, 32], "V2": [32, 128]}
NTOK_B = 2048
PAIRS = [[0, 1], [2, 3], [4, 5], [6, 7]]
DEPTH = 2


def _build_fused():
    nc = bass.Bass("TRN2", target_bir_lowering=False)
    D = lambda name, shape, kind="ExternalInput": nc.dram_tensor(name, shape, F32, kind=kind).ap()
    xT = D("xT", [1024, 4096])
    xown = D("xown", [1024, NTOK_B])
    memT = D("memT", [1024, 256])
    selv = D("selv", [128, 2])
    A_in, B_in = [], []
    for l in range(DEPTH):
        names = ["RA", "RI", "WA2", "G2"] + (["V1", "V2"] if l > 0 else [])
        A_in.append({"Wown": D("Wown%d" % l, [1024, NCOLS_A]), "pv": D("pv%d" % l, [128, NPV]),
                     "mats": {n: (D("%s_%d" % (n, l), MATSH[n]), MATSH[n]) for n in names}})
        W = {n: D("%s_%d" % (n, l), [1024, 1024]) for n in ("w_out", "wq", "wk", "wv", "wo")}
        W["w1"] = D("w1_%d" % l, [1024, 4096])
        W["w2"] = D("w2_%d" % l, [4096, 1024])
        B_in.append({"W": W, "gain": D("gain%d" % l, [128, 6, 8])})
    xo = D("xo", [1024, NTOK_B], "ExternalOutput")
    with contextlib.ExitStack() as st:
        k = K(nc, st)
        c = Ctx(k)
        ca = ConstA(k, c)
        sel = k.sb("selv_sb", [128, 2], F32)
        k.dma("sp", sel[:], selv, writes=[sel])
        xo_t = T(xo, "xo")
        xT_t, xown_t, memT_t = T(xT, "xT"), T(xown, "xown"), T(memT, "memT")
        vf_t = k.dram("vf_scr", [128, 4096])
        WB = [None] * DEPTH
        XO = k.dram("xo_scr", [1024, NTOK_B])
        XG = None
        for l in range(DEPTH):
            CA = [k.dram("ca%d_%d" % (l, i), [512, TT]) for i in range(8)]
            CG = [k.dram("cg%d_%d" % (l, i), [1024, TT]) for i in range(8)]
            with k.scope():
                a = A_in[l]
                sa = StageA(k, c, ca, l, xT_t, None, a["Wown"], a["pv"], a["mats"],
                            vfirst_in=vf_t if l > 0 else None, vfirst_out=vf_t if l == 0 else None)
                wb = {}
                for n, ap in B_in[l]["W"].items():
                    R_, C_ = ap.shape
                    t = k.dram("wb_%s_%d" % (n, l), [R_, C_], BF16)
                    src, dst = ap, t.ap
                    if C_ > 1024:
                        src = src.rearrange("r (a c) -> (r a) c", c=1024)
                        dst = dst.rearrange("r (a c) -> (r a) c", c=1024)
                    for r0 in range(0, src.shape[0], 512):
                        k.dma("pool", dst[r0:r0 + 512, :], src[r0:r0 + 512, :], writes=[(t, r0)])
                    wb[n] = t
                WB[l] = wb
                if l > 0:
                    XGl = XG

                    def x_src(ti, xt, XGl=XGl):
                        g = XGl[ti % 4]
                        r = ti // 4
                        k.dma("sp", xt[:], g[r * 1024:(r + 1) * 1024, :].rearrange("(a p) c -> p a c", p=128), reads=[g], writes=[xt])
                    sa.x_src = x_src

                def out_sink(ti, mi, o, CA=CA):
                    k.dma("sp", CA[ti][mi * 128:(mi + 1) * 128, :], o[:], reads=[o], writes=[(CA[ti], mi)])

                def tile_done(ti, CA=CA, CG=CG):
                    k.cc_allgather(CA[ti], CG[ti], PAIRS)
                sa.out_sink, sa.tile_done = out_sink, tile_done
                sa.interleave = ("ret",)
                sa.run(8)
            with k.scope():
                b = B_in[l]
                ws = WStream(k, nbuf=4)
                gain = k.sb("gain_sb%d" % l, [128, 6, 8], F32)
                k.dma("sp", gain[:], b["gain"], writes=[gain])
                ctb = k.sb("ctb_%d" % l, [128, 8, TT], BF16)
                last = (l == DEPTH - 1)
                XA = [k.dram("xa%d_%d" % (l, i), [1024, TT]) for i in range(4)] if not last else None
                XGn = [k.dram("xg%d_%d" % (l, i), [2048, TT]) for i in range(4)] if not last else None
                x_res = xown_t if l == 0 else XO

                def cat_load(j, ct, CG=CG, ctb=ctb):
                    k.dma("pool", ct[:], CG[j][:, :].rearrange("(a p) c -> p a c", p=128), reads=[CG[j]], writes=[ct])
                    k.dma("pool", ctb[:], CG[4 + j][:, :].rearrange("(a p) c -> p a c", p=128), reads=[CG[4 + j]], writes=[ctb])
                    fl = lambda t: t[:].rearrange("p a c -> p (a c)")
                    k.ts(fl(ct), fl(ct), sel[:, 1:2], ALU.mult, reads=[ct, sel], writes=[ct])
                    k.stt(fl(ct), fl(ctb), sel[:, 0:1], fl(ct), ALU.mult, ALU.add, reads=[ctb, sel, ct], writes=[ct])

                def x_in(j, xt, x_res=x_res):
                    k.dma("sp", xt[:], x_res[:, j * TT:(j + 1) * TT].rearrange("(a p) c -> p a c", p=128), reads=[(x_res, j)], writes=[xt])

                def x_out(j, xt, last=last, XA=XA, XGn=XGn):
                    if last:
                        k.dma("sp", xo_t[:, j * TT:(j + 1) * TT].rearrange("(a p) c -> p a c", p=128), xt[:], reads=[xt], writes=[(xo_t, j)])
                    else:
                        k.dma("sp", XO[:, j * TT:(j + 1) * TT].rearrange("(a p) c -> p a c", p=128), xt[:], reads=[xt], writes=[(XO, j)])
                        k.dma("sp", XA[j][:, :].rearrange("(a p) c -> p a c", p=128), xt[:], reads=[xt], writes=[XA[j]])
                        k.cc_allgather(XA[j], XGn[j], PAIRS)

                stage_b(k, c, ws, l, None, None, None, memT_t, WB[l], gain, NTOK_B, cat_load=cat_load, x_in=x_in, x_out=x_out)
                XG = XGn
        k.finish("sp", [xo_t])
        k.finish("pool", [xo_t])
    return nc


def prep_B_gain(d, l):
    gl = lambda name: np.ascontiguousarray(d[name][l].reshape(8, 128).T)
    return np.ascontiguousarray(np.stack([gl(n) for n in ("norm_mix_post", "norm_xa_pre", "norm_xa_post", "norm_mem", "norm_mlp_pre", "norm_mlp_post")], axis=1))


def kernel(**inputs):
    d = {k_: np.asarray(v, dtype=np.float32) for k_, v in inputs.items()}
    B = 4
    nc = _build_fused()
    perm = np.concatenate([np.arange(mi * 256 + r * 128, mi * 256 + r * 128 + 128) for r in range(2) for mi in range(4)])
    shared = {}
    for l in range(DEPTH):
        shared["w_out_%d" % l] = np.ascontiguousarray(d["w_out"][l][perm])
        shared["wq_%d" % l] = d["xa_wq"][l]; shared["wk_%d" % l] = d["xa_wk"][l]
        shared["wv_%d" % l] = d["xa_wv"][l]; shared["wo_%d" % l] = d["xa_wo"][l]
        shared["w1_%d" % l] = d["mlp_w1"][l]; shared["w2_%d" % l] = d["mlp_w2"][l]
        shared["gain%d" % l] = prep_B_gain(d, l)
    prepA = {(l, hh): prep_A(d, l, hh) for l in range(DEPTH) for hh in range(2)}
    ins = []
    for cid in range(8):
        b, r = cid // 2, cid % 2
        xTb = np.ascontiguousarray(d["x"][b].T)
        m = dict(shared)
        m["xT"] = xTb
        m["xown"] = np.ascontiguousarray(xTb[:, r * NTOK_B:(r + 1) * NTOK_B])
        m["memT"] = np.ascontiguousarray(d["mem"][b].T)
        m["selv"] = np.ascontiguousarray(np.tile(np.array([[float(r), 1.0 - float(r)]], np.float32), (128, 1)))
        for l in range(DEPTH):
            Wown, pv, mats = prepA[(l, r)]
            m["Wown%d" % l] = Wown
            m["pv%d" % l] = pv
            for n, v in mats.items():
                m["%s_%d" % (n, l)] = v
        ins.append(m)
    res = run_bass_kernel_spmd(nc, ins, core_ids=list(range(8)))
    out = np.empty((B, 4096, 1024), np.float32)
    for cid in range(8):
        b, r = cid // 2, cid % 2
        out[b, r * NTOK_B:(r + 1) * NTOK_B, :] = res.results[cid]["xo"].T
    return out
```

```python
import contextlib, math
import numpy as np
import concourse.bass as bass
import concourse.mybir as mybir
from concourse.bass_utils import run_bass_kernel_spmd


F32 = mybir.dt.float32
BF16 = mybir.dt.bfloat16
I32 = mybir.dt.int32
AF = mybir.ActivationFunctionType
ALU = mybir.AluOpType


class Reg:
    __slots__ = ("last_w", "reads")

    def __init__(self):
        self.last_w = None
        self.reads = []


class T:
    def __init__(self, ap, name):
        self.ap = ap
        self.name = name
        self.regs = {None: Reg()}

    def __getitem__(self, idx):
        return self.ap[idx]


class K:
    ENGS = ("pe", "dve", "act", "pool", "sp")

    def __init__(self, nc, stack, n_dma_sems=6):
        self.nc = nc
        self.stack = stack
        self.eng = {"pe": nc.tensor, "dve": nc.vector, "act": nc.scalar, "pool": nc.gpsimd, "sp": nc.sync}
        self.sem = {}
        self.tick = {}
        for e in self.ENGS:
            self.sem[e] = stack.enter_context(nc.semaphore("s_" + e))
            self.tick[e] = 0
        self.dq = {}
        for q in ("sp", "pool", "act"):
            lst = []
            for i in range(n_dma_sems):
                key = "d_%s%d" % (q, i)
                self.sem[key] = stack.enter_context(nc.semaphore(key))
                self.tick[key] = 0
                lst.append(key)
            self.dq[q] = [lst, 0]
        self.sem["cc"] = stack.enter_context(nc.semaphore("s_cc"))
        self.tick["cc"] = 0
        self.waited = {e: {} for e in self.ENGS}
        self.ninst = {e: 0 for e in self.ENGS}
        self.nwait = 0

    @contextlib.contextmanager
    def scope(self):
        old = self.stack
        with contextlib.ExitStack() as st:
            self.stack = st
            try:
                yield
            finally:
                self.barrier()
                self.stack = old

    def barrier(self):
        deps = [(sk, v) for sk, v in self.tick.items() if v > 0]
        for e in self.ENGS:
            self._wait(e, [d for d in deps if d[0] != e])

    def cc_allgather(self, in_t, out_t, groups):
        reads = self._norm([in_t])
        writes = self._norm([out_t])
        self._wait("pool", self._deps(reads, writes))
        inst = self.eng["pool"].collective_compute("AllGather", ALU.bypass, replica_groups=groups, ins=[in_t.ap], outs=[out_t.ap])
        self.tick["cc"] += 1
        inst.then_inc(self.sem["cc"], 1)
        self.ninst["pool"] += 1
        self._record(("cc", self.tick["cc"]), reads, writes)
        return inst

    _uid = 0

    def sb(self, name, shape, dtype=F32):
        K._uid += 1
        name = "%s_u%d" % (name, K._uid)
        return T(self.stack.enter_context(self.nc.sbuf_tensor(name, list(shape), dtype)), name)

    def ps(self, name, shape, dtype=F32):
        return T(self.stack.enter_context(self.nc.psum_tensor(name, list(shape), dtype)), name)

    def dram(self, name, shape, dtype=F32, kind="Internal"):
        return T(self.nc.dram_tensor(name, list(shape), dtype, kind=kind).ap(), name)

    @staticmethod
    def _norm(lst):
        out = []
        for x in lst:
            if x is None:
                continue
            if isinstance(x, T):
                out.append((x, None))
            else:
                out.append(x)
        return out

    def _deps(self, reads, writes):
        deps = []
        for (t, k) in reads:
            if k is None:
                for r in t.regs.values():
                    if r.last_w:
                        deps.append(r.last_w)
            else:
                r = t.regs.get(k)
                if r is not None and r.last_w:
                    deps.append(r.last_w)
                if t.regs[None].last_w:
                    deps.append(t.regs[None].last_w)
        for (t, k) in writes:
            if k is None:
                rs = list(t.regs.values())
            else:
                rs = [t.regs[None]]
                if k in t.regs:
                    rs.append(t.regs[k])
            for r in rs:
                if r.last_w:
                    deps.append(r.last_w)
                deps.extend(r.reads)
        return deps

    def _record(self, me, reads, writes):
        for (t, k) in reads:
            t.regs.setdefault(k, Reg()).reads.append(me)
        for (t, k) in writes:
            if k is None:
                for kk in list(t.regs.keys()):
                    if kk is not None:
                        del t.regs[kk]
                r = t.regs[None]
            else:
                r = t.regs.setdefault(k, Reg())
            r.last_w = me
            r.reads = []

    def _wait(self, e, deps):
        best = {}
        for (sk, v) in deps:
            if sk == e and e == "pe":
                continue
            if best.get(sk, 0) < v:
                best[sk] = v
        w = self.waited[e]
        for sk, v in best.items():
            if w.get(sk, 0) < v:
                self.eng[e].wait_ge(self.sem[sk], v)
                w[sk] = v
                self.nwait += 1

    def op(self, e, fn, reads=(), writes=()):
        reads = self._norm(reads)
        writes = self._norm(writes)
        self._wait(e, self._deps(reads, writes))
        inst = fn(self.eng[e])
        self.tick[e] += 1
        inst.then_inc(self.sem[e], 1)
        self.ninst[e] += 1
        self._record((e, self.tick[e]), reads, writes)
        return inst

    def dma(self, q, out, in_, reads=(), writes=(), **kw):
        reads = self._norm(reads)
        writes = self._norm(writes)
        lst, i = self.dq[q]
        sk = lst[i % len(lst)]
        self.dq[q][1] = i + 1
        deps = self._deps(reads, writes)
        if self.tick[sk] > 0:
            deps.append((sk, self.tick[sk]))
        self._wait(q, deps)
        inst = self.eng[q].dma_start(out=out, in_=in_, **kw)
        self.tick[sk] += 16
        inst.then_inc(self.sem[sk], 16)
        self.ninst[q] += 1
        self._record((sk, self.tick[sk]), reads, writes)
        return inst

    def finish(self, e, tiles):
        deps = []
        for t in tiles:
            for r in t.regs.values():
                if r.last_w:
                    deps.append(r.last_w)
        self._wait(e, deps)

    def mm(self, out, lhsT, rhs, start=True, stop=True, reads=(), writes=()):
        return self.op("pe", lambda e: e.matmul(out, lhsT, rhs, start=start, stop=stop), reads, writes)

    def tr(self, out, in_, ident, reads=(), writes=()):
        return self.op("pe", lambda e: e.transpose(out, in_, ident), reads, writes)

    def act(self, out, in_, func, reads=(), writes=(), bias=None, scale=None, e="act"):
        kw = {}
        if bias is not None:
            kw["bias"] = bias
        if scale is not None:
            kw["scale"] = scale
        return self.op(e, lambda g: g.activation(out, in_, func, **kw), reads, writes)

    def tt(self, out, in0, in1, op, reads=(), writes=(), e="dve"):
        return self.op(e, lambda g: g.tensor_tensor(out, in0, in1, op), reads, writes)

    def ts(self, out, in0, s1, op0, s2=None, op1=None, reads=(), writes=(), e="dve"):
        if op1 is None:
            return self.op(e, lambda g: g.tensor_scalar(out, in0, s1, None, op0), reads, writes)
        return self.op(e, lambda g: g.tensor_scalar(out, in0, s1, s2, op0, op1), reads, writes)

    def stt(self, out, in0, scalar, in1, op0, op1, reads=(), writes=()):
        return self.op("dve", lambda g: g.scalar_tensor_tensor(out, in0, scalar, in1, op0, op1), reads, writes)

    def cp(self, out, in_, reads=(), writes=(), e="dve"):
        if e == "act":
            return self.op(e, lambda g: g.copy(out, in_), reads, writes)
        return self.op(e, lambda g: g.tensor_copy(out, in_), reads, writes)

    def memset(self, out, val, writes=(), e="pool"):
        return self.op(e, lambda g: g.memset(out, val), (), writes)


TT = 512
EPS = 1e-6


class Rot:
    def __init__(self, k, name, shape, dtype, n):
        self.t = [k.sb("%s%d" % (name, i), shape, dtype) for i in range(n)]
        self.i = 0

    def next(self):
        t = self.t[self.i % len(self.t)]
        self.i += 1
        return t


class Ctx:
    def __init__(self, k):
        self.k = k
        self.ones = k.sb("ones_bf", [128, 128], BF16)
        k.memset(self.ones[:], 1.0, writes=[self.ones])
        self.eps = k.sb("eps_c", [128, 1], F32)
        k.memset(self.eps[:], EPS, writes=[self.eps])
        self.psums = [k.ps("ps%d" % i, [128, 512], F32) for i in range(8)]
        self.pi = 0
        self.sq = Rot(k, "sq", [128, TT], BF16, 3)
        self.rstd = Rot(k, "rstd", [128, TT], F32, 2)
        self.tmp = Rot(k, "tmpf", [128, TT], F32, 3)

    def psum(self):
        p = self.psums[self.pi % 8]
        self.pi += 1
        return p


def rstd_from(c, srcs, n=TT, nfeat=1024.0):
    k = c.k
    pst = c.psum()
    nk = len(srcs)
    for i, (ap, t) in enumerate(srcs):
        sq = c.sq.next()
        k.act(sq[:, 0:n], ap, AF.Square, reads=[t], writes=[sq])
        k.mm(pst[:, 0:n], c.ones[:], sq[:, 0:n], start=(i == 0), stop=(i == nk - 1), reads=[c.ones, sq], writes=[pst])
    r = c.rstd.next()
    k.act(r[:, 0:n], pst[:, 0:n], AF.Sqrt, reads=[pst, c.eps], writes=[r], bias=c.eps[:], scale=1.0 / nfeat)
    k.op("dve", lambda g: g.reciprocal(r[:, 0:n], r[:, 0:n]), reads=[r], writes=[r])
    return r


def pre_norm(c, xt, gain, gi, out, n=TT):
    k = c.k
    r = rstd_from(c, [(xt[:, kc, 0:n], xt) for kc in range(8)], n)
    for kc in range(8):
        k.stt(out[:, kc, 0:n], xt[:, kc, 0:n], gain[:, gi, kc:kc + 1], r[:, 0:n], ALU.mult, ALU.mult,
              reads=[xt, gain, r], writes=[(out, kc)])


def post_norm_res(c, m, gain, gi, xt, n=TT):
    k = c.k
    r = rstd_from(c, [(m[:, kc, 0:n], m) for kc in range(8)], n)
    for kc in range(8):
        t = c.tmp.next()
        k.stt(t[:, 0:n], m[:, kc, 0:n], gain[:, gi, kc:kc + 1], r[:, 0:n], ALU.mult, ALU.mult,
              reads=[m, gain, r], writes=[t])
        k.tt(xt[:, kc, 0:n], xt[:, kc, 0:n], t[:, 0:n], ALU.add, reads=[t, (xt, kc)], writes=[(xt, kc)], e="pool")


class WStream:
    def __init__(self, k, nbuf=4, elems=8 * 512):
        self.k = k
        self.elems = elems
        self.pool = Rot(k, "wbuf", [128, elems], BF16, nbuf)

    def load(self, w_ap, r0, nk, c0, ncols):
        buf = self.pool.next()
        view = buf[:, 0:nk * ncols].rearrange("p (a c) -> p a c", a=nk)
        if isinstance(w_ap, T):
            src = w_ap[r0:r0 + nk * 128, c0:c0 + ncols].rearrange("(a p) c -> p a c", p=128)
            self.qi = getattr(self, "qi", 0) + 1
            self.k.dma("sp", view, src, reads=[w_ap], writes=[buf])
        else:
            src = w_ap[r0:r0 + nk * 128, c0:c0 + ncols].rearrange("(a p) c -> p a c", p=128)
            self.k.dma("pool", view, src, writes=[buf])
        return buf, view


def pre_norm_split(c, xt, gain, gi, out, n=TT):
    k = c.k
    for kc in range(8):
        k.ts(out[:, kc, 0:n], xt[:, kc, 0:n], gain[:, gi, kc:kc + 1], ALU.mult, reads=[xt, gain], writes=[(out, kc)])
    return lambda: rstd_from(c, [(xt[:, kc, 0:n], xt) for kc in range(8)], n)


def dense(c, ws, w_ap, nk, nout_chunks, rhs_fn, rhs_reads, sink, n=TT, cols_per_load=512, hook=None):
    k = c.k
    per = cols_per_load // 128
    for o0 in range(0, nout_chunks, per):
        buf, view = ws.load(w_ap, 0, nk, o0 * 128, cols_per_load)
        for j in range(per):
            oc = o0 + j
            ps = c.psum()
            for kc in range(nk):
                k.mm(ps[:, 0:n], view[:, kc, j * 128:(j + 1) * 128], rhs_fn(kc), start=(kc == 0), stop=(kc == nk - 1),
                     reads=[buf] + rhs_reads(kc), writes=[ps])
            if hook is not None and oc == 0:
                hook()
            sink(oc, ps)


def stage_b(k, c, ws, l, catT, xT_in, xT_out, memT, W, gain, NT, tok0=0, cat_load=None, x_in=None, x_out=None):
    m = k.sb("m_%d" % l, [128, 8, TT], F32)
    memt = m
    k.dma("sp", memt[:, :, 0:256], memT[:, :].rearrange("(a p) c -> p a c", p=128), reads=[memT], writes=[memt])
    mn = k.sb("mn_%d" % l, [128, 8, 256], BF16)
    pre_norm(c, memt, gain, 3, mn, n=256)
    kT = k.sb("kT_%d" % l, [128, 8, 256], BF16)
    vM = k.sb("vM_%d" % l, [128, 2, 1024], BF16)

    def sink_k(oc, ps):
        k.cp(kT[:, oc, :], ps[:, 0:256], reads=[ps], writes=[(kT, oc)], e="act")

    dense(c, ws, W["wk"], 8, 8, lambda kc: mn[:, kc, :], lambda kc: [(mn, kc)], sink_k, n=256)
    for half in range(2):
        buf, view = ws.load(W["wv"], 0, 8, half * 512, 512)
        for mc in range(2):
            ps = c.psum()
            for kc in range(8):
                k.mm(ps[:, :], mn[:, kc, mc * 128:(mc + 1) * 128], view[:, kc, :], start=(kc == 0), stop=(kc == 7),
                     reads=[buf, (mn, kc)], writes=[ps])
            k.cp(vM[:, mc, half * 512:(half + 1) * 512], ps[:, :], reads=[ps], writes=[(vM, (mc, half))], e="act")

    xpool = Rot(k, "xt_%d" % l, [128, 8, TT], F32, 2)
    cpool = Rot(k, "ct_%d" % l, [128, 8, TT], BF16, 1)
    hb = k.sb("hb_%d" % l, [128, 8, TT], BF16)
    qT = k.sb("qT_%d" % l, [128, 8, TT], BF16)
    oT = k.sb("oT_%d" % l, [128, 8, TT], BF16)
    hid = k.sb("hid_%d" % l, [128, 32, TT], BF16)
    pT = Rot(k, "pT_%d" % l, [128, TT], BF16, 4)
    relu = Rot(k, "relu_%d" % l, [128, TT], F32, 3)
    rden = Rot(k, "rden_%d" % l, [128, TT], F32, 2)
    r2buf = k.sb("r2buf_%d" % l, [128, TT], F32)

    def sink_m(oc, ps):
        k.cp(m[:, oc, :], ps[:, :], reads=[ps], writes=[(m, oc)], e="dve")

    for ti in range(NT // TT):
        t0 = tok0 + ti * TT
        xt = xpool.next()
        if x_in is None:
            k.dma("sp", xt[:], xT_in[:, t0:t0 + TT].rearrange("(a p) c -> p a c", p=128), reads=[xT_in], writes=[xt])
        elif ti == 0:
            x_in(ti, xt)
        if x_in is not None and ti + 1 < NT // TT:
            xnext = xpool.t[(xpool.i) % len(xpool.t)]
            x_in(ti + 1, xnext)
        ct = cpool.next()
        if cat_load is None:
            k.dma("pool", ct[:], catT[:, t0:t0 + TT].rearrange("(a p) c -> p a c", p=128), reads=[catT], writes=[ct])
        elif ti == 0:
            cat_load(ti, ct)
        dense(c, ws, W["w_out"], 8, 8, lambda kc: ct[:, kc, :], lambda kc: [ct], sink_m)
        if cat_load is not None and ti + 1 < NT // TT:
            cat_load(ti + 1, ct)
        post_norm_res(c, m, gain, 0, xt)
        stats_q = pre_norm_split(c, xt, gain, 1, hb)
        rq = {}

        def hook_q():
            rq["r"] = stats_q()

        def sink_q(oc, ps):
            r_ = rq["r"]
            k.tt(qT[:, oc, :], ps[:, :], r_[:], ALU.mult, reads=[ps, r_], writes=[(qT, oc)])

        dense(c, ws, W["wq"], 8, 8, lambda kc: hb[:, kc, :], lambda kc: [(hb, kc)], sink_q, hook=hook_q)
        for hd in range(4):
            pts = []
            for mc in range(2):
                ps = c.psum()
                for j in range(2):
                    dc = hd * 2 + j
                    k.mm(ps[:, :], kT[:, dc, mc * 128:(mc + 1) * 128], qT[:, dc, :], start=(j == 0), stop=(j == 1),
                         reads=[(kT, dc), (qT, dc)], writes=[ps])
                pt = pT.next()
                k.act(pt[:], ps[:, :], AF.Exp, reads=[ps], writes=[pt], scale=1.0 / 16.0)
                pts.append(pt)
            pden = c.psum()
            for mc in range(2):
                k.mm(pden[:, :], c.ones[:], pts[mc][:], start=(mc == 0), stop=(mc == 1), reads=[c.ones, pts[mc]], writes=[pden])
            rd = rden.next()
            k.op("dve", lambda g: g.reciprocal(rd[:], pden[:, :]), reads=[pden], writes=[rd])
            for j in range(2):
                dc = hd * 2 + j
                ps = c.psum()
                for mc in range(2):
                    k.mm(ps[:, :], vM[:, mc, dc * 128:(dc + 1) * 128], pts[mc][:], start=(mc == 0), stop=(mc == 1),
                         reads=[vM, pts[mc]], writes=[ps])
                k.tt(oT[:, dc, :], ps[:, :], rd[:], ALU.mult, reads=[ps, rd], writes=[(oT, dc)])
        dense(c, ws, W["wo"], 8, 8, lambda kc: oT[:, kc, :], lambda kc: [(oT, kc)], sink_m)
        post_norm_res(c, m, gain, 2, xt)
        stats_m = pre_norm_split(c, xt, gain, 4, hb)

        def hook_m():
            r_ = stats_m()
            k.tt(r2buf[:], r_[:], r_[:], ALU.mult, reads=[r_], writes=[r2buf])

        def sink_h(f, ps):
            r = relu.next()
            k.act(r[:], ps[:, :], AF.Relu, reads=[ps], writes=[r])
            k.tt(hid[:, f, :], r[:], r[:], ALU.mult, reads=[r], writes=[(hid, f)], e="pool")

        dense(c, ws, W["w1"], 8, 32, lambda kc: hb[:, kc, :], lambda kc: [(hb, kc)], sink_h, hook=hook_m)
        for fb in range(8):
            buf, view = ws.load(W["w2"], fb * 512, 4, 0, 1024)
            for oc in range(8):
                ps = c.psums[oc]
                for f4 in range(4):
                    f = fb * 4 + f4
                    k.mm(ps[:, :], view[:, f4, oc * 128:(oc + 1) * 128], hid[:, f, :], start=(f == 0), stop=(f == 31),
                         reads=[buf, (hid, f)], writes=[ps])
        for oc in range(8):
            k.tt(m[:, oc, :], c.psums[oc][:, :], r2buf[:], ALU.mult, reads=[c.psums[oc], r2buf], writes=[(m, oc)])
        post_norm_res(c, m, gain, 5, xt)
        if x_out is None:
            k.dma("sp", xT_out[:, t0:t0 + TT].rearrange("(a p) c -> p a c", p=128), xt[:], reads=[xt], writes=[xT_out])
        else:
            x_out(ti, xt)


(CH_FQ, CH_FK, CH_FV, CH_LX, CH_LY, CH_RR, CH_RK, CH_RV, CH_RWA, CH_RGD,
 CH_TQ, CH_TK, CH_TV, CH_TG, CH_TQS, CH_TKS, CH_RVO) = range(17)
NCH = 17
NCOLS_A = NCH * 128 + 2

(PV_CW0, PV_CW1, PV_CW2, PV_CW3, PV_CB, PV_RAB, PV_RIB, PV_LAM,
 PV_MU_R, PV_MU_K, PV_MU_V, PV_MU_VO, PV_MU_WA, PV_MU_GD,
 PV_W0, PV_A0, PV_KK, PV_KA, PV_GNW, PV_GNB, PV_RK, PV_V0,
 PV_TGN, PV_FB, PV_HP, PV_H0, PV_H1) = range(27)
PV_GAIN = 27
NPV = 35


class ConstA:
    def __init__(self, k, c):
        self.k = k
        nc = k.nc
        g = k.eng["pool"]
        self.ident = k.sb("identf", [128, 128], F32)
        k.memset(self.ident[:], 1.0, writes=[self.ident])
        k.op("pool", lambda e: e.affine_select(out=self.ident[:], in_=self.ident[:], pattern=[[-1, 128]], compare_op=ALU.is_equal,
                                               fill=0.0, base=0, channel_multiplier=1), reads=[self.ident], writes=[self.ident])
        self.ut_f = k.sb("ut_f", [128, 128], F32)
        k.memset(self.ut_f[:], 1.0, writes=[self.ut_f])
        k.op("pool", lambda e: e.affine_select(out=self.ut_f[:], in_=self.ut_f[:], pattern=[[1, 128]], compare_op=ALU.is_ge,
                                               fill=0.0, base=0, channel_multiplier=-1), reads=[self.ut_f], writes=[self.ut_f])
        self.ut_b = k.sb("ut_b", [128, 128], BF16)
        k.cp(self.ut_b[:], self.ut_f[:], reads=[self.ut_f], writes=[self.ut_b])
        self.blk = k.sb("blk_f", [128, 128], F32)
        k.memset(self.blk[:], 0.0, writes=[self.blk])
        k.memset(self.blk[0:64, 0:64], 1.0, writes=[self.blk])
        k.memset(self.blk[64:128, 64:128], 1.0, writes=[self.blk])
        self.ones_f = k.sb("ones_f", [128, 512], F32)
        k.memset(self.ones_f[:], 1.0, writes=[self.ones_f])
        self.one_c = k.sb("one_c", [128, 1], F32)
        k.memset(self.one_c[:], 1.0, writes=[self.one_c])
        self.zero_c = k.sb("zero_c", [128, 1], F32)
        k.memset(self.zero_c[:], 0.0, writes=[self.zero_c])


def gelu_tanh(k, c, y_ps, y_t, tmps):
    y = tmps.next()
    k.cp(y[:], y_ps, reads=[y_t], writes=[y], e="act")
    t = tmps.next()
    k.tt(t[:], y[:], y[:], ALU.mult, reads=[y], writes=[t])
    k.ts(t[:], t[:], 0.044715, ALU.mult, 1.0, ALU.add, reads=[t], writes=[t])
    k.tt(t[:], t[:], y[:], ALU.mult, reads=[t, y], writes=[t])
    k.act(t[:], t[:], AF.Sigmoid, reads=[t], writes=[t], scale=2.0 * math.sqrt(2.0 / math.pi))
    return y, t


class StageA:
    def __init__(self, k, c, ca, l, xT, outT, Wown, pvd, mats, vfirst_in=None, vfirst_out=None, do=("lru", "ret", "fox", "rwkv")):
        self.k, self.c, self.ca, self.l = k, c, ca, l
        self.xT, self.outT = xT, outT
        self.do = do
        self.x_src = self.out_sink = self.tile_done = None
        self.results = {}
        self.interleave = ()
        self.f32r = False
        self.vf_in, self.vf_out = vfirst_in, vfirst_out
        L = "a%d_" % l
        self.L = L
        self.W = k.sb(L + "W", [128, 8, NCOLS_A], BF16)
        for kc in range(8):
            k.dma("pool", self.W[:, kc, :], Wown[kc * 128:(kc + 1) * 128, :], writes=[(self.W, kc)], max_dma_last_dim=4096)
        self.pv = k.sb(L + "pv", [128, NPV], F32)
        k.dma("sp", self.pv[:], pvd, writes=[self.pv])
        self.mats = {}
        for name, (ap, shape) in mats.items():
            t = k.sb(L + name, shape, F32)
            k.dma("sp", t[:], ap, writes=[t])
            self.mats[name] = t
        self.xpool = Rot(k, L + "xt", [128, 8, TT], F32, 1)
        self.hb = k.sb(L + "hb", [128, 8, TT], BF16)
        self.tmps = Rot(k, L + "t", [128, TT], F32, 12)
        self.outs = Rot(k, L + "o", [128, TT], F32, 3)
        self.named_ps = c.psums[5:8]
        if "lru" in do:
            self.init_lru()
        if "ret" in do:
            self.init_ret()
        if "fox" in do:
            self.init_fox()
        if "rwkv" in do:
            self.init_rwkv()

    def pcol(self, i):
        return self.pv[:, i:i + 1]

    def psum(self):
        c = self.c
        p = c.psums[c.pi % 5]
        c.pi += 1
        return p

    def proj(self, ch, n0=0, n1=TT):
        k = self.k
        ps = self.psum()
        for kc in range(8):
            k.mm(ps[:, n0:n1], self.W[:, kc, ch * 128:(ch + 1) * 128], self.hb[:, kc, n0:n1], start=(kc == 0), stop=(kc == 7),
                 reads=[(self.W, kc), (self.hb, kc)], writes=[ps])
        return ps

    def init_lru(self):
        k, L = self.k, self.L
        self.l_xbuf = k.sb(L + "lxb", [128, 3 + TT], F32)
        k.memset(self.l_xbuf[:, 0:3], 0.0, writes=[self.l_xbuf])
        self.l_h = k.sb(L + "lh", [128, 1], F32)
        k.memset(self.l_h[:], 0.0, writes=[self.l_h])
        self.l_c1 = k.sb(L + "lc1", [128, 2], F32)
        k.act(self.l_c1[:, 0:1], self.pcol(PV_LAM), AF.Exp, reads=[self.pv], writes=[self.l_c1], scale=-1.0)
        k.act(self.l_c1[:, 0:1], self.l_c1[:, 0:1], AF.Ln, reads=[self.l_c1, self.ca.one_c], writes=[self.l_c1], bias=self.ca.one_c[:])
        k.ts(self.l_c1[:, 1:2], self.l_c1[:, 0:1], -16.0, ALU.mult, reads=[self.l_c1], writes=[self.l_c1])
        k.ts(self.l_c1[:, 0:1], self.l_c1[:, 0:1], -8.0, ALU.mult, reads=[self.l_c1], writes=[self.l_c1])

    def lru_tile(self, ti):
        k, c, ca = self.k, self.c, self.ca
        xb = self.l_xbuf
        ps = self.proj(CH_LX)
        k.cp(xb[:, 3:3 + TT], ps[:, :], reads=[ps], writes=[xb], e="act")
        xc = self.tmps.next()
        k.ts(xc[:], xb[:, 0:TT], self.pcol(PV_CW0), ALU.mult, self.pcol(PV_CB), ALU.add, reads=[xb, self.pv], writes=[xc])
        for j in range(1, 4):
            k.stt(xc[:], xb[:, j:j + TT], self.pcol(PV_CW0 + j), xc[:], ALU.mult, ALU.add, reads=[xb, self.pv, xc], writes=[xc])
        k.cp(xb[:, 0:3], xb[:, TT:TT + 3], reads=[xb], writes=[xb], e="dve")
        pr = self.psum()
        k.mm(pr[:, :], self.mats["RA"][:], xc[:], reads=[self.mats["RA"], xc], writes=[pr])
        pi_ = self.psum()
        k.mm(pi_[:, :], self.mats["RI"][:], xc[:], reads=[self.mats["RI"], xc], writes=[pi_])
        r = self.tmps.next()
        k.act(r[:], pr[:, :], AF.Sigmoid, reads=[pr, self.pv], writes=[r], bias=self.pcol(PV_RAB))
        ig = self.tmps.next()
        k.act(ig[:], pi_[:, :], AF.Sigmoid, reads=[pi_, self.pv], writes=[ig], bias=self.pcol(PV_RIB))
        a = self.tmps.next()
        k.act(a[:], r[:], AF.Exp, reads=[r, self.l_c1], writes=[a], scale=self.l_c1[:, 0:1])
        mult = self.tmps.next()
        k.act(mult[:], r[:], AF.Exp, reads=[r, self.l_c1], writes=[mult], scale=self.l_c1[:, 1:2])
        k.act(mult[:], mult[:], AF.Sqrt, reads=[mult, ca.one_c], writes=[mult], scale=-1.0, bias=ca.one_c[:])
        k.tt(ig[:], ig[:], xc[:], ALU.mult, reads=[ig, xc], writes=[ig])
        k.tt(ig[:], ig[:], mult[:], ALU.mult, reads=[ig, mult], writes=[ig])
        h = self.tmps.next()
        k.op("dve", lambda g: g.tensor_tensor_scan(h[:], a[:], ig[:], self.l_h[:, 0:1], ALU.mult, ALU.add),
             reads=[a, ig, self.l_h], writes=[h])
        k.cp(self.l_h[:], h[:, TT - 1:TT], reads=[h], writes=[self.l_h], e="dve")
        py = self.proj(CH_LY)
        o = self.outs.next()
        y, t = gelu_tanh(k, c, py[:, :], py, self.tmps)
        k.tt(o[:], t[:], y[:], ALU.mult, reads=[t, y], writes=[o])
        k.tt(o[:], o[:], h[:], ALU.mult, reads=[o, h], writes=[o])
        return o

    def init_ret(self):
        k, L, ca = self.k, self.L, self.ca
        self.t_lg = k.sb(L + "tlg", [128, 3], F32)
        tmp = k.sb(L + "tlgt", [128, 3], F32)
        k.ts(tmp[:], self.pv[:, PV_HP:PV_HP + 3], 5.0, ALU.add, reads=[self.pv], writes=[tmp])
        k.act(tmp[:], tmp[:], AF.Exp, reads=[tmp], writes=[tmp], scale=-math.log(2.0))
        k.act(self.t_lg[:], tmp[:], AF.Ln, reads=[tmp, ca.one_c], writes=[self.t_lg], scale=-1.0, bias=ca.one_c[:])
        ni = k.sb(L + "tni", [128, 128], I32)
        k.op("pool", lambda e: e.iota(ni[:], pattern=[[1, 128]], base=0, channel_multiplier=0), writes=[ni])
        nf = k.sb(L + "tnf", [128, 128], F32)
        k.cp(nf[:], ni[:], reads=[ni], writes=[nf])
        self.t_xi = k.sb(L + "txi", [128, 128], F32)
        self.t_zt = k.sb(L + "tzt", [128, 128], F32)
        tt_ = k.sb(L + "ttmp", [128, 128], F32)
        k.ts(tt_[:], nf[:], 1.0, ALU.add, reads=[nf], writes=[tt_])
        k.act(self.t_xi[:], tt_[:], AF.Exp, reads=[tt_, self.t_lg], writes=[self.t_xi], scale=self.t_lg[:, 0:1])
        k.ts(tt_[:], nf[:], -1.0, ALU.mult, 127.0, ALU.add, reads=[nf], writes=[tt_])
        k.act(self.t_zt[:], tt_[:], AF.Exp, reads=[tt_, self.t_lg], writes=[self.t_zt], scale=self.t_lg[:, 0:1])
        k.ts(self.t_zt[:], self.t_zt[:], 0.125, ALU.mult, reads=[self.t_zt], writes=[self.t_zt])
        self.t_gc = k.sb(L + "tgc", [128, 1], F32)
        k.act(self.t_gc[:], self.t_lg[:, 0:1], AF.Exp, reads=[self.t_lg], writes=[self.t_gc], scale=128.0)
        di = k.sb(L + "tdi", [128, 128], I32)
        k.op("pool", lambda e: e.iota(di[:], pattern=[[1, 128]], base=0, channel_multiplier=-1), writes=[di])
        df = k.sb(L + "tdf", [128, 128], F32)
        k.cp(df[:], di[:], reads=[di], writes=[df])
        k.ts(df[:], df[:], 0.0, ALU.max, reads=[df], writes=[df])
        self.t_dt = k.sb(L + "tdt", [128, 2, 128], F32)
        for h in range(2):
            k.act(self.t_dt[:, h, :], df[:], AF.Exp, reads=[df, self.t_lg], writes=[self.t_dt], scale=self.t_lg[:, 1 + h:2 + h])
            k.stt(self.t_dt[:, h, :], self.t_dt[:, h, :], 0.125, ca.ut_f[:], ALU.mult, ALU.mult, reads=[self.t_dt, ca.ut_f], writes=[self.t_dt])
        pi_ = k.sb(L + "tpi", [128, 2], I32)
        k.op("pool", lambda e: e.iota(pi_[:, 0:1], pattern=[[0, 1]], base=0, channel_multiplier=1), writes=[pi_])
        k.ts(pi_[:, 1:2], pi_[:, 0:1], 32, ALU.bitwise_and, reads=[pi_], writes=[pi_])
        k.ts(pi_[:, 0:1], pi_[:, 0:1], 31, ALU.bitwise_and, reads=[pi_], writes=[pi_])
        pf = k.sb(L + "tpf", [128, 2], F32)
        k.cp(pf[:], pi_[:], reads=[pi_], writes=[pf])
        self.t_inv = k.sb(L + "tinv", [128, 1], F32)
        k.act(self.t_inv[:], pf[:, 0:1], AF.Exp, reads=[pf], writes=[self.t_inv], scale=-math.log(RET_THETA) / 31.0)
        self.t_sgn = k.sb(L + "tsgn", [128, 1], F32)
        k.ts(self.t_sgn[:], pf[:, 1:2], 1.0 / 16.0, ALU.mult, -1.0, ALU.add, reads=[pf], writes=[self.t_sgn])
        qi = k.sb(L + "tqi", [128, TT], I32)
        k.op("pool", lambda e: e.iota(qi[:], pattern=[[1, TT]], base=0, channel_multiplier=0), writes=[qi])
        self.t_pos = k.sb(L + "tpos", [128, TT], F32)
        k.cp(self.t_pos[:], qi[:], reads=[qi], writes=[self.t_pos])
        self.t_R = k.sb(L + "tR", [128, 64], F32)
        k.memset(self.t_R[:], 0.0, writes=[self.t_R])
        self.t_Rb = k.sb(L + "tRb", [128, 64], BF16)
        k.memset(self.t_Rb[:], 0.0, writes=[self.t_Rb])
        self.t_kzt = [k.sb(L + "tkzt%d" % i, [128, 128], BF16) for i in range(4)]
        self.t_ind = [k.sb(L + "tind%d" % i, [128, 128], BF16) for i in range(8)]
        self.t_kv = k.sb(L + "tkv", [128, 4, 64], F32)
        self.t_Rb4 = k.sb(L + "tRb4", [128, 4, 64], BF16)
        self.t_vtm = Rot(k, L + "tvtm", [128, 4, 128], BF16, 1)
        self.t_bf = Rot(k, L + "tbf", [128, TT], BF16, 4)

    def sincos(self, t0):
        k, ca = self.k, self.ca
        TWO_PI = 2.0 * math.pi
        C1 = 6.28125
        C2 = TWO_PI - C1
        res = []
        for shift in (math.pi / 2.0, 0.0):
            ang = self.tmps.next()
            k.ts(ang[:], self.t_pos[:], float(t0), ALU.add, self.t_inv[:, 0:1], ALU.mult, reads=[self.t_pos, self.t_inv], writes=[ang])
            if shift:
                k.ts(ang[:], ang[:], shift, ALU.add, reads=[ang], writes=[ang])
            kf = self.tmps.next()
            ki = kf[:].bitcast(I32)
            k.ts(kf[:], ang[:], 1.0 / TWO_PI, ALU.mult, reads=[ang], writes=[kf])
            k.cp(ki, kf[:], reads=[kf], writes=[kf])
            k.cp(kf[:], ki, reads=[kf], writes=[kf])
            k.stt(ang[:], kf[:], -C1, ang[:], ALU.mult, ALU.add, reads=[kf, ang], writes=[ang])
            k.stt(ang[:], kf[:], -C2, ang[:], ALU.mult, ALU.add, reads=[kf, ang], writes=[ang])
            k.ts(kf[:], ang[:], math.pi, ALU.is_gt, -TWO_PI, ALU.mult, reads=[ang], writes=[kf])
            k.tt(ang[:], ang[:], kf[:], ALU.add, reads=[ang, kf], writes=[ang])
            k.ts(kf[:], ang[:], -math.pi, ALU.is_lt, TWO_PI, ALU.mult, reads=[ang], writes=[kf])
            k.tt(ang[:], ang[:], kf[:], ALU.add, reads=[ang, kf], writes=[ang])
            k.ts(ang[:], ang[:], math.pi, ALU.min, -math.pi, ALU.max, reads=[ang], writes=[ang])
            k.act(ang[:], ang[:], AF.Sin, reads=[ang], writes=[ang])
            res.append(ang)
        C, S = res
        k.ts(S[:], S[:], self.t_sgn[:, 0:1], ALU.mult, reads=[S, self.t_sgn], writes=[S])
        return C, S

    def ret_gen(self, ti):
        k, c, ca = self.k, self.c, self.ca
        t0 = ti * TT
        C, S = self.sincos(t0)
        yield
        rot = []
        for (cha, chs) in ((CH_TQ, CH_TQS), (CH_TK, CH_TKS)):
            pa = self.proj(cha)
            a = self.tmps.next()
            k.tt(a[:], pa[:, :], C[:], ALU.mult, reads=[pa, C], writes=[a])
            pb = self.proj(chs)
            b = self.tmps.next()
            k.tt(b[:], pb[:, :], S[:], ALU.mult, reads=[pb, S], writes=[b])
            k.tt(a[:], a[:], b[:], ALU.add, reads=[a, b], writes=[a], e="pool")
            rot.append(a)
            yield
        qr, kr = rot
        qb = self.t_bf.next()
        k.cp(qb[:], qr[:], reads=[qr], writes=[qb], e="act")
        kb = self.t_bf.next()
        k.cp(kb[:], kr[:], reads=[kr], writes=[kb], e="act")
        qx = self.t_bf.next()
        k.tt(qx[:].rearrange("p (c n) -> p c n", n=128), qr[:].rearrange("p (c n) -> p c n", n=128),
             self.t_xi[:].unsqueeze(1).to_broadcast([128, 4, 128]), ALU.mult, reads=[qr, self.t_xi], writes=[qx])
        kz = self.tmps.next()
        k.tt(kz[:].rearrange("p (c n) -> p c n", n=128), kr[:].rearrange("p (c n) -> p c n", n=128),
             self.t_zt[:].unsqueeze(1).to_broadcast([128, 4, 128]), ALU.mult, reads=[kr, self.t_zt], writes=[kz])
        vtm = self.t_vtm.next()
        for cc in range(4):
            ps = self.psum()
            for kc in range(8):
                k.mm(ps[:, 0:128], self.hb[:, kc, cc * 128:(cc + 1) * 128], self.W[:, kc, CH_TV * 128:(CH_TV + 1) * 128],
                     start=(kc == 0), stop=(kc == 7), reads=[(self.W, kc), (self.hb, kc)], writes=[ps])
            k.cp(vtm[:, cc, :], ps[:, 0:128], reads=[ps], writes=[(vtm, cc)], e="act")
            yield
        osb = self.tmps.next()
        inds = []
        for cc in range(4):
            cs = slice(cc * 128, (cc + 1) * 128)
            ptr = self.psum()
            k.tr(ptr[:, 0:128], kz[:, cs], ca.ident[:], reads=[kz, ca.ident], writes=[ptr])
            kzt = self.t_kzt[cc]
            k.cp(kzt[:], ptr[:, 0:128], reads=[ptr], writes=[kzt], e="act")
            for h in range(2):
                hp = slice(h * 64, (h + 1) * 64)
                pin = self.psum()
                k.mm(pin[:, 0:128], kb[hp, cs], qb[hp, cs], reads=[kb, qb], writes=[pin])
                ind = self.t_ind[cc * 2 + h]
                k.tt(ind[:], pin[:, 0:128], self.t_dt[:, h, :], ALU.mult, reads=[pin, self.t_dt], writes=[ind])
                inds.append(ind)
            yield
        for cc in range(4):
            pkv = self.psum()
            kzt = self.t_kzt[cc]
            for h in range(2):
                hp = slice(h * 64, (h + 1) * 64)
                k.mm(pkv[hp, 0:64], kzt[:, hp], vtm[:, cc, hp], reads=[kzt, (vtm, cc)], writes=[pkv])
            k.cp(self.t_kv[:, cc, :], pkv[:, 0:64], reads=[pkv], writes=[(self.t_kv, cc)], e="act")
            yield
        for cc in range(4):
            k.cp(self.t_Rb4[:, cc, :], self.t_R[:], reads=[self.t_R], writes=[(self.t_Rb4, cc)], e="act")
            k.stt(self.t_R[:], self.t_R[:], self.t_gc[:, 0:1], self.t_kv[:, cc, :], ALU.mult, ALU.add,
                  reads=[self.t_R, self.t_gc, (self.t_kv, cc)], writes=[self.t_R])
        yield
        for cc in range(4):
            cs = slice(cc * 128, (cc + 1) * 128)
            po = self.psum()
            for h in range(2):
                hp = slice(h * 64, (h + 1) * 64)
                ind = inds[cc * 2 + h]
                k.mm(po[hp, 0:128], vtm[:, cc, hp], ind[:], start=True, stop=False, reads=[(vtm, cc), ind], writes=[po])
                k.mm(po[hp, 0:128], self.t_Rb4[hp, cc, :], qx[hp, cs], start=False, stop=True, reads=[(self.t_Rb4, cc), qx], writes=[po])
            k.cp(osb[:, cs], po[:, 0:128], reads=[po], writes=[osb], e="act")
            yield
        sq = self.tmps.next()
        k.tt(sq[:], osb[:], osb[:], ALU.mult, reads=[osb], writes=[sq], e="pool")
        pss = self.psum()
        k.mm(pss[:, :], ca.blk[:], sq[:], reads=[ca.blk, sq], writes=[pss])
        rs = self.tmps.next()
        k.act(rs[:], pss[:, :], AF.Sqrt, reads=[pss, c.eps], writes=[rs], bias=c.eps[:], scale=1.0 / 64.0)
        k.op("dve", lambda g: g.reciprocal(rs[:], rs[:]), reads=[rs], writes=[rs])
        pg = self.proj(CH_TG)
        sg = self.tmps.next()
        k.act(sg[:], pg[:, :], AF.Silu, reads=[pg], writes=[sg])
        o = self.outs.next()
        k.stt(o[:], osb[:], self.pcol(PV_TGN), rs[:], ALU.mult, ALU.mult, reads=[osb, self.pv, rs], writes=[o])
        k.tt(o[:], o[:], sg[:], ALU.mult, reads=[o, sg], writes=[o])
        self.results["ret"] = o

    def ret_tile(self, ti):
        for _ in self.ret_gen(ti):
            pass
        return self.results["ret"]


    def init_fox(self):
        k, L, ca = self.k, self.L, self.ca
        self.f_kT = k.sb(L + "fkT", [128, 4096], BF16)
        self.f_v = k.sb(L + "fv", [128, 32, 128], BF16)
        self.f_q = k.sb(L + "fq", [128, TT], BF16)
        self.f_row = Rot(k, L + "frow", [2, TT], F32, 3)
        self.f_clast = k.sb(L + "fcl", [2, 1], F32)
        k.memset(self.f_clast[:], 0.0, writes=[self.f_clast])
        self.f_ccol = k.sb(L + "fcc", [128, 32, 2], F32)
        self.f_cend = k.sb(L + "fce", [128, 4, 2], F32)
        self.f_B = Rot(k, L + "fB", [128, 32, 4], F32, 2)
        self.f_pT = Rot(k, L + "fpT", [128, TT], BF16, 4)
        self.f_rd = k.sb(L + "frd", [128, TT], F32)
        self.f_sel = k.sb(L + "fsel", [128, 128], F32)
        k.memset(self.f_sel[:], 1.0, writes=[self.f_sel])
        k.op("pool", lambda e: e.affine_select(out=self.f_sel[:], in_=self.f_sel[:], pattern=[[0, 128]], compare_op=ALU.is_equal,
                                               fill=0.0, base=-127, channel_multiplier=1), reads=[self.f_sel], writes=[self.f_sel])
        self.f_nb = k.sb(L + "fnb", [2, 1], F32)
        k.ts(self.f_nb[:], self.pv[0:2, PV_FB:PV_FB + 1], -1.0, ALU.mult, reads=[self.pv], writes=[self.f_nb])

    def fox_gen(self, ti):
        k, c, ca = self.k, self.c, self.ca
        t0 = ti * TT
        pq = self.proj(CH_FQ)
        k.cp(self.f_q[:], pq[:, :], reads=[pq], writes=[self.f_q], e="act")
        pk = self.proj(CH_FK)
        k.cp(self.f_kT[:, t0:t0 + TT], pk[:, :], reads=[pk], writes=[(self.f_kT, ti)], e="act")
        for cc in range(4):
            ps = self.psum()
            for kc in range(8):
                k.mm(ps[:, 0:128], self.hb[:, kc, cc * 128:(cc + 1) * 128], self.W[:, kc, CH_FV * 128:(CH_FV + 1) * 128],
                     start=(kc == 0), stop=(kc == 7), reads=[(self.W, kc), (self.hb, kc)], writes=[ps])
            k.cp(self.f_v[:, ti * 4 + cc, :], ps[:, 0:128], reads=[ps], writes=[(self.f_v, ti * 4 + cc)], e="act")
        pf = self.psum()
        for kc in range(8):
            k.mm(pf[0:2, :], self.W[:, kc, NCH * 128:NCH * 128 + 2], self.hb[:, kc, :], start=(kc == 0), stop=(kc == 7),
                 reads=[(self.W, kc), (self.hb, kc)], writes=[pf])
        e_ = self.f_row.next()
        k.act(e_[:], pf[0:2, :], AF.Exp, reads=[pf, self.f_nb], writes=[e_], scale=-1.0, bias=self.f_nb[:])
        k.act(e_[:], e_[:], AF.Ln, reads=[e_, ca.one_c], writes=[e_], bias=ca.one_c[0:2, :])
        cs = self.f_row.next()
        k.op("dve", lambda g: g.tensor_tensor_scan(cs[:], ca.ones_f[0:2, :], e_[:], self.f_clast[:, 0:1], ALU.mult, ALU.add),
             reads=[ca.ones_f, e_, self.f_clast], writes=[cs])
        k.cp(self.f_clast[:], cs[:, TT - 1:TT], reads=[cs], writes=[self.f_clast])
        pc = self.psum()
        for cc in range(4):
            k.tr(pc[:, 2 * cc:2 * cc + 2], cs[:, cc * 128:(cc + 1) * 128], ca.ident[0:2, 0:2], reads=[cs, ca.ident], writes=[pc])
        k.cp(self.f_ccol[:, ti * 4:(ti + 1) * 4, :].rearrange("p a b -> p (a b)"), pc[:, 0:8], reads=[pc], writes=[(self.f_ccol, ti)])
        pe = self.psum()
        k.mm(pe[:, 0:8], self.f_sel[:], self.f_ccol[:, ti * 4:(ti + 1) * 4, :].rearrange("p a b -> p (a b)"),
             reads=[self.f_sel, (self.f_ccol, ti)], writes=[pe])
        k.cp(self.f_cend[:].rearrange("p a b -> p (a b)"), pe[:, 0:8], reads=[pe], writes=[self.f_cend])
        po, pd = self.named_ps[0], self.named_ps[1]
        nkc = 4 * (ti + 1)
        yield
        for h in range(2):
            hp = slice(h * 64, (h + 1) * 64)
            B = self.f_B.next()
            k.tt(B[:, 0:nkc, :], self.f_ccol[:, 0:nkc, h:h + 1].to_broadcast([128, nkc, 4]),
                 self.f_cend[:, :, h].unsqueeze(1).to_broadcast([128, nkc, 4]), ALU.subtract,
                 reads=[self.f_ccol, self.f_cend], writes=[B])

            def pv_step(kc, pt, col0):
                k.mm(po[hp, col0:TT], self.f_v[:, kc, hp], pt[:, col0:TT], start=(kc == 0), stop=(kc == nkc - 1),
                     reads=[(self.f_v, kc), pt], writes=[po])
                k.mm(pd[hp, col0:TT], c.ones[:, 0:64], pt[:, col0:TT], start=(kc == 0), stop=(kc == nkc - 1),
                     reads=[c.ones, pt], writes=[pd])

            pend = None
            for kc in range(nkc):
                j = kc - 4 * ti
                col0 = 128 * j if j >= 0 else 0
                ps = self.psum()
                k.mm(ps[:, col0:TT], self.f_kT[hp, kc * 128:(kc + 1) * 128], self.f_q[hp, col0:TT],
                     reads=[(self.f_kT, kc // 4), self.f_q], writes=[ps])
                pt = self.f_pT.next()
                for qbl in range(col0 // 128, 4):
                    qs = slice(qbl * 128, (qbl + 1) * 128)
                    k.act(pt[:, qs], ps[:, qs], AF.Exp, reads=[ps, B], writes=[pt], scale=0.125, bias=B[:, kc, qbl:qbl + 1])
                if j >= 0:
                    qs = slice(j * 128, (j + 1) * 128)
                    k.tt(pt[:, qs], pt[:, qs], ca.ut_b[:], ALU.mult, reads=[pt, ca.ut_b], writes=[pt], e="pool")
                if pend is not None:
                    pv_step(*pend)
                pend = (kc, pt, col0)
                yield
            pv_step(*pend)
            yield
        rd = self.f_rd
        k.op("dve", lambda g: g.reciprocal(rd[:], pd[:, :]), reads=[pd], writes=[rd])
        o = self.outs.next()
        k.tt(o[:], po[:, :], rd[:], ALU.mult, reads=[po, rd], writes=[o])
        self.results["fox"] = o

    def fox_tile(self, ti):
        for _ in self.fox_gen(ti):
            pass
        return self.results["fox"]

    def init_rwkv(self):
        k, L, ca = self.k, self.L, self.ca
        self.TR = 256
        self.NCK = self.TR // 64
        NCK = self.NCK
        self.w_carry = k.sb(L + "wcar", [128, 6], F32)
        k.memset(self.w_carry[:], 0.0, writes=[self.w_carry])
        self.w_AR = k.sb(L + "wAR", [128, NCK, 2, 2, 64], F32)
        self.w_BK = k.sb(L + "wBK", [128, NCK, 2, 2, 64], F32)
        self.w_HAT = k.sb(L + "wHAT", [128, NCK, 3, 2, 64], F32)
        for t in (self.w_AR, self.w_BK, self.w_HAT):
            k.memset(t[:], 0.0, writes=[t])
        self.w_H = k.sb(L + "wH", [128, 128], F32)
        k.memset(self.w_H[:], 0.0, writes=[self.w_H])
        self.w_ms = k.sb(L + "wms", [128, 128], F32)
        k.memset(self.w_ms[:], 1.0, writes=[self.w_ms])
        k.op("pool", lambda e: e.affine_select(out=self.w_ms[:], in_=self.w_ms[:], pattern=[[1, 128]], compare_op=ALU.is_gt,
                                               fill=0.0, base=0, channel_multiplier=-1), reads=[self.w_ms], writes=[self.w_ms])
        self.w_mst = k.sb(L + "wmst", [128, 128], F32)
        k.memset(self.w_mst[:], 1.0, writes=[self.w_mst])
        k.op("pool", lambda e: e.affine_select(out=self.w_mst[:], in_=self.w_mst[:], pattern=[[-1, 128]], compare_op=ALU.is_gt,
                                               fill=0.0, base=0, channel_multiplier=1), reads=[self.w_mst], writes=[self.w_mst])
        self.w_mist = k.sb(L + "wmist", [128, 2, 64], F32)
        for j in range(2):
            k.cp(self.w_mist[0:64, j, :], ca.ut_f[0:64, 0:64], reads=[ca.ut_f], writes=[self.w_mist])
            k.cp(self.w_mist[64:128, j, :], ca.ut_f[64:128, 64:128], reads=[ca.ut_f], writes=[self.w_mist])
        self.w_chm = k.sb(L + "wchm", [128, self.TR], F32)
        k.memset(self.w_chm[:], 1.0, writes=[self.w_chm])
        k.memset(self.w_chm[:].rearrange("p (c n) -> p c n", n=64)[:, :, 0:1], 0.0, writes=[self.w_chm])
        self.w_omka = k.sb(L + "womka", [128, 1], F32)
        k.ts(self.w_omka[:], self.pcol(PV_KA), -1.0, ALU.mult, 1.0, ALU.add, reads=[self.pv], writes=[self.w_omka])
        self.w_gc = k.sb(L + "wgc", [128, NCK], F32)
        self.w_rot = {n: Rot(k, L + "w" + n, [128, 128], F32, 2) for n in ("w1", "u")}
        self.w_out3 = {n: [k.sb(L + "w%s%d" % (n, i), [128, 128], F32) for i in range(3)] for n in ("nak", "nst", "bh", "kh", "vb", "x")}
        self.w_ptb = [[k.sb(L + "wpt%d_%d" % (i, j), [128, 128], F32) for j in range(2)] for i in range(3)]
        self.w_pxb = [[k.sb(L + "wpx%d_%d" % (i, j), [128, 256], F32) for j in range(2)] for i in range(3)]
        self.w_v32 = k.sb(L + "wv32", [32, self.TR], F32)
        self.w_gneps = k.sb(L + "wgne", [128, 1], F32)
        k.memset(self.w_gneps[:], 64e-5, writes=[self.w_gneps])

    def nb(self, i):
        TR = self.TR
        xt = self.cur_xt
        return xt[:, i // 2, (i % 2) * TR:(i % 2 + 1) * TR], (xt, ("nb", i))

    def rwkv_gen(self, ti):
        o = self.outs.next()
        for half in range(self.TT_R):
            yield from self.rwkv_half(ti, half, o)
        self.results["rwkv"] = o

    def rwkv_tile(self, ti):
        for _ in self.rwkv_gen(ti):
            pass
        return self.results["rwkv"]

    TT_R = 2

    def rwkv_half(self, ti, half, o):
        k, c, ca, l = self.k, self.c, self.ca, self.l
        TR, NCK = self.TR, self.NCK
        n0 = half * TR
        n1 = n0 + TR
        g0 = ti * TT + n0
        pv = self.pv
        (R, dR), (Kk, dK), (V, dV), (WA, dWA), (GD, dGD), (VO, dVO), (A, dA), (LW, dLW), (LG, dLG), (KKN, dKKN), (BV, dBV), \
            (RT, dRT), (TA, dTA), (TB, dTB), (TC, dTC), (Y, dY) = [self.nb(i) for i in range(16)]
        car = self.w_carry

        def lerp(X, dX, ch, mu_i, ci):
            ps = self.proj(ch, n0, n1)
            k.cp(X, ps[:, n0:n1], reads=[ps], writes=[dX], e="act")
            k.tt(TA[:, 1:TR], X[:, 0:TR - 1], X[:, 1:TR], ALU.subtract, reads=[dX], writes=[dTA])
            k.tt(TA[:, 0:1], car[:, ci:ci + 1], X[:, 0:1], ALU.subtract, reads=[dX, car], writes=[dTA])
            k.cp(car[:, ci:ci + 1], X[:, TR - 1:TR], reads=[dX], writes=[car])
            k.stt(X, TA, self.pcol(mu_i), X, ALU.mult, ALU.add, reads=[dTA, pv, dX], writes=[dX])

        lerp(R, dR, CH_RR, PV_MU_R, 0)
        yield
        lerp(Kk, dK, CH_RK, PV_MU_K, 1)
        yield
        lerp(V, dV, CH_RV, PV_MU_V, 2)
        yield
        lerp(WA, dWA, CH_RWA, PV_MU_WA, 3)
        yield
        lerp(GD, dGD, CH_RGD, PV_MU_GD, 4)
        yield
        if l > 0:
            lerp(VO, dVO, CH_RVO, PV_MU_VO, 5)
            yield
        WA2, G2 = self.mats["WA2"], self.mats["G2"]
        k.act(TB[0:64, :], WA[0:64, :], AF.Tanh, reads=[dWA], writes=[dTB])
        pz = self.psum()
        k.mm(pz[:, 0:TR], WA2[0:64, :], TB[0:64, :], reads=[WA2, dTB], writes=[pz])
        k.act(LW, pz[:, 0:TR], AF.Sigmoid, reads=[pz, pv], writes=[dLW], bias=self.pcol(PV_W0))
        k.ts(LW, LW, -math.exp(-0.5), ALU.mult, reads=[dLW], writes=[dLW])
        pa = self.psum()
        k.mm(pa[:, 0:TR], WA2[64:128, :], WA[64:128, :], reads=[WA2, dWA], writes=[pa])
        k.act(A, pa[:, 0:TR], AF.Sigmoid, reads=[pa, pv], writes=[dA], bias=self.pcol(PV_A0))
        k.act(GD, GD, AF.Sigmoid, reads=[dGD], writes=[dGD])
        pg = self.psum()
        k.mm(pg[:, 0:TR], G2[:], GD, reads=[G2, dGD], writes=[pg])
        k.cp(GD, pg[:, 0:TR], reads=[pg], writes=[dGD], e="act")
        if l > 0:
            V1, V2 = self.mats["V1"], self.mats["V2"]
            p1 = self.psum()
            k.mm(p1[0:32, 0:TR], V1[:, 0, :], V, start=True, stop=False, reads=[V1, dV], writes=[p1])
            k.mm(p1[0:32, 0:TR], V1[:, 1, :], VO, start=False, stop=True, reads=[V1, dVO], writes=[p1])
            k.cp(self.w_v32[:], p1[0:32, 0:TR], reads=[p1], writes=[self.w_v32], e="act")
            p2 = self.psum()
            k.mm(p2[:, 0:TR], V2[:], self.w_v32[:], reads=[V2, self.w_v32], writes=[p2])
            k.act(TB, p2[:, 0:TR], AF.Sigmoid, reads=[p2, pv], writes=[dTB], bias=self.pcol(PV_V0))
            k.dma("sp", TC, self.vf_in[:, g0:g0 + TR], reads=[self.vf_in], writes=[dTC])
            k.tt(TC, TC, V, ALU.subtract, reads=[dTC, dV], writes=[dTC])
            k.tt(TC, TC, TB, ALU.mult, reads=[dTC, dTB], writes=[dTC])
            k.tt(V, V, TC, ALU.add, reads=[dV, dTC], writes=[dV])
        else:
            k.dma("sp", self.vf_out[:, g0:g0 + TR], V, reads=[dV], writes=[self.vf_out])
        yield
        k.ts(KKN, Kk, self.pcol(PV_KK), ALU.mult, reads=[dK, pv], writes=[dKKN])
        k.tt(TB, KKN, KKN, ALU.mult, reads=[dKKN], writes=[dTB])
        pn = self.psum()
        k.mm(pn[:, 0:TR], ca.blk[:], TB, reads=[ca.blk, dTB], writes=[pn])
        k.act(TB, pn[:, 0:TR], AF.Sqrt, reads=[pn], writes=[dTB])
        k.ts(TB, TB, 1e-12, ALU.max, reads=[dTB], writes=[dTB])
        k.op("dve", lambda g: g.reciprocal(TB, TB), reads=[dTB], writes=[dTB])
        k.tt(KKN, KKN, TB, ALU.mult, reads=[dKKN, dTB], writes=[dKKN])
        k.ts(TB, A, self.pcol(PV_KA), ALU.mult, self.w_omka[:, 0:1], ALU.add, reads=[dA, pv, self.w_omka], writes=[dTB])
        k.tt(Kk, Kk, TB, ALU.mult, reads=[dK, dTB], writes=[dK])
        k.tt(BV, KKN, A, ALU.mult, reads=[dKKN, dA], writes=[dBV])
        yield
        k.op("dve", lambda g: g.tensor_tensor_scan(LG, self.w_chm[:], LW, 0.0, ALU.mult, ALU.add),
             reads=[self.w_chm, dLW], writes=[dLG])
        AR, BK, HAT = self.w_AR, self.w_BK, self.w_HAT
        v3 = lambda ap: ap.rearrange("p (c n) -> p c n", n=64)

        def to_blk(dst, which, src, dsrc, mul=None, dmul=None, op="mult"):
            for h in range(2):
                hp = slice(h * 64, (h + 1) * 64)
                if mul is None:
                    k.cp(dst[hp, :, which, h, :], v3(src[hp, :]), reads=[dsrc], writes=[dst])
                else:
                    k.tt(dst[hp, :, which, h, :], v3(src[hp, :]), v3(mul[hp, :]), ALU.mult, reads=[dsrc, dmul], writes=[dst])

        k.act(TB, LG, AF.Exp, reads=[dLG], writes=[dTB])
        k.tt(RT, R, TB, ALU.mult, reads=[dR, dTB], writes=[dRT])
        to_blk(AR, 1, RT, dRT)
        k.act(self.w_gc[:], v3(LG)[:, :, 63], AF.Exp, reads=[dLG], writes=[self.w_gc])
        yield
        k.tt(TB, LG, LW, ALU.subtract, reads=[dLG, dLW], writes=[dTB])
        k.act(TB, TB, AF.Exp, reads=[dTB], writes=[dTB])
        k.stt(TC, KKN, -1.0, TB, ALU.mult, ALU.mult, reads=[dKKN, dTB], writes=[dTC])
        to_blk(AR, 0, TC, dTC)
        k.act(TB, LG, AF.Exp, reads=[dLG], writes=[dTB], scale=-1.0)
        to_blk(BK, 0, BV, dBV, TB, dTB)
        to_blk(BK, 1, Kk, dK, TB, dTB)
        yield
        k.tt(v3(TB), v3(LG)[:, :, 63:64].to_broadcast([128, NCK, 64]), v3(LG), ALU.subtract, reads=[dLG], writes=[dTB])
        k.act(TB, TB, AF.Exp, reads=[dTB], writes=[dTB])
        to_blk(HAT, 0, BV, dBV, TB, dTB)
        to_blk(HAT, 1, Kk, dK, TB, dTB)
        to_blk(HAT, 2, V, dV)
        py = self.named_ps[2]
        ident = ca.ident
        m2 = lambda ap: ap.rearrange("p a b -> p (a b)")
        H = self.w_H
        P = {}
        if self.f32r:
            F32R = mybir.dt.float32r

            def mmr(out, lhsT, rhs, **kw):
                return k.mm(out, lhsT.bitcast(F32R), rhs.bitcast(F32R), **kw)
        else:
            mmr = k.mm

        def prep(cc):
            slot = cc % 3
            pxs, pts = self.w_pxb[slot], self.w_ptb[slot]
            o3 = {n: self.w_out3[n][slot] for n in self.w_out3}
            cs = slice(cc * 64, (cc + 1) * 64)
            Ab = m2(AR[:, cc, 0])
            Bb, Kb = m2(BK[:, cc, 0]), m2(BK[:, cc, 1])
            pA = self.psum()
            mmr(pA[:, 0:128], Bb, Ab, reads=[BK, AR], writes=[pA])
            mmr(pA[:, 128:256], Kb, Ab, reads=[BK, AR], writes=[pA])
            pC = self.psum()
            mmr(pC[:, 0:128], Ab, Bb, reads=[BK, AR], writes=[pC])
            pS = self.psum()
            mmr(pS[:, 0:64], Bb, RT[:, cs], reads=[BK, dRT], writes=[pS])
            mmr(pS[:, 64:128], Kb, RT[:, cs], reads=[BK, dRT], writes=[pS])
            pi_ = 0
            px = pxs[pi_]
            k.tt(px[:, 0:128], pA[:, 0:128], self.w_ms[:], ALU.mult, reads=[pA, self.w_ms], writes=[px])
            k.tt(px[:, 128:256], px[:, 0:128], ident[:], ALU.add, reads=[px, ident], writes=[px], e="pool")
            pt = pts[pi_]
            k.tt(pt[:], pC[:, 0:128], self.w_mst[:], ALU.mult, reads=[pC, self.w_mst], writes=[pt])
            nak = o3["nak"]
            k.tt(nak[:], pA[:, 128:256], self.w_ms[:], ALU.mult, reads=[pA, self.w_ms], writes=[nak])
            nst = o3["nst"]
            k.tt(nst[:], pS[:, 0:128], m2(self.w_mist[:]), ALU.mult, reads=[pS, self.w_mist], writes=[nst])
            yield
            pP = self.psum()
            mmr(pP[:, 0:128], pt[:], px[:, 0:128], reads=[pt, px], writes=[pP])
            pQ = self.psum()
            mmr(pQ[:, 0:128], px[:, 0:128], pt[:], reads=[pt, px], writes=[pQ])
            pi_ ^= 1
            px2, pt2 = pxs[pi_], pts[pi_]
            k.cp(px2[:, 0:128], pP[:, 0:128], reads=[pP], writes=[px2], e="act")
            k.cp(px2[:, 128:256], px[:, 128:256], reads=[px], writes=[px2], e="pool")
            k.cp(pt2[:], pQ[:, 0:128], reads=[pQ], writes=[pt2], e="act")
            px, pt = px2, pt2
            yield
            X = o3["x"]
            for lvl in range(1, 6):
                last = (lvl == 5)
                pP = self.psum()
                if not last:
                    mmr(pP[:, 0:256], pt[:], px[:, 0:256], reads=[pt, px], writes=[pP])
                    pQ = self.psum()
                    mmr(pQ[:, 0:128], px[:, 0:128], pt[:], reads=[pt, px], writes=[pQ])
                    pi_ ^= 1
                    px2, pt2 = pxs[pi_], pts[pi_]
                    k.tt(px2[:, 128:256], pP[:, 128:256], px[:, 128:256], ALU.add, reads=[pP, px], writes=[px2])
                    k.cp(px2[:, 0:128], pP[:, 0:128], reads=[pP], writes=[px2], e="act")
                    k.cp(pt2[:], pQ[:, 0:128], reads=[pQ], writes=[pt2], e="act")
                    px, pt = px2, pt2
                else:
                    mmr(pP[:, 128:256], pt[:], px[:, 128:256], reads=[pt, px], writes=[pP])
                    k.tt(X[:], pP[:, 128:256], px[:, 128:256], ALU.add, reads=[pP, px], writes=[X])
                yield
            tm = []
            for wi, nm in enumerate(("bh", "kh", "vb")):
                ptr = self.psum()
                k.tr(ptr[:, 0:128], m2(HAT[:, cc, wi]), ident[:], reads=[HAT, ident], writes=[ptr])
                tbuf = o3[nm]
                k.cp(tbuf[:], ptr[:, 0:128], reads=[ptr], writes=[tbuf], e="act")
                tm.append(tbuf)
                yield
            P[cc] = (X, nak, nst) + tuple(tm)

        def chain(cc):
            cs = slice(cc * 64, (cc + 1) * 64)
            Ab = m2(AR[:, cc, 0])
            X, nak, nst, bh, kh, vb = P[cc]
            pW = self.psum()
            mmr(pW[:, 0:128], Ab, H[:], start=True, stop=False, reads=[AR, H], writes=[pW])
            mmr(pW[:, 0:128], nak[:], vb[:], start=False, stop=True, reads=[nak, vb], writes=[pW])
            w1 = self.w_rot["w1"].next()
            k.cp(w1[:], pW[:, 0:128], reads=[pW], writes=[w1], e="dve")
            yield
            pU = self.psum()
            mmr(pU[:, 0:128], X[:], w1[:], reads=[X, w1], writes=[pU])
            u = self.w_rot["u"].next()
            k.cp(u[:], pU[:, 0:128], reads=[pU], writes=[u], e="dve")
            yield
            mmr(py[:, cs], H[:], RT[:, cs], start=True, stop=False, reads=[H, dRT], writes=[py])
            mmr(py[:, cs], u[:], nst[:, 0:64], start=False, stop=False, reads=[u, nst], writes=[py])
            mmr(py[:, cs], vb[:], nst[:, 64:128], start=False, stop=True, reads=[vb, nst], writes=[py])
            pH = self.psum()
            mmr(pH[:, 0:128], bh[:], u[:], start=True, stop=False, reads=[bh, u], writes=[pH])
            mmr(pH[:, 0:128], kh[:], vb[:], start=False, stop=True, reads=[kh, vb], writes=[pH])
            k.stt(H[:], H[:], self.w_gc[:, cc:cc + 1], pH[:, 0:128], ALU.mult, ALU.add, reads=[H, self.w_gc, pH], writes=[H])
            yield

        active = {}

        def start(cc_):
            if cc_ < NCK:
                active[cc_] = prep(cc_)

        def step_preps():
            for cc_ in list(active):
                try:
                    next(active[cc_])
                except StopIteration:
                    del active[cc_]

        start(0)
        start(1)
        while 0 in active:
            step_preps()
            yield
        for cc in range(NCK):
            start(cc + 2)
            while cc in active:
                step_preps()
                yield
            gc = chain(cc)
            done = False
            while not done:
                try:
                    next(gc)
                except StopIteration:
                    done = True
                step_preps()
                yield
        k.cp(Y, py[:, 0:TR], reads=[py], writes=[dY], e="act")
        pm = self.psum()
        k.mm(pm[:, 0:TR], ca.blk[:], Y, reads=[ca.blk, dY], writes=[pm])
        k.stt(Y, pm[:, 0:TR], -1.0 / 64.0, Y, ALU.mult, ALU.add, reads=[pm, dY], writes=[dY])
        k.tt(TB, Y, Y, ALU.mult, reads=[dY], writes=[dTB])
        pvv = self.psum()
        k.mm(pvv[:, 0:TR], ca.blk[:], TB, reads=[ca.blk, dTB], writes=[pvv])
        k.act(TB, pvv[:, 0:TR], AF.Sqrt, reads=[pvv, self.w_gneps], writes=[dTB], bias=self.w_gneps[:], scale=1.0 / 64.0)
        k.op("dve", lambda g: g.reciprocal(TB, TB), reads=[dTB], writes=[dTB])
        k.tt(Y, Y, TB, ALU.mult, reads=[dY, dTB], writes=[dY])
        k.ts(Y, Y, self.pcol(PV_GNW), ALU.mult, self.pcol(PV_GNB), ALU.add, reads=[dY, pv], writes=[dY])
        yield
        k.stt(TB, R, self.pcol(PV_RK), Kk, ALU.mult, ALU.mult, reads=[dR, pv, dK], writes=[dTB])
        pb = self.psum()
        k.mm(pb[:, 0:TR], ca.blk[:], TB, reads=[ca.blk, dTB], writes=[pb])
        k.tt(TB, pb[:, 0:TR], V, ALU.mult, reads=[pb, dV], writes=[dTB])
        k.tt(Y, Y, TB, ALU.add, reads=[dY, dTB], writes=[dY])
        k.tt(o[:, n0:n1], Y, GD, ALU.mult, reads=[dY, dGD], writes=[o])

    def run(self, ntiles):
        k, c = self.k, self.c
        gainA = self.pv
        for ti in range(ntiles):
            t0 = ti * TT
            xt = self.xpool.next()
            self.cur_xt = xt
            if self.x_src is None:
                k.dma("sp", xt[:], self.xT[:, t0:t0 + TT].rearrange("(a p) c -> p a c", p=128), reads=[self.xT], writes=[xt])
            else:
                self.x_src(ti, xt)
            r = rstd_from(c, [(xt[:, kc, :], xt) for kc in range(8)])
            for kc in range(8):
                k.stt(self.hb[:, kc, :], xt[:, kc, :], self.pv[:, PV_GAIN + kc:PV_GAIN + kc + 1], r[:], ALU.mult, ALU.mult,
                      reads=[xt, self.pv, r], writes=[(self.hb, kc)])
            def emit(mi, o):
                if self.out_sink is None:
                    k.dma("sp", self.outT[mi * 128:(mi + 1) * 128, t0:t0 + TT], o[:], reads=[o], writes=[self.outT])
                else:
                    self.out_sink(ti, mi, o)
            for mi, name in ((1, "lru"), (3, "ret")):
                if name in self.do and name not in self.interleave:
                    emit(mi, getattr(self, name + "_tile")(ti))
            gens = []
            if "ret" in self.do and "ret" in self.interleave:
                gens.append((3, "ret", self.ret_gen(ti)))
            if "fox" in self.do:
                gens.append((0, "fox", self.fox_gen(ti)))
            if "rwkv" in self.do:
                gens.append((2, "rwkv", self.rwkv_gen(ti)))
            while gens:
                for item in list(gens):
                    try:
                        next(item[2])
                    except StopIteration:
                        gens.remove(item)
                        emit(item[0], self.results[item[1]])
            if self.tile_done is not None:
                self.tile_done(ti)


RET_THETA = 10000.0

G = 256
FOX_OFF = 0; LRU_OFF = 772; RWKV_OFF = 1284; RET_OFF = 2308
NCH = 17
NPV = 35

def prep_A(d, l, hh):
    w_in = d["w_in"][l]
    own = np.arange(hh * 128, hh * 128 + 128)
    oth = np.arange((1 - hh) * 128, (1 - hh) * 128 + 128)
    swap = np.concatenate([own[h * 64:(h + 1) * 64][np.r_[32:64, 0:32]] for h in range(2)])
    cols = [FOX_OFF + own, FOX_OFF + G + own, FOX_OFF + 2 * G + own,
            LRU_OFF + own, LRU_OFF + G + own,
            RWKV_OFF + own, RWKV_OFF + G + own, RWKV_OFF + 2 * G + own, RWKV_OFF + 3 * G + np.arange(128), RWKV_OFF + 3 * G + 128 + np.arange(128),
            RET_OFF + own, RET_OFF + G + own, RET_OFF + 2 * G + own, RET_OFF + 3 * G + own, RET_OFF + swap, RET_OFF + G + swap,
            RWKV_OFF + 2 * G + oth,
            FOX_OFF + 3 * G + hh * 2 + np.arange(2)]
    cols = np.concatenate(cols)
    Wown = np.ascontiguousarray(w_in[:, cols])
    pv = np.zeros((128, NPV), np.float32)
    c = 0
    def put(v):
        nonlocal c
        pv[:len(v), c] = v; c += 1
    for j in range(4): put(d["lru_conv_w"][l, j, own])
    put(d["lru_conv_b"][l, own]); put(d["lru_ra_b"][l, own]); put(d["lru_ri_b"][l, own]); put(d["lru_lambda"][l, own])
    mu = d["rwkv_mu"][l]
    put(mu[own]); put(mu[G + own]); put(mu[2 * G + own]); put(mu[2 * G + oth]); put(mu[3 * G:3 * G + 128]); put(mu[3 * G + 128:3 * G + 256])
    put(d["rwkv_w0"][l, own]); put(d["rwkv_a0"][l, own]); put(d["rwkv_k_k"][l, own]); put(d["rwkv_k_a"][l, own])
    put(d["rwkv_gn_w"][l, own]); put(d["rwkv_gn_b"][l, own]); put(d["rwkv_r_k"][l].reshape(-1)[own])
    put(d["rwkv_v0"][l - 1, own] if l > 0 else np.zeros(128, np.float32))
    put(d["ret_gn_w"][l, own])
    put(d["fox_f_bias"][l, hh * 2:hh * 2 + 2])
    put((np.arange(128) // 64 + 2 * hh).astype(np.float32)); put(np.full(128, 2 * hh, np.float32)); put(np.full(128, 2 * hh + 1, np.float32))
    assert c == 27
    pv[:, 27:35] = d["norm_mix_pre"][l].reshape(8, 128).T
    def blockdiag(w):
        m = np.zeros((128, 128), np.float32)
        for h in range(2):
            m[h * 64:(h + 1) * 64, h * 64:(h + 1) * 64] = w[2 * hh + h]
        return m
    mats = {"RA": blockdiag(d["lru_ra_w"][l]), "RI": blockdiag(d["lru_ri_w"][l]),
            "WA2": np.ascontiguousarray(np.concatenate([d["rwkv_w2"][l][:, own], d["rwkv_a2"][l][:, own]], axis=0)),
            "G2": np.ascontiguousarray(d["rwkv_g2"][l][:, own])}
    if l > 0:
        v1 = d["rwkv_v1"][l - 1]
        mats["V1"] = np.ascontiguousarray(np.stack([v1[own], v1[oth]], axis=1))
        mats["V2"] = np.ascontiguousarray(d["rwkv_v2"][l - 1][:, own])
    return Wown, pv, mats

MATSH = {"RA": [128, 128], "RI": [128, 128], "WA2": [128, 128], "G2": [128, 128], "V1": [128, 2, 32], "V2": [32, 128]}
NTOK_B = 2048
PAIRS = [[0, 1], [2, 3], [4, 5], [6, 7]]
DEPTH = 2


def _build_fused():
    nc = bass.Bass("TRN2", target_bir_lowering=False)
    D = lambda name, shape, kind="ExternalInput": nc.dram_tensor(name, shape, F32, kind=kind).ap()
    xT = D("xT", [1024, 4096])
    xown = D("xown", [1024, NTOK_B])
    memT = D("memT", [1024, 256])
    selv = D("selv", [128, 2])
    A_in, B_in = [], []
    for l in range(DEPTH):
        names = ["RA", "RI", "WA2", "G2"] + (["V1", "V2"] if l > 0 else [])
        A_in.append({"Wown": D("Wown%d" % l, [1024, NCOLS_A]), "pv": D("pv%d" % l, [128, NPV]),
                     "mats": {n: (D("%s_%d" % (n, l), MATSH[n]), MATSH[n]) for n in names}})
        W = {n: D("%s_%d" % (n, l), [1024, 1024]) for n in ("w_out", "wq", "wk", "wv", "wo")}
        W["w1"] = D("w1_%d" % l, [1024, 4096])
        W["w2"] = D("w2_%d" % l, [4096, 1024])
        B_in.append({"W": W, "gain": D("gain%d" % l, [128, 6, 8])})
    xo = D("xo", [1024, NTOK_B], "ExternalOutput")
    with contextlib.ExitStack() as st:
        k = K(nc, st)
        c = Ctx(k)
        ca = ConstA(k, c)
        sel = k.sb("selv_sb", [128, 2], F32)
        k.dma("sp", sel[:], selv, writes=[sel])
        xo_t = T(xo, "xo")
        xT_t, xown_t, memT_t = T(xT, "xT"), T(xown, "xown"), T(memT, "memT")
        vf_t = k.dram("vf_scr", [128, 4096])
        WB = [None] * DEPTH
        XO = k.dram("xo_scr", [1024, NTOK_B])
        XG = None
        for l in range(DEPTH):
            CA = [k.dram("ca%d_%d" % (l, i), [512, TT]) for i in range(8)]
            CG = [k.dram("cg%d_%d" % (l, i), [1024, TT]) for i in range(8)]
            with k.scope():
                a = A_in[l]
                sa = StageA(k, c, ca, l, xT_t, None, a["Wown"], a["pv"], a["mats"],
                            vfirst_in=vf_t if l > 0 else None, vfirst_out=vf_t if l == 0 else None)
                wb = {}
                for n, ap in B_in[l]["W"].items():
                    R_, C_ = ap.shape
                    t = k.dram("wb_%s_%d" % (n, l), [R_, C_], BF16)
                    src, dst = ap, t.ap
                    if C_ > 1024:
                        src = src.rearrange("r (a c) -> (r a) c", c=1024)
                        dst = dst.rearrange("r (a c) -> (r a) c", c=1024)
                    for r0 in range(0, src.shape[0], 512):
                        k.dma("pool", dst[r0:r0 + 512, :], src[r0:r0 + 512, :], writes=[(t, r0)])
                    wb[n] = t
                WB[l] = wb
                if l > 0:
                    XGl = XG

                    def x_src(ti, xt, XGl=XGl):
                        g = XGl[ti % 4]
                        r = ti // 4
                        k.dma("sp", xt[:], g[r * 1024:(r + 1) * 1024, :].rearrange("(a p) c -> p a c", p=128), reads=[g], writes=[xt])
                    sa.x_src = x_src

                def out_sink(ti, mi, o, CA=CA):
                    k.dma("sp", CA[ti][mi * 128:(mi + 1) * 128, :], o[:], reads=[o], writes=[(CA[ti], mi)])

                def tile_done(ti, CA=CA, CG=CG):
                    k.cc_allgather(CA[ti], CG[ti], PAIRS)
                sa.out_sink, sa.tile_done = out_sink, tile_done
                sa.run(8)
            with k.scope():
                b = B_in[l]
                ws = WStream(k, nbuf=4)
                gain = k.sb("gain_sb%d" % l, [128, 6, 8], F32)
                k.dma("sp", gain[:], b["gain"], writes=[gain])
                ctb = k.sb("ctb_%d" % l, [128, 8, TT], BF16)
                last = (l == DEPTH - 1)
                XA = [k.dram("xa%d_%d" % (l, i), [1024, TT]) for i in range(4)] if not last else None
                XGn = [k.dram("xg%d_%d" % (l, i), [2048, TT]) for i in range(4)] if not last else None
                x_res = xown_t if l == 0 else XO

                def cat_load(j, ct, CG=CG, ctb=ctb):
                    k.dma("pool", ct[:], CG[j][:, :].rearrange("(a p) c -> p a c", p=128), reads=[CG[j]], writes=[ct])
                    k.dma("pool", ctb[:], CG[4 + j][:, :].rearrange("(a p) c -> p a c", p=128), reads=[CG[4 + j]], writes=[ctb])
                    fl = lambda t: t[:].rearrange("p a c -> p (a c)")
                    k.ts(fl(ct), fl(ct), sel[:, 1:2], ALU.mult, reads=[ct, sel], writes=[ct])
                    k.stt(fl(ct), fl(ctb), sel[:, 0:1], fl(ct), ALU.mult, ALU.add, reads=[ctb, sel, ct], writes=[ct])

                def x_in(j, xt, x_res=x_res):
                    k.dma("sp", xt[:], x_res[:, j * TT:(j + 1) * TT].rearrange("(a p) c -> p a c", p=128), reads=[(x_res, j)], writes=[xt])

                def x_out(j, xt, last=last, XA=XA, XGn=XGn):
                    if last:
                        k.dma("sp", xo_t[:, j * TT:(j + 1) * TT].rearrange("(a p) c -> p a c", p=128), xt[:], reads=[xt], writes=[(xo_t, j)])
                    else:
                        k.dma("sp", XO[:, j * TT:(j + 1) * TT].rearrange("(a p) c -> p a c", p=128), xt[:], reads=[xt], writes=[(XO, j)])
                        k.dma("sp", XA[j][:, :].rearrange("(a p) c -> p a c", p=128), xt[:], reads=[xt], writes=[XA[j]])
                        k.cc_allgather(XA[j], XGn[j], PAIRS)

                stage_b(k, c, ws, l, None, None, None, memT_t, WB[l], gain, NTOK_B, cat_load=cat_load, x_in=x_in, x_out=x_out)
                XG = XGn
        k.finish("sp", [xo_t])
        k.finish("pool", [xo_t])
    return nc


def prep_B_gain(d, l):
    gl = lambda name: np.ascontiguousarray(d[name][l].reshape(8, 128).T)
    return np.ascontiguousarray(np.stack([gl(n) for n in ("norm_mix_post", "norm_xa_pre", "norm_xa_post", "norm_mem", "norm_mlp_pre", "norm_mlp_post")], axis=1))


def kernel(**inputs):
    d = {k_: np.asarray(v, dtype=np.float32) for k_, v in inputs.items()}
    B = 4
    nc = _build_fused()
    perm = np.concatenate([np.arange(mi * 256 + r * 128, mi * 256 + r * 128 + 128) for r in range(2) for mi in range(4)])
    shared = {}
    for l in range(DEPTH):
        shared["w_out_%d" % l] = np.ascontiguousarray(d["w_out"][l][perm])
        shared["wq_%d" % l] = d["xa_wq"][l]; shared["wk_%d" % l] = d["xa_wk"][l]
        shared["wv_%d" % l] = d["xa_wv"][l]; shared["wo_%d" % l] = d["xa_wo"][l]
        shared["w1_%d" % l] = d["mlp_w1"][l]; shared["w2_%d" % l] = d["mlp_w2"][l]
        shared["gain%d" % l] = prep_B_gain(d, l)
    prepA = {(l, hh): prep_A(d, l, hh) for l in range(DEPTH) for hh in range(2)}
    ins = []
    for cid in range(8):
        b, r = cid // 2, cid % 2
        xTb = np.ascontiguousarray(d["x"][b].T)
        m = dict(shared)
        m["xT"] = xTb
        m["xown"] = np.ascontiguousarray(xTb[:, r * NTOK_B:(r + 1) * NTOK_B])
        m["memT"] = np.ascontiguousarray(d["mem"][b].T)
        m["selv"] = np.ascontiguousarray(np.tile(np.array([[float(r), 1.0 - float(r)]], np.float32), (128, 1)))
        for l in range(DEPTH):
            Wown, pv, mats = prepA[(l, r)]
            m["Wown%d" % l] = Wown
            m["pv%d" % l] = pv
            for n, v in mats.items():
                m["%s_%d" % (n, l)] = v
        ins.append(m)
    res = run_bass_kernel_spmd(nc, ins, core_ids=list(range(8)))
    out = np.empty((B, 4096, 1024), np.float32)
    for cid in range(8):
        b, r = cid // 2, cid % 2
        out[b, r * NTOK_B:(r + 1) * NTOK_B, :] = res.results[cid]["xo"].T
    return out
```

```python
import contextlib, math
import numpy as np
import concourse.bass as bass
import concourse.mybir as mybir
from concourse.bass_utils import run_bass_kernel_spmd


F32 = mybir.dt.float32
BF16 = mybir.dt.bfloat16
I32 = mybir.dt.int32
AF = mybir.ActivationFunctionType
ALU = mybir.AluOpType


class Reg:
    __slots__ = ("last_w", "reads")

    def __init__(self):
        self.last_w = None
        self.reads = []


class T:
    def __init__(self, ap, name):
        self.ap = ap
        self.name = name
        self.regs = {None: Reg()}

    def __getitem__(self, idx):
        return self.ap[idx]


class K:
    ENGS = ("pe", "dve", "act", "pool", "sp")

    def __init__(self, nc, stack, n_dma_sems=6):
        self.nc = nc
        self.stack = stack
        self.eng = {"pe": nc.tensor, "dve": nc.vector, "act": nc.scalar, "pool": nc.gpsimd, "sp": nc.sync}
        self.sem = {}
        self.tick = {}
        for e in self.ENGS:
            self.sem[e] = stack.enter_context(nc.semaphore("s_" + e))
            self.tick[e] = 0
        self.dq = {}
        for q in ("sp", "pool", "act"):
            lst = []
            for i in range(n_dma_sems):
                key = "d_%s%d" % (q, i)
                self.sem[key] = stack.enter_context(nc.semaphore(key))
                self.tick[key] = 0
                lst.append(key)
            self.dq[q] = [lst, 0]
        self.sem["cc"] = stack.enter_context(nc.semaphore("s_cc"))
        self.tick["cc"] = 0
        self.waited = {e: {} for e in self.ENGS}
        self.ninst = {e: 0 for e in self.ENGS}
        self.nwait = 0

    @contextlib.contextmanager
    def scope(self):
        old = self.stack
        with contextlib.ExitStack() as st:
            self.stack = st
            try:
                yield
            finally:
                self.barrier()
                self.stack = old

    def barrier(self):
        deps = [(sk, v) for sk, v in self.tick.items() if v > 0]
        for e in self.ENGS:
            self._wait(e, [d for d in deps if d[0] != e])

    def cc_allgather(self, in_t, out_t, groups):
        reads = self._norm([in_t])
        writes = self._norm([out_t])
        self._wait("pool", self._deps(reads, writes))
        inst = self.eng["pool"].collective_compute("AllGather", ALU.bypass, replica_groups=groups, ins=[in_t.ap], outs=[out_t.ap])
        self.tick["cc"] += 1
        inst.then_inc(self.sem["cc"], 1)
        self.ninst["pool"] += 1
        self._record(("cc", self.tick["cc"]), reads, writes)
        return inst

    _uid = 0

    def sb(self, name, shape, dtype=F32):
        K._uid += 1
        name = "%s_u%d" % (name, K._uid)
        return T(self.stack.enter_context(self.nc.sbuf_tensor(name, list(shape), dtype)), name)

    def ps(self, name, shape, dtype=F32):
        return T(self.stack.enter_context(self.nc.psum_tensor(name, list(shape), dtype)), name)

    def dram(self, name, shape, dtype=F32, kind="Internal"):
        return T(self.nc.dram_tensor(name, list(shape), dtype, kind=kind).ap(), name)

    @staticmethod
    def _norm(lst):
        out = []
        for x in lst:
            if x is None:
                continue
            if isinstance(x, T):
                out.append((x, None))
            else:
                out.append(x)
        return out

    def _deps(self, reads, writes):
        deps = []
        for (t, k) in reads:
            if k is None:
                for r in t.regs.values():
                    if r.last_w:
                        deps.append(r.last_w)
            else:
                r = t.regs.get(k)
                if r is not None and r.last_w:
                    deps.append(r.last_w)
                if t.regs[None].last_w:
                    deps.append(t.regs[None].last_w)
        for (t, k) in writes:
            if k is None:
                rs = list(t.regs.values())
            else:
                rs = [t.regs[None]]
                if k in t.regs:
                    rs.append(t.regs[k])
            for r in rs:
                if r.last_w:
                    deps.append(r.last_w)
                deps.extend(r.reads)
        return deps

    def _record(self, me, reads, writes):
        for (t, k) in reads:
            t.regs.setdefault(k, Reg()).reads.append(me)
        for (t, k) in writes:
            if k is None:
                for kk in list(t.regs.keys()):
                    if kk is not None:
                        del t.regs[kk]
                r = t.regs[None]
            else:
                r = t.regs.setdefault(k, Reg())
            r.last_w = me
            r.reads = []

    def _wait(self, e, deps):
        best = {}
        for (sk, v) in deps:
            if sk == e and e == "pe":
                continue
            if best.get(sk, 0) < v:
                best[sk] = v
        w = self.waited[e]
        for sk, v in best.items():
            if w.get(sk, 0) < v:
                self.eng[e].wait_ge(self.sem[sk], v)
                w[sk] = v
                self.nwait += 1

    def op(self, e, fn, reads=(), writes=()):
        reads = self._norm(reads)
        writes = self._norm(writes)
        self._wait(e, self._deps(reads, writes))
        inst = fn(self.eng[e])
        self.tick[e] += 1
        inst.then_inc(self.sem[e], 1)
        self.ninst[e] += 1
        self._record((e, self.tick[e]), reads, writes)
        return inst

    def dma(self, q, out, in_, reads=(), writes=(), **kw):
        reads = self._norm(reads)
        writes = self._norm(writes)
        lst, i = self.dq[q]
        sk = lst[i % len(lst)]
        self.dq[q][1] = i + 1
        deps = self._deps(reads, writes)
        if self.tick[sk] > 0:
            deps.append((sk, self.tick[sk]))
        self._wait(q, deps)
        inst = self.eng[q].dma_start(out=out, in_=in_, **kw)
        self.tick[sk] += 16
        inst.then_inc(self.sem[sk], 16)
        self.ninst[q] += 1
        self._record((sk, self.tick[sk]), reads, writes)
        return inst

    def finish(self, e, tiles):
        deps = []
        for t in tiles:
            for r in t.regs.values():
                if r.last_w:
                    deps.append(r.last_w)
        self._wait(e, deps)

    def mm(self, out, lhsT, rhs, start=True, stop=True, reads=(), writes=()):
        return self.op("pe", lambda e: e.matmul(out, lhsT, rhs, start=start, stop=stop), reads, writes)

    def tr(self, out, in_, ident, reads=(), writes=()):
        return self.op("pe", lambda e: e.transpose(out, in_, ident), reads, writes)

    def act(self, out, in_, func, reads=(), writes=(), bias=None, scale=None, e="act"):
        kw = {}
        if bias is not None:
            kw["bias"] = bias
        if scale is not None:
            kw["scale"] = scale
        return self.op(e, lambda g: g.activation(out, in_, func, **kw), reads, writes)

    def tt(self, out, in0, in1, op, reads=(), writes=(), e="dve"):
        return self.op(e, lambda g: g.tensor_tensor(out, in0, in1, op), reads, writes)

    def ts(self, out, in0, s1, op0, s2=None, op1=None, reads=(), writes=(), e="dve"):
        if op1 is None:
            return self.op(e, lambda g: g.tensor_scalar(out, in0, s1, None, op0), reads, writes)
        return self.op(e, lambda g: g.tensor_scalar(out, in0, s1, s2, op0, op1), reads, writes)

    def stt(self, out, in0, scalar, in1, op0, op1, reads=(), writes=()):
        return self.op("dve", lambda g: g.scalar_tensor_tensor(out, in0, scalar, in1, op0, op1), reads, writes)

    def cp(self, out, in_, reads=(), writes=(), e="dve"):
        if e == "act":
            return self.op(e, lambda g: g.copy(out, in_), reads, writes)
        return self.op(e, lambda g: g.tensor_copy(out, in_), reads, writes)

    def memset(self, out, val, writes=(), e="pool"):
        return self.op(e, lambda g: g.memset(out, val), (), writes)


TT = 512
EPS = 1e-6


class Rot:
    def __init__(self, k, name, shape, dtype, n):
        self.t = [k.sb("%s%d" % (name, i), shape, dtype) for i in range(n)]
        self.i = 0

    def next(self):
        t = self.t[self.i % len(self.t)]
        self.i += 1
        return t


class Ctx:
    def __init__(self, k):
        self.k = k
        self.ones = k.sb("ones_bf", [128, 128], BF16)
        k.memset(self.ones[:], 1.0, writes=[self.ones])
        self.eps = k.sb("eps_c", [128, 1], F32)
        k.memset(self.eps[:], EPS, writes=[self.eps])
        self.psums = [k.ps("ps%d" % i, [128, 512], F32) for i in range(8)]
        self.pi = 0
        self.sq = Rot(k, "sq", [128, TT], BF16, 3)
        self.rstd = Rot(k, "rstd", [128, TT], F32, 2)
        self.tmp = Rot(k, "tmpf", [128, TT], F32, 3)

    def psum(self):
        p = self.psums[self.pi % 8]
        self.pi += 1
        return p


def rstd_from(c, srcs, n=TT, nfeat=1024.0):
    k = c.k
    pst = c.psum()
    nk = len(srcs)
    for i, (ap, t) in enumerate(srcs):
        sq = c.sq.next()
        k.act(sq[:, 0:n], ap, AF.Square, reads=[t], writes=[sq])
        k.mm(pst[:, 0:n], c.ones[:], sq[:, 0:n], start=(i == 0), stop=(i == nk - 1), reads=[c.ones, sq], writes=[pst])
    r = c.rstd.next()
    k.act(r[:, 0:n], pst[:, 0:n], AF.Sqrt, reads=[pst, c.eps], writes=[r], bias=c.eps[:], scale=1.0 / nfeat)
    k.op("dve", lambda g: g.reciprocal(r[:, 0:n], r[:, 0:n]), reads=[r], writes=[r])
    return r


def pre_norm(c, xt, gain, gi, out, n=TT):
    k = c.k
    r = rstd_from(c, [(xt[:, kc, 0:n], xt) for kc in range(8)], n)
    for kc in range(8):
        k.stt(out[:, kc, 0:n], xt[:, kc, 0:n], gain[:, gi, kc:kc + 1], r[:, 0:n], ALU.mult, ALU.mult,
              reads=[xt, gain, r], writes=[(out, kc)])


def post_norm_res(c, m, gain, gi, xt, n=TT):
    k = c.k
    r = rstd_from(c, [(m[:, kc, 0:n], (m, kc)) for kc in range(8)], n)
    for kc in range(8):
        t = c.tmp.next()
        k.tt(t[:, 0:n], m[:, kc, 0:n], r[:, 0:n], ALU.mult, reads=[(m, kc), r], writes=[t])
        k.stt(xt[:, kc, 0:n], t[:, 0:n], gain[:, gi, kc:kc + 1], xt[:, kc, 0:n], ALU.mult, ALU.add,
              reads=[t, gain, (xt, kc)], writes=[(xt, kc)])


class WStream:
    def __init__(self, k, nbuf=4, elems=8 * 512):
        self.k = k
        self.elems = elems
        self.pool = Rot(k, "wbuf", [128, elems], BF16, nbuf)

    def load(self, w_ap, r0, nk, c0, ncols):
        buf = self.pool.next()
        view = buf[:, 0:nk * ncols].rearrange("p (a c) -> p a c", a=nk)
        if isinstance(w_ap, T):
            src = w_ap[r0:r0 + nk * 128, c0:c0 + ncols].rearrange("(a p) c -> p a c", p=128)
            self.qi = getattr(self, "qi", 0) + 1
            self.k.dma("sp", view, src, reads=[w_ap], writes=[buf])
        else:
            src = w_ap[r0:r0 + nk * 128, c0:c0 + ncols].rearrange("(a p) c -> p a c", p=128)
            self.k.dma("pool", view, src, writes=[buf])
        return buf, view


def pre_norm_split(c, xt, gain, gi, out, n=TT):
    k = c.k
    for kc in range(8):
        k.ts(out[:, kc, 0:n], xt[:, kc, 0:n], gain[:, gi, kc:kc + 1], ALU.mult, reads=[(xt, kc), gain], writes=[(out, kc)])
    return lambda: rstd_from(c, [(xt[:, kc, 0:n], (xt, kc)) for kc in range(8)], n)


def dense(c, ws, w_ap, nk, nout_chunks, rhs_fn, rhs_reads, sink, n=TT, cols_per_load=512, hook=None):
    k = c.k
    per = cols_per_load // 128
    for o0 in range(0, nout_chunks, per):
        buf, view = ws.load(w_ap, 0, nk, o0 * 128, cols_per_load)
        for j in range(per):
            oc = o0 + j
            ps = c.psum()
            for kc in range(nk):
                k.mm(ps[:, 0:n], view[:, kc, j * 128:(j + 1) * 128], rhs_fn(kc), start=(kc == 0), stop=(kc == nk - 1),
                     reads=[buf] + rhs_reads(kc), writes=[ps])
            if hook is not None and oc == 0:
                hook()
            sink(oc, ps)


def stage_b(k, c, ws, l, catT, xT_in, xT_out, memT, W, gain, NT, tok0=0, cat_load=None, x_in=None, x_out=None):
    m = k.sb("m_%d" % l, [128, 8, TT], F32)
    memt = m
    k.dma("sp", memt[:, :, 0:256], memT[:, :].rearrange("(a p) c -> p a c", p=128), reads=[memT], writes=[memt])
    mn = k.sb("mn_%d" % l, [128, 8, 256], BF16)
    pre_norm(c, memt, gain, 3, mn, n=256)
    kT = k.sb("kT_%d" % l, [128, 8, 256], BF16)
    vM = k.sb("vM_%d" % l, [128, 2, 1024], BF16)

    def sink_k(oc, ps):
        k.cp(kT[:, oc, :], ps[:, 0:256], reads=[ps], writes=[(kT, oc)], e="act")

    dense(c, ws, W["wk"], 8, 8, lambda kc: mn[:, kc, :], lambda kc: [(mn, kc)], sink_k, n=256)
    for half in range(2):
        buf, view = ws.load(W["wv"], 0, 8, half * 512, 512)
        for mc in range(2):
            ps = c.psum()
            for kc in range(8):
                k.mm(ps[:, :], mn[:, kc, mc * 128:(mc + 1) * 128], view[:, kc, :], start=(kc == 0), stop=(kc == 7),
                     reads=[buf, (mn, kc)], writes=[ps])
            k.cp(vM[:, mc, half * 512:(half + 1) * 512], ps[:, :], reads=[ps], writes=[(vM, (mc, half))], e="act")

    xpool = Rot(k, "xt_%d" % l, [128, 8, TT], F32, 2)
    cpool = Rot(k, "ct_%d" % l, [128, 8, TT], BF16, 1)
    hb = k.sb("hb_%d" % l, [128, 8, TT], BF16)
    qT = k.sb("qT_%d" % l, [128, 8, TT], BF16)
    oT = k.sb("oT_%d" % l, [128, 8, TT], BF16)
    hid = k.sb("hid_%d" % l, [128, 32, TT], BF16)
    pT = Rot(k, "pT_%d" % l, [128, TT], BF16, 4)
    relu = Rot(k, "relu_%d" % l, [128, TT], F32, 3)
    rden = Rot(k, "rden_%d" % l, [128, TT], F32, 2)
    r2buf = k.sb("r2buf_%d" % l, [128, TT], F32)

    def sink_m(oc, ps):
        k.cp(m[:, oc, :], ps[:, :], reads=[ps], writes=[(m, oc)], e="dve")

    for ti in range(NT // TT):
        t0 = tok0 + ti * TT
        xt = xpool.next()
        if x_in is None:
            k.dma("sp", xt[:], xT_in[:, t0:t0 + TT].rearrange("(a p) c -> p a c", p=128), reads=[xT_in], writes=[xt])
        elif ti == 0:
            x_in(ti, xt)
        if x_in is not None and ti + 1 < NT // TT:
            xnext = xpool.t[(xpool.i) % len(xpool.t)]
            x_in(ti + 1, xnext)
        ct = cpool.next()
        if cat_load is None:
            k.dma("pool", ct[:], catT[:, t0:t0 + TT].rearrange("(a p) c -> p a c", p=128), reads=[catT], writes=[ct])
        elif ti == 0:
            cat_load(ti, ct)
        dense(c, ws, W["w_out"], 8, 8, lambda kc: ct[:, kc, :], lambda kc: [ct], sink_m)
        if cat_load is not None and ti + 1 < NT // TT:
            cat_load(ti + 1, ct)
        post_norm_res(c, m, gain, 0, xt)
        stats_q = pre_norm_split(c, xt, gain, 1, hb)
        rq = {}

        def hook_q():
            rq["r"] = stats_q()

        def sink_q(oc, ps):
            r_ = rq["r"]
            k.tt(qT[:, oc, :], ps[:, :], r_[:], ALU.mult, reads=[ps, r_], writes=[(qT, oc)])

        dense(c, ws, W["wq"], 8, 8, lambda kc: hb[:, kc, :], lambda kc: [(hb, kc)], sink_q, hook=hook_q)
        for hd in range(4):
            pts = []
            for mc in range(2):
                ps = c.psum()
                for j in range(2):
                    dc = hd * 2 + j
                    k.mm(ps[:, :], kT[:, dc, mc * 128:(mc + 1) * 128], qT[:, dc, :], start=(j == 0), stop=(j == 1),
                         reads=[(kT, dc), (qT, dc)], writes=[ps])
                pt = pT.next()
                k.act(pt[:], ps[:, :], AF.Exp, reads=[ps], writes=[pt], scale=1.0 / 16.0)
                pts.append(pt)
            pden = c.psum()
            for mc in range(2):
                k.mm(pden[:, :], c.ones[:], pts[mc][:], start=(mc == 0), stop=(mc == 1), reads=[c.ones, pts[mc]], writes=[pden])
            rd = rden.next()
            k.op("dve", lambda g: g.reciprocal(rd[:], pden[:, :]), reads=[pden], writes=[rd])
            for j in range(2):
                dc = hd * 2 + j
                ps = c.psum()
                for mc in range(2):
                    k.mm(ps[:, :], vM[:, mc, dc * 128:(dc + 1) * 128], pts[mc][:], start=(mc == 0), stop=(mc == 1),
                         reads=[vM, pts[mc]], writes=[ps])
                k.tt(oT[:, dc, :], ps[:, :], rd[:], ALU.mult, reads=[ps, rd], writes=[(oT, dc)])
        dense(c, ws, W["wo"], 8, 8, lambda kc: oT[:, kc, :], lambda kc: [(oT, kc)], sink_m)
        post_norm_res(c, m, gain, 2, xt)
        stats_m = pre_norm_split(c, xt, gain, 4, hb)

        def hook_m():
            r_ = stats_m()
            k.tt(r2buf[:], r_[:], r_[:], ALU.mult, reads=[r_], writes=[r2buf])

        def sink_h(f, ps):
            r = relu.next()
            k.act(r[:], ps[:, :], AF.Relu, reads=[ps], writes=[r])
            k.tt(hid[:, f, :], r[:], r[:], ALU.mult, reads=[r], writes=[(hid, f)], e="pool")

        dense(c, ws, W["w1"], 8, 32, lambda kc: hb[:, kc, :], lambda kc: [(hb, kc)], sink_h, hook=hook_m)
        for fb in range(8):
            buf, view = ws.load(W["w2"], fb * 512, 4, 0, 1024)
            for oc in range(8):
                ps = c.psums[oc]
                for f4 in range(4):
                    f = fb * 4 + f4
                    k.mm(ps[:, :], view[:, f4, oc * 128:(oc + 1) * 128], hid[:, f, :], start=(f == 0), stop=(f == 31),
                         reads=[buf, (hid, f)], writes=[ps])
        for oc in range(8):
            k.tt(m[:, oc, :], c.psums[oc][:, :], r2buf[:], ALU.mult, reads=[c.psums[oc], r2buf], writes=[(m, oc)])
        post_norm_res(c, m, gain, 5, xt)
        if x_out is None:
            k.dma("sp", xT_out[:, t0:t0 + TT].rearrange("(a p) c -> p a c", p=128), xt[:], reads=[xt], writes=[xT_out])
        else:
            x_out(ti, xt)


(CH_FQ, CH_FK, CH_FV, CH_LX, CH_LY, CH_RR, CH_RK, CH_RV, CH_RWA, CH_RGD,
 CH_TQ, CH_TK, CH_TV, CH_TG, CH_TQS, CH_TKS, CH_RVO) = range(17)
NCH = 17
NCOLS_A = NCH * 128 + 2

(PV_CW0, PV_CW1, PV_CW2, PV_CW3, PV_CB, PV_RAB, PV_RIB, PV_LAM,
 PV_MU_R, PV_MU_K, PV_MU_V, PV_MU_VO, PV_MU_WA, PV_MU_GD,
 PV_W0, PV_A0, PV_KK, PV_KA, PV_GNW, PV_GNB, PV_RK, PV_V0,
 PV_TGN, PV_FB, PV_HP, PV_H0, PV_H1) = range(27)
PV_GAIN = 27
NPV = 35


class ConstA:
    def __init__(self, k, c):
        self.k = k
        nc = k.nc
        g = k.eng["pool"]
        self.ident = k.sb("identf", [128, 128], F32)
        k.memset(self.ident[:], 1.0, writes=[self.ident])
        k.op("pool", lambda e: e.affine_select(out=self.ident[:], in_=self.ident[:], pattern=[[-1, 128]], compare_op=ALU.is_equal,
                                               fill=0.0, base=0, channel_multiplier=1), reads=[self.ident], writes=[self.ident])
        self.ut_f = k.sb("ut_f", [128, 128], F32)
        k.memset(self.ut_f[:], 1.0, writes=[self.ut_f])
        k.op("pool", lambda e: e.affine_select(out=self.ut_f[:], in_=self.ut_f[:], pattern=[[1, 128]], compare_op=ALU.is_ge,
                                               fill=0.0, base=0, channel_multiplier=-1), reads=[self.ut_f], writes=[self.ut_f])
        self.ut_b = k.sb("ut_b", [128, 128], BF16)
        k.cp(self.ut_b[:], self.ut_f[:], reads=[self.ut_f], writes=[self.ut_b])
        self.blk = k.sb("blk_f", [128, 128], F32)
        k.memset(self.blk[:], 0.0, writes=[self.blk])
        k.memset(self.blk[0:64, 0:64], 1.0, writes=[self.blk])
        k.memset(self.blk[64:128, 64:128], 1.0, writes=[self.blk])
        self.ones_f = k.sb("ones_f", [128, 512], F32)
        k.memset(self.ones_f[:], 1.0, writes=[self.ones_f])
        self.one_c = k.sb("one_c", [128, 1], F32)
        k.memset(self.one_c[:], 1.0, writes=[self.one_c])
        self.zero_c = k.sb("zero_c", [128, 1], F32)
        k.memset(self.zero_c[:], 0.0, writes=[self.zero_c])


def gelu_tanh(k, c, y_ps, y_t, tmps):
    y = tmps.next()
    k.cp(y[:], y_ps, reads=[y_t], writes=[y], e="act")
    t = tmps.next()
    k.tt(t[:], y[:], y[:], ALU.mult, reads=[y], writes=[t])
    k.ts(t[:], t[:], 0.044715, ALU.mult, 1.0, ALU.add, reads=[t], writes=[t])
    k.tt(t[:], t[:], y[:], ALU.mult, reads=[t, y], writes=[t])
    k.act(t[:], t[:], AF.Sigmoid, reads=[t], writes=[t], scale=2.0 * math.sqrt(2.0 / math.pi))
    return y, t


class StageA:
    def __init__(self, k, c, ca, l, xT, outT, Wown, pvd, mats, vfirst_in=None, vfirst_out=None, do=("lru", "ret", "fox", "rwkv")):
        self.k, self.c, self.ca, self.l = k, c, ca, l
        self.xT, self.outT = xT, outT
        self.do = do
        self.x_src = self.out_sink = self.tile_done = None
        self.results = {}
        self.interleave = ()
        self.f32r = False
        self.vf_in, self.vf_out = vfirst_in, vfirst_out
        L = "a%d_" % l
        self.L = L
        self.W = k.sb(L + "W", [128, 8, NCOLS_A], BF16)
        for kc in range(8):
            k.dma("pool", self.W[:, kc, :], Wown[kc * 128:(kc + 1) * 128, :], writes=[(self.W, kc)], max_dma_last_dim=4096)
        self.pv = k.sb(L + "pv", [128, NPV], F32)
        k.dma("sp", self.pv[:], pvd, writes=[self.pv])
        self.mats = {}
        for name, (ap, shape) in mats.items():
            t = k.sb(L + name, shape, F32)
            k.dma("sp", t[:], ap, writes=[t])
            self.mats[name] = t
        self.xpool = Rot(k, L + "xt", [128, 8, TT], F32, 1)
        self.hb = k.sb(L + "hb", [128, 8, TT], BF16)
        self.tmps = Rot(k, L + "t", [128, TT], F32, 12)
        self.outs = Rot(k, L + "o", [128, TT], F32, 3)
        self.named_ps = c.psums[5:8]
        if "lru" in do:
            self.init_lru()
        if "ret" in do:
            self.init_ret()
        if "fox" in do:
            self.init_fox()
        if "rwkv" in do:
            self.init_rwkv()

    def pcol(self, i):
        return self.pv[:, i:i + 1]

    def psum(self):
        c = self.c
        p = c.psums[c.pi % 5]
        c.pi += 1
        return p

    def proj(self, ch, n0=0, n1=TT):
        k = self.k
        ps = self.psum()
        for kc in range(8):
            k.mm(ps[:, n0:n1], self.W[:, kc, ch * 128:(ch + 1) * 128], self.hb[:, kc, n0:n1], start=(kc == 0), stop=(kc == 7),
                 reads=[(self.W, kc), (self.hb, kc)], writes=[ps])
        return ps

    def init_lru(self):
        k, L = self.k, self.L
        self.l_xbuf = k.sb(L + "lxb", [128, 3 + TT], F32)
        k.memset(self.l_xbuf[:, 0:3], 0.0, writes=[self.l_xbuf])
        self.l_h = k.sb(L + "lh", [128, 1], F32)
        k.memset(self.l_h[:], 0.0, writes=[self.l_h])
        self.l_c1 = k.sb(L + "lc1", [128, 2], F32)
        k.act(self.l_c1[:, 0:1], self.pcol(PV_LAM), AF.Exp, reads=[self.pv], writes=[self.l_c1], scale=-1.0)
        k.act(self.l_c1[:, 0:1], self.l_c1[:, 0:1], AF.Ln, reads=[self.l_c1, self.ca.one_c], writes=[self.l_c1], bias=self.ca.one_c[:])
        k.ts(self.l_c1[:, 1:2], self.l_c1[:, 0:1], -16.0, ALU.mult, reads=[self.l_c1], writes=[self.l_c1])
        k.ts(self.l_c1[:, 0:1], self.l_c1[:, 0:1], -8.0, ALU.mult, reads=[self.l_c1], writes=[self.l_c1])

    def lru_tile(self, ti):
        k, c, ca = self.k, self.c, self.ca
        xb = self.l_xbuf
        ps = self.proj(CH_LX)
        k.cp(xb[:, 3:3 + TT], ps[:, :], reads=[ps], writes=[xb], e="act")
        xc = self.tmps.next()
        k.ts(xc[:], xb[:, 0:TT], self.pcol(PV_CW0), ALU.mult, self.pcol(PV_CB), ALU.add, reads=[xb, self.pv], writes=[xc])
        for j in range(1, 4):
            k.stt(xc[:], xb[:, j:j + TT], self.pcol(PV_CW0 + j), xc[:], ALU.mult, ALU.add, reads=[xb, self.pv, xc], writes=[xc])
        k.cp(xb[:, 0:3], xb[:, TT:TT + 3], reads=[xb], writes=[xb], e="dve")
        pr = self.psum()
        k.mm(pr[:, :], self.mats["RA"][:], xc[:], reads=[self.mats["RA"], xc], writes=[pr])
        pi_ = self.psum()
        k.mm(pi_[:, :], self.mats["RI"][:], xc[:], reads=[self.mats["RI"], xc], writes=[pi_])
        r = self.tmps.next()
        k.act(r[:], pr[:, :], AF.Sigmoid, reads=[pr, self.pv], writes=[r], bias=self.pcol(PV_RAB))
        ig = self.tmps.next()
        k.act(ig[:], pi_[:, :], AF.Sigmoid, reads=[pi_, self.pv], writes=[ig], bias=self.pcol(PV_RIB))
        a = self.tmps.next()
        k.act(a[:], r[:], AF.Exp, reads=[r, self.l_c1], writes=[a], scale=self.l_c1[:, 0:1])
        mult = self.tmps.next()
        k.act(mult[:], r[:], AF.Exp, reads=[r, self.l_c1], writes=[mult], scale=self.l_c1[:, 1:2])
        k.act(mult[:], mult[:], AF.Sqrt, reads=[mult, ca.one_c], writes=[mult], scale=-1.0, bias=ca.one_c[:])
        k.tt(ig[:], ig[:], xc[:], ALU.mult, reads=[ig, xc], writes=[ig])
        k.tt(ig[:], ig[:], mult[:], ALU.mult, reads=[ig, mult], writes=[ig])
        h = self.tmps.next()
        k.op("dve", lambda g: g.tensor_tensor_scan(h[:], a[:], ig[:], self.l_h[:, 0:1], ALU.mult, ALU.add),
             reads=[a, ig, self.l_h], writes=[h])
        k.cp(self.l_h[:], h[:, TT - 1:TT], reads=[h], writes=[self.l_h], e="dve")
        py = self.proj(CH_LY)
        o = self.outs.next()
        y, t = gelu_tanh(k, c, py[:, :], py, self.tmps)
        k.tt(o[:], t[:], y[:], ALU.mult, reads=[t, y], writes=[o])
        k.tt(o[:], o[:], h[:], ALU.mult, reads=[o, h], writes=[o])
        return o

    def init_ret(self):
        k, L, ca = self.k, self.L, self.ca
        self.t_lg = k.sb(L + "tlg", [128, 3], F32)
        tmp = k.sb(L + "tlgt", [128, 3], F32)
        k.ts(tmp[:], self.pv[:, PV_HP:PV_HP + 3], 5.0, ALU.add, reads=[self.pv], writes=[tmp])
        k.act(tmp[:], tmp[:], AF.Exp, reads=[tmp], writes=[tmp], scale=-math.log(2.0))
        k.act(self.t_lg[:], tmp[:], AF.Ln, reads=[tmp, ca.one_c], writes=[self.t_lg], scale=-1.0, bias=ca.one_c[:])
        ni = k.sb(L + "tni", [128, 128], I32)
        k.op("pool", lambda e: e.iota(ni[:], pattern=[[1, 128]], base=0, channel_multiplier=0), writes=[ni])
        nf = k.sb(L + "tnf", [128, 128], F32)
        k.cp(nf[:], ni[:], reads=[ni], writes=[nf])
        self.t_xi = k.sb(L + "txi", [128, 128], F32)
        self.t_zt = k.sb(L + "tzt", [128, 128], F32)
        tt_ = k.sb(L + "ttmp", [128, 128], F32)
        k.ts(tt_[:], nf[:], 1.0, ALU.add, reads=[nf], writes=[tt_])
        k.act(self.t_xi[:], tt_[:], AF.Exp, reads=[tt_, self.t_lg], writes=[self.t_xi], scale=self.t_lg[:, 0:1])
        k.ts(tt_[:], nf[:], -1.0, ALU.mult, 127.0, ALU.add, reads=[nf], writes=[tt_])
        k.act(self.t_zt[:], tt_[:], AF.Exp, reads=[tt_, self.t_lg], writes=[self.t_zt], scale=self.t_lg[:, 0:1])
        k.ts(self.t_zt[:], self.t_zt[:], 0.125, ALU.mult, reads=[self.t_zt], writes=[self.t_zt])
        self.t_gc = k.sb(L + "tgc", [128, 1], F32)
        k.act(self.t_gc[:], self.t_lg[:, 0:1], AF.Exp, reads=[self.t_lg], writes=[self.t_gc], scale=128.0)
        di = k.sb(L + "tdi", [128, 128], I32)
        k.op("pool", lambda e: e.iota(di[:], pattern=[[1, 128]], base=0, channel_multiplier=-1), writes=[di])
        df = k.sb(L + "tdf", [128, 128], F32)
        k.cp(df[:], di[:], reads=[di], writes=[df])
        k.ts(df[:], df[:], 0.0, ALU.max, reads=[df], writes=[df])
        self.t_dt = k.sb(L + "tdt", [128, 2, 128], F32)
        for h in range(2):
            k.act(self.t_dt[:, h, :], df[:], AF.Exp, reads=[df, self.t_lg], writes=[self.t_dt], scale=self.t_lg[:, 1 + h:2 + h])
            k.stt(self.t_dt[:, h, :], self.t_dt[:, h, :], 0.125, ca.ut_f[:], ALU.mult, ALU.mult, reads=[self.t_dt, ca.ut_f], writes=[self.t_dt])
        pi_ = k.sb(L + "tpi", [128, 2], I32)
        k.op("pool", lambda e: e.iota(pi_[:, 0:1], pattern=[[0, 1]], base=0, channel_multiplier=1), writes=[pi_])
        k.ts(pi_[:, 1:2], pi_[:, 0:1], 32, ALU.bitwise_and, reads=[pi_], writes=[pi_])
        k.ts(pi_[:, 0:1], pi_[:, 0:1], 31, ALU.bitwise_and, reads=[pi_], writes=[pi_])
        pf = k.sb(L + "tpf", [128, 2], F32)
        k.cp(pf[:], pi_[:], reads=[pi_], writes=[pf])
        self.t_inv = k.sb(L + "tinv", [128, 1], F32)
        k.act(self.t_inv[:], pf[:, 0:1], AF.Exp, reads=[pf], writes=[self.t_inv], scale=-math.log(RET_THETA) / 31.0)
        self.t_sgn = k.sb(L + "tsgn", [128, 1], F32)
        k.ts(self.t_sgn[:], pf[:, 1:2], 1.0 / 16.0, ALU.mult, -1.0, ALU.add, reads=[pf], writes=[self.t_sgn])
        qi = k.sb(L + "tqi", [128, TT], I32)
        k.op("pool", lambda e: e.iota(qi[:], pattern=[[1, TT]], base=0, channel_multiplier=0), writes=[qi])
        self.t_pos = k.sb(L + "tpos", [128, TT], F32)
        k.cp(self.t_pos[:], qi[:], reads=[qi], writes=[self.t_pos])
        self.t_R = k.sb(L + "tR", [128, 64], F32)
        k.memset(self.t_R[:], 0.0, writes=[self.t_R])
        self.t_Rb = k.sb(L + "tRb", [128, 64], BF16)
        k.memset(self.t_Rb[:], 0.0, writes=[self.t_Rb])
        self.t_kzt = [k.sb(L + "tkzt%d" % i, [128, 128], BF16) for i in range(4)]
        self.t_ind = [k.sb(L + "tind%d" % i, [128, 128], BF16) for i in range(8)]
        self.t_kv = k.sb(L + "tkv", [128, 4, 64], F32)
        self.t_Rb4 = k.sb(L + "tRb4", [128, 4, 64], BF16)
        self.t_vtm = Rot(k, L + "tvtm", [128, 4, 128], BF16, 1)
        self.t_bf = Rot(k, L + "tbf", [128, TT], BF16, 4)

    def sincos(self, t0):
        k, ca = self.k, self.ca
        TWO_PI = 2.0 * math.pi
        C1 = 6.28125
        C2 = TWO_PI - C1
        res = []
        for shift in (math.pi / 2.0, 0.0):
            ang = self.tmps.next()
            k.ts(ang[:], self.t_pos[:], float(t0), ALU.add, self.t_inv[:, 0:1], ALU.mult, reads=[self.t_pos, self.t_inv], writes=[ang])
            if shift:
                k.ts(ang[:], ang[:], shift, ALU.add, reads=[ang], writes=[ang])
            kf = self.tmps.next()
            ki = kf[:].bitcast(I32)
            k.ts(kf[:], ang[:], 1.0 / TWO_PI, ALU.mult, reads=[ang], writes=[kf])
            k.cp(ki, kf[:], reads=[kf], writes=[kf])
            k.cp(kf[:], ki, reads=[kf], writes=[kf])
            k.stt(ang[:], kf[:], -C1, ang[:], ALU.mult, ALU.add, reads=[kf, ang], writes=[ang])
            k.stt(ang[:], kf[:], -C2, ang[:], ALU.mult, ALU.add, reads=[kf, ang], writes=[ang])
            k.ts(kf[:], ang[:], math.pi, ALU.is_gt, -TWO_PI, ALU.mult, reads=[ang], writes=[kf])
            k.tt(ang[:], ang[:], kf[:], ALU.add, reads=[ang, kf], writes=[ang])
            k.ts(kf[:], ang[:], -math.pi, ALU.is_lt, TWO_PI, ALU.mult, reads=[ang], writes=[kf])
            k.tt(ang[:], ang[:], kf[:], ALU.add, reads=[ang, kf], writes=[ang])
            k.ts(ang[:], ang[:], math.pi, ALU.min, -math.pi, ALU.max, reads=[ang], writes=[ang])
            k.act(ang[:], ang[:], AF.Sin, reads=[ang], writes=[ang])
            res.append(ang)
        C, S = res
        k.ts(S[:], S[:], self.t_sgn[:, 0:1], ALU.mult, reads=[S, self.t_sgn], writes=[S])
        return C, S

    def ret_gen(self, ti):
        k, c, ca = self.k, self.c, self.ca
        t0 = ti * TT
        C, S = self.sincos(t0)
        yield
        rot = []
        for (cha, chs) in ((CH_TQ, CH_TQS), (CH_TK, CH_TKS)):
            pa = self.proj(cha)
            a = self.tmps.next()
            k.tt(a[:], pa[:, :], C[:], ALU.mult, reads=[pa, C], writes=[a])
            pb = self.proj(chs)
            b = self.tmps.next()
            k.tt(b[:], pb[:, :], S[:], ALU.mult, reads=[pb, S], writes=[b])
            k.tt(a[:], a[:], b[:], ALU.add, reads=[a, b], writes=[a], e="pool")
            rot.append(a)
            yield
        qr, kr = rot
        qb = self.t_bf.next()
        k.cp(qb[:], qr[:], reads=[qr], writes=[qb], e="act")
        kb = self.t_bf.next()
        k.cp(kb[:], kr[:], reads=[kr], writes=[kb], e="act")
        qx = self.t_bf.next()
        k.tt(qx[:].rearrange("p (c n) -> p c n", n=128), qr[:].rearrange("p (c n) -> p c n", n=128),
             self.t_xi[:].unsqueeze(1).to_broadcast([128, 4, 128]), ALU.mult, reads=[qr, self.t_xi], writes=[qx])
        kz = self.tmps.next()
        k.tt(kz[:].rearrange("p (c n) -> p c n", n=128), kr[:].rearrange("p (c n) -> p c n", n=128),
             self.t_zt[:].unsqueeze(1).to_broadcast([128, 4, 128]), ALU.mult, reads=[kr, self.t_zt], writes=[kz])
        vtm = self.t_vtm.next()
        for cc in range(4):
            ps = self.psum()
            for kc in range(8):
                k.mm(ps[:, 0:128], self.hb[:, kc, cc * 128:(cc + 1) * 128], self.W[:, kc, CH_TV * 128:(CH_TV + 1) * 128],
                     start=(kc == 0), stop=(kc == 7), reads=[(self.W, kc), (self.hb, kc)], writes=[ps])
            k.cp(vtm[:, cc, :], ps[:, 0:128], reads=[ps], writes=[(vtm, cc)], e="act")
            yield
        osb = self.tmps.next()
        inds = []
        for cc in range(4):
            cs = slice(cc * 128, (cc + 1) * 128)
            ptr = self.psum()
            k.tr(ptr[:, 0:128], kz[:, cs], ca.ident[:], reads=[kz, ca.ident], writes=[ptr])
            kzt = self.t_kzt[cc]
            k.cp(kzt[:], ptr[:, 0:128], reads=[ptr], writes=[kzt], e="act")
            for h in range(2):
                hp = slice(h * 64, (h + 1) * 64)
                pin = self.psum()
                k.mm(pin[:, 0:128], kb[hp, cs], qb[hp, cs], reads=[kb, qb], writes=[pin])
                ind = self.t_ind[cc * 2 + h]
                k.tt(ind[:], pin[:, 0:128], self.t_dt[:, h, :], ALU.mult, reads=[pin, self.t_dt], writes=[ind])
                inds.append(ind)
            yield
        for cc in range(4):
            pkv = self.psum()
            kzt = self.t_kzt[cc]
            for h in range(2):
                hp = slice(h * 64, (h + 1) * 64)
                k.mm(pkv[hp, 0:64], kzt[:, hp], vtm[:, cc, hp], reads=[kzt, (vtm, cc)], writes=[pkv])
            k.cp(self.t_kv[:, cc, :], pkv[:, 0:64], reads=[pkv], writes=[(self.t_kv, cc)], e="act")
            yield
        for cc in range(4):
            k.cp(self.t_Rb4[:, cc, :], self.t_R[:], reads=[self.t_R], writes=[(self.t_Rb4, cc)], e="act")
            k.stt(self.t_R[:], self.t_R[:], self.t_gc[:, 0:1], self.t_kv[:, cc, :], ALU.mult, ALU.add,
                  reads=[self.t_R, self.t_gc, (self.t_kv, cc)], writes=[self.t_R])
        yield
        for cc in range(4):
            cs = slice(cc * 128, (cc + 1) * 128)
            po = self.psum()
            for h in range(2):
                hp = slice(h * 64, (h + 1) * 64)
                ind = inds[cc * 2 + h]
                k.mm(po[hp, 0:128], vtm[:, cc, hp], ind[:], start=True, stop=False, reads=[(vtm, cc), ind], writes=[po])
                k.mm(po[hp, 0:128], self.t_Rb4[hp, cc, :], qx[hp, cs], start=False, stop=True, reads=[(self.t_Rb4, cc), qx], writes=[po])
            k.cp(osb[:, cs], po[:, 0:128], reads=[po], writes=[osb], e="act")
            yield
        sq = self.tmps.next()
        k.tt(sq[:], osb[:], osb[:], ALU.mult, reads=[osb], writes=[sq], e="pool")
        pss = self.psum()
        k.mm(pss[:, :], ca.blk[:], sq[:], reads=[ca.blk, sq], writes=[pss])
        rs = self.tmps.next()
        k.act(rs[:], pss[:, :], AF.Sqrt, reads=[pss, c.eps], writes=[rs], bias=c.eps[:], scale=1.0 / 64.0)
        k.op("dve", lambda g: g.reciprocal(rs[:], rs[:]), reads=[rs], writes=[rs])
        pg = self.proj(CH_TG)
        sg = self.tmps.next()
        k.act(sg[:], pg[:, :], AF.Silu, reads=[pg], writes=[sg])
        o = self.outs.next()
        k.stt(o[:], osb[:], self.pcol(PV_TGN), rs[:], ALU.mult, ALU.mult, reads=[osb, self.pv, rs], writes=[o])
        k.tt(o[:], o[:], sg[:], ALU.mult, reads=[o, sg], writes=[o])
        self.results["ret"] = o

    def ret_tile(self, ti):
        for _ in self.ret_gen(ti):
            pass
        return self.results["ret"]


    def init_fox(self):
        k, L, ca = self.k, self.L, self.ca
        self.f_kT = k.sb(L + "fkT", [128, 4096], BF16)
        self.f_v = k.sb(L + "fv", [128, 32, 128], BF16)
        self.f_q = k.sb(L + "fq", [128, TT], BF16)
        self.f_row = Rot(k, L + "frow", [2, TT], F32, 3)
        self.f_clast = k.sb(L + "fcl", [2, 1], F32)
        k.memset(self.f_clast[:], 0.0, writes=[self.f_clast])
        self.f_ccol = k.sb(L + "fcc", [128, 32, 2], F32)
        self.f_cend = k.sb(L + "fce", [128, 4, 2], F32)
        self.f_B = Rot(k, L + "fB", [128, 32, 4], F32, 2)
        self.f_pT = Rot(k, L + "fpT", [128, TT], BF16, 4)
        self.f_rd = k.sb(L + "frd", [128, TT], F32)
        self.f_sel = k.sb(L + "fsel", [128, 128], F32)
        k.memset(self.f_sel[:], 1.0, writes=[self.f_sel])
        k.op("pool", lambda e: e.affine_select(out=self.f_sel[:], in_=self.f_sel[:], pattern=[[0, 128]], compare_op=ALU.is_equal,
                                               fill=0.0, base=-127, channel_multiplier=1), reads=[self.f_sel], writes=[self.f_sel])
        self.f_nb = k.sb(L + "fnb", [2, 1], F32)
        k.ts(self.f_nb[:], self.pv[0:2, PV_FB:PV_FB + 1], -1.0, ALU.mult, reads=[self.pv], writes=[self.f_nb])

    def fox_gen(self, ti):
        k, c, ca = self.k, self.c, self.ca
        t0 = ti * TT
        pq = self.proj(CH_FQ)
        k.cp(self.f_q[:], pq[:, :], reads=[pq], writes=[self.f_q], e="act")
        pk = self.proj(CH_FK)
        k.cp(self.f_kT[:, t0:t0 + TT], pk[:, :], reads=[pk], writes=[(self.f_kT, ti)], e="act")
        for cc in range(4):
            ps = self.psum()
            for kc in range(8):
                k.mm(ps[:, 0:128], self.hb[:, kc, cc * 128:(cc + 1) * 128], self.W[:, kc, CH_FV * 128:(CH_FV + 1) * 128],
                     start=(kc == 0), stop=(kc == 7), reads=[(self.W, kc), (self.hb, kc)], writes=[ps])
            k.cp(self.f_v[:, ti * 4 + cc, :], ps[:, 0:128], reads=[ps], writes=[(self.f_v, ti * 4 + cc)], e="act")
        pf = self.psum()
        for kc in range(8):
            k.mm(pf[0:2, :], self.W[:, kc, NCH * 128:NCH * 128 + 2], self.hb[:, kc, :], start=(kc == 0), stop=(kc == 7),
                 reads=[(self.W, kc), (self.hb, kc)], writes=[pf])
        e_ = self.f_row.next()
        k.act(e_[:], pf[0:2, :], AF.Exp, reads=[pf, self.f_nb], writes=[e_], scale=-1.0, bias=self.f_nb[:])
        k.act(e_[:], e_[:], AF.Ln, reads=[e_, ca.one_c], writes=[e_], bias=ca.one_c[0:2, :])
        cs = self.f_row.next()
        k.op("dve", lambda g: g.tensor_tensor_scan(cs[:], ca.ones_f[0:2, :], e_[:], self.f_clast[:, 0:1], ALU.mult, ALU.add),
             reads=[ca.ones_f, e_, self.f_clast], writes=[cs])
        k.cp(self.f_clast[:], cs[:, TT - 1:TT], reads=[cs], writes=[self.f_clast])
        pc = self.psum()
        for cc in range(4):
            k.tr(pc[:, 2 * cc:2 * cc + 2], cs[:, cc * 128:(cc + 1) * 128], ca.ident[0:2, 0:2], reads=[cs, ca.ident], writes=[pc])
        k.cp(self.f_ccol[:, ti * 4:(ti + 1) * 4, :].rearrange("p a b -> p (a b)"), pc[:, 0:8], reads=[pc], writes=[(self.f_ccol, ti)])
        pe = self.psum()
        k.mm(pe[:, 0:8], self.f_sel[:], self.f_ccol[:, ti * 4:(ti + 1) * 4, :].rearrange("p a b -> p (a b)"),
             reads=[self.f_sel, (self.f_ccol, ti)], writes=[pe])
        k.cp(self.f_cend[:].rearrange("p a b -> p (a b)"), pe[:, 0:8], reads=[pe], writes=[self.f_cend])
        po, pd = self.named_ps[0], self.named_ps[1]
        nkc = 4 * (ti + 1)
        yield
        for h in range(2):
            hp = slice(h * 64, (h + 1) * 64)
            B = self.f_B.next()
            k.tt(B[:, 0:nkc, :], self.f_ccol[:, 0:nkc, h:h + 1].to_broadcast([128, nkc, 4]),
                 self.f_cend[:, :, h].unsqueeze(1).to_broadcast([128, nkc, 4]), ALU.subtract,
                 reads=[self.f_ccol, self.f_cend], writes=[B])

            def pv_step(kc, pt, col0):
                k.mm(po[hp, col0:TT], self.f_v[:, kc, hp], pt[:, col0:TT], start=(kc == 0), stop=(kc == nkc - 1),
                     reads=[(self.f_v, kc), pt], writes=[po])
                k.mm(pd[hp, col0:TT], c.ones[:, 0:64], pt[:, col0:TT], start=(kc == 0), stop=(kc == nkc - 1),
                     reads=[c.ones, pt], writes=[pd])

            pend = None
            for kc in range(nkc):
                j = kc - 4 * ti
                col0 = 128 * j if j >= 0 else 0
                ps = self.psum()
                k.mm(ps[:, col0:TT], self.f_kT[hp, kc * 128:(kc + 1) * 128], self.f_q[hp, col0:TT],
                     reads=[(self.f_kT, kc // 4), self.f_q], writes=[ps])
                pt = self.f_pT.next()
                for qbl in range(col0 // 128, 4):
                    qs = slice(qbl * 128, (qbl + 1) * 128)
                    k.act(pt[:, qs], ps[:, qs], AF.Exp, reads=[ps, B], writes=[pt], scale=0.125, bias=B[:, kc, qbl:qbl + 1])
                if j >= 0:
                    qs = slice(j * 128, (j + 1) * 128)
                    k.tt(pt[:, qs], pt[:, qs], ca.ut_b[:], ALU.mult, reads=[pt, ca.ut_b], writes=[pt], e="pool")
                if pend is not None:
                    pv_step(*pend)
                pend = (kc, pt, col0)
                yield
            pv_step(*pend)
            yield
        rd = self.f_rd
        k.op("dve", lambda g: g.reciprocal(rd[:], pd[:, :]), reads=[pd], writes=[rd])
        o = self.outs.next()
        k.tt(o[:], po[:, :], rd[:], ALU.mult, reads=[po, rd], writes=[o])
        self.results["fox"] = o

    def fox_tile(self, ti):
        for _ in self.fox_gen(ti):
            pass
        return self.results["fox"]

    def init_rwkv(self):
        k, L, ca = self.k, self.L, self.ca
        self.TR = 256
        self.NCK = self.TR // 64
        NCK = self.NCK
        self.w_carry = k.sb(L + "wcar", [128, 6], F32)
        k.memset(self.w_carry[:], 0.0, writes=[self.w_carry])
        self.w_AR = k.sb(L + "wAR", [128, NCK, 2, 2, 64], F32)
        self.w_BK = k.sb(L + "wBK", [128, NCK, 2, 2, 64], F32)
        self.w_HAT = k.sb(L + "wHAT", [128, NCK, 3, 2, 64], F32)
        for t in (self.w_AR, self.w_BK, self.w_HAT):
            k.memset(t[:], 0.0, writes=[t])
        self.w_H = k.sb(L + "wH", [128, 128], F32)
        k.memset(self.w_H[:], 0.0, writes=[self.w_H])
        self.w_ms = k.sb(L + "wms", [128, 128], F32)
        k.memset(self.w_ms[:], 1.0, writes=[self.w_ms])
        k.op("pool", lambda e: e.affine_select(out=self.w_ms[:], in_=self.w_ms[:], pattern=[[1, 128]], compare_op=ALU.is_gt,
                                               fill=0.0, base=0, channel_multiplier=-1), reads=[self.w_ms], writes=[self.w_ms])
        self.w_mst = k.sb(L + "wmst", [128, 128], F32)
        k.memset(self.w_mst[:], 1.0, writes=[self.w_mst])
        k.op("pool", lambda e: e.affine_select(out=self.w_mst[:], in_=self.w_mst[:], pattern=[[-1, 128]], compare_op=ALU.is_gt,
                                               fill=0.0, base=0, channel_multiplier=1), reads=[self.w_mst], writes=[self.w_mst])
        self.w_mist = k.sb(L + "wmist", [128, 2, 64], F32)
        for j in range(2):
            k.cp(self.w_mist[0:64, j, :], ca.ut_f[0:64, 0:64], reads=[ca.ut_f], writes=[self.w_mist])
            k.cp(self.w_mist[64:128, j, :], ca.ut_f[64:128, 64:128], reads=[ca.ut_f], writes=[self.w_mist])
        self.w_chm = k.sb(L + "wchm", [128, self.TR], F32)
        k.memset(self.w_chm[:], 1.0, writes=[self.w_chm])
        k.memset(self.w_chm[:].rearrange("p (c n) -> p c n", n=64)[:, :, 0:1], 0.0, writes=[self.w_chm])
        self.w_omka = k.sb(L + "womka", [128, 1], F32)
        k.ts(self.w_omka[:], self.pcol(PV_KA), -1.0, ALU.mult, 1.0, ALU.add, reads=[self.pv], writes=[self.w_omka])
        self.w_gc = k.sb(L + "wgc", [128, NCK], F32)
        self.w_rot = {n: Rot(k, L + "w" + n, [128, 128], F32, 2) for n in ("w1", "u")}
        self.w_out3 = {n: [k.sb(L + "w%s%d" % (n, i), [128, 128], F32) for i in range(3)] for n in ("nak", "nst", "bh", "kh", "vb", "x")}
        self.w_ptb = [[k.sb(L + "wpt%d_%d" % (i, j), [128, 128], F32) for j in range(2)] for i in range(3)]
        self.w_pxb = [[k.sb(L + "wpx%d_%d" % (i, j), [128, 256], F32) for j in range(2)] for i in range(3)]
        self.w_v32 = k.sb(L + "wv32", [32, self.TR], F32)
        self.w_gneps = k.sb(L + "wgne", [128, 1], F32)
        k.memset(self.w_gneps[:], 64e-5, writes=[self.w_gneps])

    def nb(self, i):
        TR = self.TR
        xt = self.cur_xt
        return xt[:, i // 2, (i % 2) * TR:(i % 2 + 1) * TR], (xt, ("nb", i))

    def rwkv_gen(self, ti):
        o = self.outs.next()
        for half in range(self.TT_R):
            yield from self.rwkv_half(ti, half, o)
        self.results["rwkv"] = o

    def rwkv_tile(self, ti):
        for _ in self.rwkv_gen(ti):
            pass
        return self.results["rwkv"]

    TT_R = 2

    def rwkv_half(self, ti, half, o):
        k, c, ca, l = self.k, self.c, self.ca, self.l
        TR, NCK = self.TR, self.NCK
        n0 = half * TR
        n1 = n0 + TR
        g0 = ti * TT + n0
        pv = self.pv
        (R, dR), (Kk, dK), (V, dV), (WA, dWA), (GD, dGD), (VO, dVO), (A, dA), (LW, dLW), (LG, dLG), (KKN, dKKN), (BV, dBV), \
            (RT, dRT), (TA, dTA), (TB, dTB), (TC, dTC), (Y, dY) = [self.nb(i) for i in range(16)]
        car = self.w_carry

        def lerp(X, dX, ch, mu_i, ci):
            ps = self.proj(ch, n0, n1)
            k.cp(X, ps[:, n0:n1], reads=[ps], writes=[dX], e="act")
            k.tt(TA[:, 1:TR], X[:, 0:TR - 1], X[:, 1:TR], ALU.subtract, reads=[dX], writes=[dTA])
            k.tt(TA[:, 0:1], car[:, ci:ci + 1], X[:, 0:1], ALU.subtract, reads=[dX, car], writes=[dTA])
            k.cp(car[:, ci:ci + 1], X[:, TR - 1:TR], reads=[dX], writes=[car])
            k.stt(X, TA, self.pcol(mu_i), X, ALU.mult, ALU.add, reads=[dTA, pv, dX], writes=[dX])

        lerp(R, dR, CH_RR, PV_MU_R, 0)
        yield
        lerp(Kk, dK, CH_RK, PV_MU_K, 1)
        yield
        lerp(V, dV, CH_RV, PV_MU_V, 2)
        yield
        lerp(WA, dWA, CH_RWA, PV_MU_WA, 3)
        yield
        lerp(GD, dGD, CH_RGD, PV_MU_GD, 4)
        yield
        if l > 0:
            lerp(VO, dVO, CH_RVO, PV_MU_VO, 5)
            yield
        WA2, G2 = self.mats["WA2"], self.mats["G2"]
        k.act(TB[0:64, :], WA[0:64, :], AF.Tanh, reads=[dWA], writes=[dTB])
        pz = self.psum()
        k.mm(pz[:, 0:TR], WA2[0:64, :], TB[0:64, :], reads=[WA2, dTB], writes=[pz])
        k.act(LW, pz[:, 0:TR], AF.Sigmoid, reads=[pz, pv], writes=[dLW], bias=self.pcol(PV_W0))
        k.ts(LW, LW, -math.exp(-0.5), ALU.mult, reads=[dLW], writes=[dLW])
        pa = self.psum()
        k.mm(pa[:, 0:TR], WA2[64:128, :], WA[64:128, :], reads=[WA2, dWA], writes=[pa])
        k.act(A, pa[:, 0:TR], AF.Sigmoid, reads=[pa, pv], writes=[dA], bias=self.pcol(PV_A0))
        k.act(GD, GD, AF.Sigmoid, reads=[dGD], writes=[dGD])
        pg = self.psum()
        k.mm(pg[:, 0:TR], G2[:], GD, reads=[G2, dGD], writes=[pg])
        k.cp(GD, pg[:, 0:TR], reads=[pg], writes=[dGD], e="act")
        if l > 0:
            V1, V2 = self.mats["V1"], self.mats["V2"]
            p1 = self.psum()
            k.mm(p1[0:32, 0:TR], V1[:, 0, :], V, start=True, stop=False, reads=[V1, dV], writes=[p1])
            k.mm(p1[0:32, 0:TR], V1[:, 1, :], VO, start=False, stop=True, reads=[V1, dVO], writes=[p1])
            k.cp(self.w_v32[:], p1[0:32, 0:TR], reads=[p1], writes=[self.w_v32], e="act")
            p2 = self.psum()
            k.mm(p2[:, 0:TR], V2[:], self.w_v32[:], reads=[V2, self.w_v32], writes=[p2])
            k.act(TB, p2[:, 0:TR], AF.Sigmoid, reads=[p2, pv], writes=[dTB], bias=self.pcol(PV_V0))
            k.dma("sp", TC, self.vf_in[:, g0:g0 + TR], reads=[self.vf_in], writes=[dTC])
            k.tt(TC, TC, V, ALU.subtract, reads=[dTC, dV], writes=[dTC])
            k.tt(TC, TC, TB, ALU.mult, reads=[dTC, dTB], writes=[dTC])
            k.tt(V, V, TC, ALU.add, reads=[dV, dTC], writes=[dV])
        else:
            k.dma("sp", self.vf_out[:, g0:g0 + TR], V, reads=[dV], writes=[self.vf_out])
        yield
        k.ts(KKN, Kk, self.pcol(PV_KK), ALU.mult, reads=[dK, pv], writes=[dKKN])
        k.tt(TB, KKN, KKN, ALU.mult, reads=[dKKN], writes=[dTB])
        pn = self.psum()
        k.mm(pn[:, 0:TR], ca.blk[:], TB, reads=[ca.blk, dTB], writes=[pn])
        k.act(TB, pn[:, 0:TR], AF.Sqrt, reads=[pn], writes=[dTB])
        k.ts(TB, TB, 1e-12, ALU.max, reads=[dTB], writes=[dTB])
        k.op("dve", lambda g: g.reciprocal(TB, TB), reads=[dTB], writes=[dTB])
        k.tt(KKN, KKN, TB, ALU.mult, reads=[dKKN, dTB], writes=[dKKN])
        k.ts(TB, A, self.pcol(PV_KA), ALU.mult, self.w_omka[:, 0:1], ALU.add, reads=[dA, pv, self.w_omka], writes=[dTB])
        k.tt(Kk, Kk, TB, ALU.mult, reads=[dK, dTB], writes=[dK])
        k.tt(BV, KKN, A, ALU.mult, reads=[dKKN, dA], writes=[dBV])
        yield
        k.op("dve", lambda g: g.tensor_tensor_scan(LG, self.w_chm[:], LW, 0.0, ALU.mult, ALU.add),
             reads=[self.w_chm, dLW], writes=[dLG])
        AR, BK, HAT = self.w_AR, self.w_BK, self.w_HAT
        v3 = lambda ap: ap.rearrange("p (c n) -> p c n", n=64)

        def to_blk(dst, which, src, dsrc, mul=None, dmul=None, op="mult"):
            for h in range(2):
                hp = slice(h * 64, (h + 1) * 64)
                if mul is None:
                    k.cp(dst[hp, :, which, h, :], v3(src[hp, :]), reads=[dsrc], writes=[dst])
                else:
                    k.tt(dst[hp, :, which, h, :], v3(src[hp, :]), v3(mul[hp, :]), ALU.mult, reads=[dsrc, dmul], writes=[dst])

        k.act(TB, LG, AF.Exp, reads=[dLG], writes=[dTB])
        k.tt(RT, R, TB, ALU.mult, reads=[dR, dTB], writes=[dRT])
        to_blk(AR, 1, RT, dRT)
        k.act(self.w_gc[:], v3(LG)[:, :, 63], AF.Exp, reads=[dLG], writes=[self.w_gc])
        yield
        k.tt(TB, LG, LW, ALU.subtract, reads=[dLG, dLW], writes=[dTB])
        k.act(TB, TB, AF.Exp, reads=[dTB], writes=[dTB])
        k.stt(TC, KKN, -1.0, TB, ALU.mult, ALU.mult, reads=[dKKN, dTB], writes=[dTC])
        to_blk(AR, 0, TC, dTC)
        k.act(TB, LG, AF.Exp, reads=[dLG], writes=[dTB], scale=-1.0)
        to_blk(BK, 0, BV, dBV, TB, dTB)
        to_blk(BK, 1, Kk, dK, TB, dTB)
        yield
        k.tt(v3(TB), v3(LG)[:, :, 63:64].to_broadcast([128, NCK, 64]), v3(LG), ALU.subtract, reads=[dLG], writes=[dTB])
        k.act(TB, TB, AF.Exp, reads=[dTB], writes=[dTB])
        to_blk(HAT, 0, BV, dBV, TB, dTB)
        to_blk(HAT, 1, Kk, dK, TB, dTB)
        to_blk(HAT, 2, V, dV)
        py = self.named_ps[2]
        ident = ca.ident
        m2 = lambda ap: ap.rearrange("p a b -> p (a b)")
        H = self.w_H
        P = {}
        if self.f32r:
            F32R = mybir.dt.float32r

            def mmr(out, lhsT, rhs, **kw):
                return k.mm(out, lhsT.bitcast(F32R), rhs.bitcast(F32R), **kw)
        else:
            mmr = k.mm

        def prep(cc):
            slot = cc % 3
            pxs, pts = self.w_pxb[slot], self.w_ptb[slot]
            o3 = {n: self.w_out3[n][slot] for n in self.w_out3}
            cs = slice(cc * 64, (cc + 1) * 64)
            Ab = m2(AR[:, cc, 0])
            Bb, Kb = m2(BK[:, cc, 0]), m2(BK[:, cc, 1])
            pA = self.psum()
            mmr(pA[:, 0:128], Bb, Ab, reads=[BK, AR], writes=[pA])
            mmr(pA[:, 128:256], Kb, Ab, reads=[BK, AR], writes=[pA])
            pC = self.psum()
            mmr(pC[:, 0:128], Ab, Bb, reads=[BK, AR], writes=[pC])
            pS = self.psum()
            mmr(pS[:, 0:64], Bb, RT[:, cs], reads=[BK, dRT], writes=[pS])
            mmr(pS[:, 64:128], Kb, RT[:, cs], reads=[BK, dRT], writes=[pS])
            pi_ = 0
            px = pxs[pi_]
            k.tt(px[:, 0:128], pA[:, 0:128], self.w_ms[:], ALU.mult, reads=[pA, self.w_ms], writes=[px])
            k.tt(px[:, 128:256], px[:, 0:128], ident[:], ALU.add, reads=[px, ident], writes=[px], e="pool")
            pt = pts[pi_]
            k.tt(pt[:], pC[:, 0:128], self.w_mst[:], ALU.mult, reads=[pC, self.w_mst], writes=[pt])
            nak = o3["nak"]
            k.tt(nak[:], pA[:, 128:256], self.w_ms[:], ALU.mult, reads=[pA, self.w_ms], writes=[nak])
            nst = o3["nst"]
            k.tt(nst[:], pS[:, 0:128], m2(self.w_mist[:]), ALU.mult, reads=[pS, self.w_mist], writes=[nst])
            yield
            pP = self.psum()
            mmr(pP[:, 0:128], pt[:], px[:, 0:128], reads=[pt, px], writes=[pP])
            pQ = self.psum()
            mmr(pQ[:, 0:128], px[:, 0:128], pt[:], reads=[pt, px], writes=[pQ])
            pi_ ^= 1
            px2, pt2 = pxs[pi_], pts[pi_]
            k.cp(px2[:, 0:128], pP[:, 0:128], reads=[pP], writes=[px2], e="act")
            k.cp(px2[:, 128:256], px[:, 128:256], reads=[px], writes=[px2], e="pool")
            k.cp(pt2[:], pQ[:, 0:128], reads=[pQ], writes=[pt2], e="act")
            px, pt = px2, pt2
            yield
            X = o3["x"]
            for lvl in range(1, 6):
                last = (lvl == 5)
                pP = self.psum()
                if not last:
                    mmr(pP[:, 0:256], pt[:], px[:, 0:256], reads=[pt, px], writes=[pP])
                    pQ = self.psum()
                    mmr(pQ[:, 0:128], px[:, 0:128], pt[:], reads=[pt, px], writes=[pQ])
                    pi_ ^= 1
                    px2, pt2 = pxs[pi_], pts[pi_]
                    k.tt(px2[:, 128:256], pP[:, 128:256], px[:, 128:256], ALU.add, reads=[pP, px], writes=[px2])
                    k.cp(px2[:, 0:128], pP[:, 0:128], reads=[pP], writes=[px2], e="act")
                    k.cp(pt2[:], pQ[:, 0:128], reads=[pQ], writes=[pt2], e="act")
                    px, pt = px2, pt2
                else:
                    mmr(pP[:, 128:256], pt[:], px[:, 128:256], reads=[pt, px], writes=[pP])
                    k.tt(X[:], pP[:, 128:256], px[:, 128:256], ALU.add, reads=[pP, px], writes=[X])
                yield
            tm = []
            for wi, nm in enumerate(("bh", "kh", "vb")):
                ptr = self.psum()
                k.tr(ptr[:, 0:128], m2(HAT[:, cc, wi]), ident[:], reads=[HAT, ident], writes=[ptr])
                tbuf = o3[nm]
                k.cp(tbuf[:], ptr[:, 0:128], reads=[ptr], writes=[tbuf], e="act")
                tm.append(tbuf)
                yield
            P[cc] = (X, nak, nst) + tuple(tm)

        def chain(cc):
            cs = slice(cc * 64, (cc + 1) * 64)
            Ab = m2(AR[:, cc, 0])
            X, nak, nst, bh, kh, vb = P[cc]
            pW = self.psum()
            mmr(pW[:, 0:128], Ab, H[:], start=True, stop=False, reads=[AR, H], writes=[pW])
            mmr(pW[:, 0:128], nak[:], vb[:], start=False, stop=True, reads=[nak, vb], writes=[pW])
            w1 = self.w_rot["w1"].next()
            k.cp(w1[:], pW[:, 0:128], reads=[pW], writes=[w1], e="dve")
            yield
            pU = self.psum()
            mmr(pU[:, 0:128], X[:], w1[:], reads=[X, w1], writes=[pU])
            u = self.w_rot["u"].next()
            k.cp(u[:], pU[:, 0:128], reads=[pU], writes=[u], e="dve")
            yield
            mmr(py[:, cs], H[:], RT[:, cs], start=True, stop=False, reads=[H, dRT], writes=[py])
            mmr(py[:, cs], u[:], nst[:, 0:64], start=False, stop=False, reads=[u, nst], writes=[py])
            mmr(py[:, cs], vb[:], nst[:, 64:128], start=False, stop=True, reads=[vb, nst], writes=[py])
            pH = self.psum()
            mmr(pH[:, 0:128], bh[:], u[:], start=True, stop=False, reads=[bh, u], writes=[pH])
            mmr(pH[:, 0:128], kh[:], vb[:], start=False, stop=True, reads=[kh, vb], writes=[pH])
            k.stt(H[:], H[:], self.w_gc[:, cc:cc + 1], pH[:, 0:128], ALU.mult, ALU.add, reads=[H, self.w_gc, pH], writes=[H])
            yield

        active = {}

        def start(cc_):
            if cc_ < NCK:
                active[cc_] = prep(cc_)

        def step_preps():
            for cc_ in list(active):
                try:
                    next(active[cc_])
                except StopIteration:
                    del active[cc_]

        start(0)
        start(1)
        while 0 in active:
            step_preps()
            yield
        for cc in range(NCK):
            start(cc + 2)
            while cc in active:
                step_preps()
                yield
            gc = chain(cc)
            done = False
            while not done:
                try:
                    next(gc)
                except StopIteration:
                    done = True
                step_preps()
                yield
        k.cp(Y, py[:, 0:TR], reads=[py], writes=[dY], e="act")
        pm = self.psum()
        k.mm(pm[:, 0:TR], ca.blk[:], Y, reads=[ca.blk, dY], writes=[pm])
        k.stt(Y, pm[:, 0:TR], -1.0 / 64.0, Y, ALU.mult, ALU.add, reads=[pm, dY], writes=[dY])
        k.tt(TB, Y, Y, ALU.mult, reads=[dY], writes=[dTB])
        pvv = self.psum()
        k.mm(pvv[:, 0:TR], ca.blk[:], TB, reads=[ca.blk, dTB], writes=[pvv])
        k.act(TB, pvv[:, 0:TR], AF.Sqrt, reads=[pvv, self.w_gneps], writes=[dTB], bias=self.w_gneps[:], scale=1.0 / 64.0)
        k.op("dve", lambda g: g.reciprocal(TB, TB), reads=[dTB], writes=[dTB])
        k.tt(Y, Y, TB, ALU.mult, reads=[dY, dTB], writes=[dY])
        k.ts(Y, Y, self.pcol(PV_GNW), ALU.mult, self.pcol(PV_GNB), ALU.add, reads=[dY, pv], writes=[dY])
        yield
        k.stt(TB, R, self.pcol(PV_RK), Kk, ALU.mult, ALU.mult, reads=[dR, pv, dK], writes=[dTB])
        pb = self.psum()
        k.mm(pb[:, 0:TR], ca.blk[:], TB, reads=[ca.blk, dTB], writes=[pb])
        k.tt(TB, pb[:, 0:TR], V, ALU.mult, reads=[pb, dV], writes=[dTB])
        k.tt(Y, Y, TB, ALU.add, reads=[dY, dTB], writes=[dY])
        k.tt(o[:, n0:n1], Y, GD, ALU.mult, reads=[dY, dGD], writes=[o])

    def run(self, ntiles):
        k, c = self.k, self.c
        gainA = self.pv
        for ti in range(ntiles):
            t0 = ti * TT
            xt = self.xpool.next()
            self.cur_xt = xt
            if self.x_src is None:
                k.dma("sp", xt[:], self.xT[:, t0:t0 + TT].rearrange("(a p) c -> p a c", p=128), reads=[self.xT], writes=[xt])
            else:
                self.x_src(ti, xt)
            r = rstd_from(c, [(xt[:, kc, :], xt) for kc in range(8)])
            for kc in range(8):
                k.stt(self.hb[:, kc, :], xt[:, kc, :], self.pv[:, PV_GAIN + kc:PV_GAIN + kc + 1], r[:], ALU.mult, ALU.mult,
                      reads=[xt, self.pv, r], writes=[(self.hb, kc)])
            def emit(mi, o):
                if self.out_sink is None:
                    k.dma("sp", self.outT[mi * 128:(mi + 1) * 128, t0:t0 + TT], o[:], reads=[o], writes=[self.outT])
                else:
                    self.out_sink(ti, mi, o)
            for mi, name in ((1, "lru"), (3, "ret")):
                if name in self.do and name not in self.interleave:
                    emit(mi, getattr(self, name + "_tile")(ti))
            gens = []
            if "ret" in self.do and "ret" in self.interleave:
                gens.append((3, "ret", self.ret_gen(ti)))
            if "fox" in self.do:
                gens.append((0, "fox", self.fox_gen(ti)))
            if "rwkv" in self.do:
                gens.append((2, "rwkv", self.rwkv_gen(ti)))
            while gens:
                for item in list(gens):
                    try:
                        next(item[2])
                    except StopIteration:
                        gens.remove(item)
                        emit(item[0], self.results[item[1]])
            if self.tile_done is not None:
                self.tile_done(ti)


RET_THETA = 10000.0

G = 256
FOX_OFF = 0; LRU_OFF = 772; RWKV_OFF = 1284; RET_OFF = 2308
NCH = 17
NPV = 35

def prep_A(d, l, hh):
    w_in = d["w_in"][l]
    own = np.arange(hh * 128, hh * 128 + 128)
    oth = np.arange((1 - hh) * 128, (1 - hh) * 128 + 128)
    swap = np.concatenate([own[h * 64:(h + 1) * 64][np.r_[32:64, 0:32]] for h in range(2)])
    cols = [FOX_OFF + own, FOX_OFF + G + own, FOX_OFF + 2 * G + own,
            LRU_OFF + own, LRU_OFF + G + own,
            RWKV_OFF + own, RWKV_OFF + G + own, RWKV_OFF + 2 * G + own, RWKV_OFF + 3 * G + np.arange(128), RWKV_OFF + 3 * G + 128 + np.arange(128),
            RET_OFF + own, RET_OFF + G + own, RET_OFF + 2 * G + own, RET_OFF + 3 * G + own, RET_OFF + swap, RET_OFF + G + swap,
            RWKV_OFF + 2 * G + oth,
            FOX_OFF + 3 * G + hh * 2 + np.arange(2)]
    cols = np.concatenate(cols)
    Wown = np.ascontiguousarray(w_in[:, cols])
    pv = np.zeros((128, NPV), np.float32)
    c = 0
    def put(v):
        nonlocal c
        pv[:len(v), c] = v; c += 1
    for j in range(4): put(d["lru_conv_w"][l, j, own])
    put(d["lru_conv_b"][l, own]); put(d["lru_ra_b"][l, own]); put(d["lru_ri_b"][l, own]); put(d["lru_lambda"][l, own])
    mu = d["rwkv_mu"][l]
    put(mu[own]); put(mu[G + own]); put(mu[2 * G + own]); put(mu[2 * G + oth]); put(mu[3 * G:3 * G + 128]); put(mu[3 * G + 128:3 * G + 256])
    put(d["rwkv_w0"][l, own]); put(d["rwkv_a0"][l, own]); put(d["rwkv_k_k"][l, own]); put(d["rwkv_k_a"][l, own])
    put(d["rwkv_gn_w"][l, own]); put(d["rwkv_gn_b"][l, own]); put(d["rwkv_r_k"][l].reshape(-1)[own])
    put(d["rwkv_v0"][l - 1, own] if l > 0 else np.zeros(128, np.float32))
    put(d["ret_gn_w"][l, own])
    put(d["fox_f_bias"][l, hh * 2:hh * 2 + 2])
    put((np.arange(128) // 64 + 2 * hh).astype(np.float32)); put(np.full(128, 2 * hh, np.float32)); put(np.full(128, 2 * hh + 1, np.float32))
    assert c == 27
    pv[:, 27:35] = d["norm_mix_pre"][l].reshape(8, 128).T
    def blockdiag(w):
        m = np.zeros((128, 128), np.float32)
        for h in range(2):
            m[h * 64:(h + 1) * 64, h * 64:(h + 1) * 64] = w[2 * hh + h]
        return m
    mats = {"RA": blockdiag(d["lru_ra_w"][l]), "RI": blockdiag(d["lru_ri_w"][l]),
            "WA2": np.ascontiguousarray(np.concatenate([d["rwkv_w2"][l][:, own], d["rwkv_a2"][l][:, own]], axis=0)),
            "G2": np.ascontiguousarray(d["rwkv_g2"][l][:, own])}
    if l > 0:
        v1 = d["rwkv_v1"][l - 1]
        mats["V1"] = np.ascontiguousarray(np.stack([v1[own], v1[oth]], axis=1))
        mats["V2"] = np.ascontiguousarray(d["rwkv_v2"][l - 1][:, own])
    return Wown, pv, mats

MATSH = {"RA": [128, 128], "RI": [128, 128], "WA2": [128, 128], "G2": [128, 128], "V1": [128, 2, 32], "V2": [32, 128]}
NTOK_B = 2048
PAIRS = [[0, 1], [2, 3], [4, 5], [6, 7]]
DEPTH = 2


def _build_fused():
    nc = bass.Bass("TRN2", target_bir_lowering=False)
    D = lambda name, shape, kind="ExternalInput": nc.dram_tensor(name, shape, F32, kind=kind).ap()
    xT = D("xT", [1024, 4096])
    xown = D("xown", [1024, NTOK_B])
    memT = D("memT", [1024, 256])
    selv = D("selv", [128, 2])
    A_in, B_in = [], []
    for l in range(DEPTH):
        names = ["RA", "RI", "WA2", "G2"] + (["V1", "V2"] if l > 0 else [])
        A_in.append({"Wown": D("Wown%d" % l, [1024, NCOLS_A]), "pv": D("pv%d" % l, [128, NPV]),
                     "mats": {n: (D("%s_%d" % (n, l), MATSH[n]), MATSH[n]) for n in names}})
        W = {n: D("%s_%d" % (n, l), [1024, 1024]) for n in ("w_out", "wq", "wk", "wv", "wo")}
        W["w1"] = D("w1_%d" % l, [1024, 4096])
        W["w2"] = D("w2_%d" % l, [4096, 1024])
        B_in.append({"W": W, "gain": D("gain%d" % l, [128, 6, 8])})
    xo = D("xo", [1024, NTOK_B], "ExternalOutput")
    with contextlib.ExitStack() as st:
        k = K(nc, st)
        c = Ctx(k)
        ca = ConstA(k, c)
        sel = k.sb("selv_sb", [128, 2], F32)
        k.dma("sp", sel[:], selv, writes=[sel])
        xo_t = T(xo, "xo")
        xT_t, xown_t, memT_t = T(xT, "xT"), T(xown, "xown"), T(memT, "memT")
        vf_t = k.dram("vf_scr", [128, 4096])
        WB = [None] * DEPTH
        XO = k.dram("xo_scr", [1024, NTOK_B])
        XG = None
        for l in range(DEPTH):
            CA = [k.dram("ca%d_%d" % (l, i), [512, TT]) for i in range(8)]
            CG = [k.dram("cg%d_%d" % (l, i), [1024, TT]) for i in range(8)]
            with k.scope():
                a = A_in[l]
                sa = StageA(k, c, ca, l, xT_t, None, a["Wown"], a["pv"], a["mats"],
                            vfirst_in=vf_t if l > 0 else None, vfirst_out=vf_t if l == 0 else None)
                wb = {}
                for n, ap in B_in[l]["W"].items():
                    R_, C_ = ap.shape
                    t = k.dram("wb_%s_%d" % (n, l), [R_, C_], BF16)
                    src, dst = ap, t.ap
                    if C_ > 1024:
                        src = src.rearrange("r (a c) -> (r a) c", c=1024)
                        dst = dst.rearrange("r (a c) -> (r a) c", c=1024)
                    for r0 in range(0, src.shape[0], 512):
                        k.dma("pool", dst[r0:r0 + 512, :], src[r0:r0 + 512, :], writes=[(t, r0)])
                    wb[n] = t
                WB[l] = wb
                if l > 0:
                    XGl = XG

                    def x_src(ti, xt, XGl=XGl):
                        g = XGl[ti % 4]
                        r = ti // 4
                        k.dma("sp", xt[:], g[r * 1024:(r + 1) * 1024, :].rearrange("(a p) c -> p a c", p=128), reads=[g], writes=[xt])
                    sa.x_src = x_src

                def out_sink(ti, mi, o, CA=CA):
                    k.dma("sp", CA[ti][mi * 128:(mi + 1) * 128, :], o[:], reads=[o], writes=[(CA[ti], mi)])

                def tile_done(ti, CA=CA, CG=CG):
                    k.cc_allgather(CA[ti], CG[ti], PAIRS)
                sa.out_sink, sa.tile_done = out_sink, tile_done
                sa.run(8)
            with k.scope():
                b = B_in[l]
                ws = WStream(k, nbuf=4)
                gain = k.sb("gain_sb%d" % l, [128, 6, 8], F32)
                k.dma("sp", gain[:], b["gain"], writes=[gain])
                ctb = k.sb("ctb_%d" % l, [128, 8, TT], BF16)
                last = (l == DEPTH - 1)
                XA = [k.dram("xa%d_%d" % (l, i), [1024, TT]) for i in range(4)] if not last else None
                XGn = [k.dram("xg%d_%d" % (l, i), [2048, TT]) for i in range(4)] if not last else None
                x_res = xown_t if l == 0 else XO

                def cat_load(j, ct, CG=CG, ctb=ctb):
                    k.dma("pool", ct[:], CG[j][:, :].rearrange("(a p) c -> p a c", p=128), reads=[CG[j]], writes=[ct])
                    k.dma("pool", ctb[:], CG[4 + j][:, :].rearrange("(a p) c -> p a c", p=128), reads=[CG[4 + j]], writes=[ctb])
                    fl = lambda t: t[:].rearrange("p a c -> p (a c)")
                    k.ts(fl(ct), fl(ct), sel[:, 1:2], ALU.mult, reads=[ct, sel], writes=[ct])
                    k.stt(fl(ct), fl(ctb), sel[:, 0:1], fl(ct), ALU.mult, ALU.add, reads=[ctb, sel, ct], writes=[ct])

                def x_in(j, xt, x_res=x_res):
                    k.dma("sp", xt[:], x_res[:, j * TT:(j + 1) * TT].rearrange("(a p) c -> p a c", p=128), reads=[(x_res, j)], writes=[xt])

                def x_out(j, xt, last=last, XA=XA, XGn=XGn):
                    if last:
                        k.dma("pool", xo_t[:, j * TT:(j + 1) * TT].rearrange("(a p) c -> p a c", p=128), xt[:], reads=[xt], writes=[(xo_t, j)])
                    else:
                        k.dma("pool", XO[:, j * TT:(j + 1) * TT].rearrange("(a p) c -> p a c", p=128), xt[:], reads=[xt], writes=[(XO, j)])
                        k.dma("pool", XA[j][:, :].rearrange("(a p) c -> p a c", p=128), xt[:], reads=[xt], writes=[XA[j]])
                        k.cc_allgather(XA[j], XGn[j], PAIRS)

                stage_b(k, c, ws, l, None, None, None, memT_t, WB[l], gain, NTOK_B, cat_load=cat_load, x_in=x_in, x_out=x_out)
                XG = XGn
        k.finish("sp", [xo_t])
        k.finish("pool", [xo_t])
    return nc


def prep_B_gain(d, l):
    gl = lambda name: np.ascontiguousarray(d[name][l].reshape(8, 128).T)
    return np.ascontiguousarray(np.stack([gl(n) for n in ("norm_mix_post", "norm_xa_pre", "norm_xa_post", "norm_mem", "norm_mlp_pre", "norm_mlp_post")], axis=1))


def kernel(**inputs):
    d = {k_: np.asarray(v, dtype=np.float32) for k_, v in inputs.items()}
    B = 4
    nc = _build_fused()
    perm = np.concatenate([np.arange(mi * 256 + r * 128, mi * 256 + r * 128 + 128) for r in range(2) for mi in range(4)])
    shared = {}
    for l in range(DEPTH):
        shared["w_out_%d" % l] = np.ascontiguousarray(d["w_out"][l][perm])
        shared["wq_%d" % l] = d["xa_wq"][l]; shared["wk_%d" % l] = d["xa_wk"][l]
        shared["wv_%d" % l] = d["xa_wv"][l]; shared["wo_%d" % l] = d["xa_wo"][l]
        shared["w1_%d" % l] = d["mlp_w1"][l]; shared["w2_%d" % l] = d["mlp_w2"][l]
        shared["gain%d" % l] = prep_B_gain(d, l)
    prepA = {(l, hh): prep_A(d, l, hh) for l in range(DEPTH) for hh in range(2)}
    ins = []
    for cid in range(8):
        b, r = cid // 2, cid % 2
        xTb = np.ascontiguousarray(d["x"][b].T)
        m = dict(shared)
        m["xT"] = xTb
        m["xown"] = np.ascontiguousarray(xTb[:, r * NTOK_B:(r + 1) * NTOK_B])
        m["memT"] = np.ascontiguousarray(d["mem"][b].T)
        m["selv"] = np.ascontiguousarray(np.tile(np.array([[float(r), 1.0 - float(r)]], np.float32), (128, 1)))
        for l in range(DEPTH):
            Wown, pv, mats = prepA[(l, r)]
            m["Wown%d" % l] = Wown
            m["pv%d" % l] = pv
            for n, v in mats.items():
                m["%s_%d" % (n, l)] = v
        ins.append(m)
    res = run_bass_kernel_spmd(nc, ins, core_ids=list(range(8)))
    out = np.empty((B, 4096, 1024), np.float32)
    for cid in range(8):
        b, r = cid // 2, cid % 2
        out[b, r * NTOK_B:(r + 1) * NTOK_B, :] = res.results[cid]["xo"].T
    return out
```

```python
import contextlib, math
import numpy as np
import concourse.bass as bass
import concourse.mybir as mybir
from concourse.bass_utils import run_bass_kernel_spmd


F32 = mybir.dt.float32
BF16 = mybir.dt.bfloat16
I32 = mybir.dt.int32
AF = mybir.ActivationFunctionType
ALU = mybir.AluOpType


class Reg:
    __slots__ = ("last_w", "reads")

    def __init__(self):
        self.last_w = None
        self.reads = []


class T:
    def __init__(self, ap, name):
        self.ap = ap
        self.name = name
        self.regs = {None: Reg()}

    def __getitem__(self, idx):
        return self.ap[idx]


class K:
    ENGS = ("pe", "dve", "act", "pool", "sp")

    def __init__(self, nc, stack, n_dma_sems=6):
        self.nc = nc
        self.stack = stack
        self.eng = {"pe": nc.tensor, "dve": nc.vector, "act": nc.scalar, "pool": nc.gpsimd, "sp": nc.sync}
        self.sem = {}
        self.tick = {}
        for e in self.ENGS:
            self.sem[e] = stack.enter_context(nc.semaphore("s_" + e))
            self.tick[e] = 0
        self.dq = {}
        for q in ("sp", "pool", "act"):
            lst = []
            for i in range(n_dma_sems):
                key = "d_%s%d" % (q, i)
                self.sem[key] = stack.enter_context(nc.semaphore(key))
                self.tick[key] = 0
                lst.append(key)
            self.dq[q] = [lst, 0]
        self.sem["cc"] = stack.enter_context(nc.semaphore("s_cc"))
        self.tick["cc"] = 0
        self.waited = {e: {} for e in self.ENGS}
        self.ninst = {e: 0 for e in self.ENGS}
        self.nwait = 0

    @contextlib.contextmanager
    def scope(self):
        old = self.stack
        with contextlib.ExitStack() as st:
            self.stack = st
            try:
                yield
            finally:
                self.barrier()
                self.stack = old

    def barrier(self):
        deps = [(sk, v) for sk, v in self.tick.items() if v > 0]
        for e in self.ENGS:
            self._wait(e, [d for d in deps if d[0] != e])

    def cc_allgather(self, in_t, out_t, groups):
        reads = self._norm([in_t])
        writes = self._norm([out_t])
        self._wait("pool", self._deps(reads, writes))
        inst = self.eng["pool"].collective_compute("AllGather", ALU.bypass, replica_groups=groups, ins=[in_t.ap], outs=[out_t.ap])
        self.tick["cc"] += 1
        inst.then_inc(self.sem["cc"], 1)
        self.ninst["pool"] += 1
        self._record(("cc", self.tick["cc"]), reads, writes)
        return inst

    _uid = 0

    def sb(self, name, shape, dtype=F32):
        K._uid += 1
        name = "%s_u%d" % (name, K._uid)
        return T(self.stack.enter_context(self.nc.sbuf_tensor(name, list(shape), dtype)), name)

    def ps(self, name, shape, dtype=F32):
        return T(self.stack.enter_context(self.nc.psum_tensor(name, list(shape), dtype)), name)

    def dram(self, name, shape, dtype=F32, kind="Internal"):
        return T(self.nc.dram_tensor(name, list(shape), dtype, kind=kind).ap(), name)

    @staticmethod
    def _norm(lst):
        out = []
        for x in lst:
            if x is None:
                continue
            if isinstance(x, T):
                out.append((x, None))
            else:
                out.append(x)
        return out

    def _deps(self, reads, writes):
        deps = []
        for (t, k) in reads:
            if k is None:
                for r in t.regs.values():
                    if r.last_w:
                        deps.append(r.last_w)
            else:
                r = t.regs.get(k)
                if r is not None and r.last_w:
                    deps.append(r.last_w)
                if t.regs[None].last_w:
                    deps.append(t.regs[None].last_w)
        for (t, k) in writes:
            if k is None:
                rs = list(t.regs.values())
            else:
                rs = [t.regs[None]]
                if k in t.regs:
                    rs.append(t.regs[k])
            for r in rs:
                if r.last_w:
                    deps.append(r.last_w)
                deps.extend(r.reads)
        return deps

    def _record(self, me, reads, writes):
        for (t, k) in reads:
            t.regs.setdefault(k, Reg()).reads.append(me)
        for (t, k) in writes:
            if k is None:
                for kk in list(t.regs.keys()):
                    if kk is not None:
                        del t.regs[kk]
                r = t.regs[None]
            else:
                r = t.regs.setdefault(k, Reg())
            r.last_w = me
            r.reads = []

    def _wait(self, e, deps):
        best = {}
        for (sk, v) in deps:
            if sk == e and e == "pe":
                continue
            if best.get(sk, 0) < v:
                best[sk] = v
        w = self.waited[e]
        for sk, v in best.items():
            if w.get(sk, 0) < v:
                self.eng[e].wait_ge(self.sem[sk], v)
                w[sk] = v
                self.nwait += 1

    def op(self, e, fn, reads=(), writes=()):
        reads = self._norm(reads)
        writes = self._norm(writes)
        self._wait(e, self._deps(reads, writes))
        inst = fn(self.eng[e])
        self.tick[e] += 1
        inst.then_inc(self.sem[e], 1)
        self.ninst[e] += 1
        self._record((e, self.tick[e]), reads, writes)
        return inst

    def dma(self, q, out, in_, reads=(), writes=(), **kw):
        reads = self._norm(reads)
        writes = self._norm(writes)
        lst, i = self.dq[q]
        sk = lst[i % len(lst)]
        self.dq[q][1] = i + 1
        deps = self._deps(reads, writes)
        if self.tick[sk] > 0:
            deps.append((sk, self.tick[sk]))
        self._wait(q, deps)
        inst = self.eng[q].dma_start(out=out, in_=in_, **kw)
        self.tick[sk] += 16
        inst.then_inc(self.sem[sk], 16)
        self.ninst[q] += 1
        self._record((sk, self.tick[sk]), reads, writes)
        return inst

    def finish(self, e, tiles):
        deps = []
        for t in tiles:
            for r in t.regs.values():
                if r.last_w:
                    deps.append(r.last_w)
        self._wait(e, deps)

    def mm(self, out, lhsT, rhs, start=True, stop=True, reads=(), writes=()):
        return self.op("pe", lambda e: e.matmul(out, lhsT, rhs, start=start, stop=stop), reads, writes)

    def tr(self, out, in_, ident, reads=(), writes=()):
        return self.op("pe", lambda e: e.transpose(out, in_, ident), reads, writes)

    def act(self, out, in_, func, reads=(), writes=(), bias=None, scale=None, e="act"):
        kw = {}
        if bias is not None:
            kw["bias"] = bias
        if scale is not None:
            kw["scale"] = scale
        return self.op(e, lambda g: g.activation(out, in_, func, **kw), reads, writes)

    def tt(self, out, in0, in1, op, reads=(), writes=(), e="dve"):
        return self.op(e, lambda g: g.tensor_tensor(out, in0, in1, op), reads, writes)

    def ts(self, out, in0, s1, op0, s2=None, op1=None, reads=(), writes=(), e="dve"):
        if op1 is None:
            return self.op(e, lambda g: g.tensor_scalar(out, in0, s1, None, op0), reads, writes)
        return self.op(e, lambda g: g.tensor_scalar(out, in0, s1, s2, op0, op1), reads, writes)

    def stt(self, out, in0, scalar, in1, op0, op1, reads=(), writes=()):
        return self.op("dve", lambda g: g.scalar_tensor_tensor(out, in0, scalar, in1, op0, op1), reads, writes)

    def cp(self, out, in_, reads=(), writes=(), e="dve"):
        if e == "act":
            return self.op(e, lambda g: g.copy(out, in_), reads, writes)
        return self.op(e, lambda g: g.tensor_copy(out, in_), reads, writes)

    def memset(self, out, val, writes=(), e="pool"):
        return self.op(e, lambda g: g.memset(out, val), (), writes)


TT = 512
EPS = 1e-6


class Rot:
    def __init__(self, k, name, shape, dtype, n):
        self.t = [k.sb("%s%d" % (name, i), shape, dtype) for i in range(n)]
        self.i = 0

    def next(self):
        t = self.t[self.i % len(self.t)]
        self.i += 1
        return t


class Ctx:
    def __init__(self, k):
        self.k = k
        self.ones = k.sb("ones_bf", [128, 128], BF16)
        k.memset(self.ones[:], 1.0, writes=[self.ones])
        self.eps = k.sb("eps_c", [128, 1], F32)
        k.memset(self.eps[:], EPS, writes=[self.eps])
        self.psums = [k.ps("ps%d" % i, [128, 512], F32) for i in range(8)]
        self.pi = 0
        self.sq = Rot(k, "sq", [128, TT], BF16, 3)
        self.rstd = Rot(k, "rstd", [128, TT], F32, 2)
        self.tmp = Rot(k, "tmpf", [128, TT], F32, 3)

    def psum(self):
        p = self.psums[self.pi % 8]
        self.pi += 1
        return p


def rstd_from(c, srcs, n=TT, nfeat=1024.0):
    k = c.k
    pst = c.psum()
    nk = len(srcs)
    for i, (ap, t) in enumerate(srcs):
        sq = c.sq.next()
        k.act(sq[:, 0:n], ap, AF.Square, reads=[t], writes=[sq])
        k.mm(pst[:, 0:n], c.ones[:], sq[:, 0:n], start=(i == 0), stop=(i == nk - 1), reads=[c.ones, sq], writes=[pst])
    r = c.rstd.next()
    k.act(r[:, 0:n], pst[:, 0:n], AF.Sqrt, reads=[pst, c.eps], writes=[r], bias=c.eps[:], scale=1.0 / nfeat)
    k.op("dve", lambda g: g.reciprocal(r[:, 0:n], r[:, 0:n]), reads=[r], writes=[r])
    return r


def pre_norm(c, xt, gain, gi, out, n=TT):
    k = c.k
    r = rstd_from(c, [(xt[:, kc, 0:n], xt) for kc in range(8)], n)
    for kc in range(8):
        k.stt(out[:, kc, 0:n], xt[:, kc, 0:n], gain[:, gi, kc:kc + 1], r[:, 0:n], ALU.mult, ALU.mult,
              reads=[xt, gain, r], writes=[(out, kc)])


def post_norm_res(c, m, gain, gi, xt, n=TT, r=None):
    k = c.k
    if r is None:
        r = rstd_from(c, [(m[:, kc, 0:n], (m, kc)) for kc in range(8)], n)
    for kc in range(8):
        t = c.tmp.next()
        k.tt(t[:, 0:n], m[:, kc, 0:n], r[:, 0:n], ALU.mult, reads=[(m, kc), r], writes=[t])
        k.stt(xt[:, kc, 0:n], t[:, 0:n], gain[:, gi, kc:kc + 1], xt[:, kc, 0:n], ALU.mult, ALU.add,
              reads=[t, gain, (xt, kc)], writes=[(xt, kc)])


class WStream:
    def __init__(self, k, nbuf=4, elems=8 * 512):
        self.k = k
        self.elems = elems
        self.pool = Rot(k, "wbuf", [128, elems], BF16, nbuf)

    def load(self, w_ap, r0, nk, c0, ncols):
        buf = self.pool.next()
        view = buf[:, 0:nk * ncols].rearrange("p (a c) -> p a c", a=nk)
        if isinstance(w_ap, T):
            src = w_ap[r0:r0 + nk * 128, c0:c0 + ncols].rearrange("(a p) c -> p a c", p=128)
            self.qi = getattr(self, "qi", 0) + 1
            self.k.dma("sp", view, src, reads=[w_ap], writes=[buf])
        else:
            src = w_ap[r0:r0 + nk * 128, c0:c0 + ncols].rearrange("(a p) c -> p a c", p=128)
            self.k.dma("pool", view, src, writes=[buf])
        return buf, view


class StatAcc:
    def __init__(self, c, n=TT, nfeat=1024.0):
        self.c, self.n, self.nfeat = c, n, nfeat
        self.pst = c.psums[7]
        self.pending = None
        self.cnt = 0

    def _flush(self, last):
        k, c, n = self.c.k, self.c, self.n
        if self.pending is not None:
            sq, first = self.pending
            k.mm(self.pst[:, 0:n], c.ones[:], sq[:, 0:n], start=first, stop=last, reads=[c.ones, sq], writes=[self.pst])
            self.pending = None

    def add(self, ap, dep):
        k, c, n = self.c.k, self.c, self.n
        self._flush(False)
        sq = c.sq.next()
        k.act(sq[:, 0:n], ap, AF.Square, reads=[dep], writes=[sq])
        self.pending = (sq, self.cnt == 0)
        self.cnt += 1

    def finish(self):
        k, c, n = self.c.k, self.c, self.n
        self._flush(True)
        r = c.rstd.next()
        k.act(r[:, 0:n], self.pst[:, 0:n], AF.Sqrt, reads=[self.pst, c.eps], writes=[r], bias=c.eps[:], scale=1.0 / self.nfeat)
        k.op("dve", lambda g: g.reciprocal(r[:, 0:n], r[:, 0:n]), reads=[r], writes=[r])
        return r


def pre_norm_split(c, xt, gain, gi, out, n=TT):
    k = c.k
    for kc in range(8):
        k.ts(out[:, kc, 0:n], xt[:, kc, 0:n], gain[:, gi, kc:kc + 1], ALU.mult, reads=[(xt, kc), gain], writes=[(out, kc)])
    return lambda: rstd_from(c, [(xt[:, kc, 0:n], (xt, kc)) for kc in range(8)], n)


def dense(c, ws, w_ap, nk, nout_chunks, rhs_fn, rhs_reads, sink, n=TT, cols_per_load=512, hook=None):
    k = c.k
    per = cols_per_load // 128
    for o0 in range(0, nout_chunks, per):
        buf, view = ws.load(w_ap, 0, nk, o0 * 128, cols_per_load)
        for j in range(per):
            oc = o0 + j
            ps = c.psum()
            for kc in range(nk):
                k.mm(ps[:, 0:n], view[:, kc, j * 128:(j + 1) * 128], rhs_fn(kc), start=(kc == 0), stop=(kc == nk - 1),
                     reads=[buf] + rhs_reads(kc), writes=[ps])
            if hook is not None and oc == 0:
                hook()
            sink(oc, ps)


def stage_b(k, c, ws, l, catT, xT_in, xT_out, memT, W, gain, NT, tok0=0, cat_load=None, x_in=None, x_out=None):
    m = k.sb("m_%d" % l, [128, 8, TT], F32)
    memt = m
    k.dma("sp", memt[:, :, 0:256], memT[:, :].rearrange("(a p) c -> p a c", p=128), reads=[memT], writes=[memt])
    mn = k.sb("mn_%d" % l, [128, 8, 256], BF16)
    pre_norm(c, memt, gain, 3, mn, n=256)
    kT = k.sb("kT_%d" % l, [128, 8, 256], BF16)
    vM = k.sb("vM_%d" % l, [128, 2, 1024], BF16)

    def sink_k(oc, ps):
        k.cp(kT[:, oc, :], ps[:, 0:256], reads=[ps], writes=[(kT, oc)], e="act")

    dense(c, ws, W["wk"], 8, 8, lambda kc: mn[:, kc, :], lambda kc: [(mn, kc)], sink_k, n=256)
    for half in range(2):
        buf, view = ws.load(W["wv"], 0, 8, half * 512, 512)
        for mc in range(2):
            ps = c.psum()
            for kc in range(8):
                k.mm(ps[:, :], mn[:, kc, mc * 128:(mc + 1) * 128], view[:, kc, :], start=(kc == 0), stop=(kc == 7),
                     reads=[buf, (mn, kc)], writes=[ps])
            k.cp(vM[:, mc, half * 512:(half + 1) * 512], ps[:, :], reads=[ps], writes=[(vM, (mc, half))], e="act")

    xpool = Rot(k, "xt_%d" % l, [128, 8, TT], F32, 2)
    cpool = Rot(k, "ct_%d" % l, [128, 8, TT], BF16, 1)
    hb = k.sb("hb_%d" % l, [128, 8, TT], BF16)
    qT = k.sb("qT_%d" % l, [128, 8, TT], BF16)
    oT = k.sb("oT_%d" % l, [128, 8, TT], BF16)
    hid = k.sb("hid_%d" % l, [128, 32, TT], BF16)
    pT = Rot(k, "pT_%d" % l, [128, TT], BF16, 4)
    relu = Rot(k, "relu_%d" % l, [128, TT], F32, 3)
    rden = Rot(k, "rden_%d" % l, [128, TT], F32, 2)
    r2buf = k.sb("r2buf_%d" % l, [128, TT], F32)

    def sink_m(oc, ps):
        k.cp(m[:, oc, :], ps[:, :], reads=[ps], writes=[(m, oc)], e="dve")

    for ti in range(NT // TT):
        t0 = tok0 + ti * TT
        xt = xpool.next()
        if x_in is None:
            k.dma("sp", xt[:], xT_in[:, t0:t0 + TT].rearrange("(a p) c -> p a c", p=128), reads=[xT_in], writes=[xt])
        elif ti == 0:
            x_in(ti, xt)
        if x_in is not None and ti + 1 < NT // TT:
            xnext = xpool.t[(xpool.i) % len(xpool.t)]
            x_in(ti + 1, xnext)
        ct = cpool.next()
        if cat_load is None:
            k.dma("pool", ct[:], catT[:, t0:t0 + TT].rearrange("(a p) c -> p a c", p=128), reads=[catT], writes=[ct])
        elif ti == 0:
            cat_load(ti, ct)
        dense(c, ws, W["w_out"], 8, 8, lambda kc: ct[:, kc, :], lambda kc: [ct], sink_m)
        if cat_load is not None and ti + 1 < NT // TT:
            cat_load(ti + 1, ct)
        post_norm_res(c, m, gain, 0, xt)
        stats_q = pre_norm_split(c, xt, gain, 1, hb)
        rq = {}

        def hook_q():
            rq["r"] = stats_q()

        def sink_q(oc, ps):
            r_ = rq["r"]
            k.tt(qT[:, oc, :], ps[:, :], r_[:], ALU.mult, reads=[ps, r_], writes=[(qT, oc)])

        dense(c, ws, W["wq"], 8, 8, lambda kc: hb[:, kc, :], lambda kc: [(hb, kc)], sink_q, hook=hook_q)
        for hd in range(4):
            pts = []
            for mc in range(2):
                ps = c.psum()
                for j in range(2):
                    dc = hd * 2 + j
                    k.mm(ps[:, :], kT[:, dc, mc * 128:(mc + 1) * 128], qT[:, dc, :], start=(j == 0), stop=(j == 1),
                         reads=[(kT, dc), (qT, dc)], writes=[ps])
                pt = pT.next()
                k.act(pt[:], ps[:, :], AF.Exp, reads=[ps], writes=[pt], scale=1.0 / 16.0)
                pts.append(pt)
            pden = c.psum()
            for mc in range(2):
                k.mm(pden[:, :], c.ones[:], pts[mc][:], start=(mc == 0), stop=(mc == 1), reads=[c.ones, pts[mc]], writes=[pden])
            rd = rden.next()
            k.op("dve", lambda g: g.reciprocal(rd[:], pden[:, :]), reads=[pden], writes=[rd])
            for j in range(2):
                dc = hd * 2 + j
                ps = c.psum()
                for mc in range(2):
                    k.mm(ps[:, :], vM[:, mc, dc * 128:(dc + 1) * 128], pts[mc][:], start=(mc == 0), stop=(mc == 1),
                         reads=[vM, pts[mc]], writes=[ps])
                k.tt(oT[:, dc, :], ps[:, :], rd[:], ALU.mult, reads=[ps, rd], writes=[(oT, dc)])
        dense(c, ws, W["wo"], 8, 8, lambda kc: oT[:, kc, :], lambda kc: [(oT, kc)], sink_m)
        post_norm_res(c, m, gain, 2, xt)
        stats_m = pre_norm_split(c, xt, gain, 4, hb)

        def hook_m():
            r_ = stats_m()
            k.tt(r2buf[:], r_[:], r_[:], ALU.mult, reads=[r_], writes=[r2buf])

        def sink_h(f, ps):
            r = relu.next()
            k.act(r[:], ps[:, :], AF.Relu, reads=[ps], writes=[r])
            k.tt(hid[:, f, :], r[:], r[:], ALU.mult, reads=[r], writes=[(hid, f)], e="pool")

        dense(c, ws, W["w1"], 8, 32, lambda kc: hb[:, kc, :], lambda kc: [(hb, kc)], sink_h, hook=hook_m)
        for fb in range(8):
            buf, view = ws.load(W["w2"], fb * 512, 4, 0, 1024)
            for oc in range(8):
                ps = c.psums[oc]
                for f4 in range(4):
                    f = fb * 4 + f4
                    k.mm(ps[:, :], view[:, f4, oc * 128:(oc + 1) * 128], hid[:, f, :], start=(f == 0), stop=(f == 31),
                         reads=[buf, (hid, f)], writes=[ps])
        for oc in range(8):
            k.tt(m[:, oc, :], c.psums[oc][:, :], r2buf[:], ALU.mult, reads=[c.psums[oc], r2buf], writes=[(m, oc)])
        post_norm_res(c, m, gain, 5, xt)
        if x_out is None:
            k.dma("sp", xT_out[:, t0:t0 + TT].rearrange("(a p) c -> p a c", p=128), xt[:], reads=[xt], writes=[xT_out])
        else:
            x_out(ti, xt)


(CH_FQ, CH_FK, CH_FV, CH_LX, CH_LY, CH_RR, CH_RK, CH_RV, CH_RWA, CH_RGD,
 CH_TQ, CH_TK, CH_TV, CH_TG, CH_TQS, CH_TKS, CH_RVO) = range(17)
NCH = 17
NCOLS_A = NCH * 128 + 2

(PV_CW0, PV_CW1, PV_CW2, PV_CW3, PV_CB, PV_RAB, PV_RIB, PV_LAM,
 PV_MU_R, PV_MU_K, PV_MU_V, PV_MU_VO, PV_MU_WA, PV_MU_GD,
 PV_W0, PV_A0, PV_KK, PV_KA, PV_GNW, PV_GNB, PV_RK, PV_V0,
 PV_TGN, PV_FB, PV_HP, PV_H0, PV_H1) = range(27)
PV_GAIN = 27
NPV = 35


class ConstA:
    def __init__(self, k, c):
        self.k = k
        nc = k.nc
        g = k.eng["pool"]
        self.ident = k.sb("identf", [128, 128], F32)
        k.memset(self.ident[:], 1.0, writes=[self.ident])
        k.op("pool", lambda e: e.affine_select(out=self.ident[:], in_=self.ident[:], pattern=[[-1, 128]], compare_op=ALU.is_equal,
                                               fill=0.0, base=0, channel_multiplier=1), reads=[self.ident], writes=[self.ident])
        self.ut_f = k.sb("ut_f", [128, 128], F32)
        k.memset(self.ut_f[:], 1.0, writes=[self.ut_f])
        k.op("pool", lambda e: e.affine_select(out=self.ut_f[:], in_=self.ut_f[:], pattern=[[1, 128]], compare_op=ALU.is_ge,
                                               fill=0.0, base=0, channel_multiplier=-1), reads=[self.ut_f], writes=[self.ut_f])
        self.ut_b = k.sb("ut_b", [128, 128], BF16)
        k.cp(self.ut_b[:], self.ut_f[:], reads=[self.ut_f], writes=[self.ut_b])
        self.blk = k.sb("blk_f", [128, 128], F32)
        k.memset(self.blk[:], 0.0, writes=[self.blk])
        k.memset(self.blk[0:64, 0:64], 1.0, writes=[self.blk])
        k.memset(self.blk[64:128, 64:128], 1.0, writes=[self.blk])
        self.ones_f = k.sb("ones_f", [128, 512], F32)
        k.memset(self.ones_f[:], 1.0, writes=[self.ones_f])
        self.one_c = k.sb("one_c", [128, 1], F32)
        k.memset(self.one_c[:], 1.0, writes=[self.one_c])
        self.zero_c = k.sb("zero_c", [128, 1], F32)
        k.memset(self.zero_c[:], 0.0, writes=[self.zero_c])


def gelu_tanh(k, c, y_ps, y_t, tmps):
    y = tmps.next()
    k.cp(y[:], y_ps, reads=[y_t], writes=[y], e="act")
    t = tmps.next()
    k.tt(t[:], y[:], y[:], ALU.mult, reads=[y], writes=[t])
    k.ts(t[:], t[:], 0.044715, ALU.mult, 1.0, ALU.add, reads=[t], writes=[t])
    k.tt(t[:], t[:], y[:], ALU.mult, reads=[t, y], writes=[t])
    k.act(t[:], t[:], AF.Sigmoid, reads=[t], writes=[t], scale=2.0 * math.sqrt(2.0 / math.pi))
    return y, t


class StageA:
    def __init__(self, k, c, ca, l, xT, outT, Wown, pvd, mats, vfirst_in=None, vfirst_out=None, do=("lru", "ret", "fox", "rwkv")):
        self.k, self.c, self.ca, self.l = k, c, ca, l
        self.xT, self.outT = xT, outT
        self.do = do
        self.x_src = self.out_sink = self.tile_done = None
        self.results = {}
        self.interleave = ()
        self.f32r = False
        self.vf_in, self.vf_out = vfirst_in, vfirst_out
        L = "a%d_" % l
        self.L = L
        self.W = k.sb(L + "W", [128, 8, NCOLS_A], BF16)
        for kc in range(8):
            k.dma("pool", self.W[:, kc, :], Wown[kc * 128:(kc + 1) * 128, :], writes=[(self.W, kc)], max_dma_last_dim=4096)
        self.pv = k.sb(L + "pv", [128, NPV], F32)
        k.dma("sp", self.pv[:], pvd, writes=[self.pv])
        self.mats = {}
        for name, (ap, shape) in mats.items():
            t = k.sb(L + name, shape, F32)
            k.dma("sp", t[:], ap, writes=[t])
            self.mats[name] = t
        self.xpool = Rot(k, L + "xt", [128, 8, TT], F32, 1)
        self.hb = k.sb(L + "hb", [128, 8, TT], BF16)
        self.tmps = Rot(k, L + "t", [128, TT], F32, 12)
        self.outs = Rot(k, L + "o", [128, TT], F32, 3)
        self.named_ps = c.psums[5:8]
        if "lru" in do:
            self.init_lru()
        if "ret" in do:
            self.init_ret()
        if "fox" in do:
            self.init_fox()
        if "rwkv" in do:
            self.init_rwkv()

    def pcol(self, i):
        return self.pv[:, i:i + 1]

    def psum(self):
        c = self.c
        p = c.psums[c.pi % 5]
        c.pi += 1
        return p

    def proj(self, ch, n0=0, n1=TT):
        k = self.k
        ps = self.psum()
        for kc in range(8):
            k.mm(ps[:, n0:n1], self.W[:, kc, ch * 128:(ch + 1) * 128], self.hb[:, kc, n0:n1], start=(kc == 0), stop=(kc == 7),
                 reads=[(self.W, kc), (self.hb, kc)], writes=[ps])
        return ps

    def init_lru(self):
        k, L = self.k, self.L
        self.l_xbuf = k.sb(L + "lxb", [128, 3 + TT], F32)
        k.memset(self.l_xbuf[:, 0:3], 0.0, writes=[self.l_xbuf])
        self.l_h = k.sb(L + "lh", [128, 1], F32)
        k.memset(self.l_h[:], 0.0, writes=[self.l_h])
        self.l_c1 = k.sb(L + "lc1", [128, 2], F32)
        k.act(self.l_c1[:, 0:1], self.pcol(PV_LAM), AF.Exp, reads=[self.pv], writes=[self.l_c1], scale=-1.0)
        k.act(self.l_c1[:, 0:1], self.l_c1[:, 0:1], AF.Ln, reads=[self.l_c1, self.ca.one_c], writes=[self.l_c1], bias=self.ca.one_c[:])
        k.ts(self.l_c1[:, 1:2], self.l_c1[:, 0:1], -16.0, ALU.mult, reads=[self.l_c1], writes=[self.l_c1])
        k.ts(self.l_c1[:, 0:1], self.l_c1[:, 0:1], -8.0, ALU.mult, reads=[self.l_c1], writes=[self.l_c1])

    def lru_tile(self, ti):
        k, c, ca = self.k, self.c, self.ca
        xb = self.l_xbuf
        ps = self.proj(CH_LX)
        k.cp(xb[:, 3:3 + TT], ps[:, :], reads=[ps], writes=[xb], e="act")
        xc = self.tmps.next()
        k.ts(xc[:], xb[:, 0:TT], self.pcol(PV_CW0), ALU.mult, self.pcol(PV_CB), ALU.add, reads=[xb, self.pv], writes=[xc])
        for j in range(1, 4):
            k.stt(xc[:], xb[:, j:j + TT], self.pcol(PV_CW0 + j), xc[:], ALU.mult, ALU.add, reads=[xb, self.pv, xc], writes=[xc])
        k.cp(xb[:, 0:3], xb[:, TT:TT + 3], reads=[xb], writes=[xb], e="dve")
        pr = self.psum()
        k.mm(pr[:, :], self.mats["RA"][:], xc[:], reads=[self.mats["RA"], xc], writes=[pr])
        pi_ = self.psum()
        k.mm(pi_[:, :], self.mats["RI"][:], xc[:], reads=[self.mats["RI"], xc], writes=[pi_])
        r = self.tmps.next()
        k.act(r[:], pr[:, :], AF.Sigmoid, reads=[pr, self.pv], writes=[r], bias=self.pcol(PV_RAB))
        ig = self.tmps.next()
        k.act(ig[:], pi_[:, :], AF.Sigmoid, reads=[pi_, self.pv], writes=[ig], bias=self.pcol(PV_RIB))
        a = self.tmps.next()
        k.act(a[:], r[:], AF.Exp, reads=[r, self.l_c1], writes=[a], scale=self.l_c1[:, 0:1])
        mult = self.tmps.next()
        k.act(mult[:], r[:], AF.Exp, reads=[r, self.l_c1], writes=[mult], scale=self.l_c1[:, 1:2])
        k.act(mult[:], mult[:], AF.Sqrt, reads=[mult, ca.one_c], writes=[mult], scale=-1.0, bias=ca.one_c[:])
        k.tt(ig[:], ig[:], xc[:], ALU.mult, reads=[ig, xc], writes=[ig])
        k.tt(ig[:], ig[:], mult[:], ALU.mult, reads=[ig, mult], writes=[ig])
        h = self.tmps.next()
        k.op("dve", lambda g: g.tensor_tensor_scan(h[:], a[:], ig[:], self.l_h[:, 0:1], ALU.mult, ALU.add),
             reads=[a, ig, self.l_h], writes=[h])
        k.cp(self.l_h[:], h[:, TT - 1:TT], reads=[h], writes=[self.l_h], e="dve")
        py = self.proj(CH_LY)
        o = self.outs.next()
        y, t = gelu_tanh(k, c, py[:, :], py, self.tmps)
        k.tt(o[:], t[:], y[:], ALU.mult, reads=[t, y], writes=[o])
        k.tt(o[:], o[:], h[:], ALU.mult, reads=[o, h], writes=[o])
        return o

    def init_ret(self):
        k, L, ca = self.k, self.L, self.ca
        self.t_lg = k.sb(L + "tlg", [128, 3], F32)
        tmp = k.sb(L + "tlgt", [128, 3], F32)
        k.ts(tmp[:], self.pv[:, PV_HP:PV_HP + 3], 5.0, ALU.add, reads=[self.pv], writes=[tmp])
        k.act(tmp[:], tmp[:], AF.Exp, reads=[tmp], writes=[tmp], scale=-math.log(2.0))
        k.act(self.t_lg[:], tmp[:], AF.Ln, reads=[tmp, ca.one_c], writes=[self.t_lg], scale=-1.0, bias=ca.one_c[:])
        ni = k.sb(L + "tni", [128, 128], I32)
        k.op("pool", lambda e: e.iota(ni[:], pattern=[[1, 128]], base=0, channel_multiplier=0), writes=[ni])
        nf = k.sb(L + "tnf", [128, 128], F32)
        k.cp(nf[:], ni[:], reads=[ni], writes=[nf])
        self.t_xi = k.sb(L + "txi", [128, 128], F32)
        self.t_zt = k.sb(L + "tzt", [128, 128], F32)
        tt_ = k.sb(L + "ttmp", [128, 128], F32)
        k.ts(tt_[:], nf[:], 1.0, ALU.add, reads=[nf], writes=[tt_])
        k.act(self.t_xi[:], tt_[:], AF.Exp, reads=[tt_, self.t_lg], writes=[self.t_xi], scale=self.t_lg[:, 0:1])
        k.ts(tt_[:], nf[:], -1.0, ALU.mult, 127.0, ALU.add, reads=[nf], writes=[tt_])
        k.act(self.t_zt[:], tt_[:], AF.Exp, reads=[tt_, self.t_lg], writes=[self.t_zt], scale=self.t_lg[:, 0:1])
        k.ts(self.t_zt[:], self.t_zt[:], 0.125, ALU.mult, reads=[self.t_zt], writes=[self.t_zt])
        self.t_gc = k.sb(L + "tgc", [128, 1], F32)
        k.act(self.t_gc[:], self.t_lg[:, 0:1], AF.Exp, reads=[self.t_lg], writes=[self.t_gc], scale=128.0)
        di = k.sb(L + "tdi", [128, 128], I32)
        k.op("pool", lambda e: e.iota(di[:], pattern=[[1, 128]], base=0, channel_multiplier=-1), writes=[di])
        df = k.sb(L + "tdf", [128, 128], F32)
        k.cp(df[:], di[:], reads=[di], writes=[df])
        k.ts(df[:], df[:], 0.0, ALU.max, reads=[df], writes=[df])
        self.t_dt = k.sb(L + "tdt", [128, 2, 128], F32)
        for h in range(2):
            k.act(self.t_dt[:, h, :], df[:], AF.Exp, reads=[df, self.t_lg], writes=[self.t_dt], scale=self.t_lg[:, 1 + h:2 + h])
            k.stt(self.t_dt[:, h, :], self.t_dt[:, h, :], 0.125, ca.ut_f[:], ALU.mult, ALU.mult, reads=[self.t_dt, ca.ut_f], writes=[self.t_dt])
        pi_ = k.sb(L + "tpi", [128, 2], I32)
        k.op("pool", lambda e: e.iota(pi_[:, 0:1], pattern=[[0, 1]], base=0, channel_multiplier=1), writes=[pi_])
        k.ts(pi_[:, 1:2], pi_[:, 0:1], 32, ALU.bitwise_and, reads=[pi_], writes=[pi_])
        k.ts(pi_[:, 0:1], pi_[:, 0:1], 31, ALU.bitwise_and, reads=[pi_], writes=[pi_])
        pf = k.sb(L + "tpf", [128, 2], F32)
        k.cp(pf[:], pi_[:], reads=[pi_], writes=[pf])
        self.t_inv = k.sb(L + "tinv", [128, 1], F32)
        k.act(self.t_inv[:], pf[:, 0:1], AF.Exp, reads=[pf], writes=[self.t_inv], scale=-math.log(RET_THETA) / 31.0)
        self.t_sgn = k.sb(L + "tsgn", [128, 1], F32)
        k.ts(self.t_sgn[:], pf[:, 1:2], 1.0 / 16.0, ALU.mult, -1.0, ALU.add, reads=[pf], writes=[self.t_sgn])
        qi = k.sb(L + "tqi", [128, TT], I32)
        k.op("pool", lambda e: e.iota(qi[:], pattern=[[1, TT]], base=0, channel_multiplier=0), writes=[qi])
        self.t_pos = k.sb(L + "tpos", [128, TT], F32)
        k.cp(self.t_pos[:], qi[:], reads=[qi], writes=[self.t_pos])
        self.t_R = k.sb(L + "tR", [128, 64], F32)
        k.memset(self.t_R[:], 0.0, writes=[self.t_R])
        self.t_Rb = k.sb(L + "tRb", [128, 64], BF16)
        k.memset(self.t_Rb[:], 0.0, writes=[self.t_Rb])
        self.t_kzt = [k.sb(L + "tkzt%d" % i, [128, 128], BF16) for i in range(4)]
        self.t_ind = [k.sb(L + "tind%d" % i, [128, 128], BF16) for i in range(8)]
        self.t_kv = k.sb(L + "tkv", [128, 4, 64], F32)
        self.t_Rb4 = k.sb(L + "tRb4", [128, 4, 64], BF16)
        self.t_vtm = Rot(k, L + "tvtm", [128, 4, 128], BF16, 1)
        self.t_bf = Rot(k, L + "tbf", [128, TT], BF16, 4)

    def sincos(self, t0):
        k, ca = self.k, self.ca
        TWO_PI = 2.0 * math.pi
        C1 = 6.28125
        C2 = TWO_PI - C1
        res = []
        for shift in (math.pi / 2.0, 0.0):
            ang = self.tmps.next()
            k.ts(ang[:], self.t_pos[:], float(t0), ALU.add, self.t_inv[:, 0:1], ALU.mult, reads=[self.t_pos, self.t_inv], writes=[ang])
            if shift:
                k.ts(ang[:], ang[:], shift, ALU.add, reads=[ang], writes=[ang])
            kf = self.tmps.next()
            ki = kf[:].bitcast(I32)
            k.ts(kf[:], ang[:], 1.0 / TWO_PI, ALU.mult, reads=[ang], writes=[kf])
            k.cp(ki, kf[:], reads=[kf], writes=[kf])
            k.cp(kf[:], ki, reads=[kf], writes=[kf])
            k.stt(ang[:], kf[:], -C1, ang[:], ALU.mult, ALU.add, reads=[kf, ang], writes=[ang])
            k.stt(ang[:], kf[:], -C2, ang[:], ALU.mult, ALU.add, reads=[kf, ang], writes=[ang])
            k.ts(kf[:], ang[:], math.pi, ALU.is_gt, -TWO_PI, ALU.mult, reads=[ang], writes=[kf])
            k.tt(ang[:], ang[:], kf[:], ALU.add, reads=[ang, kf], writes=[ang])
            k.ts(kf[:], ang[:], -math.pi, ALU.is_lt, TWO_PI, ALU.mult, reads=[ang], writes=[kf])
            k.tt(ang[:], ang[:], kf[:], ALU.add, reads=[ang, kf], writes=[ang])
            k.ts(ang[:], ang[:], math.pi, ALU.min, -math.pi, ALU.max, reads=[ang], writes=[ang])
            k.act(ang[:], ang[:], AF.Sin, reads=[ang], writes=[ang])
            res.append(ang)
        C, S = res
        k.ts(S[:], S[:], self.t_sgn[:, 0:1], ALU.mult, reads=[S, self.t_sgn], writes=[S])
        return C, S

    def ret_gen(self, ti):
        k, c, ca = self.k, self.c, self.ca
        t0 = ti * TT
        C, S = self.sincos(t0)
        yield
        rot = []
        for (cha, chs) in ((CH_TQ, CH_TQS), (CH_TK, CH_TKS)):
            pa = self.proj(cha)
            a = self.tmps.next()
            k.tt(a[:], pa[:, :], C[:], ALU.mult, reads=[pa, C], writes=[a])
            pb = self.proj(chs)
            b = self.tmps.next()
            k.tt(b[:], pb[:, :], S[:], ALU.mult, reads=[pb, S], writes=[b])
            k.tt(a[:], a[:], b[:], ALU.add, reads=[a, b], writes=[a], e="pool")
            rot.append(a)
            yield
        qr, kr = rot
        qb = self.t_bf.next()
        k.cp(qb[:], qr[:], reads=[qr], writes=[qb], e="act")
        kb = self.t_bf.next()
        k.cp(kb[:], kr[:], reads=[kr], writes=[kb], e="act")
        qx = self.t_bf.next()
        k.tt(qx[:].rearrange("p (c n) -> p c n", n=128), qr[:].rearrange("p (c n) -> p c n", n=128),
             self.t_xi[:].unsqueeze(1).to_broadcast([128, 4, 128]), ALU.mult, reads=[qr, self.t_xi], writes=[qx])
        kz = self.tmps.next()
        k.tt(kz[:].rearrange("p (c n) -> p c n", n=128), kr[:].rearrange("p (c n) -> p c n", n=128),
             self.t_zt[:].unsqueeze(1).to_broadcast([128, 4, 128]), ALU.mult, reads=[kr, self.t_zt], writes=[kz])
        vtm = self.t_vtm.next()
        for cc in range(4):
            ps = self.psum()
            for kc in range(8):
                k.mm(ps[:, 0:128], self.hb[:, kc, cc * 128:(cc + 1) * 128], self.W[:, kc, CH_TV * 128:(CH_TV + 1) * 128],
                     start=(kc == 0), stop=(kc == 7), reads=[(self.W, kc), (self.hb, kc)], writes=[ps])
            k.cp(vtm[:, cc, :], ps[:, 0:128], reads=[ps], writes=[(vtm, cc)], e="act")
            yield
        osb = self.tmps.next()
        inds = []
        for cc in range(4):
            cs = slice(cc * 128, (cc + 1) * 128)
            ptr = self.psum()
            k.tr(ptr[:, 0:128], kz[:, cs], ca.ident[:], reads=[kz, ca.ident], writes=[ptr])
            kzt = self.t_kzt[cc]
            k.cp(kzt[:], ptr[:, 0:128], reads=[ptr], writes=[kzt], e="act")
            for h in range(2):
                hp = slice(h * 64, (h + 1) * 64)
                pin = self.psum()
                k.mm(pin[:, 0:128], kb[hp, cs], qb[hp, cs], reads=[kb, qb], writes=[pin])
                ind = self.t_ind[cc * 2 + h]
                k.tt(ind[:], pin[:, 0:128], self.t_dt[:, h, :], ALU.mult, reads=[pin, self.t_dt], writes=[ind])
                inds.append(ind)
            yield
        for cc in range(4):
            pkv = self.psum()
            kzt = self.t_kzt[cc]
            for h in range(2):
                hp = slice(h * 64, (h + 1) * 64)
                k.mm(pkv[hp, 0:64], kzt[:, hp], vtm[:, cc, hp], reads=[kzt, (vtm, cc)], writes=[pkv])
            k.cp(self.t_kv[:, cc, :], pkv[:, 0:64], reads=[pkv], writes=[(self.t_kv, cc)], e="act")
            yield
        for cc in range(4):
            k.cp(self.t_Rb4[:, cc, :], self.t_R[:], reads=[self.t_R], writes=[(self.t_Rb4, cc)], e="act")
            k.stt(self.t_R[:], self.t_R[:], self.t_gc[:, 0:1], self.t_kv[:, cc, :], ALU.mult, ALU.add,
                  reads=[self.t_R, self.t_gc, (self.t_kv, cc)], writes=[self.t_R])
        yield
        for cc in range(4):
            cs = slice(cc * 128, (cc + 1) * 128)
            po = self.psum()
            for h in range(2):
                hp = slice(h * 64, (h + 1) * 64)
                ind = inds[cc * 2 + h]
                k.mm(po[hp, 0:128], vtm[:, cc, hp], ind[:], start=True, stop=False, reads=[(vtm, cc), ind], writes=[po])
                k.mm(po[hp, 0:128], self.t_Rb4[hp, cc, :], qx[hp, cs], start=False, stop=True, reads=[(self.t_Rb4, cc), qx], writes=[po])
            k.cp(osb[:, cs], po[:, 0:128], reads=[po], writes=[osb], e="act")
            yield
        sq = self.tmps.next()
        k.tt(sq[:], osb[:], osb[:], ALU.mult, reads=[osb], writes=[sq], e="pool")
        pss = self.psum()
        k.mm(pss[:, :], ca.blk[:], sq[:], reads=[ca.blk, sq], writes=[pss])
        rs = self.tmps.next()
        k.act(rs[:], pss[:, :], AF.Sqrt, reads=[pss, c.eps], writes=[rs], bias=c.eps[:], scale=1.0 / 64.0)
        k.op("dve", lambda g: g.reciprocal(rs[:], rs[:]), reads=[rs], writes=[rs])
        pg = self.proj(CH_TG)
        sg = self.tmps.next()
        k.act(sg[:], pg[:, :], AF.Silu, reads=[pg], writes=[sg])
        o = self.outs.next()
        k.stt(o[:], osb[:], self.pcol(PV_TGN), rs[:], ALU.mult, ALU.mult, reads=[osb, self.pv, rs], writes=[o])
        k.tt(o[:], o[:], sg[:], ALU.mult, reads=[o, sg], writes=[o])
        self.results["ret"] = o

    def ret_tile(self, ti):
        for _ in self.ret_gen(ti):
            pass
        return self.results["ret"]


    def init_fox(self):
        k, L, ca = self.k, self.L, self.ca
        self.f_kT = k.sb(L + "fkT", [128, 4096], BF16)
        self.f_v = k.sb(L + "fv", [128, 32, 128], BF16)
        self.f_q = k.sb(L + "fq", [128, TT], BF16)
        self.f_row = Rot(k, L + "frow", [2, TT], F32, 3)
        self.f_clast = k.sb(L + "fcl", [2, 1], F32)
        k.memset(self.f_clast[:], 0.0, writes=[self.f_clast])
        self.f_ccol = k.sb(L + "fcc", [128, 32, 2], F32)
        self.f_cend = k.sb(L + "fce", [128, 4, 2], F32)
        self.f_B = Rot(k, L + "fB", [128, 32, 4], F32, 2)
        self.f_pT = Rot(k, L + "fpT", [128, TT], BF16, 4)
        self.f_rd = k.sb(L + "frd", [128, TT], F32)
        self.f_sel = k.sb(L + "fsel", [128, 128], F32)
        k.memset(self.f_sel[:], 1.0, writes=[self.f_sel])
        k.op("pool", lambda e: e.affine_select(out=self.f_sel[:], in_=self.f_sel[:], pattern=[[0, 128]], compare_op=ALU.is_equal,
                                               fill=0.0, base=-127, channel_multiplier=1), reads=[self.f_sel], writes=[self.f_sel])
        self.f_nb = k.sb(L + "fnb", [2, 1], F32)
        k.ts(self.f_nb[:], self.pv[0:2, PV_FB:PV_FB + 1], -1.0, ALU.mult, reads=[self.pv], writes=[self.f_nb])

    def fox_gen(self, ti):
        k, c, ca = self.k, self.c, self.ca
        t0 = ti * TT
        pq = self.proj(CH_FQ)
        k.cp(self.f_q[:], pq[:, :], reads=[pq], writes=[self.f_q], e="act")
        pk = self.proj(CH_FK)
        k.cp(self.f_kT[:, t0:t0 + TT], pk[:, :], reads=[pk], writes=[(self.f_kT, ti)], e="act")
        for cc in range(4):
            ps = self.psum()
            for kc in range(8):
                k.mm(ps[:, 0:128], self.hb[:, kc, cc * 128:(cc + 1) * 128], self.W[:, kc, CH_FV * 128:(CH_FV + 1) * 128],
                     start=(kc == 0), stop=(kc == 7), reads=[(self.W, kc), (self.hb, kc)], writes=[ps])
            k.cp(self.f_v[:, ti * 4 + cc, :], ps[:, 0:128], reads=[ps], writes=[(self.f_v, ti * 4 + cc)], e="act")
        pf = self.psum()
        for kc in range(8):
            k.mm(pf[0:2, :], self.W[:, kc, NCH * 128:NCH * 128 + 2], self.hb[:, kc, :], start=(kc == 0), stop=(kc == 7),
                 reads=[(self.W, kc), (self.hb, kc)], writes=[pf])
        e_ = self.f_row.next()
        k.act(e_[:], pf[0:2, :], AF.Exp, reads=[pf, self.f_nb], writes=[e_], scale=-1.0, bias=self.f_nb[:])
        k.act(e_[:], e_[:], AF.Ln, reads=[e_, ca.one_c], writes=[e_], bias=ca.one_c[0:2, :])
        cs = self.f_row.next()
        k.op("dve", lambda g: g.tensor_tensor_scan(cs[:], ca.ones_f[0:2, :], e_[:], self.f_clast[:, 0:1], ALU.mult, ALU.add),
             reads=[ca.ones_f, e_, self.f_clast], writes=[cs])
        k.cp(self.f_clast[:], cs[:, TT - 1:TT], reads=[cs], writes=[self.f_clast])
        pc = self.psum()
        for cc in range(4):
            k.tr(pc[:, 2 * cc:2 * cc + 2], cs[:, cc * 128:(cc + 1) * 128], ca.ident[0:2, 0:2], reads=[cs, ca.ident], writes=[pc])
        k.cp(self.f_ccol[:, ti * 4:(ti + 1) * 4, :].rearrange("p a b -> p (a b)"), pc[:, 0:8], reads=[pc], writes=[(self.f_ccol, ti)])
        pe = self.psum()
        k.mm(pe[:, 0:8], self.f_sel[:], self.f_ccol[:, ti * 4:(ti + 1) * 4, :].rearrange("p a b -> p (a b)"),
             reads=[self.f_sel, (self.f_ccol, ti)], writes=[pe])
        k.cp(self.f_cend[:].rearrange("p a b -> p (a b)"), pe[:, 0:8], reads=[pe], writes=[self.f_cend])
        po, pd = self.named_ps[0], self.named_ps[1]
        nkc = 4 * (ti + 1)
        yield
        for h in range(2):
            hp = slice(h * 64, (h + 1) * 64)
            B = self.f_B.next()
            k.tt(B[:, 0:nkc, :], self.f_ccol[:, 0:nkc, h:h + 1].to_broadcast([128, nkc, 4]),
                 self.f_cend[:, :, h].unsqueeze(1).to_broadcast([128, nkc, 4]), ALU.subtract,
                 reads=[self.f_ccol, self.f_cend], writes=[B])

            def pv_step(kc, pt, col0):
                k.mm(po[hp, col0:TT], self.f_v[:, kc, hp], pt[:, col0:TT], start=(kc == 0), stop=(kc == nkc - 1),
                     reads=[(self.f_v, kc), pt], writes=[po])
                k.mm(pd[hp, col0:TT], c.ones[:, 0:64], pt[:, col0:TT], start=(kc == 0), stop=(kc == nkc - 1),
                     reads=[c.ones, pt], writes=[pd])

            pend = None
            for kc in range(nkc):
                j = kc - 4 * ti
                col0 = 128 * j if j >= 0 else 0
                ps = self.psum()
                k.mm(ps[:, col0:TT], self.f_kT[hp, kc * 128:(kc + 1) * 128], self.f_q[hp, col0:TT],
                     reads=[(self.f_kT, kc // 4), self.f_q], writes=[ps])
                pt = self.f_pT.next()
                for qbl in range(col0 // 128, 4):
                    qs = slice(qbl * 128, (qbl + 1) * 128)
                    k.act(pt[:, qs], ps[:, qs], AF.Exp, reads=[ps, B], writes=[pt], scale=0.125, bias=B[:, kc, qbl:qbl + 1])
                if j >= 0:
                    qs = slice(j * 128, (j + 1) * 128)
                    k.tt(pt[:, qs], pt[:, qs], ca.ut_b[:], ALU.mult, reads=[pt, ca.ut_b], writes=[pt], e="pool")
                if pend is not None:
                    pv_step(*pend)
                pend = (kc, pt, col0)
                yield
            pv_step(*pend)
            yield
        rd = self.f_rd
        k.op("dve", lambda g: g.reciprocal(rd[:], pd[:, :]), reads=[pd], writes=[rd])
        o = self.outs.next()
        k.tt(o[:], po[:, :], rd[:], ALU.mult, reads=[po, rd], writes=[o])
        self.results["fox"] = o

    def fox_tile(self, ti):
        for _ in self.fox_gen(ti):
            pass
        return self.results["fox"]

    def init_rwkv(self):
        k, L, ca = self.k, self.L, self.ca
        self.TR = 256
        self.NCK = self.TR // 64
        NCK = self.NCK
        self.w_carry = k.sb(L + "wcar", [128, 6], F32)
        k.memset(self.w_carry[:], 0.0, writes=[self.w_carry])
        self.w_AR = k.sb(L + "wAR", [128, NCK, 2, 2, 64], F32)
        self.w_BK = k.sb(L + "wBK", [128, NCK, 2, 2, 64], F32)
        self.w_HAT = k.sb(L + "wHAT", [128, NCK, 3, 2, 64], F32)
        for t in (self.w_AR, self.w_BK, self.w_HAT):
            k.memset(t[:], 0.0, writes=[t])
        self.w_H = k.sb(L + "wH", [128, 128], F32)
        k.memset(self.w_H[:], 0.0, writes=[self.w_H])
        self.w_ms = k.sb(L + "wms", [128, 128], F32)
        k.memset(self.w_ms[:], 1.0, writes=[self.w_ms])
        k.op("pool", lambda e: e.affine_select(out=self.w_ms[:], in_=self.w_ms[:], pattern=[[1, 128]], compare_op=ALU.is_gt,
                                               fill=0.0, base=0, channel_multiplier=-1), reads=[self.w_ms], writes=[self.w_ms])
        self.w_mst = k.sb(L + "wmst", [128, 128], F32)
        k.memset(self.w_mst[:], 1.0, writes=[self.w_mst])
        k.op("pool", lambda e: e.affine_select(out=self.w_mst[:], in_=self.w_mst[:], pattern=[[-1, 128]], compare_op=ALU.is_gt,
                                               fill=0.0, base=0, channel_multiplier=1), reads=[self.w_mst], writes=[self.w_mst])
        self.w_mist = k.sb(L + "wmist", [128, 2, 64], F32)
        for j in range(2):
            k.cp(self.w_mist[0:64, j, :], ca.ut_f[0:64, 0:64], reads=[ca.ut_f], writes=[self.w_mist])
            k.cp(self.w_mist[64:128, j, :], ca.ut_f[64:128, 64:128], reads=[ca.ut_f], writes=[self.w_mist])
        self.w_chm = k.sb(L + "wchm", [128, self.TR], F32)
        k.memset(self.w_chm[:], 1.0, writes=[self.w_chm])
        k.memset(self.w_chm[:].rearrange("p (c n) -> p c n", n=64)[:, :, 0:1], 0.0, writes=[self.w_chm])
        self.w_omka = k.sb(L + "womka", [128, 1], F32)
        k.ts(self.w_omka[:], self.pcol(PV_KA), -1.0, ALU.mult, 1.0, ALU.add, reads=[self.pv], writes=[self.w_omka])
        self.w_gc = k.sb(L + "wgc", [128, NCK], F32)
        self.w_rot = {n: Rot(k, L + "w" + n, [128, 128], F32, 2) for n in ("w1", "u")}
        self.w_out3 = {n: [k.sb(L + "w%s%d" % (n, i), [128, 128], F32) for i in range(3)] for n in ("nak", "nst", "bh", "kh", "vb", "x")}
        self.w_ptb = [[k.sb(L + "wpt%d_%d" % (i, j), [128, 128], F32) for j in range(2)] for i in range(3)]
        self.w_pxb = [[k.sb(L + "wpx%d_%d" % (i, j), [128, 256], F32) for j in range(2)] for i in range(3)]
        self.w_v32 = k.sb(L + "wv32", [32, self.TR], F32)
        self.w_gneps = k.sb(L + "wgne", [128, 1], F32)
        k.memset(self.w_gneps[:], 64e-5, writes=[self.w_gneps])

    def nb(self, i):
        TR = self.TR
        xt = self.cur_xt
        return xt[:, i // 2, (i % 2) * TR:(i % 2 + 1) * TR], (xt, ("nb", i))

    def rwkv_gen(self, ti):
        o = self.outs.next()
        for half in range(self.TT_R):
            yield from self.rwkv_half(ti, half, o)
        self.results["rwkv"] = o

    def rwkv_tile(self, ti):
        for _ in self.rwkv_gen(ti):
            pass
        return self.results["rwkv"]

    TT_R = 2

    def rwkv_half(self, ti, half, o):
        k, c, ca, l = self.k, self.c, self.ca, self.l
        TR, NCK = self.TR, self.NCK
        n0 = half * TR
        n1 = n0 + TR
        g0 = ti * TT + n0
        pv = self.pv
        (R, dR), (Kk, dK), (V, dV), (WA, dWA), (GD, dGD), (VO, dVO), (A, dA), (LW, dLW), (LG, dLG), (KKN, dKKN), (BV, dBV), \
            (RT, dRT), (TA, dTA), (TB, dTB), (TC, dTC), (Y, dY) = [self.nb(i) for i in range(16)]
        car = self.w_carry

        def lerp(X, dX, ch, mu_i, ci):
            ps = self.proj(ch, n0, n1)
            k.cp(X, ps[:, n0:n1], reads=[ps], writes=[dX], e="act")
            k.tt(TA[:, 1:TR], X[:, 0:TR - 1], X[:, 1:TR], ALU.subtract, reads=[dX], writes=[dTA])
            k.tt(TA[:, 0:1], car[:, ci:ci + 1], X[:, 0:1], ALU.subtract, reads=[dX, car], writes=[dTA])
            k.cp(car[:, ci:ci + 1], X[:, TR - 1:TR], reads=[dX], writes=[car])
            k.stt(X, TA, self.pcol(mu_i), X, ALU.mult, ALU.add, reads=[dTA, pv, dX], writes=[dX])

        lerp(WA, dWA, CH_RWA, PV_MU_WA, 3)
        yield
        lerp(Kk, dK, CH_RK, PV_MU_K, 1)
        yield
        lerp(R, dR, CH_RR, PV_MU_R, 0)
        yield
        lerp(V, dV, CH_RV, PV_MU_V, 2)
        yield
        lerp(GD, dGD, CH_RGD, PV_MU_GD, 4)
        yield
        if l > 0:
            lerp(VO, dVO, CH_RVO, PV_MU_VO, 5)
            yield
        WA2, G2 = self.mats["WA2"], self.mats["G2"]
        k.act(TB[0:64, :], WA[0:64, :], AF.Tanh, reads=[dWA], writes=[dTB])
        pz = self.psum()
        k.mm(pz[:, 0:TR], WA2[0:64, :], TB[0:64, :], reads=[WA2, dTB], writes=[pz])
        k.act(LW, pz[:, 0:TR], AF.Sigmoid, reads=[pz, pv], writes=[dLW], bias=self.pcol(PV_W0))
        k.ts(LW, LW, -math.exp(-0.5), ALU.mult, reads=[dLW], writes=[dLW])
        pa = self.psum()
        k.mm(pa[:, 0:TR], WA2[64:128, :], WA[64:128, :], reads=[WA2, dWA], writes=[pa])
        k.act(A, pa[:, 0:TR], AF.Sigmoid, reads=[pa, pv], writes=[dA], bias=self.pcol(PV_A0))
        k.act(GD, GD, AF.Sigmoid, reads=[dGD], writes=[dGD])
        pg = self.psum()
        k.mm(pg[:, 0:TR], G2[:], GD, reads=[G2, dGD], writes=[pg])
        k.cp(GD, pg[:, 0:TR], reads=[pg], writes=[dGD], e="act")
        if l > 0:
            V1, V2 = self.mats["V1"], self.mats["V2"]
            p1 = self.psum()
            k.mm(p1[0:32, 0:TR], V1[:, 0, :], V, start=True, stop=False, reads=[V1, dV], writes=[p1])
            k.mm(p1[0:32, 0:TR], V1[:, 1, :], VO, start=False, stop=True, reads=[V1, dVO], writes=[p1])
            k.cp(self.w_v32[:], p1[0:32, 0:TR], reads=[p1], writes=[self.w_v32], e="act")
            p2 = self.psum()
            k.mm(p2[:, 0:TR], V2[:], self.w_v32[:], reads=[V2, self.w_v32], writes=[p2])
            k.act(TB, p2[:, 0:TR], AF.Sigmoid, reads=[p2, pv], writes=[dTB], bias=self.pcol(PV_V0))
            k.dma("sp", TC, self.vf_in[:, g0:g0 + TR], reads=[self.vf_in], writes=[dTC])
            k.tt(TC, TC, V, ALU.subtract, reads=[dTC, dV], writes=[dTC])
            k.tt(TC, TC, TB, ALU.mult, reads=[dTC, dTB], writes=[dTC])
            k.tt(V, V, TC, ALU.add, reads=[dV, dTC], writes=[dV])
        else:
            k.dma("sp", self.vf_out[:, g0:g0 + TR], V, reads=[dV], writes=[self.vf_out])
        yield
        k.ts(KKN, Kk, self.pcol(PV_KK), ALU.mult, reads=[dK, pv], writes=[dKKN])
        k.tt(TB, KKN, KKN, ALU.mult, reads=[dKKN], writes=[dTB])
        pn = self.psum()
        k.mm(pn[:, 0:TR], ca.blk[:], TB, reads=[ca.blk, dTB], writes=[pn])
        k.act(TB, pn[:, 0:TR], AF.Sqrt, reads=[pn], writes=[dTB])
        k.ts(TB, TB, 1e-12, ALU.max, reads=[dTB], writes=[dTB])
        k.op("dve", lambda g: g.reciprocal(TB, TB), reads=[dTB], writes=[dTB])
        k.tt(KKN, KKN, TB, ALU.mult, reads=[dKKN, dTB], writes=[dKKN])
        k.ts(TB, A, self.pcol(PV_KA), ALU.mult, self.w_omka[:, 0:1], ALU.add, reads=[dA, pv, self.w_omka], writes=[dTB])
        k.tt(Kk, Kk, TB, ALU.mult, reads=[dK, dTB], writes=[dK])
        k.tt(BV, KKN, A, ALU.mult, reads=[dKKN, dA], writes=[dBV])
        yield
        k.op("dve", lambda g: g.tensor_tensor_scan(LG, self.w_chm[:], LW, 0.0, ALU.mult, ALU.add),
             reads=[self.w_chm, dLW], writes=[dLG])
        AR, BK, HAT = self.w_AR, self.w_BK, self.w_HAT
        v3 = lambda ap: ap.rearrange("p (c n) -> p c n", n=64)

        def to_blk(dst, which, src, dsrc, mul=None, dmul=None, op="mult"):
            for h in range(2):
                hp = slice(h * 64, (h + 1) * 64)
                if mul is None:
                    k.cp(dst[hp, :, which, h, :], v3(src[hp, :]), reads=[dsrc], writes=[dst])
                else:
                    k.tt(dst[hp, :, which, h, :], v3(src[hp, :]), v3(mul[hp, :]), ALU.mult, reads=[dsrc, dmul], writes=[dst])

        k.act(TB, LG, AF.Exp, reads=[dLG], writes=[dTB])
        k.tt(RT, R, TB, ALU.mult, reads=[dR, dTB], writes=[dRT])
        to_blk(AR, 1, RT, dRT)
        k.act(self.w_gc[:], v3(LG)[:, :, 63], AF.Exp, reads=[dLG], writes=[self.w_gc])
        yield
        k.tt(TB, LG, LW, ALU.subtract, reads=[dLG, dLW], writes=[dTB])
        k.act(TB, TB, AF.Exp, reads=[dTB], writes=[dTB])
        k.stt(TC, KKN, -1.0, TB, ALU.mult, ALU.mult, reads=[dKKN, dTB], writes=[dTC])
        to_blk(AR, 0, TC, dTC)
        k.act(TB, LG, AF.Exp, reads=[dLG], writes=[dTB], scale=-1.0)
        to_blk(BK, 0, BV, dBV, TB, dTB)
        to_blk(BK, 1, Kk, dK, TB, dTB)
        yield
        k.tt(v3(TB), v3(LG)[:, :, 63:64].to_broadcast([128, NCK, 64]), v3(LG), ALU.subtract, reads=[dLG], writes=[dTB])
        k.act(TB, TB, AF.Exp, reads=[dTB], writes=[dTB])
        to_blk(HAT, 0, BV, dBV, TB, dTB)
        to_blk(HAT, 1, Kk, dK, TB, dTB)
        to_blk(HAT, 2, V, dV)
        py = self.named_ps[2]
        ident = ca.ident
        m2 = lambda ap: ap.rearrange("p a b -> p (a b)")
        H = self.w_H
        P = {}
        if self.f32r:
            F32R = mybir.dt.float32r

            def mmr(out, lhsT, rhs, **kw):
                return k.mm(out, lhsT.bitcast(F32R), rhs.bitcast(F32R), **kw)
        else:
            mmr = k.mm

        def prep(cc):
            slot = cc % 3
            pxs, pts = self.w_pxb[slot], self.w_ptb[slot]
            o3 = {n: self.w_out3[n][slot] for n in self.w_out3}
            cs = slice(cc * 64, (cc + 1) * 64)
            Ab = m2(AR[:, cc, 0])
            Bb, Kb = m2(BK[:, cc, 0]), m2(BK[:, cc, 1])
            pA = self.psum()
            mmr(pA[:, 0:128], Bb, Ab, reads=[BK, AR], writes=[pA])
            mmr(pA[:, 128:256], Kb, Ab, reads=[BK, AR], writes=[pA])
            pC = self.psum()
            mmr(pC[:, 0:128], Ab, Bb, reads=[BK, AR], writes=[pC])
            pS = self.psum()
            mmr(pS[:, 0:64], Bb, RT[:, cs], reads=[BK, dRT], writes=[pS])
            mmr(pS[:, 64:128], Kb, RT[:, cs], reads=[BK, dRT], writes=[pS])
            pi_ = 0
            px = pxs[pi_]
            k.tt(px[:, 0:128], pA[:, 0:128], self.w_ms[:], ALU.mult, reads=[pA, self.w_ms], writes=[px])
            k.tt(px[:, 128:256], px[:, 0:128], ident[:], ALU.add, reads=[px, ident], writes=[px], e="pool")
            pt = pts[pi_]
            k.tt(pt[:], pC[:, 0:128], self.w_mst[:], ALU.mult, reads=[pC, self.w_mst], writes=[pt])
            nak = o3["nak"]
            k.tt(nak[:], pA[:, 128:256], self.w_ms[:], ALU.mult, reads=[pA, self.w_ms], writes=[nak])
            nst = o3["nst"]
            k.tt(nst[:], pS[:, 0:128], m2(self.w_mist[:]), ALU.mult, reads=[pS, self.w_mist], writes=[nst])
            yield
            pP = self.psum()
            mmr(pP[:, 0:128], pt[:], px[:, 0:128], reads=[pt, px], writes=[pP])
            pQ = self.psum()
            mmr(pQ[:, 0:128], px[:, 0:128], pt[:], reads=[pt, px], writes=[pQ])
            pi_ ^= 1
            px2, pt2 = pxs[pi_], pts[pi_]
            k.cp(px2[:, 0:128], pP[:, 0:128], reads=[pP], writes=[px2], e="act")
            k.cp(px2[:, 128:256], px[:, 128:256], reads=[px], writes=[px2], e="pool")
            k.cp(pt2[:], pQ[:, 0:128], reads=[pQ], writes=[pt2], e="act")
            px, pt = px2, pt2
            yield
            X = o3["x"]
            for lvl in range(1, 6):
                last = (lvl == 5)
                pP = self.psum()
                if not last:
                    mmr(pP[:, 0:256], pt[:], px[:, 0:256], reads=[pt, px], writes=[pP])
                    pQ = self.psum()
                    mmr(pQ[:, 0:128], px[:, 0:128], pt[:], reads=[pt, px], writes=[pQ])
                    pi_ ^= 1
                    px2, pt2 = pxs[pi_], pts[pi_]
                    k.tt(px2[:, 128:256], pP[:, 128:256], px[:, 128:256], ALU.add, reads=[pP, px], writes=[px2])
                    k.cp(px2[:, 0:128], pP[:, 0:128], reads=[pP], writes=[px2], e="act")
                    k.cp(pt2[:], pQ[:, 0:128], reads=[pQ], writes=[pt2], e="act")
                    px, pt = px2, pt2
                else:
                    mmr(pP[:, 128:256], pt[:], px[:, 128:256], reads=[pt, px], writes=[pP])
                    k.tt(X[:], pP[:, 128:256], px[:, 128:256], ALU.add, reads=[pP, px], writes=[X])
                yield
            tm = []
            for wi, nm in enumerate(("bh", "kh", "vb")):
                ptr = self.psum()
                k.tr(ptr[:, 0:128], m2(HAT[:, cc, wi]), ident[:], reads=[HAT, ident], writes=[ptr])
                tbuf = o3[nm]
                k.cp(tbuf[:], ptr[:, 0:128], reads=[ptr], writes=[tbuf], e="act")
                tm.append(tbuf)
                yield
            P[cc] = (X, nak, nst) + tuple(tm)

        def chain(cc):
            cs = slice(cc * 64, (cc + 1) * 64)
            Ab = m2(AR[:, cc, 0])
            X, nak, nst, bh, kh, vb = P[cc]
            pW = self.psum()
            mmr(pW[:, 0:128], Ab, H[:], start=True, stop=False, reads=[AR, H], writes=[pW])
            mmr(pW[:, 0:128], nak[:], vb[:], start=False, stop=True, reads=[nak, vb], writes=[pW])
            w1 = self.w_rot["w1"].next()
            k.cp(w1[:], pW[:, 0:128], reads=[pW], writes=[w1], e="dve")
            yield
            pU = self.psum()
            mmr(pU[:, 0:128], X[:], w1[:], reads=[X, w1], writes=[pU])
            u = self.w_rot["u"].next()
            k.cp(u[:], pU[:, 0:128], reads=[pU], writes=[u], e="dve")
            yield
            mmr(py[:, cs], H[:], RT[:, cs], start=True, stop=False, reads=[H, dRT], writes=[py])
            mmr(py[:, cs], u[:], nst[:, 0:64], start=False, stop=False, reads=[u, nst], writes=[py])
            mmr(py[:, cs], vb[:], nst[:, 64:128], start=False, stop=True, reads=[vb, nst], writes=[py])
            pH = self.psum()
            mmr(pH[:, 0:128], bh[:], u[:], start=True, stop=False, reads=[bh, u], writes=[pH])
            mmr(pH[:, 0:128], kh[:], vb[:], start=False, stop=True, reads=[kh, vb], writes=[pH])
            k.stt(H[:], H[:], self.w_gc[:, cc:cc + 1], pH[:, 0:128], ALU.mult, ALU.add, reads=[H, self.w_gc, pH], writes=[H])
            yield

        active = {}

        def start(cc_):
            if cc_ < NCK:
                active[cc_] = prep(cc_)

        def step_preps():
            for cc_ in list(active):
                try:
                    next(active[cc_])
                except StopIteration:
                    del active[cc_]

        start(0)
        start(1)
        while 0 in active:
            step_preps()
            yield
        for cc in range(NCK):
            start(cc + 2)
            while cc in active:
                step_preps()
                yield
            gc = chain(cc)
            done = False
            while not done:
                try:
                    next(gc)
                except StopIteration:
                    done = True
                step_preps()
                yield
        k.cp(Y, py[:, 0:TR], reads=[py], writes=[dY], e="act")
        pm = self.psum()
        k.mm(pm[:, 0:TR], ca.blk[:], Y, reads=[ca.blk, dY], writes=[pm])
        k.stt(Y, pm[:, 0:TR], -1.0 / 64.0, Y, ALU.mult, ALU.add, reads=[pm, dY], writes=[dY])
        k.tt(TB, Y, Y, ALU.mult, reads=[dY], writes=[dTB])
        pvv = self.psum()
        k.mm(pvv[:, 0:TR], ca.blk[:], TB, reads=[ca.blk, dTB], writes=[pvv])
        k.act(TB, pvv[:, 0:TR], AF.Sqrt, reads=[pvv, self.w_gneps], writes=[dTB], bias=self.w_gneps[:], scale=1.0 / 64.0)
        k.op("dve", lambda g: g.reciprocal(TB, TB), reads=[dTB], writes=[dTB])
        k.tt(Y, Y, TB, ALU.mult, reads=[dY, dTB], writes=[dY])
        k.ts(Y, Y, self.pcol(PV_GNW), ALU.mult, self.pcol(PV_GNB), ALU.add, reads=[dY, pv], writes=[dY])
        yield
        k.stt(TB, R, self.pcol(PV_RK), Kk, ALU.mult, ALU.mult, reads=[dR, pv, dK], writes=[dTB])
        pb = self.psum()
        k.mm(pb[:, 0:TR], ca.blk[:], TB, reads=[ca.blk, dTB], writes=[pb])
        k.tt(TB, pb[:, 0:TR], V, ALU.mult, reads=[pb, dV], writes=[dTB])
        k.tt(Y, Y, TB, ALU.add, reads=[dY, dTB], writes=[dY])
        k.tt(o[:, n0:n1], Y, GD, ALU.mult, reads=[dY, dGD], writes=[o])

    def run(self, ntiles):
        k, c = self.k, self.c
        gainA = self.pv
        for ti in range(ntiles):
            t0 = ti * TT
            xt = self.xpool.next()
            self.cur_xt = xt
            if self.x_src is None:
                k.dma("sp", xt[:], self.xT[:, t0:t0 + TT].rearrange("(a p) c -> p a c", p=128), reads=[self.xT], writes=[xt])
            else:
                self.x_src(ti, xt)
            r = rstd_from(c, [(xt[:, kc, :], xt) for kc in range(8)])
            for kc in range(8):
                k.stt(self.hb[:, kc, :], xt[:, kc, :], self.pv[:, PV_GAIN + kc:PV_GAIN + kc + 1], r[:], ALU.mult, ALU.mult,
                      reads=[xt, self.pv, r], writes=[(self.hb, kc)])
            def emit(mi, o):
                if self.out_sink is None:
                    k.dma("sp", self.outT[mi * 128:(mi + 1) * 128, t0:t0 + TT], o[:], reads=[o], writes=[self.outT])
                else:
                    self.out_sink(ti, mi, o)
            for mi, name in ((1, "lru"), (3, "ret")):
                if name in self.do and name not in self.interleave:
                    emit(mi, getattr(self, name + "_tile")(ti))
            gens = []
            if "ret" in self.do and "ret" in self.interleave:
                gens.append((3, "ret", self.ret_gen(ti)))
            if "fox" in self.do:
                gens.append((0, "fox", self.fox_gen(ti)))
            if "rwkv" in self.do:
                gens.append((2, "rwkv", self.rwkv_gen(ti)))
            while gens:
                for item in list(gens):
                    try:
                        next(item[2])
                    except StopIteration:
                        gens.remove(item)
                        emit(item[0], self.results[item[1]])
            if self.tile_done is not None:
                self.tile_done(ti)


RET_THETA = 10000.0

G = 256
FOX_OFF = 0; LRU_OFF = 772; RWKV_OFF = 1284; RET_OFF = 2308
NCH = 17
NPV = 35

def prep_A(d, l, hh):
    w_in = d["w_in"][l]
    own = np.arange(hh * 128, hh * 128 + 128)
    oth = np.arange((1 - hh) * 128, (1 - hh) * 128 + 128)
    swap = np.concatenate([own[h * 64:(h + 1) * 64][np.r_[32:64, 0:32]] for h in range(2)])
    cols = [FOX_OFF + own, FOX_OFF + G + own, FOX_OFF + 2 * G + own,
            LRU_OFF + own, LRU_OFF + G + own,
            RWKV_OFF + own, RWKV_OFF + G + own, RWKV_OFF + 2 * G + own, RWKV_OFF + 3 * G + np.arange(128), RWKV_OFF + 3 * G + 128 + np.arange(128),
            RET_OFF + own, RET_OFF + G + own, RET_OFF + 2 * G + own, RET_OFF + 3 * G + own, RET_OFF + swap, RET_OFF + G + swap,
            RWKV_OFF + 2 * G + oth,
            FOX_OFF + 3 * G + hh * 2 + np.arange(2)]
    cols = np.concatenate(cols)
    Wown = np.ascontiguousarray(w_in[:, cols])
    pv = np.zeros((128, NPV), np.float32)
    c = 0
    def put(v):
        nonlocal c
        pv[:len(v), c] = v; c += 1
    for j in range(4): put(d["lru_conv_w"][l, j, own])
    put(d["lru_conv_b"][l, own]); put(d["lru_ra_b"][l, own]); put(d["lru_ri_b"][l, own]); put(d["lru_lambda"][l, own])
    mu = d["rwkv_mu"][l]
    put(mu[own]); put(mu[G + own]); put(mu[2 * G + own]); put(mu[2 * G + oth]); put(mu[3 * G:3 * G + 128]); put(mu[3 * G + 128:3 * G + 256])
    put(d["rwkv_w0"][l, own]); put(d["rwkv_a0"][l, own]); put(d["rwkv_k_k"][l, own]); put(d["rwkv_k_a"][l, own])
    put(d["rwkv_gn_w"][l, own]); put(d["rwkv_gn_b"][l, own]); put(d["rwkv_r_k"][l].reshape(-1)[own])
    put(d["rwkv_v0"][l - 1, own] if l > 0 else np.zeros(128, np.float32))
    put(d["ret_gn_w"][l, own])
    put(d["fox_f_bias"][l, hh * 2:hh * 2 + 2])
    put((np.arange(128) // 64 + 2 * hh).astype(np.float32)); put(np.full(128, 2 * hh, np.float32)); put(np.full(128, 2 * hh + 1, np.float32))
    assert c == 27
    pv[:, 27:35] = d["norm_mix_pre"][l].reshape(8, 128).T
    def blockdiag(w):
        m = np.zeros((128, 128), np.float32)
        for h in range(2):
            m[h * 64:(h + 1) * 64, h * 64:(h + 1) * 64] = w[2 * hh + h]
        return m
    mats = {"RA": blockdiag(d["lru_ra_w"][l]), "RI": blockdiag(d["lru_ri_w"][l]),
            "WA2": np.ascontiguousarray(np.concatenate([d["rwkv_w2"][l][:, own], d["rwkv_a2"][l][:, own]], axis=0)),
            "G2": np.ascontiguousarray(d["rwkv_g2"][l][:, own])}
    if l > 0:
        v1 = d["rwkv_v1"][l - 1]
        mats["V1"] = np.ascontiguousarray(np.stack([v1[own], v1[oth]], axis=1))
        mats["V2"] = np.ascontiguousarray(d["rwkv_v2"][l - 1][:, own])
    return Wown, pv, mats

MATSH = {"RA": [128, 128], "RI": [128, 128], "WA2": [128, 128], "G2": [128, 128], "V1": [128, 2, 32], "V2": [32, 128]}
NTOK_B = 2048
PAIRS = [[0, 1], [2, 3], [4, 5], [6, 7]]
DEPTH = 2


def _build_fused():
    nc = bass.Bass("TRN2", target_bir_lowering=False)
    D = lambda name, shape, kind="ExternalInput": nc.dram_tensor(name, shape, F32, kind=kind).ap()
    xT = D("xT", [1024, 4096])
    xown = D("xown", [1024, NTOK_B])
    memT = D("memT", [1024, 256])
    selv = D("selv", [128, 2])
    A_in, B_in = [], []
    for l in range(DEPTH):
        names = ["RA", "RI", "WA2", "G2"] + (["V1", "V2"] if l > 0 else [])
        A_in.append({"Wown": D("Wown%d" % l, [1024, NCOLS_A]), "pv": D("pv%d" % l, [128, NPV]),
                     "mats": {n: (D("%s_%d" % (n, l), MATSH[n]), MATSH[n]) for n in names}})
        W = {n: D("%s_%d" % (n, l), [1024, 1024]) for n in ("w_out", "wq", "wk", "wv", "wo")}
        W["w1"] = D("w1_%d" % l, [1024, 4096])
        W["w2"] = D("w2_%d" % l, [4096, 1024])
        B_in.append({"W": W, "gain": D("gain%d" % l, [128, 6, 8])})
    xo = D("xo", [1024, NTOK_B], "ExternalOutput")
    with contextlib.ExitStack() as st:
        k = K(nc, st)
        c = Ctx(k)
        ca = ConstA(k, c)
        sel = k.sb("selv_sb", [128, 2], F32)
        k.dma("sp", sel[:], selv, writes=[sel])
        xo_t = T(xo, "xo")
        xT_t, xown_t, memT_t = T(xT, "xT"), T(xown, "xown"), T(memT, "memT")
        vf_t = k.dram("vf_scr", [128, 4096])
        WB = [None] * DEPTH
        XO = k.dram("xo_scr", [1024, NTOK_B])
        XG = None
        for l in range(DEPTH):
            CA = [k.dram("ca%d_%d" % (l, i), [512, TT]) for i in range(8)]
            CG = [k.dram("cg%d_%d" % (l, i), [1024, TT]) for i in range(8)]
            with k.scope():
                a = A_in[l]
                sa = StageA(k, c, ca, l, xT_t, None, a["Wown"], a["pv"], a["mats"],
                            vfirst_in=vf_t if l > 0 else None, vfirst_out=vf_t if l == 0 else None)
                wb = {}
                for n, ap in B_in[l]["W"].items():
                    R_, C_ = ap.shape
                    t = k.dram("wb_%s_%d" % (n, l), [R_, C_], BF16)
                    src, dst = ap, t.ap
                    if C_ > 1024:
                        src = src.rearrange("r (a c) -> (r a) c", c=1024)
                        dst = dst.rearrange("r (a c) -> (r a) c", c=1024)
                    for r0 in range(0, src.shape[0], 512):
                        k.dma("pool", dst[r0:r0 + 512, :], src[r0:r0 + 512, :], writes=[(t, r0)])
                    wb[n] = t
                WB[l] = wb
                if l > 0:
                    XGl = XG

                    def x_src(ti, xt, XGl=XGl):
                        g = XGl[ti % 4]
                        r = ti // 4
                        k.dma("sp", xt[:], g[r * 1024:(r + 1) * 1024, :].rearrange("(a p) c -> p a c", p=128), reads=[g], writes=[xt])
                    sa.x_src = x_src

                def out_sink(ti, mi, o, CA=CA):
                    k.dma("sp", CA[ti][mi * 128:(mi + 1) * 128, :], o[:], reads=[o], writes=[(CA[ti], mi)])

                def tile_done(ti, CA=CA, CG=CG):
                    k.cc_allgather(CA[ti], CG[ti], PAIRS)
                sa.out_sink, sa.tile_done = out_sink, tile_done
                sa.run(8)
            with k.scope():
                b = B_in[l]
                ws = WStream(k, nbuf=4)
                gain = k.sb("gain_sb%d" % l, [128, 6, 8], F32)
                k.dma("sp", gain[:], b["gain"], writes=[gain])
                ctb = k.sb("ctb_%d" % l, [128, 8, TT], BF16)
                last = (l == DEPTH - 1)
                XA = [k.dram("xa%d_%d" % (l, i), [1024, TT]) for i in range(4)] if not last else None
                XGn = [k.dram("xg%d_%d" % (l, i), [2048, TT]) for i in range(4)] if not last else None
                x_res = xown_t if l == 0 else XO

                def cat_load(j, ct, CG=CG, ctb=ctb):
                    k.dma("pool", ct[:], CG[j][:, :].rearrange("(a p) c -> p a c", p=128), reads=[CG[j]], writes=[ct])
                    k.dma("pool", ctb[:], CG[4 + j][:, :].rearrange("(a p) c -> p a c", p=128), reads=[CG[4 + j]], writes=[ctb])
                    fl = lambda t: t[:].rearrange("p a c -> p (a c)")
                    k.ts(fl(ct), fl(ct), sel[:, 1:2], ALU.mult, reads=[ct, sel], writes=[ct])
                    k.stt(fl(ct), fl(ctb), sel[:, 0:1], fl(ct), ALU.mult, ALU.add, reads=[ctb, sel, ct], writes=[ct])

                def x_in(j, xt, x_res=x_res):
                    k.dma("sp", xt[:], x_res[:, j * TT:(j + 1) * TT].rearrange("(a p) c -> p a c", p=128), reads=[(x_res, j)], writes=[xt])

                def x_out(j, xt, last=last, XA=XA, XGn=XGn):
                    if last:
                        k.dma("pool", xo_t[:, j * TT:(j + 1) * TT].rearrange("(a p) c -> p a c", p=128), xt[:], reads=[xt], writes=[(xo_t, j)])
                    else:
                        k.dma("pool", XO[:, j * TT:(j + 1) * TT].rearrange("(a p) c -> p a c", p=128), xt[:], reads=[xt], writes=[(XO, j)])
                        k.dma("pool", XA[j][:, :].rearrange("(a p) c -> p a c", p=128), xt[:], reads=[xt], writes=[XA[j]])
                        k.cc_allgather(XA[j], XGn[j], PAIRS)

                stage_b(k, c, ws, l, None, None, None, memT_t, WB[l], gain, NTOK_B, cat_load=cat_load, x_in=x_in, x_out=x_out)
                XG = XGn
        k.finish("sp", [xo_t])
        k.finish("pool", [xo_t])
    return nc


def prep_B_gain(d, l):
    gl = lambda name: np.ascontiguousarray(d[name][l].reshape(8, 128).T)
    return np.ascontiguousarray(np.stack([gl(n) for n in ("norm_mix_post", "norm_xa_pre", "norm_xa_post", "norm_mem", "norm_mlp_pre", "norm_mlp_post")], axis=1))


def kernel(**inputs):
    d = {k_: np.asarray(v, dtype=np.float32) for k_, v in inputs.items()}
    B = 4
    nc = _build_fused()
    perm = np.concatenate([np.arange(mi * 256 + r * 128, mi * 256 + r * 128 + 128) for r in range(2) for mi in range(4)])
    shared = {}
    for l in range(DEPTH):
        shared["w_out_%d" % l] = np.ascontiguousarray(d["w_out"][l][perm])
        shared["wq_%d" % l] = d["xa_wq"][l]; shared["wk_%d" % l] = d["xa_wk"][l]
        shared["wv_%d" % l] = d["xa_wv"][l]; shared["wo_%d" % l] = d["xa_wo"][l]
        shared["w1_%d" % l] = d["mlp_w1"][l]; shared["w2_%d" % l] = d["mlp_w2"][l]
        shared["gain%d" % l] = prep_B_gain(d, l)
    prepA = {(l, hh): prep_A(d, l, hh) for l in range(DEPTH) for hh in range(2)}
    ins = []
    for cid in range(8):
        b, r = cid // 2, cid % 2
        xTb = np.ascontiguousarray(d["x"][b].T)
        m = dict(shared)
        m["xT"] = xTb
        m["xown"] = np.ascontiguousarray(xTb[:, r * NTOK_B:(r + 1) * NTOK_B])
        m["memT"] = np.ascontiguousarray(d["mem"][b].T)
        m["selv"] = np.ascontiguousarray(np.tile(np.array([[float(r), 1.0 - float(r)]], np.float32), (128, 1)))
        for l in range(DEPTH):
            Wown, pv, mats = prepA[(l, r)]
            m["Wown%d" % l] = Wown
            m["pv%d" % l] = pv
            for n, v in mats.items():
                m["%s_%d" % (n, l)] = v
        ins.append(m)
    res = run_bass_kernel_spmd(nc, ins, core_ids=list(range(8)))
    out = np.empty((B, 4096, 1024), np.float32)
    for cid in range(8):
        b, r = cid // 2, cid % 2
        out[b, r * NTOK_B:(r + 1) * NTOK_B, :] = res.results[cid]["xo"].T
    return out
```

```python
import contextlib, math
import numpy as np
import concourse.bass as bass
import concourse.mybir as mybir
from concourse.bass_utils import run_bass_kernel_spmd


F32 = mybir.dt.float32
BF16 = mybir.dt.bfloat16
I32 = mybir.dt.int32
AF = mybir.ActivationFunctionType
ALU = mybir.AluOpType


class Reg:
    __slots__ = ("last_w", "reads")

    def __init__(self):
        self.last_w = None
        self.reads = []


class T:
    def __init__(self, ap, name):
        self.ap = ap
        self.name = name
        self.regs = {None: Reg()}

    def __getitem__(self, idx):
        return self.ap[idx]


class K:
    ENGS = ("pe", "dve", "act", "pool", "sp")

    def __init__(self, nc, stack, n_dma_sems=6):
        self.nc = nc
        self.stack = stack
        self.eng = {"pe": nc.tensor, "dve": nc.vector, "act": nc.scalar, "pool": nc.gpsimd, "sp": nc.sync}
        self.sem = {}
        self.tick = {}
        for e in self.ENGS:
            self.sem[e] = stack.enter_context(nc.semaphore("s_" + e))
            self.tick[e] = 0
        self.dq = {}
        for q in ("sp", "pool", "act"):
            lst = []
            for i in range(n_dma_sems):
                key = "d_%s%d" % (q, i)
                self.sem[key] = stack.enter_context(nc.semaphore(key))
                self.tick[key] = 0
                lst.append(key)
            self.dq[q] = [lst, 0]
        self.sem["cc"] = stack.enter_context(nc.semaphore("s_cc"))
        self.tick["cc"] = 0
        self.waited = {e: {} for e in self.ENGS}
        self.ninst = {e: 0 for e in self.ENGS}
        self.nwait = 0

    @contextlib.contextmanager
    def scope(self):
        old = self.stack
        with contextlib.ExitStack() as st:
            self.stack = st
            try:
                yield
            finally:
                self.barrier()
                self.stack = old

    def barrier(self):
        deps = [(sk, v) for sk, v in self.tick.items() if v > 0]
        for e in self.ENGS:
            self._wait(e, [d for d in deps if d[0] != e])

    def cc_allgather(self, in_t, out_t, groups):
        reads = self._norm([in_t])
        writes = self._norm([out_t])
        self._wait("pool", self._deps(reads, writes))
        inst = self.eng["pool"].collective_compute("AllGather", ALU.bypass, replica_groups=groups, ins=[in_t.ap], outs=[out_t.ap])
        self.tick["cc"] += 1
        inst.then_inc(self.sem["cc"], 1)
        self.ninst["pool"] += 1
        self._record(("cc", self.tick["cc"]), reads, writes)
        return inst

    _uid = 0

    def sb(self, name, shape, dtype=F32):
        K._uid += 1
        name = "%s_u%d" % (name, K._uid)
        return T(self.stack.enter_context(self.nc.sbuf_tensor(name, list(shape), dtype)), name)

    def ps(self, name, shape, dtype=F32):
        return T(self.stack.enter_context(self.nc.psum_tensor(name, list(shape), dtype)), name)

    def dram(self, name, shape, dtype=F32, kind="Internal"):
        return T(self.nc.dram_tensor(name, list(shape), dtype, kind=kind).ap(), name)

    @staticmethod
    def _norm(lst):
        out = []
        for x in lst:
            if x is None:
                continue
            if isinstance(x, T):
                out.append((x, None))
            else:
                out.append(x)
        return out

    def _deps(self, reads, writes):
        deps = []
        for (t, k) in reads:
            if k is None:
                for r in t.regs.values():
                    if r.last_w:
                        deps.append(r.last_w)
            else:
                r = t.regs.get(k)
                if r is not None and r.last_w:
                    deps.append(r.last_w)
                if t.regs[None].last_w:
                    deps.append(t.regs[None].last_w)
        for (t, k) in writes:
            if k is None:
                rs = list(t.regs.values())
            else:
                rs = [t.regs[None]]
                if k in t.regs:
                    rs.append(t.regs[k])
            for r in rs:
                if r.last_w:
                    deps.append(r.last_w)
                deps.extend(r.reads)
        return deps

    def _record(self, me, reads, writes):
        for (t, k) in reads:
            t.regs.setdefault(k, Reg()).reads.append(me)
        for (t, k) in writes:
            if k is None:
                for kk in list(t.regs.keys()):
                    if kk is not None:
                        del t.regs[kk]
                r = t.regs[None]
            else:
                r = t.regs.setdefault(k, Reg())
            r.last_w = me
            r.reads = []

    def _wait(self, e, deps):
        best = {}
        for (sk, v) in deps:
            if sk == e and e == "pe":
                continue
            if best.get(sk, 0) < v:
                best[sk] = v
        w = self.waited[e]
        for sk, v in best.items():
            if w.get(sk, 0) < v:
                self.eng[e].wait_ge(self.sem[sk], v)
                w[sk] = v
                self.nwait += 1

    def op(self, e, fn, reads=(), writes=()):
        reads = self._norm(reads)
        writes = self._norm(writes)
        self._wait(e, self._deps(reads, writes))
        inst = fn(self.eng[e])
        self.tick[e] += 1
        inst.then_inc(self.sem[e], 1)
        self.ninst[e] += 1
        self._record((e, self.tick[e]), reads, writes)
        return inst

    def dma(self, q, out, in_, reads=(), writes=(), **kw):
        reads = self._norm(reads)
        writes = self._norm(writes)
        lst, i = self.dq[q]
        sk = lst[i % len(lst)]
        self.dq[q][1] = i + 1
        deps = self._deps(reads, writes)
        if self.tick[sk] > 0:
            deps.append((sk, self.tick[sk]))
        self._wait(q, deps)
        inst = self.eng[q].dma_start(out=out, in_=in_, **kw)
        self.tick[sk] += 16
        inst.then_inc(self.sem[sk], 16)
        self.ninst[q] += 1
        self._record((sk, self.tick[sk]), reads, writes)
        return inst

    def finish(self, e, tiles):
        deps = []
        for t in tiles:
            for r in t.regs.values():
                if r.last_w:
                    deps.append(r.last_w)
        self._wait(e, deps)

    def mm(self, out, lhsT, rhs, start=True, stop=True, reads=(), writes=()):
        return self.op("pe", lambda e: e.matmul(out, lhsT, rhs, start=start, stop=stop), reads, writes)

    def tr(self, out, in_, ident, reads=(), writes=()):
        return self.op("pe", lambda e: e.transpose(out, in_, ident), reads, writes)

    def act(self, out, in_, func, reads=(), writes=(), bias=None, scale=None, e="act"):
        kw = {}
        if bias is not None:
            kw["bias"] = bias
        if scale is not None:
            kw["scale"] = scale
        return self.op(e, lambda g: g.activation(out, in_, func, **kw), reads, writes)

    def tt(self, out, in0, in1, op, reads=(), writes=(), e="dve"):
        return self.op(e, lambda g: g.tensor_tensor(out, in0, in1, op), reads, writes)

    def ts(self, out, in0, s1, op0, s2=None, op1=None, reads=(), writes=(), e="dve"):
        if op1 is None:
            return self.op(e, lambda g: g.tensor_scalar(out, in0, s1, None, op0), reads, writes)
        return self.op(e, lambda g: g.tensor_scalar(out, in0, s1, s2, op0, op1), reads, writes)

    def stt(self, out, in0, scalar, in1, op0, op1, reads=(), writes=()):
        return self.op("dve", lambda g: g.scalar_tensor_tensor(out, in0, scalar, in1, op0, op1), reads, writes)

    def cp(self, out, in_, reads=(), writes=(), e="dve"):
        if e == "act":
            return self.op(e, lambda g: g.copy(out, in_), reads, writes)
        return self.op(e, lambda g: g.tensor_copy(out, in_), reads, writes)

    def memset(self, out, val, writes=(), e="pool"):
        return self.op(e, lambda g: g.memset(out, val), (), writes)


TT = 512
EPS = 1e-6


class Rot:
    def __init__(self, k, name, shape, dtype, n):
        self.t = [k.sb("%s%d" % (name, i), shape, dtype) for i in range(n)]
        self.i = 0

    def next(self):
        t = self.t[self.i % len(self.t)]
        self.i += 1
        return t


class Ctx:
    def __init__(self, k):
        self.k = k
        self.ones = k.sb("ones_bf", [128, 128], BF16)
        k.memset(self.ones[:], 1.0, writes=[self.ones])
        self.eps = k.sb("eps_c", [128, 1], F32)
        k.memset(self.eps[:], EPS, writes=[self.eps])
        self.psums = [k.ps("ps%d" % i, [128, 512], F32) for i in range(8)]
        self.pi = 0
        self.sq = Rot(k, "sq", [128, TT], BF16, 3)
        self.rstd = Rot(k, "rstd", [128, TT], F32, 2)
        self.tmp = Rot(k, "tmpf", [128, TT], F32, 3)

    def psum(self):
        p = self.psums[self.pi % 8]
        self.pi += 1
        return p


def rstd_from(c, srcs, n=TT, nfeat=1024.0):
    k = c.k
    pst = c.psum()
    nk = len(srcs)
    for i, (ap, t) in enumerate(srcs):
        sq = c.sq.next()
        k.act(sq[:, 0:n], ap, AF.Square, reads=[t], writes=[sq])
        k.mm(pst[:, 0:n], c.ones[:], sq[:, 0:n], start=(i == 0), stop=(i == nk - 1), reads=[c.ones, sq], writes=[pst])
    r = c.rstd.next()
    k.act(r[:, 0:n], pst[:, 0:n], AF.Sqrt, reads=[pst, c.eps], writes=[r], bias=c.eps[:], scale=1.0 / nfeat)
    k.op("dve", lambda g: g.reciprocal(r[:, 0:n], r[:, 0:n]), reads=[r], writes=[r])
    return r


def pre_norm(c, xt, gain, gi, out, n=TT):
    k = c.k
    r = rstd_from(c, [(xt[:, kc, 0:n], xt) for kc in range(8)], n)
    for kc in range(8):
        k.stt(out[:, kc, 0:n], xt[:, kc, 0:n], gain[:, gi, kc:kc + 1], r[:, 0:n], ALU.mult, ALU.mult,
              reads=[xt, gain, r], writes=[(out, kc)])


def post_norm_res(c, m, gain, gi, xt, n=TT, r=None):
    k = c.k
    if r is None:
        r = rstd_from(c, [(m[:, kc, 0:n], (m, kc)) for kc in range(8)], n)
    for kc in range(8):
        t = c.tmp.next()
        k.tt(t[:, 0:n], m[:, kc, 0:n], r[:, 0:n], ALU.mult, reads=[(m, kc), r], writes=[t])
        k.stt(xt[:, kc, 0:n], t[:, 0:n], gain[:, gi, kc:kc + 1], xt[:, kc, 0:n], ALU.mult, ALU.add,
              reads=[t, gain, (xt, kc)], writes=[(xt, kc)])


class WStream:
    def __init__(self, k, nbuf=4, elems=8 * 512):
        self.k = k
        self.elems = elems
        self.pool = Rot(k, "wbuf", [128, elems], BF16, nbuf)

    def load(self, w_ap, r0, nk, c0, ncols):
        buf = self.pool.next()
        view = buf[:, 0:nk * ncols].rearrange("p (a c) -> p a c", a=nk)
        if isinstance(w_ap, T):
            src = w_ap[r0:r0 + nk * 128, c0:c0 + ncols].rearrange("(a p) c -> p a c", p=128)
            self.qi = getattr(self, "qi", 0) + 1
            self.k.dma("sp", view, src, reads=[w_ap], writes=[buf])
        else:
            src = w_ap[r0:r0 + nk * 128, c0:c0 + ncols].rearrange("(a p) c -> p a c", p=128)
            self.k.dma("pool", view, src, writes=[buf])
        return buf, view


class StatAcc:
    def __init__(self, c, n=TT, nfeat=1024.0):
        self.c, self.n, self.nfeat = c, n, nfeat
        self.pst = c.psums[7]
        self.pending = None
        self.cnt = 0

    def _flush(self, last):
        k, c, n = self.c.k, self.c, self.n
        if self.pending is not None:
            sq, first = self.pending
            k.mm(self.pst[:, 0:n], c.ones[:], sq[:, 0:n], start=first, stop=last, reads=[c.ones, sq], writes=[self.pst])
            self.pending = None

    def add(self, ap, dep):
        k, c, n = self.c.k, self.c, self.n
        self._flush(False)
        sq = c.sq.next()
        k.act(sq[:, 0:n], ap, AF.Square, reads=[dep], writes=[sq])
        self.pending = (sq, self.cnt == 0)
        self.cnt += 1

    def finish(self):
        k, c, n = self.c.k, self.c, self.n
        self._flush(True)
        r = c.rstd.next()
        k.act(r[:, 0:n], self.pst[:, 0:n], AF.Sqrt, reads=[self.pst, c.eps], writes=[r], bias=c.eps[:], scale=1.0 / self.nfeat)
        k.op("dve", lambda g: g.reciprocal(r[:, 0:n], r[:, 0:n]), reads=[r], writes=[r])
        return r


def pre_norm_split(c, xt, gain, gi, out, n=TT):
    k = c.k
    for kc in range(8):
        k.ts(out[:, kc, 0:n], xt[:, kc, 0:n], gain[:, gi, kc:kc + 1], ALU.mult, reads=[(xt, kc), gain], writes=[(out, kc)])
    return lambda: rstd_from(c, [(xt[:, kc, 0:n], (xt, kc)) for kc in range(8)], n)


def dense(c, ws, w_ap, nk, nout_chunks, rhs_fn, rhs_reads, sink, n=TT, cols_per_load=512, hook=None):
    k = c.k
    per = cols_per_load // 128
    for o0 in range(0, nout_chunks, per):
        buf, view = ws.load(w_ap, 0, nk, o0 * 128, cols_per_load)
        for j in range(per):
            oc = o0 + j
            ps = c.psum()
            for kc in range(nk):
                k.mm(ps[:, 0:n], view[:, kc, j * 128:(j + 1) * 128], rhs_fn(kc), start=(kc == 0), stop=(kc == nk - 1),
                     reads=[buf] + rhs_reads(kc), writes=[ps])
            if hook is not None and oc == 0:
                hook()
            sink(oc, ps)


def stage_b(k, c, ws, l, catT, xT_in, xT_out, memT, W, gain, NT, tok0=0, cat_load=None, x_in=None, x_out=None):
    m = k.sb("m_%d" % l, [128, 8, TT], F32)
    memt = m
    k.dma("sp", memt[:, :, 0:256], memT[:, :].rearrange("(a p) c -> p a c", p=128), reads=[memT], writes=[memt])
    mn = k.sb("mn_%d" % l, [128, 8, 256], BF16)
    pre_norm(c, memt, gain, 3, mn, n=256)
    kT = k.sb("kT_%d" % l, [128, 8, 256], BF16)
    vM = k.sb("vM_%d" % l, [128, 2, 1024], BF16)

    def sink_k(oc, ps):
        k.cp(kT[:, oc, :], ps[:, 0:256], reads=[ps], writes=[(kT, oc)], e="act")

    dense(c, ws, W["wk"], 8, 8, lambda kc: mn[:, kc, :], lambda kc: [(mn, kc)], sink_k, n=256)
    for half in range(2):
        buf, view = ws.load(W["wv"], 0, 8, half * 512, 512)
        for mc in range(2):
            ps = c.psum()
            for kc in range(8):
                k.mm(ps[:, :], mn[:, kc, mc * 128:(mc + 1) * 128], view[:, kc, :], start=(kc == 0), stop=(kc == 7),
                     reads=[buf, (mn, kc)], writes=[ps])
            k.cp(vM[:, mc, half * 512:(half + 1) * 512], ps[:, :], reads=[ps], writes=[(vM, (mc, half))], e="act")

    xpool = Rot(k, "xt_%d" % l, [128, 8, TT], F32, 2)
    cpool = Rot(k, "ct_%d" % l, [128, 8, TT], BF16, 1)
    hb = k.sb("hb_%d" % l, [128, 8, TT], BF16)
    qT = k.sb("qT_%d" % l, [128, 8, TT], BF16)
    oT = k.sb("oT_%d" % l, [128, 8, TT], BF16)
    hid = k.sb("hid_%d" % l, [128, 32, TT], BF16)
    pT = Rot(k, "pT_%d" % l, [128, TT], BF16, 4)
    relu = Rot(k, "relu_%d" % l, [128, TT], F32, 3)
    rden = Rot(k, "rden_%d" % l, [128, TT], F32, 2)
    r2buf = k.sb("r2buf_%d" % l, [128, TT], F32)

    def sink_m(oc, ps):
        k.cp(m[:, oc, :], ps[:, :], reads=[ps], writes=[(m, oc)], e="dve")

    for ti in range(NT // TT):
        t0 = tok0 + ti * TT
        xt = xpool.next()
        if x_in is None:
            k.dma("sp", xt[:], xT_in[:, t0:t0 + TT].rearrange("(a p) c -> p a c", p=128), reads=[xT_in], writes=[xt])
        elif ti == 0:
            x_in(ti, xt)
        if x_in is not None and ti + 1 < NT // TT:
            xnext = xpool.t[(xpool.i) % len(xpool.t)]
            x_in(ti + 1, xnext)
        ct = cpool.next()
        if cat_load is None:
            k.dma("pool", ct[:], catT[:, t0:t0 + TT].rearrange("(a p) c -> p a c", p=128), reads=[catT], writes=[ct])
        elif ti == 0:
            cat_load(ti, ct)
        dense(c, ws, W["w_out"], 8, 8, lambda kc: ct[:, kc, :], lambda kc: [ct], sink_m)
        if cat_load is not None and ti + 1 < NT // TT:
            cat_load(ti + 1, ct)
        post_norm_res(c, m, gain, 0, xt)
        stats_q = pre_norm_split(c, xt, gain, 1, hb)
        rq = {}

        def hook_q():
            rq["r"] = stats_q()

        def sink_q(oc, ps):
            r_ = rq["r"]
            k.tt(qT[:, oc, :], ps[:, :], r_[:], ALU.mult, reads=[ps, r_], writes=[(qT, oc)])

        dense(c, ws, W["wq"], 8, 8, lambda kc: hb[:, kc, :], lambda kc: [(hb, kc)], sink_q, hook=hook_q)
        for hd in range(4):
            pts = []
            for mc in range(2):
                ps = c.psum()
                for j in range(2):
                    dc = hd * 2 + j
                    k.mm(ps[:, :], kT[:, dc, mc * 128:(mc + 1) * 128], qT[:, dc, :], start=(j == 0), stop=(j == 1),
                         reads=[(kT, dc), (qT, dc)], writes=[ps])
                pt = pT.next()
                k.act(pt[:], ps[:, :], AF.Exp, reads=[ps], writes=[pt], scale=1.0 / 16.0)
                pts.append(pt)
            pden = c.psum()
            for mc in range(2):
                k.mm(pden[:, :], c.ones[:], pts[mc][:], start=(mc == 0), stop=(mc == 1), reads=[c.ones, pts[mc]], writes=[pden])
            rd = rden.next()
            k.op("dve", lambda g: g.reciprocal(rd[:], pden[:, :]), reads=[pden], writes=[rd])
            for j in range(2):
                dc = hd * 2 + j
                ps = c.psum()
                for mc in range(2):
                    k.mm(ps[:, :], vM[:, mc, dc * 128:(dc + 1) * 128], pts[mc][:], start=(mc == 0), stop=(mc == 1),
                         reads=[vM, pts[mc]], writes=[ps])
                k.tt(oT[:, dc, :], ps[:, :], rd[:], ALU.mult, reads=[ps, rd], writes=[(oT, dc)])
        dense(c, ws, W["wo"], 8, 8, lambda kc: oT[:, kc, :], lambda kc: [(oT, kc)], sink_m)
        post_norm_res(c, m, gain, 2, xt)
        stats_m = pre_norm_split(c, xt, gain, 4, hb)

        def hook_m():
            r_ = stats_m()
            k.tt(r2buf[:], r_[:], r_[:], ALU.mult, reads=[r_], writes=[r2buf])

        def sink_h(f, ps):
            r = relu.next()
            k.act(r[:], ps[:, :], AF.Relu, reads=[ps], writes=[r])
            k.tt(hid[:, f, :], r[:], r[:], ALU.mult, reads=[r], writes=[(hid, f)], e="pool")

        dense(c, ws, W["w1"], 8, 32, lambda kc: hb[:, kc, :], lambda kc: [(hb, kc)], sink_h, hook=hook_m)
        for fb in range(8):
            buf, view = ws.load(W["w2"], fb * 512, 4, 0, 1024)
            for oc in range(8):
                ps = c.psums[oc]
                for f4 in range(4):
                    f = fb * 4 + f4
                    k.mm(ps[:, :], view[:, f4, oc * 128:(oc + 1) * 128], hid[:, f, :], start=(f == 0), stop=(f == 31),
                         reads=[buf, (hid, f)], writes=[ps])
        for oc in range(8):
            k.tt(m[:, oc, :], c.psums[oc][:, :], r2buf[:], ALU.mult, reads=[c.psums[oc], r2buf], writes=[(m, oc)])
        post_norm_res(c, m, gain, 5, xt)
        if x_out is None:
            k.dma("sp", xT_out[:, t0:t0 + TT].rearrange("(a p) c -> p a c", p=128), xt[:], reads=[xt], writes=[xT_out])
        else:
            x_out(ti, xt)


(CH_FQ, CH_FK, CH_FV, CH_LX, CH_LY, CH_RR, CH_RK, CH_RV, CH_RWA, CH_RGD,
 CH_TQ, CH_TK, CH_TV, CH_TG, CH_TQS, CH_TKS, CH_RVO) = range(17)
NCH = 17
NCOLS_A = NCH * 128 + 2

(PV_CW0, PV_CW1, PV_CW2, PV_CW3, PV_CB, PV_RAB, PV_RIB, PV_LAM,
 PV_MU_R, PV_MU_K, PV_MU_V, PV_MU_VO, PV_MU_WA, PV_MU_GD,
 PV_W0, PV_A0, PV_KK, PV_KA, PV_GNW, PV_GNB, PV_RK, PV_V0,
 PV_TGN, PV_FB, PV_HP, PV_H0, PV_H1) = range(27)
PV_GAIN = 27
NPV = 35


class ConstA:
    def __init__(self, k, c):
        self.k = k
        nc = k.nc
        g = k.eng["pool"]
        self.ident = k.sb("identf", [128, 128], F32)
        k.memset(self.ident[:], 1.0, writes=[self.ident])
        k.op("pool", lambda e: e.affine_select(out=self.ident[:], in_=self.ident[:], pattern=[[-1, 128]], compare_op=ALU.is_equal,
                                               fill=0.0, base=0, channel_multiplier=1), reads=[self.ident], writes=[self.ident])
        self.ut_f = k.sb("ut_f", [128, 128], F32)
        k.memset(self.ut_f[:], 1.0, writes=[self.ut_f])
        k.op("pool", lambda e: e.affine_select(out=self.ut_f[:], in_=self.ut_f[:], pattern=[[1, 128]], compare_op=ALU.is_ge,
                                               fill=0.0, base=0, channel_multiplier=-1), reads=[self.ut_f], writes=[self.ut_f])
        self.ut_b = k.sb("ut_b", [128, 128], BF16)
        k.cp(self.ut_b[:], self.ut_f[:], reads=[self.ut_f], writes=[self.ut_b])
        self.blk = k.sb("blk_f", [128, 128], F32)
        k.memset(self.blk[:], 0.0, writes=[self.blk])
        k.memset(self.blk[0:64, 0:64], 1.0, writes=[self.blk])
        k.memset(self.blk[64:128, 64:128], 1.0, writes=[self.blk])
        self.ones_f = k.sb("ones_f", [128, 512], F32)
        k.memset(self.ones_f[:], 1.0, writes=[self.ones_f])
        self.one_c = k.sb("one_c", [128, 1], F32)
        k.memset(self.one_c[:], 1.0, writes=[self.one_c])
        self.zero_c = k.sb("zero_c", [128, 1], F32)
        k.memset(self.zero_c[:], 0.0, writes=[self.zero_c])


def gelu_tanh(k, c, y_ps, y_t, tmps):
    y = tmps.next()
    k.cp(y[:], y_ps, reads=[y_t], writes=[y], e="act")
    t = tmps.next()
    k.tt(t[:], y[:], y[:], ALU.mult, reads=[y], writes=[t])
    k.ts(t[:], t[:], 0.044715, ALU.mult, 1.0, ALU.add, reads=[t], writes=[t])
    k.tt(t[:], t[:], y[:], ALU.mult, reads=[t, y], writes=[t])
    k.act(t[:], t[:], AF.Sigmoid, reads=[t], writes=[t], scale=2.0 * math.sqrt(2.0 / math.pi))
    return y, t


class StageA:
    def __init__(self, k, c, ca, l, xT, outT, Wown, pvd, mats, vfirst_in=None, vfirst_out=None, do=("lru", "ret", "fox", "rwkv")):
        self.k, self.c, self.ca, self.l = k, c, ca, l
        self.xT, self.outT = xT, outT
        self.do = do
        self.x_src = self.out_sink = self.tile_done = None
        self.results = {}
        self.interleave = ()
        self.f32r = False
        self.vf_in, self.vf_out = vfirst_in, vfirst_out
        L = "a%d_" % l
        self.L = L
        self.W = k.sb(L + "W", [128, 8, NCOLS_A], BF16)
        for kc in range(8):
            k.dma("pool", self.W[:, kc, :], Wown[kc * 128:(kc + 1) * 128, :], writes=[(self.W, kc)], max_dma_last_dim=4096)
        self.pv = k.sb(L + "pv", [128, NPV], F32)
        k.dma("sp", self.pv[:], pvd, writes=[self.pv])
        self.mats = {}
        for name, (ap, shape) in mats.items():
            t = k.sb(L + name, shape, F32)
            k.dma("sp", t[:], ap, writes=[t])
            self.mats[name] = t
        self.xpool = Rot(k, L + "xt", [128, 8, TT], F32, 1)
        self.hb = k.sb(L + "hb", [128, 8, TT], BF16)
        self.tmps = Rot(k, L + "t", [128, TT], F32, 12)
        self.outs = Rot(k, L + "o", [128, TT], F32, 3)
        self.named_ps = c.psums[5:8]
        if "lru" in do:
            self.init_lru()
        if "ret" in do:
            self.init_ret()
        if "fox" in do:
            self.init_fox()
        if "rwkv" in do:
            self.init_rwkv()

    def pcol(self, i):
        return self.pv[:, i:i + 1]

    def psum(self):
        c = self.c
        p = c.psums[c.pi % 5]
        c.pi += 1
        return p

    def proj(self, ch, n0=0, n1=TT):
        k = self.k
        ps = self.psum()
        for kc in range(8):
            k.mm(ps[:, n0:n1], self.W[:, kc, ch * 128:(ch + 1) * 128], self.hb[:, kc, n0:n1], start=(kc == 0), stop=(kc == 7),
                 reads=[(self.W, kc), (self.hb, kc)], writes=[ps])
        return ps

    def init_lru(self):
        k, L = self.k, self.L
        self.l_xbuf = k.sb(L + "lxb", [128, 3 + TT], F32)
        k.memset(self.l_xbuf[:, 0:3], 0.0, writes=[self.l_xbuf])
        self.l_h = k.sb(L + "lh", [128, 1], F32)
        k.memset(self.l_h[:], 0.0, writes=[self.l_h])
        self.l_c1 = k.sb(L + "lc1", [128, 2], F32)
        k.act(self.l_c1[:, 0:1], self.pcol(PV_LAM), AF.Exp, reads=[self.pv], writes=[self.l_c1], scale=-1.0)
        k.act(self.l_c1[:, 0:1], self.l_c1[:, 0:1], AF.Ln, reads=[self.l_c1, self.ca.one_c], writes=[self.l_c1], bias=self.ca.one_c[:])
        k.ts(self.l_c1[:, 1:2], self.l_c1[:, 0:1], -16.0, ALU.mult, reads=[self.l_c1], writes=[self.l_c1])
        k.ts(self.l_c1[:, 0:1], self.l_c1[:, 0:1], -8.0, ALU.mult, reads=[self.l_c1], writes=[self.l_c1])

    def lru_tile(self, ti):
        k, c, ca = self.k, self.c, self.ca
        xb = self.l_xbuf
        ps = self.proj(CH_LX)
        k.cp(xb[:, 3:3 + TT], ps[:, :], reads=[ps], writes=[xb], e="act")
        xc = self.tmps.next()
        k.ts(xc[:], xb[:, 0:TT], self.pcol(PV_CW0), ALU.mult, self.pcol(PV_CB), ALU.add, reads=[xb, self.pv], writes=[xc])
        for j in range(1, 4):
            k.stt(xc[:], xb[:, j:j + TT], self.pcol(PV_CW0 + j), xc[:], ALU.mult, ALU.add, reads=[xb, self.pv, xc], writes=[xc])
        k.cp(xb[:, 0:3], xb[:, TT:TT + 3], reads=[xb], writes=[xb], e="dve")
        pr = self.psum()
        k.mm(pr[:, :], self.mats["RA"][:], xc[:], reads=[self.mats["RA"], xc], writes=[pr])
        pi_ = self.psum()
        k.mm(pi_[:, :], self.mats["RI"][:], xc[:], reads=[self.mats["RI"], xc], writes=[pi_])
        r = self.tmps.next()
        k.act(r[:], pr[:, :], AF.Sigmoid, reads=[pr, self.pv], writes=[r], bias=self.pcol(PV_RAB))
        ig = self.tmps.next()
        k.act(ig[:], pi_[:, :], AF.Sigmoid, reads=[pi_, self.pv], writes=[ig], bias=self.pcol(PV_RIB))
        a = self.tmps.next()
        k.act(a[:], r[:], AF.Exp, reads=[r, self.l_c1], writes=[a], scale=self.l_c1[:, 0:1])
        mult = self.tmps.next()
        k.act(mult[:], r[:], AF.Exp, reads=[r, self.l_c1], writes=[mult], scale=self.l_c1[:, 1:2])
        k.act(mult[:], mult[:], AF.Sqrt, reads=[mult, ca.one_c], writes=[mult], scale=-1.0, bias=ca.one_c[:])
        k.tt(ig[:], ig[:], xc[:], ALU.mult, reads=[ig, xc], writes=[ig])
        k.tt(ig[:], ig[:], mult[:], ALU.mult, reads=[ig, mult], writes=[ig])
        h = self.tmps.next()
        k.op("dve", lambda g: g.tensor_tensor_scan(h[:], a[:], ig[:], self.l_h[:, 0:1], ALU.mult, ALU.add),
             reads=[a, ig, self.l_h], writes=[h])
        k.cp(self.l_h[:], h[:, TT - 1:TT], reads=[h], writes=[self.l_h], e="dve")
        py = self.proj(CH_LY)
        o = self.outs.next()
        y, t = gelu_tanh(k, c, py[:, :], py, self.tmps)
        k.tt(o[:], t[:], y[:], ALU.mult, reads=[t, y], writes=[o])
        k.tt(o[:], o[:], h[:], ALU.mult, reads=[o, h], writes=[o])
        return o

    def init_ret(self):
        k, L, ca = self.k, self.L, self.ca
        self.t_lg = k.sb(L + "tlg", [128, 3], F32)
        tmp = k.sb(L + "tlgt", [128, 3], F32)
        k.ts(tmp[:], self.pv[:, PV_HP:PV_HP + 3], 5.0, ALU.add, reads=[self.pv], writes=[tmp])
        k.act(tmp[:], tmp[:], AF.Exp, reads=[tmp], writes=[tmp], scale=-math.log(2.0))
        k.act(self.t_lg[:], tmp[:], AF.Ln, reads=[tmp, ca.one_c], writes=[self.t_lg], scale=-1.0, bias=ca.one_c[:])
        ni = k.sb(L + "tni", [128, 128], I32)
        k.op("pool", lambda e: e.iota(ni[:], pattern=[[1, 128]], base=0, channel_multiplier=0), writes=[ni])
        nf = k.sb(L + "tnf", [128, 128], F32)
        k.cp(nf[:], ni[:], reads=[ni], writes=[nf])
        self.t_xi = k.sb(L + "txi", [128, 128], F32)
        self.t_zt = k.sb(L + "tzt", [128, 128], F32)
        tt_ = k.sb(L + "ttmp", [128, 128], F32)
        k.ts(tt_[:], nf[:], 1.0, ALU.add, reads=[nf], writes=[tt_])
        k.act(self.t_xi[:], tt_[:], AF.Exp, reads=[tt_, self.t_lg], writes=[self.t_xi], scale=self.t_lg[:, 0:1])
        k.ts(tt_[:], nf[:], -1.0, ALU.mult, 127.0, ALU.add, reads=[nf], writes=[tt_])
        k.act(self.t_zt[:], tt_[:], AF.Exp, reads=[tt_, self.t_lg], writes=[self.t_zt], scale=self.t_lg[:, 0:1])
        k.ts(self.t_zt[:], self.t_zt[:], 0.125, ALU.mult, reads=[self.t_zt], writes=[self.t_zt])
        self.t_gc = k.sb(L + "tgc", [128, 1], F32)
        k.act(self.t_gc[:], self.t_lg[:, 0:1], AF.Exp, reads=[self.t_lg], writes=[self.t_gc], scale=128.0)
        di = k.sb(L + "tdi", [128, 128], I32)
        k.op("pool", lambda e: e.iota(di[:], pattern=[[1, 128]], base=0, channel_multiplier=-1), writes=[di])
        df = k.sb(L + "tdf", [128, 128], F32)
        k.cp(df[:], di[:], reads=[di], writes=[df])
        k.ts(df[:], df[:], 0.0, ALU.max, reads=[df], writes=[df])
        self.t_dt = k.sb(L + "tdt", [128, 2, 128], F32)
        for h in range(2):
            k.act(self.t_dt[:, h, :], df[:], AF.Exp, reads=[df, self.t_lg], writes=[self.t_dt], scale=self.t_lg[:, 1 + h:2 + h])
            k.stt(self.t_dt[:, h, :], self.t_dt[:, h, :], 0.125, ca.ut_f[:], ALU.mult, ALU.mult, reads=[self.t_dt, ca.ut_f], writes=[self.t_dt])
        pi_ = k.sb(L + "tpi", [128, 2], I32)
        k.op("pool", lambda e: e.iota(pi_[:, 0:1], pattern=[[0, 1]], base=0, channel_multiplier=1), writes=[pi_])
        k.ts(pi_[:, 1:2], pi_[:, 0:1], 32, ALU.bitwise_and, reads=[pi_], writes=[pi_])
        k.ts(pi_[:, 0:1], pi_[:, 0:1], 31, ALU.bitwise_and, reads=[pi_], writes=[pi_])
        pf = k.sb(L + "tpf", [128, 2], F32)
        k.cp(pf[:], pi_[:], reads=[pi_], writes=[pf])
        self.t_inv = k.sb(L + "tinv", [128, 1], F32)
        k.act(self.t_inv[:], pf[:, 0:1], AF.Exp, reads=[pf], writes=[self.t_inv], scale=-math.log(RET_THETA) / 31.0)
        self.t_sgn = k.sb(L + "tsgn", [128, 1], F32)
        k.ts(self.t_sgn[:], pf[:, 1:2], 1.0 / 16.0, ALU.mult, -1.0, ALU.add, reads=[pf], writes=[self.t_sgn])
        qi = k.sb(L + "tqi", [128, TT], I32)
        k.op("pool", lambda e: e.iota(qi[:], pattern=[[1, TT]], base=0, channel_multiplier=0), writes=[qi])
        self.t_pos = k.sb(L + "tpos", [128, TT], F32)
        k.cp(self.t_pos[:], qi[:], reads=[qi], writes=[self.t_pos])
        self.t_R = k.sb(L + "tR", [128, 64], F32)
        k.memset(self.t_R[:], 0.0, writes=[self.t_R])
        self.t_Rb = k.sb(L + "tRb", [128, 64], BF16)
        k.memset(self.t_Rb[:], 0.0, writes=[self.t_Rb])
        self.t_kzt = [k.sb(L + "tkzt%d" % i, [128, 128], BF16) for i in range(4)]
        self.t_ind = [k.sb(L + "tind%d" % i, [128, 128], BF16) for i in range(8)]
        self.t_kv = k.sb(L + "tkv", [128, 4, 64], F32)
        self.t_Rb4 = k.sb(L + "tRb4", [128, 4, 64], BF16)
        self.t_vtm = Rot(k, L + "tvtm", [128, 4, 128], BF16, 1)
        self.t_bf = Rot(k, L + "tbf", [128, TT], BF16, 4)

    def sincos(self, t0):
        k, ca = self.k, self.ca
        TWO_PI = 2.0 * math.pi
        C1 = 6.28125
        C2 = TWO_PI - C1
        res = []
        for shift in (math.pi / 2.0, 0.0):
            ang = self.tmps.next()
            k.ts(ang[:], self.t_pos[:], float(t0), ALU.add, self.t_inv[:, 0:1], ALU.mult, reads=[self.t_pos, self.t_inv], writes=[ang])
            if shift:
                k.ts(ang[:], ang[:], shift, ALU.add, reads=[ang], writes=[ang])
            kf = self.tmps.next()
            ki = kf[:].bitcast(I32)
            k.ts(kf[:], ang[:], 1.0 / TWO_PI, ALU.mult, reads=[ang], writes=[kf])
            k.cp(ki, kf[:], reads=[kf], writes=[kf])
            k.cp(kf[:], ki, reads=[kf], writes=[kf])
            k.stt(ang[:], kf[:], -C1, ang[:], ALU.mult, ALU.add, reads=[kf, ang], writes=[ang])
            k.stt(ang[:], kf[:], -C2, ang[:], ALU.mult, ALU.add, reads=[kf, ang], writes=[ang])
            k.ts(kf[:], ang[:], math.pi, ALU.is_gt, -TWO_PI, ALU.mult, reads=[ang], writes=[kf])
            k.tt(ang[:], ang[:], kf[:], ALU.add, reads=[ang, kf], writes=[ang])
            k.ts(kf[:], ang[:], -math.pi, ALU.is_lt, TWO_PI, ALU.mult, reads=[ang], writes=[kf])
            k.tt(ang[:], ang[:], kf[:], ALU.add, reads=[ang, kf], writes=[ang])
            k.ts(ang[:], ang[:], math.pi, ALU.min, -math.pi, ALU.max, reads=[ang], writes=[ang])
            k.act(ang[:], ang[:], AF.Sin, reads=[ang], writes=[ang])
            res.append(ang)
        C, S = res
        k.ts(S[:], S[:], self.t_sgn[:, 0:1], ALU.mult, reads=[S, self.t_sgn], writes=[S])
        return C, S

    def ret_gen(self, ti):
        k, c, ca = self.k, self.c, self.ca
        t0 = ti * TT
        C, S = self.sincos(t0)
        yield
        rot = []
        for (cha, chs) in ((CH_TQ, CH_TQS), (CH_TK, CH_TKS)):
            pa = self.proj(cha)
            a = self.tmps.next()
            k.tt(a[:], pa[:, :], C[:], ALU.mult, reads=[pa, C], writes=[a])
            pb = self.proj(chs)
            b = self.tmps.next()
            k.tt(b[:], pb[:, :], S[:], ALU.mult, reads=[pb, S], writes=[b])
            k.tt(a[:], a[:], b[:], ALU.add, reads=[a, b], writes=[a], e="pool")
            rot.append(a)
            yield
        qr, kr = rot
        qb = self.t_bf.next()
        k.cp(qb[:], qr[:], reads=[qr], writes=[qb], e="act")
        kb = self.t_bf.next()
        k.cp(kb[:], kr[:], reads=[kr], writes=[kb], e="act")
        qx = self.t_bf.next()
        k.tt(qx[:].rearrange("p (c n) -> p c n", n=128), qr[:].rearrange("p (c n) -> p c n", n=128),
             self.t_xi[:].unsqueeze(1).to_broadcast([128, 4, 128]), ALU.mult, reads=[qr, self.t_xi], writes=[qx])
        kz = self.tmps.next()
        k.tt(kz[:].rearrange("p (c n) -> p c n", n=128), kr[:].rearrange("p (c n) -> p c n", n=128),
             self.t_zt[:].unsqueeze(1).to_broadcast([128, 4, 128]), ALU.mult, reads=[kr, self.t_zt], writes=[kz])
        vtm = self.t_vtm.next()
        for cc in range(4):
            ps = self.psum()
            for kc in range(8):
                k.mm(ps[:, 0:128], self.hb[:, kc, cc * 128:(cc + 1) * 128], self.W[:, kc, CH_TV * 128:(CH_TV + 1) * 128],
                     start=(kc == 0), stop=(kc == 7), reads=[(self.W, kc), (self.hb, kc)], writes=[ps])
            k.cp(vtm[:, cc, :], ps[:, 0:128], reads=[ps], writes=[(vtm, cc)], e="act")
            yield
        osb = self.tmps.next()
        inds = []
        for cc in range(4):
            cs = slice(cc * 128, (cc + 1) * 128)
            ptr = self.psum()
            k.tr(ptr[:, 0:128], kz[:, cs], ca.ident[:], reads=[kz, ca.ident], writes=[ptr])
            kzt = self.t_kzt[cc]
            k.cp(kzt[:], ptr[:, 0:128], reads=[ptr], writes=[kzt], e="act")
            for h in range(2):
                hp = slice(h * 64, (h + 1) * 64)
                pin = self.psum()
                k.mm(pin[:, 0:128], kb[hp, cs], qb[hp, cs], reads=[kb, qb], writes=[pin])
                ind = self.t_ind[cc * 2 + h]
                k.tt(ind[:], pin[:, 0:128], self.t_dt[:, h, :], ALU.mult, reads=[pin, self.t_dt], writes=[ind])
                inds.append(ind)
            yield
        for cc in range(4):
            pkv = self.psum()
            kzt = self.t_kzt[cc]
            for h in range(2):
                hp = slice(h * 64, (h + 1) * 64)
                k.mm(pkv[hp, 0:64], kzt[:, hp], vtm[:, cc, hp], reads=[kzt, (vtm, cc)], writes=[pkv])
            k.cp(self.t_kv[:, cc, :], pkv[:, 0:64], reads=[pkv], writes=[(self.t_kv, cc)], e="act")
            yield
        for cc in range(4):
            k.cp(self.t_Rb4[:, cc, :], self.t_R[:], reads=[self.t_R], writes=[(self.t_Rb4, cc)], e="act")
            k.stt(self.t_R[:], self.t_R[:], self.t_gc[:, 0:1], self.t_kv[:, cc, :], ALU.mult, ALU.add,
                  reads=[self.t_R, self.t_gc, (self.t_kv, cc)], writes=[self.t_R])
        yield
        for cc in range(4):
            cs = slice(cc * 128, (cc + 1) * 128)
            po = self.psum()
            for h in range(2):
                hp = slice(h * 64, (h + 1) * 64)
                ind = inds[cc * 2 + h]
                k.mm(po[hp, 0:128], vtm[:, cc, hp], ind[:], start=True, stop=False, reads=[(vtm, cc), ind], writes=[po])
                k.mm(po[hp, 0:128], self.t_Rb4[hp, cc, :], qx[hp, cs], start=False, stop=True, reads=[(self.t_Rb4, cc), qx], writes=[po])
            k.cp(osb[:, cs], po[:, 0:128], reads=[po], writes=[osb], e="act")
            yield
        sq = self.tmps.next()
        k.tt(sq[:], osb[:], osb[:], ALU.mult, reads=[osb], writes=[sq], e="pool")
        pss = self.psum()
        k.mm(pss[:, :], ca.blk[:], sq[:], reads=[ca.blk, sq], writes=[pss])
        rs = self.tmps.next()
        k.act(rs[:], pss[:, :], AF.Sqrt, reads=[pss, c.eps], writes=[rs], bias=c.eps[:], scale=1.0 / 64.0)
        k.op("dve", lambda g: g.reciprocal(rs[:], rs[:]), reads=[rs], writes=[rs])
        pg = self.proj(CH_TG)
        sg = self.tmps.next()
        k.act(sg[:], pg[:, :], AF.Silu, reads=[pg], writes=[sg])
        o = self.outs.next()
        k.stt(o[:], osb[:], self.pcol(PV_TGN), rs[:], ALU.mult, ALU.mult, reads=[osb, self.pv, rs], writes=[o])
        k.tt(o[:], o[:], sg[:], ALU.mult, reads=[o, sg], writes=[o])
        self.results["ret"] = o

    def ret_tile(self, ti):
        for _ in self.ret_gen(ti):
            pass
        return self.results["ret"]


    def init_fox(self):
        k, L, ca = self.k, self.L, self.ca
        self.f_kT = k.sb(L + "fkT", [128, 4096], BF16)
        self.f_v = k.sb(L + "fv", [128, 32, 128], BF16)
        self.f_q = k.sb(L + "fq", [128, TT], BF16)
        self.f_row = Rot(k, L + "frow", [2, TT], F32, 3)
        self.f_clast = k.sb(L + "fcl", [2, 1], F32)
        k.memset(self.f_clast[:], 0.0, writes=[self.f_clast])
        self.f_ccol = k.sb(L + "fcc", [128, 32, 2], F32)
        self.f_cend = k.sb(L + "fce", [128, 4, 2], F32)
        self.f_B = Rot(k, L + "fB", [128, 32, 4], F32, 2)
        self.f_pT = Rot(k, L + "fpT", [128, TT], BF16, 4)
        self.f_rd = k.sb(L + "frd", [128, TT], F32)
        self.f_sel = k.sb(L + "fsel", [128, 128], F32)
        k.memset(self.f_sel[:], 1.0, writes=[self.f_sel])
        k.op("pool", lambda e: e.affine_select(out=self.f_sel[:], in_=self.f_sel[:], pattern=[[0, 128]], compare_op=ALU.is_equal,
                                               fill=0.0, base=-127, channel_multiplier=1), reads=[self.f_sel], writes=[self.f_sel])
        self.f_nb = k.sb(L + "fnb", [2, 1], F32)
        k.ts(self.f_nb[:], self.pv[0:2, PV_FB:PV_FB + 1], -1.0, ALU.mult, reads=[self.pv], writes=[self.f_nb])

    def fox_gen(self, ti):
        k, c, ca = self.k, self.c, self.ca
        t0 = ti * TT
        pq = self.proj(CH_FQ)
        k.cp(self.f_q[:], pq[:, :], reads=[pq], writes=[self.f_q], e="act")
        pk = self.proj(CH_FK)
        k.cp(self.f_kT[:, t0:t0 + TT], pk[:, :], reads=[pk], writes=[(self.f_kT, ti)], e="act")
        for cc in range(4):
            ps = self.psum()
            for kc in range(8):
                k.mm(ps[:, 0:128], self.hb[:, kc, cc * 128:(cc + 1) * 128], self.W[:, kc, CH_FV * 128:(CH_FV + 1) * 128],
                     start=(kc == 0), stop=(kc == 7), reads=[(self.W, kc), (self.hb, kc)], writes=[ps])
            k.cp(self.f_v[:, ti * 4 + cc, :], ps[:, 0:128], reads=[ps], writes=[(self.f_v, ti * 4 + cc)], e="act")
        pf = self.psum()
        for kc in range(8):
            k.mm(pf[0:2, :], self.W[:, kc, NCH * 128:NCH * 128 + 2], self.hb[:, kc, :], start=(kc == 0), stop=(kc == 7),
                 reads=[(self.W, kc), (self.hb, kc)], writes=[pf])
        e_ = self.f_row.next()
        k.act(e_[:], pf[0:2, :], AF.Exp, reads=[pf, self.f_nb], writes=[e_], scale=-1.0, bias=self.f_nb[:])
        k.act(e_[:], e_[:], AF.Ln, reads=[e_, ca.one_c], writes=[e_], bias=ca.one_c[0:2, :])
        cs = self.f_row.next()
        k.op("dve", lambda g: g.tensor_tensor_scan(cs[:], ca.ones_f[0:2, :], e_[:], self.f_clast[:, 0:1], ALU.mult, ALU.add),
             reads=[ca.ones_f, e_, self.f_clast], writes=[cs])
        k.cp(self.f_clast[:], cs[:, TT - 1:TT], reads=[cs], writes=[self.f_clast])
        pc = self.psum()
        for cc in range(4):
            k.tr(pc[:, 2 * cc:2 * cc + 2], cs[:, cc * 128:(cc + 1) * 128], ca.ident[0:2, 0:2], reads=[cs, ca.ident], writes=[pc])
        k.cp(self.f_ccol[:, ti * 4:(ti + 1) * 4, :].rearrange("p a b -> p (a b)"), pc[:, 0:8], reads=[pc], writes=[(self.f_ccol, ti)])
        pe = self.psum()
        k.mm(pe[:, 0:8], self.f_sel[:], self.f_ccol[:, ti * 4:(ti + 1) * 4, :].rearrange("p a b -> p (a b)"),
             reads=[self.f_sel, (self.f_ccol, ti)], writes=[pe])
        k.cp(self.f_cend[:].rearrange("p a b -> p (a b)"), pe[:, 0:8], reads=[pe], writes=[self.f_cend])
        po, pd = self.named_ps[0], self.named_ps[1]
        nkc = 4 * (ti + 1)
        yield
        for h in range(2):
            hp = slice(h * 64, (h + 1) * 64)
            B = self.f_B.next()
            k.tt(B[:, 0:nkc, :], self.f_ccol[:, 0:nkc, h:h + 1].to_broadcast([128, nkc, 4]),
                 self.f_cend[:, :, h].unsqueeze(1).to_broadcast([128, nkc, 4]), ALU.subtract,
                 reads=[self.f_ccol, self.f_cend], writes=[B])

            def pv_step(kc, pt, col0):
                k.mm(po[hp, col0:TT], self.f_v[:, kc, hp], pt[:, col0:TT], start=(kc == 0), stop=(kc == nkc - 1),
                     reads=[(self.f_v, kc), pt], writes=[po])
                k.mm(pd[hp, col0:TT], c.ones[:, 0:64], pt[:, col0:TT], start=(kc == 0), stop=(kc == nkc - 1),
                     reads=[c.ones, pt], writes=[pd])

            pend = None
            for kc in range(nkc):
                j = kc - 4 * ti
                col0 = 128 * j if j >= 0 else 0
                ps = self.psum()
                k.mm(ps[:, col0:TT], self.f_kT[hp, kc * 128:(kc + 1) * 128], self.f_q[hp, col0:TT],
                     reads=[(self.f_kT, kc // 4), self.f_q], writes=[ps])
                pt = self.f_pT.next()
                for qbl in range(col0 // 128, 4):
                    qs = slice(qbl * 128, (qbl + 1) * 128)
                    k.act(pt[:, qs], ps[:, qs], AF.Exp, reads=[ps, B], writes=[pt], scale=0.125, bias=B[:, kc, qbl:qbl + 1])
                if j >= 0:
                    qs = slice(j * 128, (j + 1) * 128)
                    k.tt(pt[:, qs], pt[:, qs], ca.ut_b[:], ALU.mult, reads=[pt, ca.ut_b], writes=[pt], e="pool")
                if pend is not None:
                    pv_step(*pend)
                pend = (kc, pt, col0)
                yield
            pv_step(*pend)
            yield
        rd = self.f_rd
        k.op("dve", lambda g: g.reciprocal(rd[:], pd[:, :]), reads=[pd], writes=[rd])
        o = self.outs.next()
        k.tt(o[:], po[:, :], rd[:], ALU.mult, reads=[po, rd], writes=[o])
        self.results["fox"] = o

    def fox_tile(self, ti):
        for _ in self.fox_gen(ti):
            pass
        return self.results["fox"]

    def init_rwkv(self):
        k, L, ca = self.k, self.L, self.ca
        self.TR = 256
        self.NCK = self.TR // 64
        NCK = self.NCK
        self.w_carry = k.sb(L + "wcar", [128, 6], F32)
        k.memset(self.w_carry[:], 0.0, writes=[self.w_carry])
        self.w_AR = k.sb(L + "wAR", [128, NCK, 2, 2, 64], F32)
        self.w_BK = k.sb(L + "wBK", [128, NCK, 2, 2, 64], F32)
        self.w_HAT = k.sb(L + "wHAT", [128, NCK, 3, 2, 64], F32)
        for t in (self.w_AR, self.w_BK, self.w_HAT):
            k.memset(t[:], 0.0, writes=[t])
        self.w_H = k.sb(L + "wH", [128, 128], F32)
        k.memset(self.w_H[:], 0.0, writes=[self.w_H])
        self.w_ms = k.sb(L + "wms", [128, 128], F32)
        k.memset(self.w_ms[:], 1.0, writes=[self.w_ms])
        k.op("pool", lambda e: e.affine_select(out=self.w_ms[:], in_=self.w_ms[:], pattern=[[1, 128]], compare_op=ALU.is_gt,
                                               fill=0.0, base=0, channel_multiplier=-1), reads=[self.w_ms], writes=[self.w_ms])
        self.w_mst = k.sb(L + "wmst", [128, 128], F32)
        k.memset(self.w_mst[:], 1.0, writes=[self.w_mst])
        k.op("pool", lambda e: e.affine_select(out=self.w_mst[:], in_=self.w_mst[:], pattern=[[-1, 128]], compare_op=ALU.is_gt,
                                               fill=0.0, base=0, channel_multiplier=1), reads=[self.w_mst], writes=[self.w_mst])
        self.w_mist = k.sb(L + "wmist", [128, 2, 64], F32)
        for j in range(2):
            k.cp(self.w_mist[0:64, j, :], ca.ut_f[0:64, 0:64], reads=[ca.ut_f], writes=[self.w_mist])
            k.cp(self.w_mist[64:128, j, :], ca.ut_f[64:128, 64:128], reads=[ca.ut_f], writes=[self.w_mist])
        self.w_chm = k.sb(L + "wchm", [128, self.TR], F32)
        k.memset(self.w_chm[:], 1.0, writes=[self.w_chm])
        k.memset(self.w_chm[:].rearrange("p (c n) -> p c n", n=64)[:, :, 0:1], 0.0, writes=[self.w_chm])
        self.w_omka = k.sb(L + "womka", [128, 1], F32)
        k.ts(self.w_omka[:], self.pcol(PV_KA), -1.0, ALU.mult, 1.0, ALU.add, reads=[self.pv], writes=[self.w_omka])
        self.w_gc = k.sb(L + "wgc", [128, NCK], F32)
        self.w_rot = {n: Rot(k, L + "w" + n, [128, 128], F32, 2) for n in ("w1", "u")}
        self.w_out3 = {n: [k.sb(L + "w%s%d" % (n, i), [128, 128], F32) for i in range(3)] for n in ("nak", "nst", "bh", "kh", "vb", "x")}
        self.w_ptb = [[k.sb(L + "wpt%d_%d" % (i, j), [128, 128], F32) for j in range(2)] for i in range(3)]
        self.w_pxb = [[k.sb(L + "wpx%d_%d" % (i, j), [128, 256], F32) for j in range(2)] for i in range(3)]
        self.w_v32 = k.sb(L + "wv32", [32, self.TR], F32)
        self.w_gneps = k.sb(L + "wgne", [128, 1], F32)
        k.memset(self.w_gneps[:], 64e-5, writes=[self.w_gneps])

    def nb(self, i):
        TR = self.TR
        xt = self.cur_xt
        return xt[:, i // 2, (i % 2) * TR:(i % 2 + 1) * TR], (xt, ("nb", i))

    def rwkv_gen(self, ti):
        o = self.outs.next()
        for half in range(self.TT_R):
            yield from self.rwkv_half(ti, half, o)
        self.results["rwkv"] = o

    def rwkv_tile(self, ti):
        for _ in self.rwkv_gen(ti):
            pass
        return self.results["rwkv"]

    TT_R = 2

    def rwkv_half(self, ti, half, o):
        k, c, ca, l = self.k, self.c, self.ca, self.l
        TR, NCK = self.TR, self.NCK
        n0 = half * TR
        n1 = n0 + TR
        g0 = ti * TT + n0
        pv = self.pv
        (R, dR), (Kk, dK), (V, dV), (WA, dWA), (GD, dGD), (VO, dVO), (A, dA), (LW, dLW), (LG, dLG), (KKN, dKKN), (BV, dBV), \
            (RT, dRT), (TA, dTA), (TB, dTB), (TC, dTC), (Y, dY) = [self.nb(i) for i in range(16)]
        car = self.w_carry

        def lerp(X, dX, ch, mu_i, ci):
            ps = self.proj(ch, n0, n1)
            k.cp(X, ps[:, n0:n1], reads=[ps], writes=[dX], e="act")
            k.tt(TA[:, 1:TR], X[:, 0:TR - 1], X[:, 1:TR], ALU.subtract, reads=[dX], writes=[dTA])
            k.tt(TA[:, 0:1], car[:, ci:ci + 1], X[:, 0:1], ALU.subtract, reads=[dX, car], writes=[dTA])
            k.cp(car[:, ci:ci + 1], X[:, TR - 1:TR], reads=[dX], writes=[car])
            k.stt(X, TA, self.pcol(mu_i), X, ALU.mult, ALU.add, reads=[dTA, pv, dX], writes=[dX])

        lerp(WA, dWA, CH_RWA, PV_MU_WA, 3)
        yield
        lerp(Kk, dK, CH_RK, PV_MU_K, 1)
        yield
        lerp(R, dR, CH_RR, PV_MU_R, 0)
        yield
        lerp(V, dV, CH_RV, PV_MU_V, 2)
        yield
        lerp(GD, dGD, CH_RGD, PV_MU_GD, 4)
        yield
        if l > 0:
            lerp(VO, dVO, CH_RVO, PV_MU_VO, 5)
            yield
        WA2, G2 = self.mats["WA2"], self.mats["G2"]
        k.act(TB[0:64, :], WA[0:64, :], AF.Tanh, reads=[dWA], writes=[dTB])
        pz = self.psum()
        k.mm(pz[:, 0:TR], WA2[0:64, :], TB[0:64, :], reads=[WA2, dTB], writes=[pz])
        k.act(LW, pz[:, 0:TR], AF.Sigmoid, reads=[pz, pv], writes=[dLW], bias=self.pcol(PV_W0))
        k.ts(LW, LW, -math.exp(-0.5), ALU.mult, reads=[dLW], writes=[dLW])
        pa = self.psum()
        k.mm(pa[:, 0:TR], WA2[64:128, :], WA[64:128, :], reads=[WA2, dWA], writes=[pa])
        k.act(A, pa[:, 0:TR], AF.Sigmoid, reads=[pa, pv], writes=[dA], bias=self.pcol(PV_A0))
        k.act(GD, GD, AF.Sigmoid, reads=[dGD], writes=[dGD])
        pg = self.psum()
        k.mm(pg[:, 0:TR], G2[:], GD, reads=[G2, dGD], writes=[pg])
        k.cp(GD, pg[:, 0:TR], reads=[pg], writes=[dGD], e="act")
        if l > 0:
            V1, V2 = self.mats["V1"], self.mats["V2"]
            p1 = self.psum()
            k.mm(p1[0:32, 0:TR], V1[:, 0, :], V, start=True, stop=False, reads=[V1, dV], writes=[p1])
            k.mm(p1[0:32, 0:TR], V1[:, 1, :], VO, start=False, stop=True, reads=[V1, dVO], writes=[p1])
            k.cp(self.w_v32[:], p1[0:32, 0:TR], reads=[p1], writes=[self.w_v32], e="act")
            p2 = self.psum()
            k.mm(p2[:, 0:TR], V2[:], self.w_v32[:], reads=[V2, self.w_v32], writes=[p2])
            k.act(TB, p2[:, 0:TR], AF.Sigmoid, reads=[p2, pv], writes=[dTB], bias=self.pcol(PV_V0))
            k.dma("sp", TC, self.vf_in[:, g0:g0 + TR], reads=[self.vf_in], writes=[dTC])
            k.tt(TC, TC, V, ALU.subtract, reads=[dTC, dV], writes=[dTC])
            k.tt(TC, TC, TB, ALU.mult, reads=[dTC, dTB], writes=[dTC])
            k.tt(V, V, TC, ALU.add, reads=[dV, dTC], writes=[dV])
        else:
            k.dma("sp", self.vf_out[:, g0:g0 + TR], V, reads=[dV], writes=[self.vf_out])
        yield
        k.ts(KKN, Kk, self.pcol(PV_KK), ALU.mult, reads=[dK, pv], writes=[dKKN])
        k.tt(TB, KKN, KKN, ALU.mult, reads=[dKKN], writes=[dTB])
        pn = self.psum()
        k.mm(pn[:, 0:TR], ca.blk[:], TB, reads=[ca.blk, dTB], writes=[pn])
        k.act(TB, pn[:, 0:TR], AF.Sqrt, reads=[pn], writes=[dTB])
        k.ts(TB, TB, 1e-12, ALU.max, reads=[dTB], writes=[dTB])
        k.op("dve", lambda g: g.reciprocal(TB, TB), reads=[dTB], writes=[dTB])
        k.tt(KKN, KKN, TB, ALU.mult, reads=[dKKN, dTB], writes=[dKKN])
        k.ts(TB, A, self.pcol(PV_KA), ALU.mult, self.w_omka[:, 0:1], ALU.add, reads=[dA, pv, self.w_omka], writes=[dTB])
        k.tt(Kk, Kk, TB, ALU.mult, reads=[dK, dTB], writes=[dK])
        k.tt(BV, KKN, A, ALU.mult, reads=[dKKN, dA], writes=[dBV])
        yield
        k.op("dve", lambda g: g.tensor_tensor_scan(LG, self.w_chm[:], LW, 0.0, ALU.mult, ALU.add),
             reads=[self.w_chm, dLW], writes=[dLG])
        AR, BK, HAT = self.w_AR, self.w_BK, self.w_HAT
        v3 = lambda ap: ap.rearrange("p (c n) -> p c n", n=64)

        def to_blk(dst, which, src, dsrc, mul=None, dmul=None, op="mult"):
            for h in range(2):
                hp = slice(h * 64, (h + 1) * 64)
                if mul is None:
                    k.cp(dst[hp, :, which, h, :], v3(src[hp, :]), reads=[dsrc], writes=[dst])
                else:
                    k.tt(dst[hp, :, which, h, :], v3(src[hp, :]), v3(mul[hp, :]), ALU.mult, reads=[dsrc, dmul], writes=[dst])

        k.act(TB, LG, AF.Exp, reads=[dLG], writes=[dTB])
        k.tt(RT, R, TB, ALU.mult, reads=[dR, dTB], writes=[dRT])
        to_blk(AR, 1, RT, dRT)
        k.act(self.w_gc[:], v3(LG)[:, :, 63], AF.Exp, reads=[dLG], writes=[self.w_gc])
        yield
        k.tt(TB, LG, LW, ALU.subtract, reads=[dLG, dLW], writes=[dTB])
        k.act(TB, TB, AF.Exp, reads=[dTB], writes=[dTB])
        k.stt(TC, KKN, -1.0, TB, ALU.mult, ALU.mult, reads=[dKKN, dTB], writes=[dTC])
        to_blk(AR, 0, TC, dTC)
        k.act(TB, LG, AF.Exp, reads=[dLG], writes=[dTB], scale=-1.0)
        to_blk(BK, 0, BV, dBV, TB, dTB)
        to_blk(BK, 1, Kk, dK, TB, dTB)
        yield
        k.tt(v3(TB), v3(LG)[:, :, 63:64].to_broadcast([128, NCK, 64]), v3(LG), ALU.subtract, reads=[dLG], writes=[dTB])
        k.act(TB, TB, AF.Exp, reads=[dTB], writes=[dTB])
        to_blk(HAT, 0, BV, dBV, TB, dTB)
        to_blk(HAT, 1, Kk, dK, TB, dTB)
        to_blk(HAT, 2, V, dV)
        py = self.named_ps[2]
        ident = ca.ident
        m2 = lambda ap: ap.rearrange("p a b -> p (a b)")
        H = self.w_H
        P = {}
        if self.f32r:
            F32R = mybir.dt.float32r

            def mmr(out, lhsT, rhs, **kw):
                return k.mm(out, lhsT.bitcast(F32R), rhs.bitcast(F32R), **kw)
        else:
            mmr = k.mm

        def prep(cc):
            slot = cc % 3
            pxs, pts = self.w_pxb[slot], self.w_ptb[slot]
            o3 = {n: self.w_out3[n][slot] for n in self.w_out3}
            cs = slice(cc * 64, (cc + 1) * 64)
            Ab = m2(AR[:, cc, 0])
            Bb, Kb = m2(BK[:, cc, 0]), m2(BK[:, cc, 1])
            pA = self.psum()
            mmr(pA[:, 0:128], Bb, Ab, reads=[BK, AR], writes=[pA])
            mmr(pA[:, 128:256], Kb, Ab, reads=[BK, AR], writes=[pA])
            pC = self.psum()
            mmr(pC[:, 0:128], Ab, Bb, reads=[BK, AR], writes=[pC])
            pS = self.psum()
            mmr(pS[:, 0:64], Bb, RT[:, cs], reads=[BK, dRT], writes=[pS])
            mmr(pS[:, 64:128], Kb, RT[:, cs], reads=[BK, dRT], writes=[pS])
            pi_ = 0
            px = pxs[pi_]
            k.tt(px[:, 0:128], pA[:, 0:128], self.w_ms[:], ALU.mult, reads=[pA, self.w_ms], writes=[px])
            k.tt(px[:, 128:256], px[:, 0:128], ident[:], ALU.add, reads=[px, ident], writes=[px], e="pool")
            pt = pts[pi_]
            k.tt(pt[:], pC[:, 0:128], self.w_mst[:], ALU.mult, reads=[pC, self.w_mst], writes=[pt])
            nak = o3["nak"]
            k.tt(nak[:], pA[:, 128:256], self.w_ms[:], ALU.mult, reads=[pA, self.w_ms], writes=[nak])
            nst = o3["nst"]
            k.tt(nst[:], pS[:, 0:128], m2(self.w_mist[:]), ALU.mult, reads=[pS, self.w_mist], writes=[nst])
            yield
            pP = self.psum()
            mmr(pP[:, 0:128], pt[:], px[:, 0:128], reads=[pt, px], writes=[pP])
            pQ = self.psum()
            mmr(pQ[:, 0:128], px[:, 0:128], pt[:], reads=[pt, px], writes=[pQ])
            pi_ ^= 1
            px2, pt2 = pxs[pi_], pts[pi_]
            k.cp(px2[:, 0:128], pP[:, 0:128], reads=[pP], writes=[px2], e="act")
            k.cp(px2[:, 128:256], px[:, 128:256], reads=[px], writes=[px2], e="pool")
            k.cp(pt2[:], pQ[:, 0:128], reads=[pQ], writes=[pt2], e="act")
            px, pt = px2, pt2
            yield
            X = o3["x"]
            for lvl in range(1, 6):
                last = (lvl == 5)
                pP = self.psum()
                if not last:
                    mmr(pP[:, 0:256], pt[:], px[:, 0:256], reads=[pt, px], writes=[pP])
                    pQ = self.psum()
                    mmr(pQ[:, 0:128], px[:, 0:128], pt[:], reads=[pt, px], writes=[pQ])
                    pi_ ^= 1
                    px2, pt2 = pxs[pi_], pts[pi_]
                    k.tt(px2[:, 128:256], pP[:, 128:256], px[:, 128:256], ALU.add, reads=[pP, px], writes=[px2])
                    k.cp(px2[:, 0:128], pP[:, 0:128], reads=[pP], writes=[px2], e="act")
                    k.cp(pt2[:], pQ[:, 0:128], reads=[pQ], writes=[pt2], e="act")
                    px, pt = px2, pt2
                else:
                    mmr(pP[:, 128:256], pt[:], px[:, 128:256], reads=[pt, px], writes=[pP])
                    k.tt(X[:], pP[:, 128:256], px[:, 128:256], ALU.add, reads=[pP, px], writes=[X])
                yield
            tm = []
            for wi, nm in enumerate(("bh", "kh", "vb")):
                ptr = self.psum()
                k.tr(ptr[:, 0:128], m2(HAT[:, cc, wi]), ident[:], reads=[HAT, ident], writes=[ptr])
                tbuf = o3[nm]
                k.cp(tbuf[:], ptr[:, 0:128], reads=[ptr], writes=[tbuf], e="act")
                tm.append(tbuf)
                yield
            P[cc] = (X, nak, nst) + tuple(tm)

        def chain(cc):
            cs = slice(cc * 64, (cc + 1) * 64)
            Ab = m2(AR[:, cc, 0])
            X, nak, nst, bh, kh, vb = P[cc]
            pW = self.psum()
            mmr(pW[:, 0:128], Ab, H[:], start=True, stop=False, reads=[AR, H], writes=[pW])
            mmr(pW[:, 0:128], nak[:], vb[:], start=False, stop=True, reads=[nak, vb], writes=[pW])
            w1 = self.w_rot["w1"].next()
            k.cp(w1[:], pW[:, 0:128], reads=[pW], writes=[w1], e="dve")
            yield
            pU = self.psum()
            mmr(pU[:, 0:128], X[:], w1[:], reads=[X, w1], writes=[pU])
            u = self.w_rot["u"].next()
            k.cp(u[:], pU[:, 0:128], reads=[pU], writes=[u], e="dve")
            yield
            mmr(py[:, cs], H[:], RT[:, cs], start=True, stop=False, reads=[H, dRT], writes=[py])
            mmr(py[:, cs], u[:], nst[:, 0:64], start=False, stop=False, reads=[u, nst], writes=[py])
            mmr(py[:, cs], vb[:], nst[:, 64:128], start=False, stop=True, reads=[vb, nst], writes=[py])
            pH = self.psum()
            mmr(pH[:, 0:128], bh[:], u[:], start=True, stop=False, reads=[bh, u], writes=[pH])
            mmr(pH[:, 0:128], kh[:], vb[:], start=False, stop=True, reads=[kh, vb], writes=[pH])
            k.stt(H[:], H[:], self.w_gc[:, cc:cc + 1], pH[:, 0:128], ALU.mult, ALU.add, reads=[H, self.w_gc, pH], writes=[H])
            yield

        active = {}

        def start(cc_):
            if cc_ < NCK:
                active[cc_] = prep(cc_)

        def step_preps():
            for cc_ in list(active):
                try:
                    next(active[cc_])
                except StopIteration:
                    del active[cc_]

        start(0)
        start(1)
        while 0 in active:
            step_preps()
            yield
        for cc in range(NCK):
            start(cc + 2)
            while cc in active:
                step_preps()
                yield
            gc = chain(cc)
            done = False
            while not done:
                try:
                    next(gc)
                except StopIteration:
                    done = True
                step_preps()
                yield
        k.cp(Y, py[:, 0:TR], reads=[py], writes=[dY], e="act")
        pm = self.psum()
        k.mm(pm[:, 0:TR], ca.blk[:], Y, reads=[ca.blk, dY], writes=[pm])
        k.stt(Y, pm[:, 0:TR], -1.0 / 64.0, Y, ALU.mult, ALU.add, reads=[pm, dY], writes=[dY])
        k.tt(TB, Y, Y, ALU.mult, reads=[dY], writes=[dTB])
        pvv = self.psum()
        k.mm(pvv[:, 0:TR], ca.blk[:], TB, reads=[ca.blk, dTB], writes=[pvv])
        k.act(TB, pvv[:, 0:TR], AF.Sqrt, reads=[pvv, self.w_gneps], writes=[dTB], bias=self.w_gneps[:], scale=1.0 / 64.0)
        k.op("dve", lambda g: g.reciprocal(TB, TB), reads=[dTB], writes=[dTB])
        k.tt(Y, Y, TB, ALU.mult, reads=[dY, dTB], writes=[dY])
        k.ts(Y, Y, self.pcol(PV_GNW), ALU.mult, self.pcol(PV_GNB), ALU.add, reads=[dY, pv], writes=[dY])
        yield
        k.stt(TB, R, self.pcol(PV_RK), Kk, ALU.mult, ALU.mult, reads=[dR, pv, dK], writes=[dTB])
        pb = self.psum()
        k.mm(pb[:, 0:TR], ca.blk[:], TB, reads=[ca.blk, dTB], writes=[pb])
        k.tt(TB, pb[:, 0:TR], V, ALU.mult, reads=[pb, dV], writes=[dTB])
        k.tt(Y, Y, TB, ALU.add, reads=[dY, dTB], writes=[dY])
        k.tt(o[:, n0:n1], Y, GD, ALU.mult, reads=[dY, dGD], writes=[o])

    def run(self, ntiles):
        k, c = self.k, self.c
        gainA = self.pv
        for ti in range(ntiles):
            t0 = ti * TT
            xt = self.xpool.next()
            self.cur_xt = xt
            if self.x_src is None:
                k.dma("sp", xt[:], self.xT[:, t0:t0 + TT].rearrange("(a p) c -> p a c", p=128), reads=[self.xT], writes=[xt])
            else:
                self.x_src(ti, xt)
            r = rstd_from(c, [(xt[:, kc, :], xt) for kc in range(8)])
            for kc in range(8):
                k.stt(self.hb[:, kc, :], xt[:, kc, :], self.pv[:, PV_GAIN + kc:PV_GAIN + kc + 1], r[:], ALU.mult, ALU.mult,
                      reads=[xt, self.pv, r], writes=[(self.hb, kc)])
            def emit(mi, o):
                if self.out_sink is None:
                    k.dma("sp", self.outT[mi * 128:(mi + 1) * 128, t0:t0 + TT], o[:], reads=[o], writes=[self.outT])
                else:
                    self.out_sink(ti, mi, o)
            for mi, name in ((1, "lru"), (3, "ret")):
                if name in self.do and name not in self.interleave:
                    emit(mi, getattr(self, name + "_tile")(ti))
            gens = []
            if "ret" in self.do and "ret" in self.interleave:
                gens.append((3, "ret", self.ret_gen(ti)))
            if "fox" in self.do:
                gens.append((0, "fox", self.fox_gen(ti)))
            if "rwkv" in self.do:
                gens.append((2, "rwkv", self.rwkv_gen(ti)))
            while gens:
                for item in list(gens):
                    try:
                        next(item[2])
                    except StopIteration:
                        gens.remove(item)
                        emit(item[0], self.results[item[1]])
            if self.tile_done is not None:
                self.tile_done(ti)


RET_THETA = 10000.0

G = 256
FOX_OFF = 0; LRU_OFF = 772; RWKV_OFF = 1284; RET_OFF = 2308
NCH = 17
NPV = 35

def prep_A(d, l, hh):
    w_in = d["w_in"][l]
    own = np.arange(hh * 128, hh * 128 + 128)
    oth = np.arange((1 - hh) * 128, (1 - hh) * 128 + 128)
    swap = np.concatenate([own[h * 64:(h + 1) * 64][np.r_[32:64, 0:32]] for h in range(2)])
    cols = [FOX_OFF + own, FOX_OFF + G + own, FOX_OFF + 2 * G + own,
            LRU_OFF + own, LRU_OFF + G + own,
            RWKV_OFF + own, RWKV_OFF + G + own, RWKV_OFF + 2 * G + own, RWKV_OFF + 3 * G + np.arange(128), RWKV_OFF + 3 * G + 128 + np.arange(128),
            RET_OFF + own, RET_OFF + G + own, RET_OFF + 2 * G + own, RET_OFF + 3 * G + own, RET_OFF + swap, RET_OFF + G + swap,
            RWKV_OFF + 2 * G + oth,
            FOX_OFF + 3 * G + hh * 2 + np.arange(2)]
    cols = np.concatenate(cols)
    Wown = np.ascontiguousarray(w_in[:, cols])
    pv = np.zeros((128, NPV), np.float32)
    c = 0
    def put(v):
        nonlocal c
        pv[:len(v), c] = v; c += 1
    for j in range(4): put(d["lru_conv_w"][l, j, own])
    put(d["lru_conv_b"][l, own]); put(d["lru_ra_b"][l, own]); put(d["lru_ri_b"][l, own]); put(d["lru_lambda"][l, own])
    mu = d["rwkv_mu"][l]
    put(mu[own]); put(mu[G + own]); put(mu[2 * G + own]); put(mu[2 * G + oth]); put(mu[3 * G:3 * G + 128]); put(mu[3 * G + 128:3 * G + 256])
    put(d["rwkv_w0"][l, own]); put(d["rwkv_a0"][l, own]); put(d["rwkv_k_k"][l, own]); put(d["rwkv_k_a"][l, own])
    put(d["rwkv_gn_w"][l, own]); put(d["rwkv_gn_b"][l, own]); put(d["rwkv_r_k"][l].reshape(-1)[own])
    put(d["rwkv_v0"][l - 1, own] if l > 0 else np.zeros(128, np.float32))
    put(d["ret_gn_w"][l, own])
    put(d["fox_f_bias"][l, hh * 2:hh * 2 + 2])
    put((np.arange(128) // 64 + 2 * hh).astype(np.float32)); put(np.full(128, 2 * hh, np.float32)); put(np.full(128, 2 * hh + 1, np.float32))
    assert c == 27
    pv[:, 27:35] = d["norm_mix_pre"][l].reshape(8, 128).T
    def blockdiag(w):
        m = np.zeros((128, 128), np.float32)
        for h in range(2):
            m[h * 64:(h + 1) * 64, h * 64:(h + 1) * 64] = w[2 * hh + h]
        return m
    mats = {"RA": blockdiag(d["lru_ra_w"][l]), "RI": blockdiag(d["lru_ri_w"][l]),
            "WA2": np.ascontiguousarray(np.concatenate([d["rwkv_w2"][l][:, own], d["rwkv_a2"][l][:, own]], axis=0)),
            "G2": np.ascontiguousarray(d["rwkv_g2"][l][:, own])}
    if l > 0:
        v1 = d["rwkv_v1"][l - 1]
        mats["V1"] = np.ascontiguousarray(np.stack([v1[own], v1[oth]], axis=1))
        mats["V2"] = np.ascontiguousarray(d["rwkv_v2"][l - 1][:, own])
    return Wown, pv, mats

MATSH = {"RA": [128, 128], "RI": [128, 128], "WA2": [128, 128], "G2": [128, 128], "V1": [128, 2, 32], "V2": [32, 128]}
NTOK_B = 2048
PAIRS = [[0, 1], [2, 3], [4, 5], [6, 7]]
DEPTH = 2


def _build_fused():
    nc = bass.Bass("TRN2", target_bir_lowering=False)
    D = lambda name, shape, kind="ExternalInput": nc.dram_tensor(name, shape, F32, kind=kind).ap()
    xT = D("xT", [1024, 4096])
    xown = D("xown", [1024, NTOK_B])
    memT = D("memT", [1024, 256])
    selv = D("selv", [128, 2])
    A_in, B_in = [], []
    for l in range(DEPTH):
        names = ["RA", "RI", "WA2", "G2"] + (["V1", "V2"] if l > 0 else [])
        A_in.append({"Wown": D("Wown%d" % l, [1024, NCOLS_A]), "pv": D("pv%d" % l, [128, NPV]),
                     "mats": {n: (D("%s_%d" % (n, l), MATSH[n]), MATSH[n]) for n in names}})
        W = {n: D("%s_%d" % (n, l), [1024, 1024]) for n in ("w_out", "wq", "wk", "wv", "wo")}
        W["w1"] = D("w1_%d" % l, [1024, 4096])
        W["w2"] = D("w2_%d" % l, [4096, 1024])
        B_in.append({"W": W, "gain": D("gain%d" % l, [128, 6, 8])})
    xo = D("xo", [1024, NTOK_B], "ExternalOutput")
    with contextlib.ExitStack() as st:
        k = K(nc, st)
        c = Ctx(k)
        ca = ConstA(k, c)
        sel = k.sb("selv_sb", [128, 2], F32)
        k.dma("sp", sel[:], selv, writes=[sel])
        xo_t = T(xo, "xo")
        xT_t, xown_t, memT_t = T(xT, "xT"), T(xown, "xown"), T(memT, "memT")
        vf_t = k.dram("vf_scr", [128, 4096])
        WB = [None] * DEPTH
        XO = k.dram("xo_scr", [1024, NTOK_B])
        XG = None
        for l in range(DEPTH):
            CA = [k.dram("ca%d_%d" % (l, i), [512, TT]) for i in range(8)]
            CG = [k.dram("cg%d_%d" % (l, i), [1024, TT]) for i in range(8)]
            with k.scope():
                a = A_in[l]
                sa = StageA(k, c, ca, l, xT_t, None, a["Wown"], a["pv"], a["mats"],
                            vfirst_in=vf_t if l > 0 else None, vfirst_out=vf_t if l == 0 else None)
                wb = {}
                for n, ap in B_in[l]["W"].items():
                    R_, C_ = ap.shape
                    t = k.dram("wb_%s_%d" % (n, l), [R_, C_], BF16)
                    src, dst = ap, t.ap
                    if C_ > 1024:
                        src = src.rearrange("r (a c) -> (r a) c", c=1024)
                        dst = dst.rearrange("r (a c) -> (r a) c", c=1024)
                    for r0 in range(0, src.shape[0], 512):
                        k.dma("pool", dst[r0:r0 + 512, :], src[r0:r0 + 512, :], writes=[(t, r0)])
                    wb[n] = t
                WB[l] = wb
                if l > 0:
                    XGl = XG

                    def x_src(ti, xt, XGl=XGl):
                        g = XGl[ti % 4]
                        r = ti // 4
                        k.dma("sp", xt[:], g[r * 1024:(r + 1) * 1024, :].rearrange("(a p) c -> p a c", p=128), reads=[g], writes=[xt])
                    sa.x_src = x_src

                def out_sink(ti, mi, o, CA=CA):
                    k.dma("sp", CA[ti][mi * 128:(mi + 1) * 128, :], o[:], reads=[o], writes=[(CA[ti], mi)])

                def tile_done(ti, CA=CA, CG=CG):
                    k.cc_allgather(CA[ti], CG[ti], PAIRS)
                sa.out_sink, sa.tile_done = out_sink, tile_done
                sa.run(8)
            with k.scope():
                b = B_in[l]
                ws = WStream(k, nbuf=4)
                gain = k.sb("gain_sb%d" % l, [128, 6, 8], F32)
                k.dma("sp", gain[:], b["gain"], writes=[gain])
                ctb = k.sb("ctb_%d" % l, [128, 8, TT], BF16)
                last = (l == DEPTH - 1)
                XA = [k.dram("xa%d_%d" % (l, i), [1024, TT]) for i in range(4)] if not last else None
                XGn = [k.dram("xg%d_%d" % (l, i), [2048, TT]) for i in range(4)] if not last else None
                x_res = xown_t if l == 0 else XO

                def cat_load(j, ct, CG=CG, ctb=ctb):
                    k.dma("pool", ct[:], CG[j][:, :].rearrange("(a p) c -> p a c", p=128), reads=[CG[j]], writes=[ct])
                    k.dma("pool", ctb[:], CG[4 + j][:, :].rearrange("(a p) c -> p a c", p=128), reads=[CG[4 + j]], writes=[ctb])
                    fl = lambda t: t[:].rearrange("p a c -> p (a c)")
                    k.ts(fl(ct), fl(ct), sel[:, 1:2], ALU.mult, reads=[ct, sel], writes=[ct])
                    k.stt(fl(ct), fl(ctb), sel[:, 0:1], fl(ct), ALU.mult, ALU.add, reads=[ctb, sel, ct], writes=[ct])

                def x_in(j, xt, x_res=x_res):
                    k.dma("pool", xt[:], x_res[:, j * TT:(j + 1) * TT].rearrange("(a p) c -> p a c", p=128), reads=[(x_res, j)], writes=[xt])

                def x_out(j, xt, last=last, XA=XA, XGn=XGn):
                    if last:
                        k.dma("pool", xo_t[:, j * TT:(j + 1) * TT].rearrange("(a p) c -> p a c", p=128), xt[:], reads=[xt], writes=[(xo_t, j)])
                    else:
                        k.dma("pool", XO[:, j * TT:(j + 1) * TT].rearrange("(a p) c -> p a c", p=128), xt[:], reads=[xt], writes=[(XO, j)])
                        k.dma("pool", XA[j][:, :].rearrange("(a p) c -> p a c", p=128), xt[:], reads=[xt], writes=[XA[j]])
                        k.cc_allgather(XA[j], XGn[j], PAIRS)

                stage_b(k, c, ws, l, None, None, None, memT_t, WB[l], gain, NTOK_B, cat_load=cat_load, x_in=x_in, x_out=x_out)
                XG = XGn
        k.finish("sp", [xo_t])
        k.finish("pool", [xo_t])
    return nc


def prep_B_gain(d, l):
    gl = lambda name: np.ascontiguousarray(d[name][l].reshape(8, 128).T)
    return np.ascontiguousarray(np.stack([gl(n) for n in ("norm_mix_post", "norm_xa_pre", "norm_xa_post", "norm_mem", "norm_mlp_pre", "norm_mlp_post")], axis=1))


def kernel(**inputs):
    d = {k_: np.asarray(v, dtype=np.float32) for k_, v in inputs.items()}
    B = 4
    nc = _build_fused()
    perm = np.concatenate([np.arange(mi * 256 + r * 128, mi * 256 + r * 128 + 128) for r in range(2) for mi in range(4)])
    shared = {}
    for l in range(DEPTH):
        shared["w_out_%d" % l] = np.ascontiguousarray(d["w_out"][l][perm])
        shared["wq_%d" % l] = d["xa_wq"][l]; shared["wk_%d" % l] = d["xa_wk"][l]
        shared["wv_%d" % l] = d["xa_wv"][l]; shared["wo_%d" % l] = d["xa_wo"][l]
        shared["w1_%d" % l] = d["mlp_w1"][l]; shared["w2_%d" % l] = d["mlp_w2"][l]
        shared["gain%d" % l] = prep_B_gain(d, l)
    prepA = {(l, hh): prep_A(d, l, hh) for l in range(DEPTH) for hh in range(2)}
    ins = []
    for cid in range(8):
        b, r = cid // 2, cid % 2
        xTb = np.ascontiguousarray(d["x"][b].T)
        m = dict(shared)
        m["xT"] = xTb
        m["xown"] = np.ascontiguousarray(xTb[:, r * NTOK_B:(r + 1) * NTOK_B])
        m["memT"] = np.ascontiguousarray(d["mem"][b].T)
        m["selv"] = np.ascontiguousarray(np.tile(np.array([[float(r), 1.0 - float(r)]], np.float32), (128, 1)))
        for l in range(DEPTH):
            Wown, pv, mats = prepA[(l, r)]
            m["Wown%d" % l] = Wown
            m["pv%d" % l] = pv
            for n, v in mats.items():
                m["%s_%d" % (n, l)] = v
        ins.append(m)
    res = run_bass_kernel_spmd(nc, ins, core_ids=list(range(8)))
    out = np.empty((B, 4096, 1024), np.float32)
    for cid in range(8):
        b, r = cid // 2, cid % 2
        out[b, r * NTOK_B:(r + 1) * NTOK_B, :] = res.results[cid]["xo"].T
    return out
```
